# Optimizing a Trainium2 kernel written in Bass

```python
import jax, jax.numpy as jnp
from jax import lax
import numpy as np

D_MODEL = 1024
BATCH = 2
SEQ = 8192
DEPTH = 2

N_MIXERS = 2
N_RWKV_LAYERS = (DEPTH + 1) // 2
N_ATTN_LAYERS = DEPTH // 2

RWKV_HEAD_SIZE = 64
RWKV_HEADS = D_MODEL // RWKV_HEAD_SIZE
DECAY_LORA = max(32, int(round(1.8 * D_MODEL ** 0.5 / 32)) * 32)
AAA_LORA = max(32, int(round(1.8 * D_MODEL ** 0.5 / 32)) * 32)
GATE_LORA = max(32, int(round(0.6 * D_MODEL ** 0.8 / 32)) * 32)
N_SHIFT_MIX = 6
GN_EPS = 64e-5

HEAD_DIM = 64
N_HEADS = D_MODEL // HEAD_DIM
N_KV_HEADS = max(1, N_HEADS // 8)
GROUP = N_HEADS // N_KV_HEADS
WINDOW = 128
BLOCK = WINDOW
ROPE_THETA = 10000.0
QKV_DIM = (N_HEADS + 2 * N_KV_HEADS) * HEAD_DIM

D_FF = 4 * D_MODEL
RMS_EPS = 1e-5

kernel_name = "hybrid_rwkv7_swa_sink_sqrelu"

F32 = jnp.float32


def rms_norm(x, g):
    xf = x.astype(F32)
    y = xf * lax.rsqrt(jnp.mean(xf * xf, axis=-1, keepdims=True) + RMS_EPS)
    return (y * g.astype(F32)).astype(x.dtype)


def wkv7_scan(r, w, k, v, a, b):
    bsz, _, nh, n = r.shape
    xs = tuple(jnp.moveaxis(t, 1, 0) for t in (r, w, k, v, a, b))

    def step(state, inp):
        r_t, w_t, k_t, v_t, a_t, b_t = inp
        sa = jnp.einsum('bhvk,bhk->bhv', state, a_t)
        state = (state * w_t[:, :, None, :]
                 + sa[..., None] * b_t[:, :, None, :]
                 + v_t[..., None] * k_t[:, :, None, :])
        y_t = jnp.einsum('bhvk,bhk->bhv', state, r_t)
        return state, y_t

    s0 = jnp.zeros((bsz, nh, n, n), F32)
    _, y = lax.scan(step, s0, xs)
    return jnp.moveaxis(y, 0, 1)


def rwkv7_time_mix(h, mu, w_r, w_k, w_v, w_o, w0, w1, w2, a0, a1, a2, g1, g2,
                   k_k, k_a, r_k, ln_w, ln_b):
    b, s, d = h.shape
    h_prev = jnp.pad(h[:, :-1], ((0, 0), (1, 0), (0, 0)))
    dx = h_prev - h
    xr = h + dx * mu[0]
    xw = h + dx * mu[1]
    xk = h + dx * mu[2]
    xv = h + dx * mu[3]
    xa = h + dx * mu[4]
    xg = h + dx * mu[5]

    r = xr @ w_r
    k = xk @ w_k
    v = xv @ w_v
    w_log = -jax.nn.softplus(-(w0 + jnp.tanh(xw @ w1) @ w2)) - 0.5
    decay = jnp.exp(-jnp.exp(w_log.astype(F32)))
    a = jax.nn.sigmoid(a0 + (xa @ a1) @ a2)
    g = jax.nn.sigmoid(xg @ g1) @ g2

    def heads(t):
        return t.reshape(b, s, RWKV_HEADS, RWKV_HEAD_SIZE).astype(F32)

    kk = heads(k * k_k)
    kk = kk * lax.rsqrt(jnp.maximum(jnp.sum(kk * kk, axis=-1, keepdims=True), 1e-24))
    k = k * (1 + (a - 1) * k_a)
    rh, kh, vh, ah, wh = heads(r), heads(k), heads(v), heads(a), heads(decay)

    y = wkv7_scan(rh, wh, kh, vh, -kk, kk * ah)
    mean = jnp.mean(y, axis=-1, keepdims=True)
    var = jnp.mean(jnp.square(y - mean), axis=-1, keepdims=True)
    yn = ((y - mean) * lax.rsqrt(var + GN_EPS)).reshape(b, s, d)
    yn = yn * ln_w.astype(F32) + ln_b.astype(F32)
    bonus = (jnp.sum(rh * kh * r_k.astype(F32), axis=-1, keepdims=True) * vh).reshape(b, s, d)
    return ((yn + bonus).astype(h.dtype) * g) @ w_o


def rope(x, cos, sin):
    half = x.shape[-1] // 2
    shape = cos.shape[:2] + (1,) * (x.ndim - 3) + (half,)
    c = cos.reshape(shape).astype(x.dtype)
    sn = sin.reshape(shape).astype(x.dtype)
    x1, x2 = x[..., :half], x[..., half:]
    return jnp.concatenate([x1 * c - x2 * sn, x2 * c + x1 * sn], axis=-1)


def sliding_window_sink_attention(q, k, v, sinks):
    b, s = q.shape[:2]
    nb = s // BLOCK
    qb = q.reshape(b, nb, BLOCK, N_KV_HEADS, GROUP, HEAD_DIM)
    kb = k.reshape(b, nb, BLOCK, N_KV_HEADS, HEAD_DIM)
    vb = v.reshape(b, nb, BLOCK, N_KV_HEADS, HEAD_DIM)
    zero = jnp.zeros_like(kb[:, :1])
    k2 = jnp.concatenate([jnp.concatenate([zero, kb[:, :-1]], axis=1), kb], axis=2)
    v2 = jnp.concatenate([jnp.concatenate([zero, vb[:, :-1]], axis=1), vb], axis=2)

    scale = HEAD_DIM ** -0.5
    scores = jnp.einsum('bnqhgd,bnkhd->bnhgqk', qb, k2, preferred_element_type=F32) * scale
    q_pos = jnp.arange(BLOCK)[:, None] + BLOCK
    k_pos = jnp.arange(2 * BLOCK)[None, :]
    rel = q_pos - k_pos
    band = (rel >= 0) & (rel < WINDOW)
    has_prev = (jnp.arange(nb) > 0)[:, None, None]
    valid = band[None] & (has_prev | (k_pos >= BLOCK)[None])
    scores = jnp.where(valid[None, :, None, None], scores, -jnp.inf)

    sink = sinks.astype(F32).reshape(N_KV_HEADS, GROUP)[None, None, :, :, None, None]
    m = jnp.maximum(jnp.max(scores, axis=-1, keepdims=True), sink)
    p = jnp.exp(scores - m)
    probs = p / (jnp.sum(p, axis=-1, keepdims=True) + jnp.exp(sink - m))
    out = jnp.einsum('bnhgqk,bnkhd->bnqhgd', probs.astype(v.dtype), v2)
    return out.reshape(b, s, N_HEADS * HEAD_DIM)


def swa_attention(h, cos, sin, w_qkv, b_qkv, sinks, w_o, b_o):
    b, s, _ = h.shape
    qkv = h @ w_qkv + b_qkv
    nq, nkv = N_HEADS * HEAD_DIM, N_KV_HEADS * HEAD_DIM
    q = qkv[..., :nq].reshape(b, s, N_KV_HEADS, GROUP, HEAD_DIM)
    k = qkv[..., nq:nq + nkv].reshape(b, s, N_KV_HEADS, HEAD_DIM)
    v = qkv[..., nq + nkv:].reshape(b, s, N_KV_HEADS, HEAD_DIM)
    q = rope(q, cos, sin)
    k = rope(k, cos, sin)
    o = sliding_window_sink_attention(q, k, v, sinks)
    return o @ w_o + b_o


def sqrelu_mlp(h, w_in, w_out):
    return jnp.square(jax.nn.relu(h @ w_in)) @ w_out


def setup_inputs(seed: int = 0) -> dict:
    key = jax.random.key(seed)
    ks = iter(jax.random.split(key, 40))
    nr, na, d = N_RWKV_LAYERS, N_ATTN_LAYERS, D_MODEL

    def nrm(shape, scale):
        return jax.random.normal(next(ks), shape, F32) * scale

    x = jax.random.normal(next(ks), (BATCH, SEQ, d), F32)
    offsets = jax.random.randint(next(ks), (BATCH, 1), 0, 1024, dtype=jnp.int32)
    positions = offsets + jnp.arange(SEQ, dtype=jnp.int32)[None, :]

    return {
        "x": x,
        "positions": positions,
        "norm_mix_g": 1.0 + nrm((DEPTH, d), 0.05),
        "norm_mlp_g": 1.0 + nrm((DEPTH, d), 0.05),
        "norm_final_g": 1.0 + nrm((d,), 0.05),
        "rwkv_mu": jax.random.uniform(next(ks), (nr, N_SHIFT_MIX, d), F32),
        "rwkv_w_r": nrm((nr, d, d), d ** -0.5),
        "rwkv_w_k": nrm((nr, d, d), d ** -0.5),
        "rwkv_w_v": nrm((nr, d, d), d ** -0.5),
        "rwkv_w_o": nrm((nr, d, d), d ** -0.5),
        "rwkv_w0": jax.random.uniform(next(ks), (nr, d), F32, -6.5, -1.5),
        "rwkv_w1": nrm((nr, d, DECAY_LORA), d ** -0.5),
        "rwkv_w2": nrm((nr, DECAY_LORA, d), 0.3 * DECAY_LORA ** -0.5),
        "rwkv_a0": nrm((nr, d), 0.1),
        "rwkv_a1": nrm((nr, d, AAA_LORA), d ** -0.5),
        "rwkv_a2": nrm((nr, AAA_LORA, d), 0.3 * AAA_LORA ** -0.5),
        "rwkv_g1": nrm((nr, d, GATE_LORA), d ** -0.5),
        "rwkv_g2": nrm((nr, GATE_LORA, d), GATE_LORA ** -0.5),
        "rwkv_k_k": 0.85 + nrm((nr, d), 0.05),
        "rwkv_k_a": 1.0 + nrm((nr, d), 0.05),
        "rwkv_r_k": -0.04 + nrm((nr, RWKV_HEADS, RWKV_HEAD_SIZE), 0.1),
        "rwkv_ln_w": 1.0 + nrm((nr, d), 0.05),
        "rwkv_ln_b": nrm((nr, d), 0.02),
        "attn_w_qkv": nrm((na, d, QKV_DIM), d ** -0.5),
        "attn_b_qkv": nrm((na, QKV_DIM), 0.02),
        "attn_sinks": nrm((na, N_HEADS), 0.5),
        "attn_w_o": nrm((na, N_HEADS * HEAD_DIM, d), (N_HEADS * HEAD_DIM) ** -0.5),
        "attn_b_o": nrm((na, d), 0.02),
        "mlp_w_in": nrm((DEPTH, d, D_FF), d ** -0.5),
        "mlp_w_out": nrm((DEPTH, D_FF, d), D_FF ** -0.5),
    }


def reference(x, positions, norm_mix_g, norm_mlp_g, norm_final_g,
              rwkv_mu, rwkv_w_r, rwkv_w_k, rwkv_w_v, rwkv_w_o, rwkv_w0, rwkv_w1, rwkv_w2,
              rwkv_a0, rwkv_a1, rwkv_a2, rwkv_g1, rwkv_g2, rwkv_k_k, rwkv_k_a, rwkv_r_k,
              rwkv_ln_w, rwkv_ln_b,
              attn_w_qkv, attn_b_qkv, attn_sinks, attn_w_o, attn_b_o,
              mlp_w_in, mlp_w_out):
    inv_freq = ROPE_THETA ** (-jnp.arange(0, HEAD_DIM, 2, dtype=F32) / HEAD_DIM)
    angles = positions.astype(F32)[..., None] * inv_freq
    cos, sin = jnp.cos(angles), jnp.sin(angles)

    for i in range(DEPTH):
        h = rms_norm(x, norm_mix_g[i])
        j = i // N_MIXERS
        if i % N_MIXERS == 0:
            mix = rwkv7_time_mix(h, rwkv_mu[j], rwkv_w_r[j], rwkv_w_k[j], rwkv_w_v[j], rwkv_w_o[j],
                                 rwkv_w0[j], rwkv_w1[j], rwkv_w2[j], rwkv_a0[j], rwkv_a1[j], rwkv_a2[j],
                                 rwkv_g1[j], rwkv_g2[j], rwkv_k_k[j], rwkv_k_a[j], rwkv_r_k[j],
                                 rwkv_ln_w[j], rwkv_ln_b[j])
        else:
            mix = swa_attention(h, cos, sin, attn_w_qkv[j], attn_b_qkv[j], attn_sinks[j],
                                attn_w_o[j], attn_b_o[j])
        x = x + mix
        x = x + sqrelu_mlp(rms_norm(x, norm_mlp_g[i]), mlp_w_in[i], mlp_w_out[i])
    return rms_norm(x, norm_final_g)
```

```python
import numpy as np
import ml_dtypes
from contextlib import ExitStack
import concourse.bass as bass
import concourse.mybir as mybir
from concourse.bass_utils import run_bass_kernel_spmd

F32 = mybir.dt.float32
BF16 = mybir.dt.bfloat16
I32 = mybir.dt.int32
AF = mybir.ActivationFunctionType
ALU = mybir.AluOpType

EPOCH = 4000
DMA_ROT = 8
NEG_C = -0.6065306597126334


class KB:
    ENGS = ("tensor", "vector", "scalar", "gpsimd", "sync")

    def __init__(self, nc, stack):
        self.nc = nc
        self.stack = stack
        self.streams = {e: [] for e in self.ENGS}
        self.count = {e: 0 for e in self.ENGS}
        self.sems = {e: [] for e in self.ENGS}
        self.dma_count = {e: 0 for e in self.ENGS}
        self.dma_sems = {e: [] for e in self.ENGS}
        self.waited = {e: {} for e in self.ENGS}
        self.last_w = {}
        self.readers = {}

    def _newsem(self, name):
        return self.stack.enter_context(self.nc.semaphore(name))

    def _compute_token(self, e):
        n = self.count[e]
        ep, idx = divmod(n, EPOCH)
        while len(self.sems[e]) <= ep:
            self.sems[e].append(self._newsem(f"s_{e}_{len(self.sems[e])}"))
        self.count[e] = n + 1
        return (self.sems[e][ep], idx + 1, 1, e)

    def _dma_token(self, e):
        n = self.dma_count[e]
        if not self.dma_sems[e]:
            self.dma_sems[e] = [self._newsem(f"d_{e}_{i}") for i in range(DMA_ROT)]
        self.dma_count[e] = n + 1
        return (self.dma_sems[e][n % DMA_ROT], 16 * (n // DMA_ROT + 1), 16, "dma_" + e)

    def op(self, e, fn, reads=(), writes=(), dma=False):
        deps = []
        for k in reads:
            t = self.last_w.get(k)
            if t is not None:
                deps.append((t, True))
        for k in writes:
            t = self.last_w.get(k)
            if t is not None:
                deps.append((t, False))
            for t in self.readers.get(k, ()):
                deps.append((t, False))
        wd = self.waited[e]
        ww = {}
        for (sem, val, _inc, src), is_raw in deps:
            if src == e and not dma and e == "tensor":
                continue
            sid = id(sem)
            if wd.get(sid, 0) >= val:
                continue
            if sid not in ww or ww[sid][1] < val:
                ww[sid] = (sem, val)
        for sid, (sem, val) in ww.items():
            wd[sid] = val
        if dma:
            n = self.dma_count[e]
            if n >= DMA_ROT:
                sem = self.dma_sems[e][n % DMA_ROT]
                val = 16 * (n // DMA_ROT)
                if wd.get(id(sem), 0) < val:
                    wd[id(sem)] = val
                    ww[id(sem)] = (sem, val)
        tok = self._dma_token(e) if dma else self._compute_token(e)
        self.streams[e].append((list(ww.values()), fn, tok))
        for k in reads:
            self.readers.setdefault(k, []).append(tok)
        for k in writes:
            self.last_w[k] = tok
            self.readers[k] = []
        return tok

    def wait_tokens(self, e, toks):
        wd = self.waited[e]
        waits = []
        for (sem, val, _i, _s) in toks:
            if wd.get(id(sem), 0) >= val:
                continue
            wd[id(sem)] = val
            waits.append((sem, val))
        self.streams[e].append((waits, None, None))

    def emit(self):
        nc = self.nc
        with nc.Block() as block:
            def mk(e):
                def body(eng):
                    for waits, fn, tok in self.streams[e]:
                        for sem, val in waits:
                            eng.wait_ge(sem, val)
                        if fn is not None:
                            ins = fn(eng)
                            ins.then_inc(tok[0], tok[2])
                return body
            for e in self.ENGS:
                if self.streams[e]:
                    getattr(block, e)(mk(e))


class V:
    __slots__ = ("ap", "keys")

    def __init__(self, ap, keys):
        self.ap = ap
        self.keys = keys


class Buf:
    def __init__(self, t, name):
        self.t = t
        self.name = name

    def v(self, idx=None, sub=None):
        ap = self.t[idx] if idx is not None else self.t[:]
        return V(ap, [(self.name, sub)])

    def w(self, ap, sub=None):
        return V(ap, [(self.name, sub)])


class G:
    def __init__(self, nc, st):
        self.nc = nc
        self.st = st
        self.kb = KB(nc, st)
        self.banks = [Buf(st.enter_context(nc.psum_tensor(f"psb{i}", [128, 512], F32)), f"ps{i}") for i in range(8)]
        self.bank_i = 0
        self.out_tokens = []

    def sb(self, name, shape, dt):
        return Buf(self.st.enter_context(self.nc.sbuf_tensor("sb_" + name, shape, dt)), name)

    def bank(self):
        b = self.banks[self.bank_i % 8]
        self.bank_i += 1
        return b

    @staticmethod
    def _k(vs):
        ks = []
        for v in vs:
            if isinstance(v, V):
                ks.extend(v.keys)
        return ks

    def mm(self, out, lhsT, rhs, start=True, stop=True):
        return self.kb.op("tensor", lambda e: e.matmul(out.ap, lhsT=lhsT.ap, rhs=rhs.ap, start=start, stop=stop),
                          reads=self._k([lhsT, rhs]), writes=out.keys)

    def tr(self, out, in_, ident):
        return self.kb.op("tensor", lambda e: e.transpose(out=out.ap, in_=in_.ap, identity=ident.ap),
                          reads=self._k([in_, ident]), writes=out.keys)

    def act(self, out, in_, func, bias=None, scale=1.0, accum=None, eng="scalar"):
        kw = {}
        if bias is not None:
            kw["bias"] = bias.ap if isinstance(bias, V) else bias
        if accum is not None:
            kw["accum_out"] = accum.ap
        sc = scale.ap if isinstance(scale, V) else scale
        return self.kb.op("scalar", lambda e: e.activation(out=out.ap, in_=in_.ap, func=func, scale=sc, **kw),
                          reads=self._k([in_, bias, scale]), writes=self._k([out, accum]))

    def tt(self, eng, out, a, b, op):
        return self.kb.op(eng, lambda e: e.tensor_tensor(out=out.ap, in0=a.ap, in1=b.ap, op=op),
                          reads=self._k([a, b]), writes=out.keys)

    def ts(self, eng, out, a, s1, op0, s2=None, op1=None):
        s1a = s1.ap if isinstance(s1, V) else s1
        s2a = s2.ap if isinstance(s2, V) else s2
        if op1 is None:
            fn = lambda e: e.tensor_scalar(out=out.ap, in0=a.ap, scalar1=s1a, scalar2=None, op0=op0)
        else:
            fn = lambda e: e.tensor_scalar(out=out.ap, in0=a.ap, scalar1=s1a, scalar2=s2a, op0=op0, op1=op1)
        return self.kb.op(eng, fn, reads=self._k([a, s1, s2]), writes=out.keys)

    def stt(self, eng, out, in0, scalar, in1, op0, op1):
        sa = scalar.ap if isinstance(scalar, V) else scalar
        return self.kb.op(eng, lambda e: e.scalar_tensor_tensor(out=out.ap, in0=in0.ap, scalar=sa, in1=in1.ap, op0=op0, op1=op1),
                          reads=self._k([in0, scalar, in1]), writes=out.keys)

    def copy(self, eng, out, in_):
        if eng == "scalar":
            return self.act(out, in_, AF.Copy)
        return self.kb.op(eng, lambda e: e.tensor_copy(out=out.ap, in_=in_.ap), reads=in_.keys, writes=out.keys)

    def memset(self, eng, out, val):
        return self.kb.op(eng, lambda e: e.memset(out.ap, val), writes=out.keys)

    def recip(self, out, in_):
        return self.kb.op("vector", lambda e: e.reciprocal(out=out.ap, in_=in_.ap), reads=in_.keys, writes=out.keys)

    def scan(self, out, d0, d1, init, op0, op1):
        return self.kb.op("vector", lambda e: e.tensor_tensor_scan(out=out.ap, data0=d0.ap, data1=d1.ap, initial=init, op0=op0, op1=op1),
                          reads=self._k([d0, d1]), writes=out.keys)

    def aselect(self, out, in_, pattern, cmp, fill, base, cm):
        return self.kb.op("gpsimd", lambda e: e.affine_select(out=out.ap, in_=in_.ap, pattern=pattern, compare_op=cmp,
                                                               fill=fill, base=base, channel_multiplier=cm),
                          reads=in_.keys, writes=out.keys)

    def dma_in(self, eng, out, in_ap, **kw):
        return self.kb.op(eng, lambda e: e.dma_start(out=out.ap, in_=in_ap, **kw), writes=out.keys, dma=True)

    def dma_out(self, eng, out_ap, in_, final=True):
        t = self.kb.op(eng, lambda e: e.dma_start(out=out_ap, in_=in_.ap), reads=in_.keys, dma=True)
        if final:
            self.out_tokens.append(t)
        return t

    def finish(self):
        self.kb.wait_tokens("sync", self.out_tokens)
        self.kb.emit()


S_LEN = 8192
D = 1024
TB = 512
NCH = TB // 128
GN_EPS = 64e-5
RMS_EPS = 1e-5
PROJ = [("r", 0, 256, 0), ("k", 2, 256, 256), ("v", 3, 256, 512), ("w1", 1, 64, 768), ("a1", 4, 64, 832), ("g1", 5, 160, 896)]
NCOL = 1056


BACK_W = 2


def build_phase1(n_tok=S_LEN, stop=None, stopargs=()):
    nc = bass.Bass("TRN2", target_bir_lowering=False)
    dr = lambda n, s, d=F32: nc.dram_tensor(n, s, d, kind="ExternalInput").ap()
    x_d = dr("x", [n_tok, D])
    gm_d = dr("gm", [128, 7, 8])
    wcat_d = dr("wcat", [D, NCOL])
    w2_d = dr("w2", [64, 256])
    a2_d = dr("a2", [64, 256])
    g2_d = dr("g2", [160, 256])
    vec_d = dr("vec", [128, 7, 2])
    yg_d = nc.dram_tensor("yg", [256, n_tok], BF16, kind="ExternalOutput").ap()
    dbg_d = nc.dram_tensor("dbg", [128, 2560], F32, kind="ExternalOutput").ap() if stop else None
    nblk = n_tok // TB

    class Stop(Exception):
        pass

    with ExitStack() as st:
        g = G(nc, st)
        sb = g.sb
        dbgt = sb("dbgt", [128, 2560], F32) if stop else None
        dbg_off = [0]

        def dump(v, n):
            o = dbg_off[0]
            p = v.ap.shape[0]
            g.copy("vector", dbgt.w(dbgt.t[0:p, o:o + n]), v)
            dbg_off[0] = o + n

        def chk(name):
            if stop == name:
                raise Stop()
        def emit_all():
            identf = sb("identf", [128, 128], F32)
            ident = sb("ident", [128, 128], BF16)
            ident4 = sb("ident4", [128, 4, 128], BF16)
            onesblk = sb("onesblk", [128, 128], F32)
            m_su = sb("m_su", [128, 4, 128], BF16)
            m_u = sb("m_u", [128, 4, 128], BF16)
            m_sl = sb("m_sl", [128, 4, 128], BF16)
            rmask = sb("rmask", [128, TB], F32)
            g.memset("gpsimd", identf.v(), 0.0)
            g.aselect(identf.v(), identf.v(), [[-1, 128]], ALU.not_equal, 1.0, 0, 1)
            g.copy("vector", ident.v(), identf.v())
            for h in range(4):
                g.copy("vector", ident4.v(np.s_[:, h, :]), identf.v())
            g.memset("gpsimd", onesblk.v(), 0.0)
            g.memset("gpsimd", onesblk.v(np.s_[0:64, 0:64]), 1.0)
            g.memset("gpsimd", onesblk.v(np.s_[64:128, 64:128]), 1.0)
            for m, cm, pat, cmp in ((m_su, -1, 1, ALU.is_gt), (m_u, -1, 1, ALU.is_ge), (m_sl, 1, -1, ALU.is_gt)):
                g.memset("gpsimd", m.v(), 1.0)
                g.aselect(m.v(), m.v(), [[0, 4], [pat, 128]], cmp, 0.0, 0, cm)
            g.memset("gpsimd", rmask.v(), 1.0)
            g.memset("gpsimd", rmask.w(rmask.t[:].rearrange("p (c t) -> p c t", t=128)[:, :, 0:1]), 0.0)

            if stop == "const":
                dump(identf.v(), 128); dump(m_su.v(np.s_[:, 1, :]), 128); dump(m_sl.v(np.s_[:, 2, :]), 128); dump(m_u.v(np.s_[:, 3, :]), 128)
                dump(onesblk.v(), 128); dump(rmask.v(), 512)
            chk("const")
            gm = sb("gm", [128, 7, 8], F32)
            coefA = sb("coefA", [128, 6, 8], F32)
            coefB = sb("coefB", [128, 6, 8], F32)
            vec = sb("vec", [128, 7, 2], F32)
            WA = sb("WA", [128, 8, NCOL], BF16)
            WB = sb("WB", [128, 8, NCOL], BF16)
            w2b = sb("w2b", [64, 256], BF16)
            a2b = sb("a2b", [128, 256], BF16)
            g2a = sb("g2a", [128, 256], BF16)
            g2b = sb("g2b", [32, 256], BF16)
            xin = sb("xin", [128, 4, 1024], F32)
            g.dma_in("sync", gm.v(), gm_d[:, :, :])
            g.dma_in("sync", vec.v(), vec_d[:, :, :])
            g.dma_in("gpsimd", w2b.v(), w2_d[:, :])
            g.dma_in("gpsimd", a2b.v(np.s_[64:128, :]), a2_d[:, :])
            g.dma_in("gpsimd", g2a.v(), g2_d[0:128, :])
            g.dma_in("gpsimd", g2b.v(), g2_d[128:160, :])
            g0b = gm.w(gm.t[:, 0:1, :].to_broadcast([128, 6, 8]))
            g.tt("vector", coefB.v(), gm.v(np.s_[:, 1:7, :]), g0b, ALU.mult)
            g.tt("vector", coefA.v(), g0b, coefB.v(), ALU.subtract)
            stage = xin
            wv = wcat_d.rearrange("(kc p) n -> p kc n", p=128)
            XK = [("xin", j) for j in range(4)]
            for (nm, mi, ncol, off) in PROJ:
                sv = stage.t[:].rearrange("p a b -> p (a b)")[:, 0:8 * ncol].rearrange("p (k n) -> p k n", k=8)
                g.kb.op("sync", lambda e, sv=sv, off=off, ncol=ncol: e.dma_start(out=sv, in_=wv[:, :, off:off + ncol]), writes=XK, dma=True)
                ca = coefA.w(coefA.t[:, mi, :].unsqueeze(2).to_broadcast([128, 8, ncol]))
                cb = coefB.w(coefB.t[:, mi, :].unsqueeze(2).to_broadcast([128, 8, ncol]))
                g.tt("vector", WA.v(np.s_[:, :, off:off + ncol]), V(sv, XK), ca, ALU.mult)
                g.tt("gpsimd", WB.v(np.s_[:, :, off:off + ncol]), V(sv, XK), cb, ALU.mult)

            if stop == "weights":
                dump(WA.v(np.s_[:, 3, 0:512]), 512); dump(WB.v(np.s_[:, 7, 544:1056]), 512); dump(coefA.v(np.s_[:, 2, :]), 8)
            chk("weights")
            ms = sb("ms", [128, 4], F32)
            xn = [sb(f"xn{i}", [128, 1024], BF16) for i in range(2)]
            hT = sb("hT", [128, 8, TB + 1], BF16)
            T = {n: sb("t_" + n, [128, TB], F32) for n in
                 ("rT", "kraw", "aT", "sgw", "cs", "Er", "En", "kk", "kp", "beta", "t1", "t2", "EC")}
            junk = Buf(T["EC"].t[:].bitcast(BF16), "t_EC")
            tw = sb("tw", [128, TB], BF16)
            sg1a = sb("sg1a", [128, TB], BF16)
            sg1b = sb("sg1b", [32, TB], BF16)
            vT = sb("vT", [128, 2, TB], F32)
            gT2 = [sb(f"gT{i}", [128, 2, TB], F32) for i in range(2)]
            bonusT2 = [sb(f"bonusT{i}", [128, 2, TB], F32) for i in range(2)]
            PCs2 = [sb(f"PCs{i}", [128, 2, NCH], F32) for i in range(2)]
            rt2 = [sb(f"rt_{i}", [128, 2, TB], BF16) for i in range(2)]
            kt2 = [sb(f"kt_{i}", [128, 2, TB], BF16) for i in range(2)]
            at2 = [sb(f"at_{i}", [128, 2, TB], BF16) for i in range(2)]
            bt2 = [sb(f"bt_{i}", [128, 2, TB], BF16) for i in range(2)]
            T2 = {n: sb("t2_" + n, [128, TB], F32) for n in ("o1", "o2")}
            khT = sb("khT", [128, 2, TB], BF16)
            bhT = sb("bhT", [128, 2, TB], BF16)
            XR2 = [sb(f"XR{i}", [128, NCH, 4, 128], BF16) for i in range(2)]
            Kpad2 = [sb(f"Kpad{i}", [128, NCH, 4, 128], BF16) for i in range(2)]
            Bpad2 = [sb(f"Bpad{i}", [128, NCH, 4, 128], BF16) for i in range(2)]
            Vpad2 = [sb(f"Vpad{i}", [128, NCH, 4, 128], BF16) for i in range(2)]
            NSL = 2
            Pb = [[sb(f"P{s}{i}", [128, 4, 128], BF16) for i in range(2)] for s in range(NSL)]
            SUt = [sb(f"SU{s}", [128, 2, 4, 128], BF16) for s in range(NSL)]
            UUt = [sb(f"UU{s}", [128, 2, 4, 128], BF16) for s in range(NSL)]
            PTb = [[Buf(SUt[s].t[:, 0], f"PT{s}0"), sb(f"PT{s}1", [128, 4, 128], BF16)] for s in range(NSL)]
            NTb = [[sb(f"NT{s}{i}", [128, 4, 128], BF16) for i in range(2)] for s in range(NSL)]
            AakT = [Buf(SUt[s].t[:, 1], f"AakT{s}") for s in range(NSL)]
            ArbT = [Buf(UUt[s].t[:, 0], f"ArbT{s}") for s in range(NSL)]
            ArkT = [Buf(UUt[s].t[:, 1], f"ArkT{s}") for s in range(NSL)]
            Apad = [sb(f"Apad{s}", [128, 4, 128], BF16) for s in range(NSL)]
            Wpad = [sb(f"Wpad{s}", [128, 4, 128], BF16) for s in range(NSL)]
            TTbd = [sb(f"TTbd{s}", [128, 2, 128], BF16) for s in range(NSL)]
            RhT = [sb(f"RhT{s}", [128, 2, 128], BF16) for s in range(NSL)]
            Sbd = [sb(f"Sbd{i}", [128, 2, 128], BF16) for i in range(2)]
            yraw = sb("yraw", [128, 2, TB], F32)
            ygo = sb("ygo", [128, 2, TB], BF16)
            for b_ in Kpad2 + Bpad2 + Vpad2:
                g.memset("gpsimd", b_.v(), 0.0)
            for s in range(NSL):
                g.memset("gpsimd", Apad[s].v(), 0.0)
                g.memset("gpsimd", Wpad[s].v(), 0.0)
            g.memset("gpsimd", Sbd[0].v(), 0.0)
            g.memset("vector", hT.v(np.s_[:, :, 0:1]), 0.0)
            s_cur = 0

            def bf(bank):
                return bank.t[:].bitcast(BF16)

            def hsl(h, j):
                cc, q = divmod(h, 2)
                return np.s_[q * 64:(q + 1) * 64, cc, j * 128:(j + 1) * 128]

            def load_x(blk):
                t0 = blk * TB
                for j in range(4):
                    g.dma_in("sync", xin.v(np.s_[:, j, :], sub=j), x_d[t0 + j * 128:t0 + (j + 1) * 128, :])

            def front(blk):
                F = blk % 2
                rt_, kt_, at_, bt_ = rt2[F], kt2[F], at2[F], bt2[F]
                XR, Kpad, Bpad, Vpad, PCs, gT, bonusT = XR2[F], Kpad2[F], Bpad2[F], Vpad2[F], PCs2[F], gT2[F], bonusT2[F]
                for j in range(4):
                    g.act(junk.v(), xin.v(np.s_[:, j, :], sub=j), AF.Square, scale=1.0 / 32, accum=ms.v(np.s_[:, j:j + 1]))
                g.ts("vector", ms.v(), ms.v(), RMS_EPS, ALU.add)
                g.act(ms.v(), ms.v(), AF.Sqrt)
                g.recip(ms.v(), ms.v())
                if blk > 0:
                    g.copy("vector", hT.v(np.s_[:, :, 0:1]), hT.v(np.s_[:, :, TB:TB + 1]))
                yield
                for j in range(4):
                    xj = xn[j % 2]
                    g.act(xj.v(), xin.v(np.s_[:, j, :], sub=j), AF.Copy, scale=ms.v(np.s_[:, j:j + 1]))
                    bk = g.bank()
                    bv = bf(bk)
                    for kc in range(8):
                        g.tr(bk.w(bv[:, kc * 128:(kc + 1) * 128]), xj.v(np.s_[:, kc * 128:(kc + 1) * 128]), ident.v())
                    g.copy("vector", hT.v(np.s_[:, :, 1 + 128 * j:1 + 128 * (j + 1)]),
                           bk.w(bv.rearrange("p (k t) -> p k t", k=8)))
                    yield
                if blk + 1 < nblk:
                    load_x(blk + 1)

                def proj_fm(off, ncol):
                    bk = g.bank()
                    for kc in range(8):
                        g.mm(bk.v(np.s_[0:ncol, :]), WA.v(np.s_[:, kc, off:off + ncol]), hT.v(np.s_[:, kc, 1:TB + 1]),
                             start=(kc == 0), stop=False)
                        g.mm(bk.v(np.s_[0:ncol, :]), WB.v(np.s_[:, kc, off:off + ncol]), hT.v(np.s_[:, kc, 0:TB]),
                             start=False, stop=(kc == 7))
                    return bk

                bk = proj_fm(768, 128)
                g.act(tw.v(np.s_[0:64, :]), bk.v(np.s_[0:64, :]), AF.Tanh)
                g.copy("vector", tw.v(np.s_[64:128, :]), bk.v(np.s_[64:128, :]))
                yield
                bk = proj_fm(896, 128)
                g.act(sg1a.v(), bk.v(), AF.Sigmoid)
                yield
                bk = proj_fm(1024, 32)
                g.act(sg1b.v(), bk.v(np.s_[0:32, :]), AF.Sigmoid)
                yield
                for j in range(4):
                    bk = g.bank()
                    for kc in range(8):
                        g.mm(bk.v(np.s_[:, 0:256]), hT.v(np.s_[:, kc, 1 + 128 * j:1 + 128 * (j + 1)]), WA.v(np.s_[:, kc, 512:768]),
                             start=(kc == 0), stop=False)
                        g.mm(bk.v(np.s_[:, 0:256]), hT.v(np.s_[:, kc, 128 * j:128 * (j + 1)]), WB.v(np.s_[:, kc, 512:768]),
                             start=False, stop=(kc == 7))
                    pv = bk.t[:, 0:256].rearrange("p (h c) -> p h c", h=4)
                    g.copy("vector", Vpad.v(np.s_[:, j, 0::2, 0:64]), bk.w(pv[:, 0::2, :]))
                    g.copy("scalar", Vpad.v(np.s_[:, j, 1::2, 64:128]), bk.w(pv[:, 1::2, :]))
                    yield

                for cc in range(2):
                    bk = proj_fm(0 + cc * 128, 128)
                    g.copy("scalar", T["rT"].v(), bk.v())
                    yield
                    bk = proj_fm(256 + cc * 128, 128)
                    g.copy("vector", T["kraw"].v(), bk.v())
                    yield
                    bk = proj_fm(512 + cc * 128, 128)
                    g.copy("scalar", vT.v(np.s_[:, cc, :]), bk.v())
                    yield
                    bk = g.bank()
                    g.mm(bk.v(), w2b.v(np.s_[0:64, cc * 128:(cc + 1) * 128]), tw.v(np.s_[0:64, :]))
                    g.act(T["sgw"].v(), bk.v(), AF.Sigmoid, bias=vec.v(np.s_[:, 0, cc:cc + 1]))
                    bk = g.bank()
                    g.mm(bk.v(), a2b.v(np.s_[64:128, cc * 128:(cc + 1) * 128]), tw.v(np.s_[64:128, :]))
                    g.act(T["aT"].v(), bk.v(), AF.Sigmoid, bias=vec.v(np.s_[:, 1, cc:cc + 1]))
                    bk = g.bank()
                    g.mm(bk.v(), g2a.v(np.s_[:, cc * 128:(cc + 1) * 128]), sg1a.v(), start=True, stop=False)
                    g.mm(bk.v(), g2b.v(np.s_[0:32, cc * 128:(cc + 1) * 128]), sg1b.v(np.s_[0:32, :]), start=False, stop=True)
                    g.copy("vector", gT.v(np.s_[:, cc, :]), bk.v())
                    yield

                    kk, kp, beta, t1, t2 = T["kk"], T["kp"], T["beta"], T["t1"], T["t2"]
                    Er, En, EC, cs = T["Er"], T["En"], T["EC"], T["cs"]
                    g.ts("gpsimd", kk.v(), T["kraw"].v(), vec.v(np.s_[:, 2, cc:cc + 1]), ALU.mult)
                    g.tt("gpsimd", t1.v(), kk.v(), kk.v(), ALU.mult)
                    bk = g.bank()
                    g.mm(bk.v(), onesblk.v(), t1.v())
                    g.scan(cs.v(), rmask.v(), T["sgw"].v(), 0.0, ALU.mult, ALU.add)
                    yield
                    g.ts("vector", t2.v(), bk.v(), 1e-24, ALU.max)
                    g.act(t2.v(), t2.v(), AF.Sqrt)
                    g.act(Er.v(), cs.v(), AF.Exp, scale=NEG_C)
                    g.act(En.v(), cs.v(), AF.Exp, scale=-NEG_C)
                    g.recip(t2.v(), t2.v())
                    yield
                    g.tt("gpsimd", kk.v(), kk.v(), t2.v(), ALU.mult)
                    g.ts("vector", t1.v(), T["aT"].v(), -1.0, ALU.add, vec.v(np.s_[:, 3, cc:cc + 1]), ALU.mult)
                    g.tt("gpsimd", beta.v(), kk.v(), T["aT"].v(), ALU.mult)
                    g.stt("vector", kp.v(), t1.v(), 1.0, T["kraw"].v(), ALU.add, ALU.mult)
                    yield
                    Er3 = Er.t[:].rearrange("p (c t) -> p c t", t=128)
                    g.copy("vector", PCs.v(np.s_[:, cc, :]), Er.w(Er3[:, :, 127]))
                    g.tt("vector", EC.w(EC.t[:].rearrange("p (c t) -> p c t", t=128)),
                         En.w(En.t[:].rearrange("p (c t) -> p c t", t=128)),
                         Er.w(Er3[:, :, 127:128].to_broadcast([128, NCH, 128])), ALU.mult)
                    g.tt("gpsimd", kt_.v(np.s_[:, cc, :]), kp.v(), En.v(), ALU.mult)
                    g.tt("vector", rt_.v(np.s_[:, cc, :]), T["rT"].v(), Er.v(), ALU.mult)
                    yield
                    g.tt("gpsimd", bt_.v(np.s_[:, cc, :]), beta.v(), En.v(), ALU.mult)
                    g.tt("vector", khT.v(np.s_[:, cc, :]), kp.v(), EC.v(), ALU.mult)
                    g.tt("gpsimd", bhT.v(np.s_[:, cc, :]), beta.v(), EC.v(), ALU.mult)
                    kk3 = kk.t[:].rearrange("p (c t) -> p c t", t=128)
                    at3 = at_.t[:, cc, :].rearrange("p (c t) -> p c t", t=128)
                    g.stt("vector", at_.w(at3[:, :, 1:128]), kk.w(kk3[:, :, 1:128]), -1.0, Er.w(Er3[:, :, 0:127]), ALU.mult, ALU.mult)
                    g.ts("vector", at_.w(at3[:, :, 0:1]), kk.w(kk3[:, :, 0:1]), -1.0, ALU.mult)
                    yield
                    g.stt("vector", t1.v(), T["rT"].v(), vec.v(np.s_[:, 4, cc:cc + 1]), kp.v(), ALU.mult, ALU.mult)
                    bk = g.bank()
                    g.mm(bk.v(), onesblk.v(), t1.v())
                    g.tt("vector", bonusT.v(np.s_[:, cc, :]), vT.v(np.s_[:, cc, :]), bk.v(), ALU.mult)
                    yield

                for (src, kind) in ((at_, "A"), (khT, "K"), (bhT, "B")):
                    bk = g.bank()
                    bv = bf(bk)
                    for j in range(NCH):
                        for cc in range(2):
                            sl = (j * 2 + cc) * 128
                            g.tr(bk.w(bv[:, sl:sl + 128]), src.v(np.s_[:, cc, j * 128:(j + 1) * 128]), ident.v())
                    if kind == "A":
                        g.copy("vector", XR.v(np.s_[:, :, :, 0:64]),
                               bk.w(bv.rearrange("p (j h c) -> p j h c", j=NCH, h=4)))
                    else:
                        dst = Kpad if kind == "K" else Bpad
                        b5 = bv.rearrange("p (j c q d) -> p j c q d", j=NCH, c=2, q=2)
                        g.copy("vector", dst.v(np.s_[:, :, 0::2, 0:64]), bk.w(b5[:, :, :, 0, :]))
                        g.copy("scalar", dst.v(np.s_[:, :, 1::2, 64:128]), bk.w(b5[:, :, :, 1, :]))
                    yield

            def back(blk):
                nonlocal s_cur
                t0 = blk * TB
                F = blk % 2
                rt_, kt_, at_, bt_ = rt2[F], kt2[F], at2[F], bt2[F]
                XR, Kpad, Bpad, Vpad, PCs, gT, bonusT = XR2[F], Kpad2[F], Bpad2[F], Vpad2[F], PCs2[F], gT2[F], bonusT2[F]

                def pre_a(j, s):
                    def grp(specs, msk, dsts_t, dkeys):
                        bks = [g.bank(), g.bank()]
                        for si, (lh, rh) in enumerate(specs):
                            for h in range(4):
                                cc, q = divmod(h, 2)
                                sl = (si * 2 + cc) * 128
                                g.mm(bks[q].v(np.s_[:, sl:sl + 128]), lh.v(hsl(h, j)), rh.v(hsl(h, j)))
                        n = len(specs)
                        for q in range(2):
                            if n == 2:
                                src = bks[q].w(bks[q].t[:].rearrange("p (a c t) -> p a c t", a=2, c=2))
                                dst = V(dsts_t[:, :, q::2, :], dkeys)
                                mk = msk.w(msk.t[:].rearrange("p (a c) t -> p a c t", a=2))
                            else:
                                src = bks[q].w(bks[q].t[:, 0:256].rearrange("p (c t) -> p c t", c=2))
                                dst = V(dsts_t[:, q::2, :], dkeys)
                                mk = msk.v(np.s_[:, 0:2, :])
                            g.tt("vector", dst, src, mk, ALU.mult)
                    grp([(bt_, at_), (kt_, at_)], m_su, SUt[s].t, [(PTb[s][0].name, None), (AakT[s].name, None)])
                    yield
                    grp([(bt_, rt_), (kt_, rt_)], m_u, UUt[s].t, [(ArbT[s].name, None), (ArkT[s].name, None)])
                    yield
                    grp([(at_, bt_)], m_sl, Pb[s][0].t, [(Pb[s][0].name, None)])
                    g.tt("gpsimd", NTb[s][0].v(), PTb[s][0].v(), ident4.v(), ALU.add)
                    yield

                def pre_dbl(s, it):
                    cur, nxt = it % 2, (it + 1) % 2
                    bk = g.bank()
                    for h in range(4):
                        g.mm(bk.v(np.s_[:, h * 128:(h + 1) * 128]), PTb[s][cur].v(np.s_[:, h, :]), Pb[s][cur].v(np.s_[:, h, :]))
                    g.copy("scalar", Pb[s][nxt].v(), bk.w(bk.t[:].rearrange("p (h t) -> p h t", h=4)))
                    if it < 5:
                        bk = g.bank()
                        for h in range(4):
                            g.mm(bk.v(np.s_[:, h * 128:(h + 1) * 128]), Pb[s][cur].v(np.s_[:, h, :]), PTb[s][cur].v(np.s_[:, h, :]))
                        g.copy("vector", PTb[s][nxt].v(), bk.w(bk.t[:].rearrange("p (h t) -> p h t", h=4)))
                    yield
                    bk = g.bank()
                    for h in range(4):
                        o = bk.v(np.s_[:, h * 128:(h + 1) * 128])
                        g.mm(o, ident.v(), NTb[s][cur].v(np.s_[:, h, :]), start=True, stop=False)
                        g.mm(o, Pb[s][nxt].v(np.s_[:, h, :]), NTb[s][cur].v(np.s_[:, h, :]), start=False, stop=True)
                    eng = "scalar" if it % 2 == 0 else "vector"
                    g.copy(eng, NTb[s][nxt].v(), bk.w(bk.t[:].rearrange("p (h t) -> p h t", h=4)))
                    yield

                def pre_b(j, s):
                    NTf = NTb[s][0]
                    bk = g.bank()
                    for h in range(4):
                        q = h % 2
                        g.mm(bk.v(np.s_[:, h * 64:(h + 1) * 64]), AakT[s].v(np.s_[:, h, :]), Vpad.v(np.s_[:, j, h, q * 64:(q + 1) * 64]))
                    g.copy("vector", XR.v(np.s_[:, j, :, 64:128]), bk.w(bk.t[:, 0:256].rearrange("p (h c) -> p h c", h=4)))
                    yield
                    bk = g.bank()
                    for h in range(4):
                        g.mm(bk.v(np.s_[:, h * 128:(h + 1) * 128]), NTf.v(np.s_[:, h, :]), XR.v(np.s_[:, j, h, :]))
                    x4 = bk.t[:].rearrange("p (h c) -> p h c", h=4)
                    g.copy("vector", Apad[s].v(np.s_[:, 0::2, 0:64]), bk.w(x4[:, 0::2, 0:64]))
                    g.copy("scalar", Apad[s].v(np.s_[:, 1::2, 64:128]), bk.w(x4[:, 1::2, 0:64]))
                    g.copy("vector", Wpad[s].v(np.s_[:, 0::2, 0:64]), bk.w(x4[:, 0::2, 64:128]))
                    g.copy("scalar", Wpad[s].v(np.s_[:, 1::2, 64:128]), bk.w(x4[:, 1::2, 64:128]))
                    yield
                    bk = g.bank()
                    for cc in range(2):
                        o = bk.v(np.s_[:, cc * 128:(cc + 1) * 128])
                        for q in range(2):
                            h = 2 * cc + q
                            g.mm(o, Apad[s].v(np.s_[:, h, :]), Bpad.v(np.s_[:, j, h, :]), start=(q == 0), stop=(q == 1))
                    for cc in range(2):
                        g.stt("vector", TTbd[s].v(np.s_[:, cc, :]), identf.v(), PCs.v(np.s_[:, cc, j:j + 1]),
                              bk.v(np.s_[:, cc * 128:(cc + 1) * 128]), ALU.mult, ALU.add)
                    bk = g.bank()
                    for cc in range(2):
                        o = bk.v(np.s_[:, cc * 128:(cc + 1) * 128])
                        for q in range(2):
                            h = 2 * cc + q
                            g.mm(o, Apad[s].v(np.s_[:, h, :]), ArbT[s].v(np.s_[:, h, :]), start=(q == 0), stop=(q == 1))
                    g.tt("vector", RhT[s].v(), bk.w(bk.t[:, 0:256].rearrange("p (c t) -> p c t", c=2)),
                         rt_.v(np.s_[:, :, j * 128:(j + 1) * 128]), ALU.add)
                    yield

                def seq(j, s):
                    nonlocal s_cur
                    Sc, Sn = Sbd[s_cur], Sbd[1 - s_cur]
                    bk = g.bank()
                    for cc in range(2):
                        o = bk.v(np.s_[:, cc * 128:(cc + 1) * 128])
                        g.mm(o, TTbd[s].v(np.s_[:, cc, :]), Sc.v(np.s_[:, cc, :]), start=True, stop=False)
                        for q in range(2):
                            h = 2 * cc + q
                            g.mm(o, Bpad.v(np.s_[:, j, h, :]), Wpad[s].v(np.s_[:, h, :]), start=False, stop=False)
                            g.mm(o, Kpad.v(np.s_[:, j, h, :]), Vpad.v(np.s_[:, j, h, :]), start=False, stop=(q == 1))
                    g.copy("vector", Sn.v(), bk.w(bk.t[:, 0:256].rearrange("p (c t) -> p c t", c=2)))
                    bk = g.bank()
                    for cc in range(2):
                        o = bk.v(np.s_[:, cc * 128:(cc + 1) * 128])
                        g.mm(o, Sc.v(np.s_[:, cc, :]), RhT[s].v(np.s_[:, cc, :]), start=True, stop=False)
                        for q in range(2):
                            h = 2 * cc + q
                            g.mm(o, Wpad[s].v(np.s_[:, h, :]), ArbT[s].v(np.s_[:, h, :]), start=False, stop=False)
                            g.mm(o, Vpad.v(np.s_[:, j, h, :]), ArkT[s].v(np.s_[:, h, :]), start=False, stop=(q == 1))
                    g.copy("scalar", yraw.v(np.s_[:, :, j * 128:(j + 1) * 128]), bk.w(bk.t[:, 0:256].rearrange("p (c t) -> p c t", c=2)))
                    s_cur = 1 - s_cur
                    yield

                def rr(gens):
                    gens = list(gens)
                    while gens:
                        for gn in list(gens):
                            try:
                                next(gn)
                            except StopIteration:
                                gens.remove(gn)
                            yield

                for jp in range(0, NCH, NSL):
                    yield from rr([pre_a(jp + s, s) for s in range(NSL)])
                    for it in range(6):
                        yield from rr([pre_dbl(s, it) for s in range(NSL)])
                    yield from rr([pre_b(jp + s, s) for s in range(NSL)])
                    for s in range(NSL):
                        yield from seq(jp + s, s)

                for cc in range(2):
                    t1, t2 = T2["o1"], T2["o2"]
                    yr = yraw.v(np.s_[:, cc, :])
                    bk = g.bank()
                    g.mm(bk.v(), onesblk.v(), yr)
                    g.stt("vector", t1.v(), bk.v(), -1.0 / 64, yr, ALU.mult, ALU.add)
                    g.tt("gpsimd", t2.v(), t1.v(), t1.v(), ALU.mult)
                    yield
                    bk = g.bank()
                    g.mm(bk.v(), onesblk.v(), t2.v())
                    g.ts("vector", t2.v(), bk.v(), 1.0 / 64, ALU.mult, GN_EPS, ALU.add)
                    g.act(t2.v(), t2.v(), AF.Sqrt)
                    g.recip(t2.v(), t2.v())
                    yield
                    g.tt("gpsimd", t1.v(), t1.v(), t2.v(), ALU.mult)
                    g.ts("vector", t1.v(), t1.v(), vec.v(np.s_[:, 5, cc:cc + 1]), ALU.mult, vec.v(np.s_[:, 6, cc:cc + 1]), ALU.add)
                    g.tt("gpsimd", t1.v(), t1.v(), bonusT.v(np.s_[:, cc, :]), ALU.add)
                    g.tt("vector", ygo.v(np.s_[:, cc, :]), t1.v(), gT.v(np.s_[:, cc, :]), ALU.mult)
                    g.dma_out("sync", yg_d[cc * 128:(cc + 1) * 128, t0:t0 + TB], ygo.v(np.s_[:, cc, :]))
                    yield

            def drive(gens, weights=None):
                gens = list(gens)
                weights = list(weights or [1] * len(gens))
                while gens:
                    for gn, w in list(zip(gens, weights)):
                        for _ in range(w):
                            try:
                                next(gn)
                            except StopIteration:
                                k = gens.index(gn)
                                gens.pop(k)
                                weights.pop(k)
                                break

            load_x(0)
            drive([front(0)])
            for blk in range(nblk):
                gs = [back(blk)]
                if blk + 1 < nblk:
                    gs.append(front(blk + 1))
                drive(gs, [BACK_W, 1])
        try:
            emit_all()
        except Stop:
            pass
        if stop:
            g.dma_out("sync", dbg_d[:, :], dbgt.v())
        g.finish()
    return nc


def phase1_inputs(inp, core):
    b, hg = divmod(core, 4)
    cs = slice(hg * 256, (hg + 1) * 256)
    pm = lambda v: np.ascontiguousarray(v.reshape(-1, 128).T)
    gm = np.stack([pm(inp["norm_mix_g"][0])] + [pm(inp["rwkv_mu"][0, i]) for i in range(6)], axis=1)
    wcat = np.concatenate([inp["rwkv_w_r"][0][:, cs], inp["rwkv_w_k"][0][:, cs], inp["rwkv_w_v"][0][:, cs],
                           inp["rwkv_w1"][0], inp["rwkv_a1"][0], inp["rwkv_g1"][0]], axis=1)
    vecs = [inp["rwkv_w0"][0][cs], inp["rwkv_a0"][0][cs], inp["rwkv_k_k"][0][cs], inp["rwkv_k_a"][0][cs],
            inp["rwkv_r_k"][0].reshape(-1)[cs], inp["rwkv_ln_w"][0][cs], inp["rwkv_ln_b"][0][cs]]
    vec = np.stack([pm(v) for v in vecs], axis=1)
    return {
        "x": np.ascontiguousarray(inp["x"][b]),
        "gm": np.ascontiguousarray(gm, dtype=np.float32),
        "wcat": np.ascontiguousarray(wcat, dtype=np.float32),
        "w2": np.ascontiguousarray(inp["rwkv_w2"][0][:, cs]),
        "a2": np.ascontiguousarray(inp["rwkv_a2"][0][:, cs]),
        "g2": np.ascontiguousarray(inp["rwkv_g2"][0][:, cs]),
        "vec": np.ascontiguousarray(vec, dtype=np.float32),
    }


NT2 = 17
NTOK2 = NT2 * 128
DFF = 4096
HG = 512
NGRP = DFF // HG
NEG_BIG = -30000.0
TWO_PI = 6.283185307179586
PI = 3.141592653589793
C1_2PI = 6.28125
C2_2PI = TWO_PI - 6.28125


def build_phase2():
    nc = bass.Bass("TRN2", target_bir_lowering=False)
    dr = lambda n, s, d=F32: nc.dram_tensor(n, s, d, kind="ExternalInput").ap()
    x_d = dr("x", [NTOK2, D])
    yg_d = dr("ygT", [D, NTOK2], BF16)
    pos_d = dr("pos", [1, NTOK2], I32)
    wo0_d = dr("wo0", [D, D])
    win_d = [dr("win0", [D, DFF]), dr("win1", [D, DFF])]
    wout_d = [dr("wout0", [DFF, D]), dr("wout1", [DFF, D])]
    wqkv_d = dr("wqkv", [D, 1280])
    wo1_d = dr("wo1", [D, D])
    gains_d = dr("gains", [128, 3, 8])
    bq_d = dr("bq", [128, 10])
    rowv_d = dr("rowv", [1, 1024 + 1024 + 128 + 16])
    cst_d = dr("cst", [128, 4])
    out_d = nc.dram_tensor("out", [16 * 128, D], F32, kind="ExternalOutput").ap()

    with ExitStack() as st:
        g = G(nc, st)
        sb = g.sb
        identf = sb("identf", [128, 128], F32)
        ident = sb("ident", [128, 128], BF16)
        g.memset("gpsimd", identf.v(), 0.0)
        g.aselect(identf.v(), identf.v(), [[-1, 128]], ALU.not_equal, 1.0, 0, 1)
        g.copy("vector", ident.v(), identf.v())
        rotf = sb("rotf", [128, 128], F32)
        rot = sb("rot", [128, 128], BF16)
        g.memset("gpsimd", rotf.v(), 0.0)
        for blk in range(2):
            o = blk * 64
            sub = rotf.v(np.s_[:, o:o + 32])
            g.aselect(sub, sub, [[-1, 32]], ALU.not_equal, -1.0, -(o + 32), 1)
            sub = rotf.v(np.s_[:, o + 32:o + 64])
            g.aselect(sub, sub, [[-1, 32]], ALU.not_equal, 1.0, -(o + 32) + 32, 1)
        g.copy("vector", rot.v(), rotf.v())
        mscr = sb("scr", [128, 1024], F32)
        maskb = Buf(mscr.t[:, 0:256], "scr_m0")
        mask1 = Buf(mscr.t[:, 256:512], "scr_m1")
        g.memset("gpsimd", maskb.v(), 0.0)
        g.aselect(maskb.v(), maskb.v(), [[1, 256]], ALU.is_gt, NEG_BIG, 0, -1)
        g.aselect(maskb.v(), maskb.v(), [[-1, 256]], ALU.is_ge, NEG_BIG, 128, 1)
        cst = sb("cst", [128, 4], F32)
        g.dma_in("sync", cst.v(), cst_d[:, :])
        g.copy("vector", mask1.v(), maskb.v())
        g.ts("vector", mask1.v(np.s_[:, 0:128]), maskb.v(np.s_[:, 0:128]), cst.v(np.s_[:, 2:3]), ALU.add)
        maskbf = sb("maskbf", [128, 256], BF16)
        mask1bf = sb("mask1bf", [128, 256], BF16)
        g.ts("vector", maskbf.v(), maskb.v(), 8.0, ALU.mult)
        g.ts("vector", mask1bf.v(), mask1.v(), 8.0, ALU.mult)
        ones_row = sb("ones_row", [1, 128], F32)
        g.memset("gpsimd", ones_row.v(), 1.0)
        gains = sb("gains", [128, 3, 8], F32)
        g.dma_in("sync", gains.v(), gains_d[:, :, :])
        bq = sb("bq", [128, 10], F32)
        g.dma_in("sync", bq.v(), bq_d[:, :])
        rowv = sb("rowv", [1, 1024], F32)
        g.dma_in("sync", rowv.v(), rowv_d[0:1, 1024:2048])
        bvb = sb("bvb", [128, 128], F32)
        sinkb = sb("sinkb", [128, 16], F32)
        g.dma_in("sync", bvb.v(), rowv_d[0:1, 2048:2176].to_broadcast([128, 128]))
        g.dma_in("sync", sinkb.v(), rowv_d[0:1, 2176:2192].to_broadcast([128, 16]))

        xres = sb("xres", [128, NT2, 1024], F32)
        hT = sb("hT", [128, 8, NTOK2], BF16)
        uT = sb("uT", [128, 4, NTOK2], BF16)
        arena = sb("arena", [128, 18432], BF16)
        junk = sb("junk", [128, 1024], BF16)
        ms = sb("ms", [128, NT2], F32)
        xn = [sb("xn0", [128, 1024], BF16)] * 2
        scr = mscr
        relu_s = [sb(f"relu{i}", [128, 512], BF16) for i in range(2)]

        def bf(bank):
            return bank.t[:].bitcast(BF16)

        NTILES = [(0, 512), (512, 512), (1024, 512), (1536, 512), (2048, 128)]

        def load_x_and_yg():
            for i in range(NT2):
                g.dma_in("sync", xres.v(np.s_[:, i, :], sub=i), x_d[i * 128:(i + 1) * 128, :])
            for kc in range(8):
                g.dma_in("sync", hT.v(np.s_[:, kc, :]), yg_d[kc * 128:(kc + 1) * 128, :])

        def wload(dst_ap, key, src_ap):
            g.kb.op("gpsimd", lambda e: e.dma_start(out=dst_ap, in_=src_ap), writes=[key], dma=True)

        def norm_to_hT(gi, tiles):
            for i in tiles:
                g.act(junk.v(), xres.v(np.s_[:, i, :], sub=i), AF.Square, scale=1.0 / 32, accum=ms.v(np.s_[:, i:i + 1], sub=i))
                g.ts("vector", ms.v(np.s_[:, i:i + 1], sub=i), ms.v(np.s_[:, i:i + 1], sub=i), RMS_EPS, ALU.add)
                g.act(ms.v(np.s_[:, i:i + 1], sub=i), ms.v(np.s_[:, i:i + 1], sub=i), AF.Sqrt)
                g.recip(ms.v(np.s_[:, i:i + 1], sub=i), ms.v(np.s_[:, i:i + 1], sub=i))
                xj = xn[i % 2]
                g.act(xj.v(), xres.v(np.s_[:, i, :], sub=i), AF.Copy, scale=ms.v(np.s_[:, i:i + 1], sub=i))
                bk = g.bank()
                bv = bf(bk)
                for kc in range(8):
                    g.tr(bk.w(bv[:, kc * 128:(kc + 1) * 128]), xj.v(np.s_[:, kc * 128:(kc + 1) * 128]), ident.v())
                g.tt("vector", hT.v(np.s_[:, :, i * 128:(i + 1) * 128]), bk.w(bv.rearrange("p (k t) -> p k t", k=8)),
                     gains.w(gains.t[:, gi, :].unsqueeze(2).to_broadcast([128, 8, 128])), ALU.mult)

        def mlp(layer):
            win, wout = win_d[layer], wout_d[layer]
            winv = win.rearrange("(kc p) n -> p kc n", p=128)
            woutv = wout.rearrange("(m p) n -> p m n", p=128)
            def WI(s):
                return arena.t[:, s * 4096:(s + 1) * 4096].rearrange("p (k n) -> p k n", k=8)

            def WO(s):
                return arena.t[:, 8192 + s * 4096:8192 + (s + 1) * 4096].rearrange("p (m n) -> p m n", m=4)

            ri = 0
            for grp in range(NGRP):
                s = grp % 2
                kwi, kwo = ("arena", "wi%d" % s), ("arena", "wo%d" % s)
                wload(WI(s), kwi, winv[:, :, grp * HG:(grp + 1) * HG])
                wload(WO(s), kwo, woutv[:, grp * 4:(grp + 1) * 4, :])
                for m in range(4):
                    for (n0, nn) in NTILES:
                        bk = g.bank()
                        for kc in range(8):
                            g.mm(bk.v(np.s_[:, 0:nn]), V(WI(s)[:, kc, m * 128:(m + 1) * 128], [kwi]), hT.v(np.s_[:, kc, n0:n0 + nn]),
                                 start=(kc == 0), stop=(kc == 7))
                        r = relu_s[ri % 2]
                        ri += 1
                        g.act(r.v(np.s_[:, 0:nn]), bk.v(np.s_[:, 0:nn]), AF.Relu)
                        g.tt("gpsimd", uT.v(np.s_[:, m, n0:n0 + nn]), r.v(np.s_[:, 0:nn]), r.v(np.s_[:, 0:nn]), ALU.mult)
                for i in range(NT2):
                    for half in range(2):
                        bk = g.bank()
                        for m in range(4):
                            g.mm(bk.v(), uT.v(np.s_[:, m, i * 128:(i + 1) * 128]), V(WO(s)[:, m, half * 512:(half + 1) * 512], [kwo]),
                                 start=(m == 0), stop=(m == 3))
                        xs = xres.v(np.s_[:, i, half * 512:(half + 1) * 512], sub=i)
                        g.tt("vector", xs, xs, bk.v(), ALU.add)

        load_x_and_yg()
        wo_v = arena.t[:, 0:8192].rearrange("p (k n) -> p k n", k=8)
        g.kb.op("gpsimd", lambda e: e.dma_start(out=wo_v, in_=wo0_d.rearrange("(kc p) n -> p kc n", p=128)),
                writes=[("arena", "wi0"), ("arena", "wi1")], dma=True)
        for i in range(NT2):
            for half in range(2):
                bk = g.bank()
                for kc in range(8):
                    g.mm(bk.v(), hT.v(np.s_[:, kc, i * 128:(i + 1) * 128]),
                         V(wo_v[:, kc, half * 512:(half + 1) * 512], [("arena", "wi0"), ("arena", "wi1")]),
                         start=(kc == 0), stop=(kc == 7))
                xs = xres.v(np.s_[:, i, half * 512:(half + 1) * 512], sub=i)
                g.tt("vector", xs, xs, bk.v(), ALU.add)

        norm_to_hT(0, range(NT2))
        mlp(0)

        norm_to_hT(1, range(NT2))
        wq_v = arena.t[:, 0:10240].rearrange("p (k n) -> p k n", k=8)
        wo1_v = arena.t[:, 10240:18432].rearrange("p (k n) -> p k n", k=8)
        KQ = [("arena", "wi0"), ("arena", "wi1"), ("arena", "wo0")]
        KO = [("arena", "wo0"), ("arena", "wo1"), ("arena", "x")]
        g.kb.op("gpsimd", lambda e: e.dma_start(out=wq_v, in_=wqkv_d.rearrange("(kc p) n -> p kc n", p=128)), writes=KQ, dma=True)
        g.kb.op("gpsimd", lambda e: e.dma_start(out=wo1_v, in_=wo1_d.rearrange("(kc p) n -> p kc n", p=128)), writes=KO, dma=True)
        tabs = uT.t[:].rearrange("p a b -> p (a b)").bitcast(F32)
        cosT = uT.w(tabs[:, 0:NTOK2])
        sinT = uT.w(tabs[:, NTOK2:2 * NTOK2])
        posi = V(scr.t[:, 0:512].bitcast(I32), [("scr", "a"), ("scr_m0", None), ("scr_m1", None)])
        for (n0, nn) in NTILES:
            pch = V(posi.ap[:, 0:nn], posi.keys)
            ach = scr.v(np.s_[:, 512:512 + nn], sub="b")
            g.dma_in("sync", pch, pos_d[0:1, n0:n0 + nn].to_broadcast([128, nn]))
            g.copy("vector", ach, pch)
            g.ts("vector", ach, ach, cst.v(np.s_[:, 0:1]), ALU.mult)
            sch = V(sinT.ap[:, n0:n0 + nn], sinT.keys)
            cch = V(cosT.ap[:, n0:n0 + nn], cosT.keys)
            T1 = V(xn[0].t[:].bitcast(F32)[:, 0:nn], [("xn0", None)])
            A2 = V(junk.t[:].bitcast(F32)[:, 0:nn], [("junk", None)])
            TI = pch
            for (src, dst, shift) in ((ach, sch, 0.0), (ach, cch, 0.5 * PI)):
                if shift:
                    g.ts("vector", A2, src, shift, ALU.add)
                    src = A2
                g.ts("vector", T1, src, 1.0 / TWO_PI, ALU.mult)
                g.copy("vector", TI, T1)
                g.copy("vector", T1, TI)
                g.stt("vector", dst, T1, -C1_2PI, src, ALU.mult, ALU.add)
                g.stt("vector", dst, T1, -C2_2PI, dst, ALU.mult, ALU.add)
                g.ts("vector", dst, dst, -PI, ALU.max, PI, ALU.min)
                g.act(dst, dst, AF.Sin)

        kr = sb("kr", [128, 2, NTOK2], BF16)
        vpad = [sb(f"vpad{i}", [128, 2, 2, 128], BF16) for i in range(3)]
        for v_ in vpad:
            g.memset("gpsimd", v_.v(), 0.0)
        qb16 = sb("qb16", [128, 512], BF16)
        SCALE = 0.125
        negsink = sb("negsink", [128, 16], F32)
        esink = sb("esink", [128, 16], F32)
        g.ts("vector", negsink.v(), sinkb.v(), -1.0, ALU.mult)
        g.act(esink.v(), sinkb.v(), AF.Exp)

        def rope_evac(bk, nn, bias_col, n0, dst, qf, qb, r2):
            g.act(qf, bk.v(np.s_[:, 0:nn]), AF.Identity, bias=bias_col)
            g.copy("gpsimd", qb, qf)
            b2 = g.bank()
            g.mm(b2.v(np.s_[:, 0:nn]), rot.v(), qb)
            g.tt("vector", r2, b2.v(np.s_[:, 0:nn]), V(sinT.ap[:, n0:n0 + nn], sinT.keys), ALU.mult)
            g.tt("gpsimd", qf, qf, V(cosT.ap[:, n0:n0 + nn], cosT.keys), ALU.mult)
            g.tt("vector", dst, qf, r2, ALU.add)

        NU = 4
        asc = sb("asc", [128, NU, 2, 256], F32)
        wkd_ap = asc.t[:].rearrange("p a b c -> p (a b c)")[:, 0:1024].bitcast(BF16).rearrange("p (k j c) -> p k j c", k=8, j=2)
        WK = [("asc", u) for u in range(NU)]
        for j in range(2):
            for dup in range(2):
                g.copy("vector", V(wkd_ap[:, :, j, dup * 64:(dup + 1) * 64], WK), V(wq_v[:, :, 1024 + j * 64:1024 + (j + 1) * 64], KQ))
        for j in range(2):
            for (n0, nn) in NTILES:
                bk = g.bank()
                for kc in range(8):
                    g.mm(bk.v(np.s_[:, 0:nn]), V(wkd_ap[:, kc, j, :], WK), hT.v(np.s_[:, kc, n0:n0 + nn]), start=(kc == 0), stop=(kc == 7))
                rope_evac(bk, nn, bq.v(np.s_[:, 8 + j:9 + j]), n0, kr.v(np.s_[:, j, n0:n0 + nn]),
                          scr.v(np.s_[:, 0:nn], sub="a"), qb16.v(np.s_[:, 0:nn]), scr.v(np.s_[:, 512:512 + nn], sub="b"))

        def make_vpad(i):
            vp = vpad[i % 3]
            bk = g.bank()
            for kc in range(8):
                g.mm(bk.v(np.s_[:, 0:128]), hT.v(np.s_[:, kc, i * 128:(i + 1) * 128]), V(wq_v[:, kc, 1152:1280], KQ), start=(kc == 0), stop=(kc == 7))
            for q2 in range(2):
                g.tt("vector", vp.v(np.s_[:, :, q2, q2 * 64:(q2 + 1) * 64]), bk.w(bk.t[:, 0:128].rearrange("p (j d) -> p j d", j=2)),
                     bvb.w(bvb.t[:].rearrange("p (j d) -> p j d", j=2)), ALU.add)
            return vp

        qr = sb("qr", [128, 8, 128], BF16)
        oT = sb("oT", [128, 8, 128], BF16)
        pn = [sb(f"pn{i}", [128, 2, 256], BF16) for i in range(NU)]
        pT = [sb(f"pT{i}", [128, 2, 2, 128], BF16) for i in range(NU)]
        stat = [sb(f"stat{i}", [128, 8], F32) for i in range(NU)]
        make_vpad(0)
        for i in range(1, NT2):
            make_vpad(i)
            vps = [vpad[(i - 1) % 3], vpad[i % 3]]
            for hh in range(2):
                bk = g.bank()
                for a4 in range(4):
                    hp = hh * 4 + a4
                    for kc in range(8):
                        g.mm(bk.v(np.s_[:, a4 * 128:(a4 + 1) * 128]), V(wq_v[:, kc, hp * 128:(hp + 1) * 128], KQ),
                             hT.v(np.s_[:, kc, i * 128:(i + 1) * 128]), start=(kc == 0), stop=(kc == 7))
                qf = scr.v(np.s_[:, 0:512], sub="a")
                r2 = scr.v(np.s_[:, 512:1024], sub="b")
                qb = qb16.v()
                qf3 = V(scr.t[:, 0:512].rearrange("p (a t) -> p a t", a=4), [("scr", "a")])
                r23 = V(scr.t[:, 512:1024].rearrange("p (a t) -> p a t", a=4), [("scr", "b")])
                g.tt("vector", qf3, bk.w(bk.t[:].rearrange("p (a t) -> p a t", a=4)),
                     bq.w(bq.t[:, hh * 4:hh * 4 + 4].unsqueeze(2).to_broadcast([128, 4, 128])), ALU.add)
                g.copy("gpsimd", qb, qf)
                b2 = g.bank()
                g.mm(b2.v(), rot.v(), qb)
                sin_b = V(sinT.ap[:, i * 128:(i + 1) * 128].unsqueeze(1).to_broadcast([128, 4, 128]), sinT.keys)
                cos_b = V(cosT.ap[:, i * 128:(i + 1) * 128].unsqueeze(1).to_broadcast([128, 4, 128]), cosT.keys)
                g.tt("vector", r23, b2.w(b2.t[:].rearrange("p (a t) -> p a t", a=4)), sin_b, ALU.mult)
                g.tt("gpsimd", qf3, qf3, cos_b, ALU.mult)
                g.tt("vector", qr.v(np.s_[:, hh * 4:hh * 4 + 4, :]), qf3, r23, ALU.add)
            mk = mask1bf if i == 1 else maskbf
            for j in range(2):
                sbank = {}
                for gp in range(2):
                    hp0 = 4 * j + 2 * gp
                    bks = [g.bank(), g.bank()]
                    for a in range(2):
                        for q2 in range(2):
                            o = bks[q2].v(np.s_[:, a * 256:(a + 1) * 256])
                            g.mm(o, ident.v(), mk.v(), start=True, stop=False)
                            g.mm(o, qr.v(np.s_[q2 * 64:(q2 + 1) * 64, hp0 + a, :]),
                                 kr.v(np.s_[q2 * 64:(q2 + 1) * 64, j, (i - 1) * 128:(i + 1) * 128]), start=False, stop=True)
                    for q2 in range(2):
                        sbank[gp * 2 + q2] = bks[q2]
                UN = range(4)
                for u in UN:
                    gp, q2 = divmod(u, 2)
                    st_ = stat[u]
                    ps3 = sbank[u].w(sbank[u].t[:].rearrange("p (a k) -> p a k", a=2))
                    g.kb.op("vector", lambda e, st_=st_, ps3=ps3: e.tensor_reduce(out=st_.t[:, 0:2], in_=ps3.ap, axis=mybir.AxisListType.X, op=ALU.max),
                            reads=ps3.keys, writes=st_.v().keys)
                    h0 = 2 * (4 * j + 2 * gp) + q2
                    g.stt("vector", st_.v(np.s_[:, 2:4]), st_.v(np.s_[:, 0:2]), -SCALE, negsink.w(negsink.t[:, h0:h0 + 3:2]), ALU.mult, ALU.min)
                for u in UN:
                    st_ = stat[u]
                    for a in range(2):
                        g.act(V(asc.t[:, u, a, :], [("asc", u)]), sbank[u].v(np.s_[:, a * 256:(a + 1) * 256]), AF.Exp, scale=SCALE,
                              bias=st_.v(np.s_[:, 2 + a:3 + a]), accum=st_.v(np.s_[:, 4 + a:5 + a]))
                    g.act(st_.v(np.s_[:, 6:8]), st_.v(np.s_[:, 2:4]), AF.Exp)
                for u in UN:
                    gp, q2 = divmod(u, 2)
                    st_ = stat[u]
                    h0 = 2 * (4 * j + 2 * gp) + q2
                    g.tt("vector", st_.v(np.s_[:, 6:8]), st_.v(np.s_[:, 6:8]), esink.w(esink.t[:, h0:h0 + 3:2]), ALU.mult)
                    g.tt("vector", st_.v(np.s_[:, 4:6]), st_.v(np.s_[:, 4:6]), st_.v(np.s_[:, 6:8]), ALU.add)
                    g.recip(st_.v(np.s_[:, 4:6]), st_.v(np.s_[:, 4:6]))
                for u in UN:
                    st_ = stat[u]
                    g.tt("gpsimd", pn[u].v(), V(asc.t[:, u], [("asc", u)]), st_.w(st_.t[:, 4:6].unsqueeze(2).to_broadcast([128, 2, 256])), ALU.mult)
                tbs = {}
                for u in UN:
                    tb = g.bank()
                    tv = bf(tb)
                    for a in range(2):
                        for kb in range(2):
                            sl = (a * 2 + kb) * 128
                            g.tr(tb.w(tv[:, sl:sl + 128]), pn[u].v(np.s_[:, a, kb * 128:(kb + 1) * 128]), ident.v())
                    tbs[u] = (tb, tv)
                for u in UN:
                    tb, tv = tbs[u]
                    g.copy("scalar" if u % 2 == 0 else "vector", pT[u].v(), tb.w(tv[:, 0:512].rearrange("p (a k q) -> p a k q", a=2, k=2)))
                for gp in range(2):
                    hp0 = 4 * j + 2 * gp
                    obk = g.bank()
                    for a in range(2):
                        first = True
                        for q2 in range(2):
                            for kb in range(2):
                                g.mm(obk.v(np.s_[:, a * 128:(a + 1) * 128]), vps[kb].v(np.s_[:, j, q2, :]), pT[gp * 2 + q2].v(np.s_[:, a, kb, :]),
                                     start=first, stop=(q2 == 1 and kb == 1))
                                first = False
                    g.copy("scalar", oT.v(np.s_[:, hp0:hp0 + 2, :]), obk.w(obk.t[:, 0:256].rearrange("p (a q) -> p a q", a=2)))
            for half in range(2):
                bk = g.bank()
                for hp in range(8):
                    g.mm(bk.v(), oT.v(np.s_[:, hp, :]), V(wo1_v[:, hp, half * 512:(half + 1) * 512], KO), start=(hp == 0), stop=False)
                g.mm(bk.v(), ones_row.v(), rowv.v(np.s_[0:1, half * 512:(half + 1) * 512]), start=False, stop=True)
                xs = xres.v(np.s_[:, i, half * 512:(half + 1) * 512], sub=i)
                g.tt("vector", xs, xs, bk.v(), ALU.add)

        norm_to_hT(2, range(1, NT2))
        mlp(1)

        gfb = V(asc.t[:].rearrange("p a b c -> p (a b c)")[:, 0:1024], [("asc", u) for u in range(NU)])
        g.dma_in("sync", gfb, rowv_d[0:1, 0:1024].to_broadcast([128, 1024]))
        for i in range(1, NT2):
            g.act(junk.v(), xres.v(np.s_[:, i, :], sub=i), AF.Square, scale=1.0 / 32, accum=ms.v(np.s_[:, i:i + 1], sub=i))
            g.ts("vector", ms.v(np.s_[:, i:i + 1], sub=i), ms.v(np.s_[:, i:i + 1], sub=i), RMS_EPS, ALU.add)
            g.act(ms.v(np.s_[:, i:i + 1], sub=i), ms.v(np.s_[:, i:i + 1], sub=i), AF.Sqrt)
            g.recip(ms.v(np.s_[:, i:i + 1], sub=i), ms.v(np.s_[:, i:i + 1], sub=i))
            o_ = V(scr.t[:], [("scr", "a"), ("scr", "b")])
            g.stt("vector", o_, xres.v(np.s_[:, i, :], sub=i), ms.v(np.s_[:, i:i + 1], sub=i), gfb, ALU.mult, ALU.mult)
            g.dma_out("sync", out_d[(i - 1) * 128:i * 128, :], o_)
        g.finish()
    return nc


def _pm(v):
    return np.ascontiguousarray(np.asarray(v).reshape(-1, 128).T)


def phase2_inputs(inp, ygT_full, core):
    b, tq = divmod(core, 4)
    t0 = tq * 2048
    x = np.zeros((NTOK2, D), np.float32)
    yg = np.zeros((D, NTOK2), ml_dtypes.bfloat16)
    pos = np.zeros((1, NTOK2), np.int32)
    lo = t0 - 128
    if tq > 0:
        x[:] = inp["x"][b, lo:lo + NTOK2]
        yg[:] = ygT_full[b][:, lo:lo + NTOK2]
        pos[0] = inp["positions"][b, lo:lo + NTOK2]
    else:
        x[128:] = inp["x"][b, 0:2048]
        yg[:, 128:] = ygT_full[b][:, 0:2048]
        pos[0, 128:] = inp["positions"][b, 0:2048]
    gains = np.stack([_pm(inp["norm_mlp_g"][0]), _pm(inp["norm_mix_g"][1]), _pm(inp["norm_mlp_g"][1])], axis=1)
    bqkv = inp["attn_b_qkv"][0]
    bq = np.zeros((128, 10), np.float32)
    bq[:, 0:8] = _pm(bqkv[0:1024])
    for j in range(2):
        bk = bqkv[1024 + j * 64:1024 + (j + 1) * 64]
        bq[:, 8 + j] = np.concatenate([bk, bk])
    rowv = np.concatenate([inp["norm_final_g"], inp["attn_b_o"][0], bqkv[1152:1280], inp["attn_sinks"][0]])[None, :]
    cst = np.zeros((128, 4), np.float32)
    p = np.arange(128)
    cst[:, 0] = (10000.0 ** (-(np.arange(0, 64, 2, dtype=np.float32)) / 64.0))[p % 32]
    cst[:, 1] = np.where((p % 64) < 32, 1.0, 1.0)
    cst[:, 2] = 0.0 if tq > 0 else NEG_BIG
    return {
        "x": x, "ygT": yg, "pos": pos,
        "wo0": np.ascontiguousarray(inp["rwkv_w_o"][0]),
        "win0": np.ascontiguousarray(inp["mlp_w_in"][0]), "win1": np.ascontiguousarray(inp["mlp_w_in"][1]),
        "wout0": np.ascontiguousarray(inp["mlp_w_out"][0]), "wout1": np.ascontiguousarray(inp["mlp_w_out"][1]),
        "wqkv": np.ascontiguousarray(inp["attn_w_qkv"][0]), "wo1": np.ascontiguousarray(inp["attn_w_o"][0]),
        "gains": np.ascontiguousarray(gains, dtype=np.float32), "bq": bq,
        "rowv": np.ascontiguousarray(rowv, dtype=np.float32), "cst": cst,
    }


_NC_CACHE = {}


def kernel(**inputs):
    inp = {k: np.asarray(v) for k, v in inputs.items()}
    if "p1" not in _NC_CACHE:
        _NC_CACHE["p1"] = build_phase1()
        _NC_CACHE["p2"] = build_phase2()
    r1 = run_bass_kernel_spmd(_NC_CACHE["p1"], [phase1_inputs(inp, c) for c in range(8)], core_ids=list(range(8)))
    ygT = np.zeros((2, D, S_LEN), ml_dtypes.bfloat16)
    for c in range(8):
        b, hg = divmod(c, 4)
        ygT[b, hg * 256:(hg + 1) * 256] = r1.results[c]["yg"]
    r2 = run_bass_kernel_spmd(_NC_CACHE["p2"], [phase2_inputs(inp, ygT, c) for c in range(8)], core_ids=list(range(8)))
    out = np.zeros((2, S_LEN, D), np.float32)
    for c in range(8):
        b, tq = divmod(c, 4)
        out[b, tq * 2048:(tq + 1) * 2048] = r2.results[c]["out"]
    return out
```

```python
import numpy as np
import ml_dtypes
from contextlib import ExitStack
import concourse.bass as bass
import concourse.mybir as mybir
from concourse.bass_utils import run_bass_kernel_spmd

F32 = mybir.dt.float32
BF16 = mybir.dt.bfloat16
I32 = mybir.dt.int32
AF = mybir.ActivationFunctionType
ALU = mybir.AluOpType

EPOCH = 4000
DMA_ROT = 8
NEG_C = -0.6065306597126334


class KB:
    ENGS = ("tensor", "vector", "scalar", "gpsimd", "sync")

    def __init__(self, nc, stack):
        self.nc = nc
        self.stack = stack
        self.streams = {e: [] for e in self.ENGS}
        self.count = {e: 0 for e in self.ENGS}
        self.sems = {e: [] for e in self.ENGS}
        self.dma_count = {e: 0 for e in self.ENGS}
        self.dma_sems = {e: [] for e in self.ENGS}
        self.waited = {e: {} for e in self.ENGS}
        self.last_w = {}
        self.readers = {}

    def _newsem(self, name):
        return self.stack.enter_context(self.nc.semaphore(name))

    def _compute_token(self, e):
        n = self.count[e]
        ep, idx = divmod(n, EPOCH)
        while len(self.sems[e]) <= ep:
            self.sems[e].append(self._newsem(f"s_{e}_{len(self.sems[e])}"))
        self.count[e] = n + 1
        return (self.sems[e][ep], idx + 1, 1, e)

    def _dma_token(self, e):
        n = self.dma_count[e]
        if not self.dma_sems[e]:
            self.dma_sems[e] = [self._newsem(f"d_{e}_{i}") for i in range(DMA_ROT)]
        self.dma_count[e] = n + 1
        return (self.dma_sems[e][n % DMA_ROT], 16 * (n // DMA_ROT + 1), 16, "dma_" + e)

    def op(self, e, fn, reads=(), writes=(), dma=False):
        deps = []
        for k in reads:
            t = self.last_w.get(k)
            if t is not None:
                deps.append((t, True))
        for k in writes:
            t = self.last_w.get(k)
            if t is not None:
                deps.append((t, False))
            for t in self.readers.get(k, ()):
                deps.append((t, False))
        wd = self.waited[e]
        ww = {}
        for (sem, val, _inc, src), is_raw in deps:
            if src == e and not dma and e == "tensor":
                continue
            sid = id(sem)
            if wd.get(sid, 0) >= val:
                continue
            if sid not in ww or ww[sid][1] < val:
                ww[sid] = (sem, val)
        for sid, (sem, val) in ww.items():
            wd[sid] = val
        if dma:
            n = self.dma_count[e]
            if n >= DMA_ROT:
                sem = self.dma_sems[e][n % DMA_ROT]
                val = 16 * (n // DMA_ROT)
                if wd.get(id(sem), 0) < val:
                    wd[id(sem)] = val
                    ww[id(sem)] = (sem, val)
        tok = self._dma_token(e) if dma else self._compute_token(e)
        self.streams[e].append((list(ww.values()), fn, tok))
        for k in reads:
            self.readers.setdefault(k, []).append(tok)
        for k in writes:
            self.last_w[k] = tok
            self.readers[k] = []
        return tok

    def wait_tokens(self, e, toks):
        wd = self.waited[e]
        waits = []
        for (sem, val, _i, _s) in toks:
            if wd.get(id(sem), 0) >= val:
                continue
            wd[id(sem)] = val
            waits.append((sem, val))
        self.streams[e].append((waits, None, None))

    def emit(self):
        nc = self.nc
        with nc.Block() as block:
            def mk(e):
                def body(eng):
                    for waits, fn, tok in self.streams[e]:
                        for sem, val in waits:
                            eng.wait_ge(sem, val)
                        if fn is not None:
                            ins = fn(eng)
                            ins.then_inc(tok[0], tok[2])
                return body
            for e in self.ENGS:
                if self.streams[e]:
                    getattr(block, e)(mk(e))


class V:
    __slots__ = ("ap", "keys")

    def __init__(self, ap, keys):
        self.ap = ap
        self.keys = keys


class Buf:
    def __init__(self, t, name):
        self.t = t
        self.name = name

    def v(self, idx=None, sub=None):
        ap = self.t[idx] if idx is not None else self.t[:]
        return V(ap, [(self.name, sub)])

    def w(self, ap, sub=None):
        return V(ap, [(self.name, sub)])


class G:
    def __init__(self, nc, st):
        self.nc = nc
        self.st = st
        self.kb = KB(nc, st)
        self.banks = [Buf(st.enter_context(nc.psum_tensor(f"psb{i}", [128, 512], F32)), f"ps{i}") for i in range(8)]
        self.bank_i = 0
        self.out_tokens = []

    def sb(self, name, shape, dt):
        return Buf(self.st.enter_context(self.nc.sbuf_tensor("sb_" + name, shape, dt)), name)

    def bank(self):
        b = self.banks[self.bank_i % 8]
        self.bank_i += 1
        return b

    @staticmethod
    def _k(vs):
        ks = []
        for v in vs:
            if isinstance(v, V):
                ks.extend(v.keys)
        return ks

    def mm(self, out, lhsT, rhs, start=True, stop=True):
        return self.kb.op("tensor", lambda e: e.matmul(out.ap, lhsT=lhsT.ap, rhs=rhs.ap, start=start, stop=stop),
                          reads=self._k([lhsT, rhs]), writes=out.keys)

    def tr(self, out, in_, ident):
        return self.kb.op("tensor", lambda e: e.transpose(out=out.ap, in_=in_.ap, identity=ident.ap),
                          reads=self._k([in_, ident]), writes=out.keys)

    def act(self, out, in_, func, bias=None, scale=1.0, accum=None, eng="scalar"):
        kw = {}
        if bias is not None:
            kw["bias"] = bias.ap if isinstance(bias, V) else bias
        if accum is not None:
            kw["accum_out"] = accum.ap
        sc = scale.ap if isinstance(scale, V) else scale
        return self.kb.op("scalar", lambda e: e.activation(out=out.ap, in_=in_.ap, func=func, scale=sc, **kw),
                          reads=self._k([in_, bias, scale]), writes=self._k([out, accum]))

    def tt(self, eng, out, a, b, op):
        return self.kb.op(eng, lambda e: e.tensor_tensor(out=out.ap, in0=a.ap, in1=b.ap, op=op),
                          reads=self._k([a, b]), writes=out.keys)

    def ts(self, eng, out, a, s1, op0, s2=None, op1=None):
        s1a = s1.ap if isinstance(s1, V) else s1
        s2a = s2.ap if isinstance(s2, V) else s2
        if op1 is None:
            fn = lambda e: e.tensor_scalar(out=out.ap, in0=a.ap, scalar1=s1a, scalar2=None, op0=op0)
        else:
            fn = lambda e: e.tensor_scalar(out=out.ap, in0=a.ap, scalar1=s1a, scalar2=s2a, op0=op0, op1=op1)
        return self.kb.op(eng, fn, reads=self._k([a, s1, s2]), writes=out.keys)

    def stt(self, eng, out, in0, scalar, in1, op0, op1):
        sa = scalar.ap if isinstance(scalar, V) else scalar
        return self.kb.op(eng, lambda e: e.scalar_tensor_tensor(out=out.ap, in0=in0.ap, scalar=sa, in1=in1.ap, op0=op0, op1=op1),
                          reads=self._k([in0, scalar, in1]), writes=out.keys)

    def copy(self, eng, out, in_):
        if eng == "scalar":
            return self.act(out, in_, AF.Copy)
        return self.kb.op(eng, lambda e: e.tensor_copy(out=out.ap, in_=in_.ap), reads=in_.keys, writes=out.keys)

    def memset(self, eng, out, val):
        return self.kb.op(eng, lambda e: e.memset(out.ap, val), writes=out.keys)

    def recip(self, out, in_):
        return self.kb.op("vector", lambda e: e.reciprocal(out=out.ap, in_=in_.ap), reads=in_.keys, writes=out.keys)

    def scan(self, out, d0, d1, init, op0, op1):
        return self.kb.op("vector", lambda e: e.tensor_tensor_scan(out=out.ap, data0=d0.ap, data1=d1.ap, initial=init, op0=op0, op1=op1),
                          reads=self._k([d0, d1]), writes=out.keys)

    def aselect(self, out, in_, pattern, cmp, fill, base, cm):
        return self.kb.op("gpsimd", lambda e: e.affine_select(out=out.ap, in_=in_.ap, pattern=pattern, compare_op=cmp,
                                                               fill=fill, base=base, channel_multiplier=cm),
                          reads=in_.keys, writes=out.keys)

    def dma_in(self, eng, out, in_ap, **kw):
        return self.kb.op(eng, lambda e: e.dma_start(out=out.ap, in_=in_ap, **kw), writes=out.keys, dma=True)

    def dma_out(self, eng, out_ap, in_, final=True):
        t = self.kb.op(eng, lambda e: e.dma_start(out=out_ap, in_=in_.ap), reads=in_.keys, dma=True)
        if final:
            self.out_tokens.append(t)
        return t

    def finish(self):
        self.kb.wait_tokens("sync", self.out_tokens)
        self.kb.emit()


S_LEN = 8192
D = 1024
TB = 512
NCH = TB // 128
GN_EPS = 64e-5
RMS_EPS = 1e-5
PROJ = [("r", 0, 256, 0), ("k", 2, 256, 256), ("v", 3, 256, 512), ("w1", 1, 64, 768), ("a1", 4, 64, 832), ("g1", 5, 160, 896)]
NCOL = 1056


BACK_W = 2


def build_phase1(n_tok=S_LEN, stop=None, stopargs=()):
    nc = bass.Bass("TRN2", target_bir_lowering=False)
    dr = lambda n, s, d=F32: nc.dram_tensor(n, s, d, kind="ExternalInput").ap()
    x_d = dr("x", [n_tok, D])
    gm_d = dr("gm", [128, 7, 8])
    wcat_d = dr("wcat", [D, NCOL])
    w2_d = dr("w2", [64, 256])
    a2_d = dr("a2", [64, 256])
    g2_d = dr("g2", [160, 256])
    vec_d = dr("vec", [128, 7, 2])
    yg_d = nc.dram_tensor("yg", [256, n_tok], BF16, kind="ExternalOutput").ap()
    dbg_d = nc.dram_tensor("dbg", [128, 2560], F32, kind="ExternalOutput").ap() if stop else None
    nblk = n_tok // TB

    class Stop(Exception):
        pass

    with ExitStack() as st:
        g = G(nc, st)
        sb = g.sb
        dbgt = sb("dbgt", [128, 2560], F32) if stop else None
        dbg_off = [0]

        def dump(v, n):
            o = dbg_off[0]
            p = v.ap.shape[0]
            g.copy("vector", dbgt.w(dbgt.t[0:p, o:o + n]), v)
            dbg_off[0] = o + n

        def chk(name):
            if stop == name:
                raise Stop()
        def emit_all():
            identf = sb("identf", [128, 128], F32)
            ident = sb("ident", [128, 128], BF16)
            ident4 = sb("ident4", [128, 4, 128], BF16)
            onesblk = sb("onesblk", [128, 128], F32)
            m_su = sb("m_su", [128, 4, 128], BF16)
            m_u = sb("m_u", [128, 4, 128], BF16)
            m_sl = sb("m_sl", [128, 4, 128], BF16)
            rmask = sb("rmask", [128, TB], F32)
            g.memset("gpsimd", identf.v(), 0.0)
            g.aselect(identf.v(), identf.v(), [[-1, 128]], ALU.not_equal, 1.0, 0, 1)
            g.copy("vector", ident.v(), identf.v())
            for h in range(4):
                g.copy("vector", ident4.v(np.s_[:, h, :]), identf.v())
            g.memset("gpsimd", onesblk.v(), 0.0)
            g.memset("gpsimd", onesblk.v(np.s_[0:64, 0:64]), 1.0)
            g.memset("gpsimd", onesblk.v(np.s_[64:128, 64:128]), 1.0)
            for m, cm, pat, cmp in ((m_su, -1, 1, ALU.is_gt), (m_u, -1, 1, ALU.is_ge), (m_sl, 1, -1, ALU.is_gt)):
                g.memset("gpsimd", m.v(), 1.0)
                g.aselect(m.v(), m.v(), [[0, 4], [pat, 128]], cmp, 0.0, 0, cm)
            g.memset("gpsimd", rmask.v(), 1.0)
            g.memset("gpsimd", rmask.w(rmask.t[:].rearrange("p (c t) -> p c t", t=128)[:, :, 0:1]), 0.0)

            if stop == "const":
                dump(identf.v(), 128); dump(m_su.v(np.s_[:, 1, :]), 128); dump(m_sl.v(np.s_[:, 2, :]), 128); dump(m_u.v(np.s_[:, 3, :]), 128)
                dump(onesblk.v(), 128); dump(rmask.v(), 512)
            chk("const")
            gm = sb("gm", [128, 7, 8], F32)
            coefA = sb("coefA", [128, 6, 8], F32)
            coefB = sb("coefB", [128, 6, 8], F32)
            vec = sb("vec", [128, 7, 2], F32)
            WA = sb("WA", [128, 8, NCOL], BF16)
            WB = sb("WB", [128, 8, NCOL], BF16)
            w2b = sb("w2b", [64, 256], BF16)
            a2b = sb("a2b", [128, 256], BF16)
            g2a = sb("g2a", [128, 256], BF16)
            g2b = sb("g2b", [32, 256], BF16)
            xin = sb("xin", [128, 4, 1024], F32)
            g.dma_in("sync", gm.v(), gm_d[:, :, :])
            g.dma_in("sync", vec.v(), vec_d[:, :, :])
            g.dma_in("gpsimd", w2b.v(), w2_d[:, :])
            g.dma_in("gpsimd", a2b.v(np.s_[64:128, :]), a2_d[:, :])
            g.dma_in("gpsimd", g2a.v(), g2_d[0:128, :])
            g.dma_in("gpsimd", g2b.v(), g2_d[128:160, :])
            g0b = gm.w(gm.t[:, 0:1, :].to_broadcast([128, 6, 8]))
            g.tt("vector", coefB.v(), gm.v(np.s_[:, 1:7, :]), g0b, ALU.mult)
            g.tt("vector", coefA.v(), g0b, coefB.v(), ALU.subtract)
            stage = xin
            wv = wcat_d.rearrange("(kc p) n -> p kc n", p=128)
            XK = [("xin", j) for j in range(4)]
            for (nm, mi, ncol, off) in PROJ:
                sv = stage.t[:].rearrange("p a b -> p (a b)")[:, 0:8 * ncol].rearrange("p (k n) -> p k n", k=8)
                g.kb.op("sync", lambda e, sv=sv, off=off, ncol=ncol: e.dma_start(out=sv, in_=wv[:, :, off:off + ncol]), writes=XK, dma=True)
                ca = coefA.w(coefA.t[:, mi, :].unsqueeze(2).to_broadcast([128, 8, ncol]))
                cb = coefB.w(coefB.t[:, mi, :].unsqueeze(2).to_broadcast([128, 8, ncol]))
                g.tt("vector", WA.v(np.s_[:, :, off:off + ncol]), V(sv, XK), ca, ALU.mult)
                g.tt("gpsimd", WB.v(np.s_[:, :, off:off + ncol]), V(sv, XK), cb, ALU.mult)

            if stop == "weights":
                dump(WA.v(np.s_[:, 3, 0:512]), 512); dump(WB.v(np.s_[:, 7, 544:1056]), 512); dump(coefA.v(np.s_[:, 2, :]), 8)
            chk("weights")
            ms = sb("ms", [128, 4], F32)
            xn = [sb(f"xn{i}", [128, 1024], BF16) for i in range(2)]
            hT = sb("hT", [128, 8, TB + 1], BF16)
            T = {n: sb("t_" + n, [128, TB], F32) for n in
                 ("rT", "kraw", "aT", "sgw", "cs", "Er", "En", "kk", "kp", "beta", "t1", "t2", "EC")}
            junk = Buf(T["EC"].t[:].bitcast(BF16), "t_EC")
            tw = sb("tw", [128, TB], BF16)
            sg1a = sb("sg1a", [128, TB], BF16)
            sg1b = sb("sg1b", [32, TB], BF16)
            vT = sb("vT", [128, 2, TB], F32)
            gT2 = [sb(f"gT{i}", [128, 2, TB], F32) for i in range(2)]
            bonusT2 = [sb(f"bonusT{i}", [128, 2, TB], F32) for i in range(2)]
            PCs2 = [sb(f"PCs{i}", [128, 2, NCH], F32) for i in range(2)]
            rt2 = [sb(f"rt_{i}", [128, 2, TB], BF16) for i in range(2)]
            kt2 = [sb(f"kt_{i}", [128, 2, TB], BF16) for i in range(2)]
            at2 = [sb(f"at_{i}", [128, 2, TB], BF16) for i in range(2)]
            bt2 = [sb(f"bt_{i}", [128, 2, TB], BF16) for i in range(2)]
            T2 = {n: sb("t2_" + n, [128, TB], F32) for n in ("o1", "o2")}
            khT = sb("khT", [128, 2, TB], BF16)
            bhT = sb("bhT", [128, 2, TB], BF16)
            XR2 = [sb(f"XR{i}", [128, NCH, 4, 128], BF16) for i in range(2)]
            Kpad2 = [sb(f"Kpad{i}", [128, NCH, 4, 128], BF16) for i in range(2)]
            Bpad2 = [sb(f"Bpad{i}", [128, NCH, 4, 128], BF16) for i in range(2)]
            Vpad2 = [sb(f"Vpad{i}", [128, NCH, 4, 128], BF16) for i in range(2)]
            NSL = 2
            Pb = [[sb(f"P{s}{i}", [128, 4, 128], BF16) for i in range(2)] for s in range(NSL)]
            SUt = [sb(f"SU{s}", [128, 2, 4, 128], BF16) for s in range(NSL)]
            UUt = [sb(f"UU{s}", [128, 2, 4, 128], BF16) for s in range(NSL)]
            PTb = [[Buf(SUt[s].t[:, 0], f"PT{s}0"), sb(f"PT{s}1", [128, 4, 128], BF16)] for s in range(NSL)]
            NTb = [[sb(f"NT{s}{i}", [128, 4, 128], BF16) for i in range(2)] for s in range(NSL)]
            AakT = [Buf(SUt[s].t[:, 1], f"AakT{s}") for s in range(NSL)]
            ArbT = [Buf(UUt[s].t[:, 0], f"ArbT{s}") for s in range(NSL)]
            ArkT = [Buf(UUt[s].t[:, 1], f"ArkT{s}") for s in range(NSL)]
            Apad = [sb(f"Apad{s}", [128, 4, 128], BF16) for s in range(NSL)]
            Wpad = [sb(f"Wpad{s}", [128, 4, 128], BF16) for s in range(NSL)]
            TTbd = [sb(f"TTbd{s}", [128, 2, 128], BF16) for s in range(NSL)]
            RhT = [sb(f"RhT{s}", [128, 2, 128], BF16) for s in range(NSL)]
            Sbd = [sb(f"Sbd{i}", [128, 2, 128], BF16) for i in range(2)]
            yraw = sb("yraw", [128, 2, TB], F32)
            ygo = sb("ygo", [128, 2, TB], BF16)
            for b_ in Kpad2 + Bpad2 + Vpad2:
                g.memset("gpsimd", b_.v(), 0.0)
            for s in range(NSL):
                g.memset("gpsimd", Apad[s].v(), 0.0)
                g.memset("gpsimd", Wpad[s].v(), 0.0)
            g.memset("gpsimd", Sbd[0].v(), 0.0)
            g.memset("vector", hT.v(np.s_[:, :, 0:1]), 0.0)
            s_cur = 0

            def bf(bank):
                return bank.t[:].bitcast(BF16)

            def hsl(h, j):
                cc, q = divmod(h, 2)
                return np.s_[q * 64:(q + 1) * 64, cc, j * 128:(j + 1) * 128]

            def load_x(blk):
                t0 = blk * TB
                for j in range(4):
                    g.dma_in("sync", xin.v(np.s_[:, j, :], sub=j), x_d[t0 + j * 128:t0 + (j + 1) * 128, :])

            def front(blk):
                F = blk % 2
                rt_, kt_, at_, bt_ = rt2[F], kt2[F], at2[F], bt2[F]
                XR, Kpad, Bpad, Vpad, PCs, gT, bonusT = XR2[F], Kpad2[F], Bpad2[F], Vpad2[F], PCs2[F], gT2[F], bonusT2[F]
                for j in range(4):
                    g.act(junk.v(), xin.v(np.s_[:, j, :], sub=j), AF.Square, scale=1.0 / 32, accum=ms.v(np.s_[:, j:j + 1]))
                g.ts("vector", ms.v(), ms.v(), RMS_EPS, ALU.add)
                g.act(ms.v(), ms.v(), AF.Sqrt)
                g.recip(ms.v(), ms.v())
                if blk > 0:
                    g.copy("vector", hT.v(np.s_[:, :, 0:1]), hT.v(np.s_[:, :, TB:TB + 1]))
                yield
                for j in range(4):
                    xj = xn[j % 2]
                    g.act(xj.v(), xin.v(np.s_[:, j, :], sub=j), AF.Copy, scale=ms.v(np.s_[:, j:j + 1]))
                    bk = g.bank()
                    bv = bf(bk)
                    for kc in range(8):
                        g.tr(bk.w(bv[:, kc * 128:(kc + 1) * 128]), xj.v(np.s_[:, kc * 128:(kc + 1) * 128]), ident.v())
                    g.copy("vector" if j % 2 == 0 else "scalar", hT.v(np.s_[:, :, 1 + 128 * j:1 + 128 * (j + 1)]),
                           bk.w(bv.rearrange("p (k t) -> p k t", k=8)))
                    yield
                if blk + 1 < nblk:
                    load_x(blk + 1)

                def proj_fm(off, ncol):
                    bk = g.bank()
                    for kc in range(8):
                        g.mm(bk.v(np.s_[0:ncol, :]), WA.v(np.s_[:, kc, off:off + ncol]), hT.v(np.s_[:, kc, 1:TB + 1]),
                             start=(kc == 0), stop=False)
                        g.mm(bk.v(np.s_[0:ncol, :]), WB.v(np.s_[:, kc, off:off + ncol]), hT.v(np.s_[:, kc, 0:TB]),
                             start=False, stop=(kc == 7))
                    return bk

                bk = proj_fm(768, 128)
                g.act(tw.v(np.s_[0:64, :]), bk.v(np.s_[0:64, :]), AF.Tanh)
                g.copy("vector", tw.v(np.s_[64:128, :]), bk.v(np.s_[64:128, :]))
                yield
                bk = proj_fm(896, 128)
                g.act(sg1a.v(), bk.v(), AF.Sigmoid)
                yield
                bk = proj_fm(1024, 32)
                g.act(sg1b.v(), bk.v(np.s_[0:32, :]), AF.Sigmoid)
                yield
                for j in range(4):
                    bk = g.bank()
                    for kc in range(8):
                        g.mm(bk.v(np.s_[:, 0:256]), hT.v(np.s_[:, kc, 1 + 128 * j:1 + 128 * (j + 1)]), WA.v(np.s_[:, kc, 512:768]),
                             start=(kc == 0), stop=False)
                        g.mm(bk.v(np.s_[:, 0:256]), hT.v(np.s_[:, kc, 128 * j:128 * (j + 1)]), WB.v(np.s_[:, kc, 512:768]),
                             start=False, stop=(kc == 7))
                    pv = bk.t[:, 0:256].rearrange("p (h c) -> p h c", h=4)
                    g.copy("vector", Vpad.v(np.s_[:, j, 0::2, 0:64]), bk.w(pv[:, 0::2, :]))
                    g.copy("scalar", Vpad.v(np.s_[:, j, 1::2, 64:128]), bk.w(pv[:, 1::2, :]))
                    yield

                for cc in range(2):
                    bk = proj_fm(0 + cc * 128, 128)
                    g.copy("scalar", T["rT"].v(), bk.v())
                    yield
                    bk = proj_fm(256 + cc * 128, 128)
                    g.copy("vector", T["kraw"].v(), bk.v())
                    yield
                    bk = proj_fm(512 + cc * 128, 128)
                    g.copy("scalar", vT.v(np.s_[:, cc, :]), bk.v())
                    yield
                    bk = g.bank()
                    g.mm(bk.v(), w2b.v(np.s_[0:64, cc * 128:(cc + 1) * 128]), tw.v(np.s_[0:64, :]))
                    g.act(T["sgw"].v(), bk.v(), AF.Sigmoid, bias=vec.v(np.s_[:, 0, cc:cc + 1]))
                    bk = g.bank()
                    g.mm(bk.v(), a2b.v(np.s_[64:128, cc * 128:(cc + 1) * 128]), tw.v(np.s_[64:128, :]))
                    g.act(T["aT"].v(), bk.v(), AF.Sigmoid, bias=vec.v(np.s_[:, 1, cc:cc + 1]))
                    bk = g.bank()
                    g.mm(bk.v(), g2a.v(np.s_[:, cc * 128:(cc + 1) * 128]), sg1a.v(), start=True, stop=False)
                    g.mm(bk.v(), g2b.v(np.s_[0:32, cc * 128:(cc + 1) * 128]), sg1b.v(np.s_[0:32, :]), start=False, stop=True)
                    g.copy("vector", gT.v(np.s_[:, cc, :]), bk.v())
                    yield

                    kk, kp, beta, t1, t2 = T["kk"], T["kp"], T["beta"], T["t1"], T["t2"]
                    Er, En, EC, cs = T["Er"], T["En"], T["EC"], T["cs"]
                    g.ts("gpsimd", kk.v(), T["kraw"].v(), vec.v(np.s_[:, 2, cc:cc + 1]), ALU.mult)
                    g.tt("gpsimd", t1.v(), kk.v(), kk.v(), ALU.mult)
                    bk = g.bank()
                    g.mm(bk.v(), onesblk.v(), t1.v())
                    g.scan(cs.v(), rmask.v(), T["sgw"].v(), 0.0, ALU.mult, ALU.add)
                    yield
                    g.ts("vector", t2.v(), bk.v(), 1e-24, ALU.max)
                    g.act(t2.v(), t2.v(), AF.Sqrt)
                    g.act(Er.v(), cs.v(), AF.Exp, scale=NEG_C)
                    g.act(En.v(), cs.v(), AF.Exp, scale=-NEG_C)
                    g.recip(t2.v(), t2.v())
                    yield
                    g.tt("gpsimd", kk.v(), kk.v(), t2.v(), ALU.mult)
                    g.ts("vector", t1.v(), T["aT"].v(), -1.0, ALU.add, vec.v(np.s_[:, 3, cc:cc + 1]), ALU.mult)
                    g.tt("gpsimd", beta.v(), kk.v(), T["aT"].v(), ALU.mult)
                    g.stt("vector", kp.v(), t1.v(), 1.0, T["kraw"].v(), ALU.add, ALU.mult)
                    yield
                    Er3 = Er.t[:].rearrange("p (c t) -> p c t", t=128)
                    g.copy("vector", PCs.v(np.s_[:, cc, :]), Er.w(Er3[:, :, 127]))
                    g.tt("vector", EC.w(EC.t[:].rearrange("p (c t) -> p c t", t=128)),
                         En.w(En.t[:].rearrange("p (c t) -> p c t", t=128)),
                         Er.w(Er3[:, :, 127:128].to_broadcast([128, NCH, 128])), ALU.mult)
                    g.tt("gpsimd", kt_.v(np.s_[:, cc, :]), kp.v(), En.v(), ALU.mult)
                    g.tt("gpsimd", rt_.v(np.s_[:, cc, :]), T["rT"].v(), Er.v(), ALU.mult)
                    yield
                    g.tt("gpsimd", bt_.v(np.s_[:, cc, :]), beta.v(), En.v(), ALU.mult)
                    g.tt("gpsimd", khT.v(np.s_[:, cc, :]), kp.v(), EC.v(), ALU.mult)
                    g.tt("gpsimd", bhT.v(np.s_[:, cc, :]), beta.v(), EC.v(), ALU.mult)
                    kk3 = kk.t[:].rearrange("p (c t) -> p c t", t=128)
                    at3 = at_.t[:, cc, :].rearrange("p (c t) -> p c t", t=128)
                    g.stt("vector", at_.w(at3[:, :, 1:128]), kk.w(kk3[:, :, 1:128]), -1.0, Er.w(Er3[:, :, 0:127]), ALU.mult, ALU.mult)
                    g.ts("vector", at_.w(at3[:, :, 0:1]), kk.w(kk3[:, :, 0:1]), -1.0, ALU.mult)
                    yield
                    g.stt("vector", t1.v(), T["rT"].v(), vec.v(np.s_[:, 4, cc:cc + 1]), kp.v(), ALU.mult, ALU.mult)
                    bk = g.bank()
                    g.mm(bk.v(), onesblk.v(), t1.v())
                    g.tt("vector", bonusT.v(np.s_[:, cc, :]), vT.v(np.s_[:, cc, :]), bk.v(), ALU.mult)
                    yield

                for (src, kind) in ((at_, "A"), (khT, "K"), (bhT, "B")):
                    bk = g.bank()
                    bv = bf(bk)
                    for j in range(NCH):
                        for cc in range(2):
                            sl = (j * 2 + cc) * 128
                            g.tr(bk.w(bv[:, sl:sl + 128]), src.v(np.s_[:, cc, j * 128:(j + 1) * 128]), ident.v())
                    if kind == "A":
                        g.copy("vector", XR.v(np.s_[:, :, :, 0:64]),
                               bk.w(bv.rearrange("p (j h c) -> p j h c", j=NCH, h=4)))
                    else:
                        dst = Kpad if kind == "K" else Bpad
                        b5 = bv.rearrange("p (j c q d) -> p j c q d", j=NCH, c=2, q=2)
                        g.copy("vector", dst.v(np.s_[:, :, 0::2, 0:64]), bk.w(b5[:, :, :, 0, :]))
                        g.copy("scalar", dst.v(np.s_[:, :, 1::2, 64:128]), bk.w(b5[:, :, :, 1, :]))
                    yield

            def back(blk):
                nonlocal s_cur
                t0 = blk * TB
                F = blk % 2
                rt_, kt_, at_, bt_ = rt2[F], kt2[F], at2[F], bt2[F]
                XR, Kpad, Bpad, Vpad, PCs, gT, bonusT = XR2[F], Kpad2[F], Bpad2[F], Vpad2[F], PCs2[F], gT2[F], bonusT2[F]

                def pre_a(j, s):
                    def grp(specs, msk, dsts_t, dkeys):
                        bks = [g.bank(), g.bank()]
                        for si, (lh, rh) in enumerate(specs):
                            for h in range(4):
                                cc, q = divmod(h, 2)
                                sl = (si * 2 + cc) * 128
                                g.mm(bks[q].v(np.s_[:, sl:sl + 128]), lh.v(hsl(h, j)), rh.v(hsl(h, j)))
                        n = len(specs)
                        for q in range(2):
                            if n == 2:
                                src = bks[q].w(bks[q].t[:].rearrange("p (a c t) -> p a c t", a=2, c=2))
                                dst = V(dsts_t[:, :, q::2, :], dkeys)
                                mk = msk.w(msk.t[:].rearrange("p (a c) t -> p a c t", a=2))
                            else:
                                src = bks[q].w(bks[q].t[:, 0:256].rearrange("p (c t) -> p c t", c=2))
                                dst = V(dsts_t[:, q::2, :], dkeys)
                                mk = msk.v(np.s_[:, 0:2, :])
                            g.tt("vector", dst, src, mk, ALU.mult)
                    grp([(bt_, at_), (kt_, at_)], m_su, SUt[s].t, [(PTb[s][0].name, None), (AakT[s].name, None)])
                    yield
                    grp([(bt_, rt_), (kt_, rt_)], m_u, UUt[s].t, [(ArbT[s].name, None), (ArkT[s].name, None)])
                    yield
                    grp([(at_, bt_)], m_sl, Pb[s][0].t, [(Pb[s][0].name, None)])
                    g.tt("gpsimd", NTb[s][0].v(), PTb[s][0].v(), ident4.v(), ALU.add)
                    yield

                def pre_dbl(s, it):
                    cur, nxt = it % 2, (it + 1) % 2
                    bk = g.bank()
                    for h in range(4):
                        g.mm(bk.v(np.s_[:, h * 128:(h + 1) * 128]), PTb[s][cur].v(np.s_[:, h, :]), Pb[s][cur].v(np.s_[:, h, :]))
                    g.copy("scalar", Pb[s][nxt].v(), bk.w(bk.t[:].rearrange("p (h t) -> p h t", h=4)))
                    if it < 5:
                        bk = g.bank()
                        for h in range(4):
                            g.mm(bk.v(np.s_[:, h * 128:(h + 1) * 128]), Pb[s][cur].v(np.s_[:, h, :]), PTb[s][cur].v(np.s_[:, h, :]))
                        g.copy("vector", PTb[s][nxt].v(), bk.w(bk.t[:].rearrange("p (h t) -> p h t", h=4)))
                    yield
                    bk = g.bank()
                    for h in range(4):
                        o = bk.v(np.s_[:, h * 128:(h + 1) * 128])
                        g.mm(o, ident.v(), NTb[s][cur].v(np.s_[:, h, :]), start=True, stop=False)
                        g.mm(o, Pb[s][nxt].v(np.s_[:, h, :]), NTb[s][cur].v(np.s_[:, h, :]), start=False, stop=True)
                    eng = "scalar" if it % 2 == 0 else "vector"
                    g.copy(eng, NTb[s][nxt].v(), bk.w(bk.t[:].rearrange("p (h t) -> p h t", h=4)))
                    yield

                def pre_b(j, s):
                    NTf = NTb[s][0]
                    bk = g.bank()
                    for h in range(4):
                        q = h % 2
                        g.mm(bk.v(np.s_[:, h * 64:(h + 1) * 64]), AakT[s].v(np.s_[:, h, :]), Vpad.v(np.s_[:, j, h, q * 64:(q + 1) * 64]))
                    g.copy("vector", XR.v(np.s_[:, j, :, 64:128]), bk.w(bk.t[:, 0:256].rearrange("p (h c) -> p h c", h=4)))
                    yield
                    bk = g.bank()
                    for h in range(4):
                        g.mm(bk.v(np.s_[:, h * 128:(h + 1) * 128]), NTf.v(np.s_[:, h, :]), XR.v(np.s_[:, j, h, :]))
                    x4 = bk.t[:].rearrange("p (h c) -> p h c", h=4)
                    g.copy("vector", Apad[s].v(np.s_[:, 0::2, 0:64]), bk.w(x4[:, 0::2, 0:64]))
                    g.copy("scalar", Apad[s].v(np.s_[:, 1::2, 64:128]), bk.w(x4[:, 1::2, 0:64]))
                    g.copy("vector", Wpad[s].v(np.s_[:, 0::2, 0:64]), bk.w(x4[:, 0::2, 64:128]))
                    g.copy("scalar", Wpad[s].v(np.s_[:, 1::2, 64:128]), bk.w(x4[:, 1::2, 64:128]))
                    yield
                    bk = g.bank()
                    for cc in range(2):
                        o = bk.v(np.s_[:, cc * 128:(cc + 1) * 128])
                        for q in range(2):
                            h = 2 * cc + q
                            g.mm(o, Apad[s].v(np.s_[:, h, :]), Bpad.v(np.s_[:, j, h, :]), start=(q == 0), stop=(q == 1))
                    for cc in range(2):
                        g.stt("vector", TTbd[s].v(np.s_[:, cc, :]), identf.v(), PCs.v(np.s_[:, cc, j:j + 1]),
                              bk.v(np.s_[:, cc * 128:(cc + 1) * 128]), ALU.mult, ALU.add)
                    bk = g.bank()
                    for cc in range(2):
                        o = bk.v(np.s_[:, cc * 128:(cc + 1) * 128])
                        for q in range(2):
                            h = 2 * cc + q
                            g.mm(o, Apad[s].v(np.s_[:, h, :]), ArbT[s].v(np.s_[:, h, :]), start=(q == 0), stop=(q == 1))
                    g.tt("vector", RhT[s].v(), bk.w(bk.t[:, 0:256].rearrange("p (c t) -> p c t", c=2)),
                         rt_.v(np.s_[:, :, j * 128:(j + 1) * 128]), ALU.add)
                    yield

                def seq(j, s):
                    nonlocal s_cur
                    Sc, Sn = Sbd[s_cur], Sbd[1 - s_cur]
                    bk = g.bank()
                    for cc in range(2):
                        o = bk.v(np.s_[:, cc * 128:(cc + 1) * 128])
                        g.mm(o, TTbd[s].v(np.s_[:, cc, :]), Sc.v(np.s_[:, cc, :]), start=True, stop=False)
                        for q in range(2):
                            h = 2 * cc + q
                            g.mm(o, Bpad.v(np.s_[:, j, h, :]), Wpad[s].v(np.s_[:, h, :]), start=False, stop=False)
                            g.mm(o, Kpad.v(np.s_[:, j, h, :]), Vpad.v(np.s_[:, j, h, :]), start=False, stop=(q == 1))
                    g.copy("vector", Sn.v(), bk.w(bk.t[:, 0:256].rearrange("p (c t) -> p c t", c=2)))
                    bk = g.bank()
                    for cc in range(2):
                        o = bk.v(np.s_[:, cc * 128:(cc + 1) * 128])
                        g.mm(o, Sc.v(np.s_[:, cc, :]), RhT[s].v(np.s_[:, cc, :]), start=True, stop=False)
                        for q in range(2):
                            h = 2 * cc + q
                            g.mm(o, Wpad[s].v(np.s_[:, h, :]), ArbT[s].v(np.s_[:, h, :]), start=False, stop=False)
                            g.mm(o, Vpad.v(np.s_[:, j, h, :]), ArkT[s].v(np.s_[:, h, :]), start=False, stop=(q == 1))
                    g.copy("scalar", yraw.v(np.s_[:, :, j * 128:(j + 1) * 128]), bk.w(bk.t[:, 0:256].rearrange("p (c t) -> p c t", c=2)))
                    s_cur = 1 - s_cur
                    yield

                def rr(gens):
                    gens = list(gens)
                    while gens:
                        for gn in list(gens):
                            try:
                                next(gn)
                            except StopIteration:
                                gens.remove(gn)
                            yield

                for jp in range(0, NCH, NSL):
                    yield from rr([pre_a(jp + s, s) for s in range(NSL)])
                    for it in range(6):
                        yield from rr([pre_dbl(s, it) for s in range(NSL)])
                    yield from rr([pre_b(jp + s, s) for s in range(NSL)])
                    for s in range(NSL):
                        yield from seq(jp + s, s)

                for cc in range(2):
                    t1, t2 = T2["o1"], T2["o2"]
                    yr = yraw.v(np.s_[:, cc, :])
                    bk = g.bank()
                    g.mm(bk.v(), onesblk.v(), yr)
                    g.stt("vector", t1.v(), bk.v(), -1.0 / 64, yr, ALU.mult, ALU.add)
                    g.tt("gpsimd", t2.v(), t1.v(), t1.v(), ALU.mult)
                    yield
                    bk = g.bank()
                    g.mm(bk.v(), onesblk.v(), t2.v())
                    g.ts("vector", t2.v(), bk.v(), 1.0 / 64, ALU.mult, GN_EPS, ALU.add)
                    g.act(t2.v(), t2.v(), AF.Sqrt)
                    g.recip(t2.v(), t2.v())
                    yield
                    g.tt("gpsimd", t1.v(), t1.v(), t2.v(), ALU.mult)
                    g.ts("vector", t1.v(), t1.v(), vec.v(np.s_[:, 5, cc:cc + 1]), ALU.mult, vec.v(np.s_[:, 6, cc:cc + 1]), ALU.add)
                    g.tt("gpsimd", t1.v(), t1.v(), bonusT.v(np.s_[:, cc, :]), ALU.add)
                    g.tt("vector", ygo.v(np.s_[:, cc, :]), t1.v(), gT.v(np.s_[:, cc, :]), ALU.mult)
                    g.dma_out("sync", yg_d[cc * 128:(cc + 1) * 128, t0:t0 + TB], ygo.v(np.s_[:, cc, :]))
                    yield

            def drive(gens, weights=None):
                gens = list(gens)
                weights = list(weights or [1] * len(gens))
                while gens:
                    for gn, w in list(zip(gens, weights)):
                        for _ in range(w):
                            try:
                                next(gn)
                            except StopIteration:
                                k = gens.index(gn)
                                gens.pop(k)
                                weights.pop(k)
                                break

            load_x(0)
            drive([front(0)])
            for blk in range(nblk):
                gs = [back(blk)]
                if blk + 1 < nblk:
                    gs.append(front(blk + 1))
                drive(gs, [BACK_W, 1])
        try:
            emit_all()
        except Stop:
            pass
        if stop:
            g.dma_out("sync", dbg_d[:, :], dbgt.v())
        g.finish()
    return nc


def phase1_inputs(inp, core):
    b, hg = divmod(core, 4)
    cs = slice(hg * 256, (hg + 1) * 256)
    pm = lambda v: np.ascontiguousarray(v.reshape(-1, 128).T)
    gm = np.stack([pm(inp["norm_mix_g"][0])] + [pm(inp["rwkv_mu"][0, i]) for i in range(6)], axis=1)
    wcat = np.concatenate([inp["rwkv_w_r"][0][:, cs], inp["rwkv_w_k"][0][:, cs], inp["rwkv_w_v"][0][:, cs],
                           inp["rwkv_w1"][0], inp["rwkv_a1"][0], inp["rwkv_g1"][0]], axis=1)
    vecs = [inp["rwkv_w0"][0][cs], inp["rwkv_a0"][0][cs], inp["rwkv_k_k"][0][cs], inp["rwkv_k_a"][0][cs],
            inp["rwkv_r_k"][0].reshape(-1)[cs], inp["rwkv_ln_w"][0][cs], inp["rwkv_ln_b"][0][cs]]
    vec = np.stack([pm(v) for v in vecs], axis=1)
    return {
        "x": np.ascontiguousarray(inp["x"][b]),
        "gm": np.ascontiguousarray(gm, dtype=np.float32),
        "wcat": np.ascontiguousarray(wcat, dtype=np.float32),
        "w2": np.ascontiguousarray(inp["rwkv_w2"][0][:, cs]),
        "a2": np.ascontiguousarray(inp["rwkv_a2"][0][:, cs]),
        "g2": np.ascontiguousarray(inp["rwkv_g2"][0][:, cs]),
        "vec": np.ascontiguousarray(vec, dtype=np.float32),
    }


NT2 = 17
NTOK2 = NT2 * 128
DFF = 4096
HG = 512
NGRP = DFF // HG
NEG_BIG = -30000.0
TWO_PI = 6.283185307179586
PI = 3.141592653589793
C1_2PI = 6.28125
C2_2PI = TWO_PI - 6.28125


def build_phase2():
    nc = bass.Bass("TRN2", target_bir_lowering=False)
    dr = lambda n, s, d=F32: nc.dram_tensor(n, s, d, kind="ExternalInput").ap()
    x_d = dr("x", [NTOK2, D])
    yg_d = dr("ygT", [D, NTOK2], BF16)
    pos_d = dr("pos", [1, NTOK2], I32)
    wo0_d = dr("wo0", [D, D])
    win_d = [dr("win0", [D, DFF]), dr("win1", [D, DFF])]
    wout_d = [dr("wout0", [DFF, D]), dr("wout1", [DFF, D])]
    wqkv_d = dr("wqkv", [D, 1280])
    wo1_d = dr("wo1", [D, D])
    gains_d = dr("gains", [128, 3, 8])
    bq_d = dr("bq", [128, 10])
    rowv_d = dr("rowv", [1, 1024 + 1024 + 128 + 16])
    cst_d = dr("cst", [128, 4])
    out_d = nc.dram_tensor("out", [16 * 128, D], F32, kind="ExternalOutput").ap()

    with ExitStack() as st:
        g = G(nc, st)
        sb = g.sb
        identf = sb("identf", [128, 128], F32)
        ident = sb("ident", [128, 128], BF16)
        g.memset("gpsimd", identf.v(), 0.0)
        g.aselect(identf.v(), identf.v(), [[-1, 128]], ALU.not_equal, 1.0, 0, 1)
        g.copy("vector", ident.v(), identf.v())
        rotf = sb("rotf", [128, 128], F32)
        rot = sb("rot", [128, 128], BF16)
        g.memset("gpsimd", rotf.v(), 0.0)
        for blk in range(2):
            o = blk * 64
            sub = rotf.v(np.s_[:, o:o + 32])
            g.aselect(sub, sub, [[-1, 32]], ALU.not_equal, -1.0, -(o + 32), 1)
            sub = rotf.v(np.s_[:, o + 32:o + 64])
            g.aselect(sub, sub, [[-1, 32]], ALU.not_equal, 1.0, -(o + 32) + 32, 1)
        g.copy("vector", rot.v(), rotf.v())
        mscr = sb("scr", [128, 1024], F32)
        maskb = Buf(mscr.t[:, 0:256], "scr_m0")
        mask1 = Buf(mscr.t[:, 256:512], "scr_m1")
        g.memset("gpsimd", maskb.v(), 0.0)
        g.aselect(maskb.v(), maskb.v(), [[1, 256]], ALU.is_gt, NEG_BIG, 0, -1)
        g.aselect(maskb.v(), maskb.v(), [[-1, 256]], ALU.is_ge, NEG_BIG, 128, 1)
        cst = sb("cst", [128, 4], F32)
        g.dma_in("sync", cst.v(), cst_d[:, :])
        g.copy("vector", mask1.v(), maskb.v())
        g.ts("vector", mask1.v(np.s_[:, 0:128]), maskb.v(np.s_[:, 0:128]), cst.v(np.s_[:, 2:3]), ALU.add)
        maskbf = sb("maskbf", [128, 256], BF16)
        mask1bf = sb("mask1bf", [128, 256], BF16)
        g.ts("vector", maskbf.v(), maskb.v(), 8.0, ALU.mult)
        g.ts("vector", mask1bf.v(), mask1.v(), 8.0, ALU.mult)
        ones_row = sb("ones_row", [1, 128], F32)
        g.memset("gpsimd", ones_row.v(), 1.0)
        gains = sb("gains", [128, 3, 8], F32)
        g.dma_in("sync", gains.v(), gains_d[:, :, :])
        bq = sb("bq", [128, 10], F32)
        g.dma_in("sync", bq.v(), bq_d[:, :])
        rowv = sb("rowv", [1, 1024], F32)
        g.dma_in("sync", rowv.v(), rowv_d[0:1, 1024:2048])
        bvb = sb("bvb", [128, 128], F32)
        sinkb = sb("sinkb", [128, 16], F32)
        g.dma_in("sync", bvb.v(), rowv_d[0:1, 2048:2176].to_broadcast([128, 128]))
        g.dma_in("sync", sinkb.v(), rowv_d[0:1, 2176:2192].to_broadcast([128, 16]))

        xres = sb("xres", [128, NT2, 1024], F32)
        hT = sb("hT", [128, 8, NTOK2], BF16)
        uT = sb("uT", [128, 4, NTOK2], BF16)
        arena = sb("arena", [128, 18432], BF16)
        junk = sb("junk", [128, 1024], BF16)
        ms = sb("ms", [128, NT2], F32)
        xn = [sb("xn0", [128, 1024], BF16)] * 2
        scr = mscr
        NU = 4
        asc = sb("asc", [128, NU, 2, 256], F32)
        relu_s = [Buf(asc.t[:, i].rearrange("p a b -> p (a b)").bitcast(BF16)[:, 0:512], f"asc_relu{i}") for i in range(2)]

        def bf(bank):
            return bank.t[:].bitcast(BF16)

        NTILES = [(0, 512), (512, 512), (1024, 512), (1536, 512), (2048, 128)]
        NTILES_M = {0: NTILES, 1: [(128, 512), (640, 512), (1152, 512), (1664, 512)]}

        def load_x_and_yg():
            for i in range(NT2):
                g.dma_in("sync", xres.v(np.s_[:, i, :], sub=i), x_d[i * 128:(i + 1) * 128, :])
            for kc in range(8):
                g.dma_in("sync", hT.v(np.s_[:, kc, :]), yg_d[kc * 128:(kc + 1) * 128, :])

        def wload(dst_ap, key, src_ap):
            g.kb.op("gpsimd", lambda e: e.dma_start(out=dst_ap, in_=src_ap), writes=[key], dma=True)

        def rstd_all(tiles):
            allk = [("ms", i) for i in tiles]
            lo, hi = tiles[0], tiles[-1] + 1
            for i in tiles:
                g.act(junk.v(), xres.v(np.s_[:, i, :], sub=i), AF.Square, scale=1.0 / 32, accum=ms.v(np.s_[:, i:i + 1], sub=i))
            mv = V(ms.t[:, lo:hi], allk)
            g.ts("vector", mv, mv, RMS_EPS, ALU.add)
            g.act(mv, mv, AF.Sqrt)
            g.recip(mv, mv)

        def norm_to_hT(gi, tiles):
            tiles = list(tiles)
            rstd_all(tiles)
            for i in tiles:
                xj = xn[i % 2]
                g.act(xj.v(), xres.v(np.s_[:, i, :], sub=i), AF.Copy, scale=ms.v(np.s_[:, i:i + 1], sub=i))
                bk = g.bank()
                bv = bf(bk)
                for kc in range(8):
                    g.tr(bk.w(bv[:, kc * 128:(kc + 1) * 128]), xj.v(np.s_[:, kc * 128:(kc + 1) * 128]), ident.v())
                g.tt("vector", hT.v(np.s_[:, :, i * 128:(i + 1) * 128]), bk.w(bv.rearrange("p (k t) -> p k t", k=8)),
                     gains.w(gains.t[:, gi, :].unsqueeze(2).to_broadcast([128, 8, 128])), ALU.mult)

        def mlp(layer, first_tile=0):
            win, wout = win_d[layer], wout_d[layer]
            ntiles = [(n0, nn) for (n0, nn) in NTILES_M[first_tile]]
            winv = win.rearrange("(kc p) n -> p kc n", p=128)
            woutv = wout.rearrange("(m p) n -> p m n", p=128)
            def WI(s):
                return arena.t[:, s * 4096:(s + 1) * 4096].rearrange("p (k n) -> p k n", k=8)

            def WO(s):
                return arena.t[:, 8192 + s * 4096:8192 + (s + 1) * 4096].rearrange("p (m n) -> p m n", m=4)

            ri = 0
            for grp in range(NGRP):
                s = grp % 2
                kwi, kwo = ("arena", "wi%d" % s), ("arena", "wo%d" % s)
                wload(WI(s), kwi, winv[:, :, grp * HG:(grp + 1) * HG])
                wload(WO(s), kwo, woutv[:, grp * 4:(grp + 1) * 4, :])
                for m in range(4):
                    for (n0, nn) in ntiles:
                        bk = g.bank()
                        for kc in range(8):
                            g.mm(bk.v(np.s_[:, 0:nn]), V(WI(s)[:, kc, m * 128:(m + 1) * 128], [kwi]), hT.v(np.s_[:, kc, n0:n0 + nn]),
                                 start=(kc == 0), stop=(kc == 7))
                        r = relu_s[ri % 2]
                        ri += 1
                        g.act(V(r.t[:, 0:nn], r.v().keys + [("asc", 0), ("asc", 1)]), bk.v(np.s_[:, 0:nn]), AF.Relu)
                        g.tt("gpsimd", uT.v(np.s_[:, m, n0:n0 + nn]), r.v(np.s_[:, 0:nn]), r.v(np.s_[:, 0:nn]), ALU.mult)
                for i in range(first_tile, NT2):
                    for half in range(2):
                        bk = g.bank()
                        for m in range(4):
                            g.mm(bk.v(), uT.v(np.s_[:, m, i * 128:(i + 1) * 128]), V(WO(s)[:, m, half * 512:(half + 1) * 512], [kwo]),
                                 start=(m == 0), stop=(m == 3))
                        xs = xres.v(np.s_[:, i, half * 512:(half + 1) * 512], sub=i)
                        g.tt("vector", xs, xs, bk.v(), ALU.add)

        load_x_and_yg()
        wo_v = arena.t[:, 0:8192].rearrange("p (k n) -> p k n", k=8)
        g.kb.op("gpsimd", lambda e: e.dma_start(out=wo_v, in_=wo0_d.rearrange("(kc p) n -> p kc n", p=128)),
                writes=[("arena", "wi0"), ("arena", "wi1")], dma=True)
        for i in range(NT2):
            for half in range(2):
                bk = g.bank()
                for kc in range(8):
                    g.mm(bk.v(), hT.v(np.s_[:, kc, i * 128:(i + 1) * 128]),
                         V(wo_v[:, kc, half * 512:(half + 1) * 512], [("arena", "wi0"), ("arena", "wi1")]),
                         start=(kc == 0), stop=(kc == 7))
                xs = xres.v(np.s_[:, i, half * 512:(half + 1) * 512], sub=i)
                g.tt("vector", xs, xs, bk.v(), ALU.add)

        norm_to_hT(0, range(NT2))
        mlp(0)

        norm_to_hT(1, range(NT2))
        wq_v = arena.t[:, 0:10240].rearrange("p (k n) -> p k n", k=8)
        wo1_v = arena.t[:, 10240:18432].rearrange("p (k n) -> p k n", k=8)
        KQ = [("arena", "wi0"), ("arena", "wi1"), ("arena", "wo0")]
        KO = [("arena", "wo0"), ("arena", "wo1"), ("arena", "x")]
        g.kb.op("gpsimd", lambda e: e.dma_start(out=wq_v, in_=wqkv_d.rearrange("(kc p) n -> p kc n", p=128)), writes=KQ, dma=True)
        g.kb.op("gpsimd", lambda e: e.dma_start(out=wo1_v, in_=wo1_d.rearrange("(kc p) n -> p kc n", p=128)), writes=KO, dma=True)
        tabs = uT.t[:].rearrange("p a b -> p (a b)").bitcast(F32)
        cosT = uT.w(tabs[:, 0:NTOK2])
        sinT = uT.w(tabs[:, NTOK2:2 * NTOK2])
        posi = V(scr.t[:, 0:512].bitcast(I32), [("scr", "a"), ("scr_m0", None), ("scr_m1", None)])
        for (n0, nn) in NTILES:
            pch = V(posi.ap[:, 0:nn], posi.keys)
            ach = scr.v(np.s_[:, 512:512 + nn], sub="b")
            g.dma_in("sync", pch, pos_d[0:1, n0:n0 + nn].to_broadcast([128, nn]))
            g.copy("vector", ach, pch)
            g.ts("vector", ach, ach, cst.v(np.s_[:, 0:1]), ALU.mult)
            sch = V(sinT.ap[:, n0:n0 + nn], sinT.keys)
            cch = V(cosT.ap[:, n0:n0 + nn], cosT.keys)
            T1 = V(xn[0].t[:].bitcast(F32)[:, 0:nn], [("xn0", None)])
            A2 = V(junk.t[:].bitcast(F32)[:, 0:nn], [("junk", None)])
            TI = pch
            for (src, dst, shift) in ((ach, sch, 0.0), (ach, cch, 0.5 * PI)):
                if shift:
                    g.ts("vector", A2, src, shift, ALU.add)
                    src = A2
                g.ts("vector", T1, src, 1.0 / TWO_PI, ALU.mult)
                g.copy("vector", TI, T1)
                g.copy("vector", T1, TI)
                g.stt("vector", dst, T1, -C1_2PI, src, ALU.mult, ALU.add)
                g.stt("vector", dst, T1, -C2_2PI, dst, ALU.mult, ALU.add)
                g.ts("vector", dst, dst, -PI, ALU.max, PI, ALU.min)
                g.act(dst, dst, AF.Sin)

        kr = sb("kr", [128, 2, NTOK2], BF16)
        vpad = [sb(f"vpad{i}", [128, 2, 2, 128], BF16) for i in range(3)]
        for v_ in vpad:
            g.memset("gpsimd", v_.v(), 0.0)
        qb16 = sb("qb16", [128, 512], BF16)
        SCALE = 0.125
        negsink = sb("negsink", [128, 16], F32)
        esink = sb("esink", [128, 16], F32)
        g.ts("vector", negsink.v(), sinkb.v(), -1.0, ALU.mult)
        g.act(esink.v(), sinkb.v(), AF.Exp)

        def rope_evac(bk, nn, bias_col, n0, dst, qf, qb, r2):
            g.act(qf, bk.v(np.s_[:, 0:nn]), AF.Identity, bias=bias_col)
            g.copy("gpsimd", qb, qf)
            b2 = g.bank()
            g.mm(b2.v(np.s_[:, 0:nn]), rot.v(), qb)
            g.tt("vector", r2, b2.v(np.s_[:, 0:nn]), V(sinT.ap[:, n0:n0 + nn], sinT.keys), ALU.mult)
            g.tt("gpsimd", qf, qf, V(cosT.ap[:, n0:n0 + nn], cosT.keys), ALU.mult)
            g.tt("vector", dst, qf, r2, ALU.add)

        wkd_ap = asc.t[:].rearrange("p a b c -> p (a b c)")[:, 0:1024].bitcast(BF16).rearrange("p (k j c) -> p k j c", k=8, j=2)
        WK = [("asc", u) for u in range(NU)] + [("asc_relu0", None), ("asc_relu1", None)]
        for j in range(2):
            for dup in range(2):
                g.copy("vector", V(wkd_ap[:, :, j, dup * 64:(dup + 1) * 64], WK), V(wq_v[:, :, 1024 + j * 64:1024 + (j + 1) * 64], KQ))
        for j in range(2):
            for (n0, nn) in NTILES:
                bk = g.bank()
                for kc in range(8):
                    g.mm(bk.v(np.s_[:, 0:nn]), V(wkd_ap[:, kc, j, :], WK), hT.v(np.s_[:, kc, n0:n0 + nn]), start=(kc == 0), stop=(kc == 7))
                rope_evac(bk, nn, bq.v(np.s_[:, 8 + j:9 + j]), n0, kr.v(np.s_[:, j, n0:n0 + nn]),
                          scr.v(np.s_[:, 0:nn], sub="a"), qb16.v(np.s_[:, 0:nn]), scr.v(np.s_[:, 512:512 + nn], sub="b"))

        def make_vpad(i):
            vp = vpad[i % 3]
            bk = g.bank()
            for kc in range(8):
                g.mm(bk.v(np.s_[:, 0:128]), hT.v(np.s_[:, kc, i * 128:(i + 1) * 128]), V(wq_v[:, kc, 1152:1280], KQ), start=(kc == 0), stop=(kc == 7))
            for q2 in range(2):
                g.tt("vector", vp.v(np.s_[:, :, q2, q2 * 64:(q2 + 1) * 64]), bk.w(bk.t[:, 0:128].rearrange("p (j d) -> p j d", j=2)),
                     bvb.w(bvb.t[:].rearrange("p (j d) -> p j d", j=2)), ALU.add)
            return vp

        qr2 = [sb(f"qr{i}", [128, 8, 128], BF16) for i in range(2)]
        oT = sb("oT", [128, 8, 128], BF16)
        pn = [sb(f"pn{i}", [128, 2, 256], BF16) for i in range(NU)]
        pT = [sb(f"pT{i}", [128, 2, 2, 128], BF16) for i in range(NU)]
        stat = [sb(f"stat{i}", [128, 8], F32) for i in range(NU)]
        def prep(i):
            make_vpad(i)
            qr = qr2[i % 2]
            yield
            for hh in range(2):
                bk = g.bank()
                for a4 in range(4):
                    hp = hh * 4 + a4
                    for kc in range(8):
                        g.mm(bk.v(np.s_[:, a4 * 128:(a4 + 1) * 128]), V(wq_v[:, kc, hp * 128:(hp + 1) * 128], KQ),
                             hT.v(np.s_[:, kc, i * 128:(i + 1) * 128]), start=(kc == 0), stop=(kc == 7))
                qf = scr.v(np.s_[:, 0:512], sub="a")
                r2 = scr.v(np.s_[:, 512:1024], sub="b")
                qb = qb16.v()
                qf3 = V(scr.t[:, 0:512].rearrange("p (a t) -> p a t", a=4), [("scr", "a")])
                r23 = V(scr.t[:, 512:1024].rearrange("p (a t) -> p a t", a=4), [("scr", "b")])
                g.tt("vector", qf3, bk.w(bk.t[:].rearrange("p (a t) -> p a t", a=4)),
                     bq.w(bq.t[:, hh * 4:hh * 4 + 4].unsqueeze(2).to_broadcast([128, 4, 128])), ALU.add)
                g.copy("gpsimd", qb, qf)
                b2 = g.bank()
                g.mm(b2.v(), rot.v(), qb)
                sin_b = V(sinT.ap[:, i * 128:(i + 1) * 128].unsqueeze(1).to_broadcast([128, 4, 128]), sinT.keys)
                cos_b = V(cosT.ap[:, i * 128:(i + 1) * 128].unsqueeze(1).to_broadcast([128, 4, 128]), cosT.keys)
                g.tt("vector", r23, b2.w(b2.t[:].rearrange("p (a t) -> p a t", a=4)), sin_b, ALU.mult)
                g.tt("gpsimd", qf3, qf3, cos_b, ALU.mult)
                g.tt("vector", qr.v(np.s_[:, hh * 4:hh * 4 + 4, :]), qf3, r23, ALU.add)
                yield

        def attn(i):
            vps = [vpad[(i - 1) % 3], vpad[i % 3]]
            qr = qr2[i % 2]
            mk = mask1bf if i == 1 else maskbf
            for j in range(2):
                sbank = {}
                for gp in range(2):
                    hp0 = 4 * j + 2 * gp
                    bks = [g.bank(), g.bank()]
                    for a in range(2):
                        for q2 in range(2):
                            o = bks[q2].v(np.s_[:, a * 256:(a + 1) * 256])
                            g.mm(o, ident.v(), mk.v(), start=True, stop=False)
                            g.mm(o, qr.v(np.s_[q2 * 64:(q2 + 1) * 64, hp0 + a, :]),
                                 kr.v(np.s_[q2 * 64:(q2 + 1) * 64, j, (i - 1) * 128:(i + 1) * 128]), start=False, stop=True)
                    for q2 in range(2):
                        sbank[gp * 2 + q2] = bks[q2]
                UN = range(4)
                yield
                for u in UN:
                    gp, q2 = divmod(u, 2)
                    st_ = stat[u]
                    ps3 = sbank[u].w(sbank[u].t[:].rearrange("p (a k) -> p a k", a=2))
                    g.kb.op("vector", lambda e, st_=st_, ps3=ps3: e.tensor_reduce(out=st_.t[:, 0:2], in_=ps3.ap, axis=mybir.AxisListType.X, op=ALU.max),
                            reads=ps3.keys, writes=st_.v().keys)
                    h0 = 2 * (4 * j + 2 * gp) + q2
                    g.stt("vector", st_.v(np.s_[:, 2:4]), st_.v(np.s_[:, 0:2]), -SCALE, negsink.w(negsink.t[:, h0:h0 + 3:2]), ALU.mult, ALU.min)
                yield
                for u in UN:
                    st_ = stat[u]
                    for a in range(2):
                        g.act(V(asc.t[:, u, a, :], [("asc", u)]), sbank[u].v(np.s_[:, a * 256:(a + 1) * 256]), AF.Exp, scale=SCALE,
                              bias=st_.v(np.s_[:, 2 + a:3 + a]), accum=st_.v(np.s_[:, 4 + a:5 + a]))
                    g.act(st_.v(np.s_[:, 6:8]), st_.v(np.s_[:, 2:4]), AF.Exp)
                yield
                for u in UN:
                    gp, q2 = divmod(u, 2)
                    st_ = stat[u]
                    h0 = 2 * (4 * j + 2 * gp) + q2
                    g.tt("vector", st_.v(np.s_[:, 6:8]), st_.v(np.s_[:, 6:8]), esink.w(esink.t[:, h0:h0 + 3:2]), ALU.mult)
                    g.tt("vector", st_.v(np.s_[:, 4:6]), st_.v(np.s_[:, 4:6]), st_.v(np.s_[:, 6:8]), ALU.add)
                    g.recip(st_.v(np.s_[:, 4:6]), st_.v(np.s_[:, 4:6]))
                for u in UN:
                    st_ = stat[u]
                    g.tt("gpsimd", pn[u].v(), V(asc.t[:, u], [("asc", u)]), st_.w(st_.t[:, 4:6].unsqueeze(2).to_broadcast([128, 2, 256])), ALU.mult)
                yield
                tbs = {}
                for u in UN:
                    tb = g.bank()
                    tv = bf(tb)
                    for a in range(2):
                        for kb in range(2):
                            sl = (a * 2 + kb) * 128
                            g.tr(tb.w(tv[:, sl:sl + 128]), pn[u].v(np.s_[:, a, kb * 128:(kb + 1) * 128]), ident.v())
                    tbs[u] = (tb, tv)
                for u in UN:
                    tb, tv = tbs[u]
                    g.copy("scalar" if u % 2 == 0 else "vector", pT[u].v(), tb.w(tv[:, 0:512].rearrange("p (a k q) -> p a k q", a=2, k=2)))
                yield
                for gp in range(2):
                    hp0 = 4 * j + 2 * gp
                    obk = g.bank()
                    for a in range(2):
                        first = True
                        for q2 in range(2):
                            for kb in range(2):
                                g.mm(obk.v(np.s_[:, a * 128:(a + 1) * 128]), vps[kb].v(np.s_[:, j, q2, :]), pT[gp * 2 + q2].v(np.s_[:, a, kb, :]),
                                     start=first, stop=(q2 == 1 and kb == 1))
                                first = False
                    g.copy("scalar", oT.v(np.s_[:, hp0:hp0 + 2, :]), obk.w(obk.t[:, 0:256].rearrange("p (a q) -> p a q", a=2)))
            for half in range(2):
                bk = g.bank()
                for hp in range(8):
                    g.mm(bk.v(), oT.v(np.s_[:, hp, :]), V(wo1_v[:, hp, half * 512:(half + 1) * 512], KO), start=(hp == 0), stop=False)
                g.mm(bk.v(), ones_row.v(), rowv.v(np.s_[0:1, half * 512:(half + 1) * 512]), start=False, stop=True)
                xs = xres.v(np.s_[:, i, half * 512:(half + 1) * 512], sub=i)
                g.tt("vector", xs, xs, bk.v(), ALU.add)

            yield

        def drive2(gens):
            gens = list(gens)
            while gens:
                for gn in list(gens):
                    try:
                        next(gn)
                    except StopIteration:
                        gens.remove(gn)

        make_vpad(0)
        drive2([prep(1)])
        for i in range(1, NT2):
            gs = [attn(i)]
            if i + 1 < NT2:
                gs.append(prep(i + 1))
            drive2(gs)

        norm_to_hT(2, range(1, NT2))
        mlp(1, first_tile=1)

        gfb = V(asc.t[:].rearrange("p a b c -> p (a b c)")[:, 0:1024], [("asc", u) for u in range(NU)] + [("asc_relu0", None), ("asc_relu1", None)])
        g.dma_in("sync", gfb, rowv_d[0:1, 0:1024].to_broadcast([128, 1024]))
        rstd_all(list(range(1, NT2)))
        for i in range(1, NT2):
            o_ = V(scr.t[:], [("scr", "a"), ("scr", "b")])
            g.stt("vector", o_, xres.v(np.s_[:, i, :], sub=i), ms.v(np.s_[:, i:i + 1], sub=i), gfb, ALU.mult, ALU.mult)
            g.dma_out("sync", out_d[(i - 1) * 128:i * 128, :], o_)
        g.finish()
    return nc


def _pm(v):
    return np.ascontiguousarray(np.asarray(v).reshape(-1, 128).T)


def phase2_inputs(inp, ygT_full, core):
    b, tq = divmod(core, 4)
    t0 = tq * 2048
    x = np.zeros((NTOK2, D), np.float32)
    yg = np.zeros((D, NTOK2), ml_dtypes.bfloat16)
    pos = np.zeros((1, NTOK2), np.int32)
    lo = t0 - 128
    if tq > 0:
        x[:] = inp["x"][b, lo:lo + NTOK2]
        yg[:] = ygT_full[b][:, lo:lo + NTOK2]
        pos[0] = inp["positions"][b, lo:lo + NTOK2]
    else:
        x[128:] = inp["x"][b, 0:2048]
        yg[:, 128:] = ygT_full[b][:, 0:2048]
        pos[0, 128:] = inp["positions"][b, 0:2048]
    gains = np.stack([_pm(inp["norm_mlp_g"][0]), _pm(inp["norm_mix_g"][1]), _pm(inp["norm_mlp_g"][1])], axis=1)
    bqkv = inp["attn_b_qkv"][0]
    bq = np.zeros((128, 10), np.float32)
    bq[:, 0:8] = _pm(bqkv[0:1024])
    for j in range(2):
        bk = bqkv[1024 + j * 64:1024 + (j + 1) * 64]
        bq[:, 8 + j] = np.concatenate([bk, bk])
    rowv = np.concatenate([inp["norm_final_g"], inp["attn_b_o"][0], bqkv[1152:1280], inp["attn_sinks"][0]])[None, :]
    cst = np.zeros((128, 4), np.float32)
    p = np.arange(128)
    cst[:, 0] = (10000.0 ** (-(np.arange(0, 64, 2, dtype=np.float32)) / 64.0))[p % 32]
    cst[:, 1] = np.where((p % 64) < 32, 1.0, 1.0)
    cst[:, 2] = 0.0 if tq > 0 else NEG_BIG
    return {
        "x": x, "ygT": yg, "pos": pos,
        "wo0": np.ascontiguousarray(inp["rwkv_w_o"][0]),
        "win0": np.ascontiguousarray(inp["mlp_w_in"][0]), "win1": np.ascontiguousarray(inp["mlp_w_in"][1]),
        "wout0": np.ascontiguousarray(inp["mlp_w_out"][0]), "wout1": np.ascontiguousarray(inp["mlp_w_out"][1]),
        "wqkv": np.ascontiguousarray(inp["attn_w_qkv"][0]), "wo1": np.ascontiguousarray(inp["attn_w_o"][0]),
        "gains": np.ascontiguousarray(gains, dtype=np.float32), "bq": bq,
        "rowv": np.ascontiguousarray(rowv, dtype=np.float32), "cst": cst,
    }


_NC_CACHE = {}


def kernel(**inputs):
    inp = {k: np.asarray(v) for k, v in inputs.items()}
    if "p1" not in _NC_CACHE:
        _NC_CACHE["p1"] = build_phase1()
        _NC_CACHE["p2"] = build_phase2()
    r1 = run_bass_kernel_spmd(_NC_CACHE["p1"], [phase1_inputs(inp, c) for c in range(8)], core_ids=list(range(8)))
    ygT = np.zeros((2, D, S_LEN), ml_dtypes.bfloat16)
    for c in range(8):
        b, hg = divmod(c, 4)
        ygT[b, hg * 256:(hg + 1) * 256] = r1.results[c]["yg"]
    r2 = run_bass_kernel_spmd(_NC_CACHE["p2"], [phase2_inputs(inp, ygT, c) for c in range(8)], core_ids=list(range(8)))
    out = np.zeros((2, S_LEN, D), np.float32)
    for c in range(8):
        b, tq = divmod(c, 4)
        out[b, tq * 2048:(tq + 1) * 2048] = r2.results[c]["out"]
    return out
```

```python
import numpy as np
import ml_dtypes
from contextlib import ExitStack
import concourse.bass as bass
import concourse.mybir as mybir
from concourse.bass_utils import run_bass_kernel_spmd

F32 = mybir.dt.float32
BF16 = mybir.dt.bfloat16
I32 = mybir.dt.int32
AF = mybir.ActivationFunctionType
ALU = mybir.AluOpType

EPOCH = 4000
DMA_ROT = 8
NEG_C = -0.6065306597126334


class KB:
    ENGS = ("tensor", "vector", "scalar", "gpsimd", "sync")

    def __init__(self, nc, stack):
        self.nc = nc
        self.stack = stack
        self.streams = {e: [] for e in self.ENGS}
        self.count = {e: 0 for e in self.ENGS}
        self.sems = {e: [] for e in self.ENGS}
        self.dma_count = {e: 0 for e in self.ENGS}
        self.dma_sems = {e: [] for e in self.ENGS}
        self.waited = {e: {} for e in self.ENGS}
        self.last_w = {}
        self.readers = {}

    def _newsem(self, name):
        return self.stack.enter_context(self.nc.semaphore(name))

    def _compute_token(self, e):
        n = self.count[e]
        ep, idx = divmod(n, EPOCH)
        while len(self.sems[e]) <= ep:
            self.sems[e].append(self._newsem(f"s_{e}_{len(self.sems[e])}"))
        self.count[e] = n + 1
        return (self.sems[e][ep], idx + 1, 1, e)

    def _dma_token(self, e):
        n = self.dma_count[e]
        if not self.dma_sems[e]:
            self.dma_sems[e] = [self._newsem(f"d_{e}_{i}") for i in range(DMA_ROT)]
        self.dma_count[e] = n + 1
        return (self.dma_sems[e][n % DMA_ROT], 16 * (n // DMA_ROT + 1), 16, "dma_" + e)

    def op(self, e, fn, reads=(), writes=(), dma=False):
        deps = []
        for k in reads:
            t = self.last_w.get(k)
            if t is not None:
                deps.append((t, True))
        for k in writes:
            t = self.last_w.get(k)
            if t is not None:
                deps.append((t, False))
            for t in self.readers.get(k, ()):
                deps.append((t, False))
        wd = self.waited[e]
        ww = {}
        for (sem, val, _inc, src), is_raw in deps:
            if src == e and not dma and e == "tensor":
                continue
            sid = id(sem)
            if wd.get(sid, 0) >= val:
                continue
            if sid not in ww or ww[sid][1] < val:
                ww[sid] = (sem, val)
        for sid, (sem, val) in ww.items():
            wd[sid] = val
        if dma:
            n = self.dma_count[e]
            if n >= DMA_ROT:
                sem = self.dma_sems[e][n % DMA_ROT]
                val = 16 * (n // DMA_ROT)
                if wd.get(id(sem), 0) < val:
                    wd[id(sem)] = val
                    ww[id(sem)] = (sem, val)
        tok = self._dma_token(e) if dma else self._compute_token(e)
        self.streams[e].append((list(ww.values()), fn, tok))
        for k in reads:
            self.readers.setdefault(k, []).append(tok)
        for k in writes:
            self.last_w[k] = tok
            self.readers[k] = []
        return tok

    def wait_tokens(self, e, toks):
        wd = self.waited[e]
        waits = []
        for (sem, val, _i, _s) in toks:
            if wd.get(id(sem), 0) >= val:
                continue
            wd[id(sem)] = val
            waits.append((sem, val))
        self.streams[e].append((waits, None, None))

    def emit(self):
        nc = self.nc
        with nc.Block() as block:
            def mk(e):
                def body(eng):
                    for waits, fn, tok in self.streams[e]:
                        for sem, val in waits:
                            eng.wait_ge(sem, val)
                        if fn is not None:
                            ins = fn(eng)
                            ins.then_inc(tok[0], tok[2])
                return body
            for e in self.ENGS:
                if self.streams[e]:
                    getattr(block, e)(mk(e))


class V:
    __slots__ = ("ap", "keys")

    def __init__(self, ap, keys):
        self.ap = ap
        self.keys = keys


class Buf:
    def __init__(self, t, name):
        self.t = t
        self.name = name

    def v(self, idx=None, sub=None):
        ap = self.t[idx] if idx is not None else self.t[:]
        return V(ap, [(self.name, sub)])

    def w(self, ap, sub=None):
        return V(ap, [(self.name, sub)])


class G:
    def __init__(self, nc, st):
        self.nc = nc
        self.st = st
        self.kb = KB(nc, st)
        self.banks = [Buf(st.enter_context(nc.psum_tensor(f"psb{i}", [128, 512], F32)), f"ps{i}") for i in range(8)]
        self.bank_i = 0
        self.out_tokens = []

    def sb(self, name, shape, dt):
        return Buf(self.st.enter_context(self.nc.sbuf_tensor("sb_" + name, shape, dt)), name)

    def bank(self):
        b = self.banks[self.bank_i % 8]
        self.bank_i += 1
        return b

    @staticmethod
    def _k(vs):
        ks = []
        for v in vs:
            if isinstance(v, V):
                ks.extend(v.keys)
        return ks

    def mm(self, out, lhsT, rhs, start=True, stop=True):
        return self.kb.op("tensor", lambda e: e.matmul(out.ap, lhsT=lhsT.ap, rhs=rhs.ap, start=start, stop=stop),
                          reads=self._k([lhsT, rhs]), writes=out.keys)

    def tr(self, out, in_, ident):
        return self.kb.op("tensor", lambda e: e.transpose(out=out.ap, in_=in_.ap, identity=ident.ap),
                          reads=self._k([in_, ident]), writes=out.keys)

    def act(self, out, in_, func, bias=None, scale=1.0, accum=None, eng="scalar"):
        kw = {}
        if bias is not None:
            kw["bias"] = bias.ap if isinstance(bias, V) else bias
        if accum is not None:
            kw["accum_out"] = accum.ap
        sc = scale.ap if isinstance(scale, V) else scale
        return self.kb.op("scalar", lambda e: e.activation(out=out.ap, in_=in_.ap, func=func, scale=sc, **kw),
                          reads=self._k([in_, bias, scale]), writes=self._k([out, accum]))

    def tt(self, eng, out, a, b, op):
        return self.kb.op(eng, lambda e: e.tensor_tensor(out=out.ap, in0=a.ap, in1=b.ap, op=op),
                          reads=self._k([a, b]), writes=out.keys)

    def ts(self, eng, out, a, s1, op0, s2=None, op1=None):
        s1a = s1.ap if isinstance(s1, V) else s1
        s2a = s2.ap if isinstance(s2, V) else s2
        if op1 is None:
            fn = lambda e: e.tensor_scalar(out=out.ap, in0=a.ap, scalar1=s1a, scalar2=None, op0=op0)
        else:
            fn = lambda e: e.tensor_scalar(out=out.ap, in0=a.ap, scalar1=s1a, scalar2=s2a, op0=op0, op1=op1)
        return self.kb.op(eng, fn, reads=self._k([a, s1, s2]), writes=out.keys)

    def stt(self, eng, out, in0, scalar, in1, op0, op1):
        sa = scalar.ap if isinstance(scalar, V) else scalar
        return self.kb.op(eng, lambda e: e.scalar_tensor_tensor(out=out.ap, in0=in0.ap, scalar=sa, in1=in1.ap, op0=op0, op1=op1),
                          reads=self._k([in0, scalar, in1]), writes=out.keys)

    def copy(self, eng, out, in_):
        if eng == "scalar":
            return self.act(out, in_, AF.Copy)
        return self.kb.op(eng, lambda e: e.tensor_copy(out=out.ap, in_=in_.ap), reads=in_.keys, writes=out.keys)

    def memset(self, eng, out, val):
        return self.kb.op(eng, lambda e: e.memset(out.ap, val), writes=out.keys)

    def recip(self, out, in_):
        return self.kb.op("vector", lambda e: e.reciprocal(out=out.ap, in_=in_.ap), reads=in_.keys, writes=out.keys)

    def scan(self, out, d0, d1, init, op0, op1):
        return self.kb.op("vector", lambda e: e.tensor_tensor_scan(out=out.ap, data0=d0.ap, data1=d1.ap, initial=init, op0=op0, op1=op1),
                          reads=self._k([d0, d1]), writes=out.keys)

    def aselect(self, out, in_, pattern, cmp, fill, base, cm):
        return self.kb.op("gpsimd", lambda e: e.affine_select(out=out.ap, in_=in_.ap, pattern=pattern, compare_op=cmp,
                                                               fill=fill, base=base, channel_multiplier=cm),
                          reads=in_.keys, writes=out.keys)

    def dma_in(self, eng, out, in_ap, **kw):
        return self.kb.op(eng, lambda e: e.dma_start(out=out.ap, in_=in_ap, **kw), writes=out.keys, dma=True)

    def dma_out(self, eng, out_ap, in_, final=True):
        t = self.kb.op(eng, lambda e: e.dma_start(out=out_ap, in_=in_.ap), reads=in_.keys, dma=True)
        if final:
            self.out_tokens.append(t)
        return t

    def finish(self):
        self.kb.wait_tokens("sync", self.out_tokens)
        self.kb.emit()


S_LEN = 8192
D = 1024
TB = 512
NCH = TB // 128
GN_EPS = 64e-5
RMS_EPS = 1e-5
PROJ = [("r", 0, 256, 0), ("k", 2, 256, 256), ("v", 3, 256, 512), ("w1", 1, 64, 768), ("a1", 4, 64, 832), ("g1", 5, 160, 896)]
NCOL = 1056


BACK_W = 4


def build_phase1(n_tok=S_LEN, stop=None, stopargs=()):
    nc = bass.Bass("TRN2", target_bir_lowering=False)
    dr = lambda n, s, d=F32: nc.dram_tensor(n, s, d, kind="ExternalInput").ap()
    x_d = dr("x", [n_tok, D])
    gm_d = dr("gm", [128, 7, 8])
    wcat_d = dr("wcat", [D, NCOL])
    w2_d = dr("w2", [64, 256])
    a2_d = dr("a2", [64, 256])
    g2_d = dr("g2", [160, 256])
    vec_d = dr("vec", [128, 7, 2])
    yg_d = nc.dram_tensor("yg", [256, n_tok], BF16, kind="ExternalOutput").ap()
    dbg_d = nc.dram_tensor("dbg", [128, 2560], F32, kind="ExternalOutput").ap() if stop else None
    nblk = n_tok // TB

    class Stop(Exception):
        pass

    with ExitStack() as st:
        g = G(nc, st)
        sb = g.sb
        dbgt = sb("dbgt", [128, 2560], F32) if stop else None
        dbg_off = [0]

        def dump(v, n):
            o = dbg_off[0]
            p = v.ap.shape[0]
            g.copy("vector", dbgt.w(dbgt.t[0:p, o:o + n]), v)
            dbg_off[0] = o + n

        def chk(name):
            if stop == name:
                raise Stop()
        def emit_all():
            identf = sb("identf", [128, 128], F32)
            ident = sb("ident", [128, 128], BF16)
            ident4 = sb("ident4", [128, 4, 128], BF16)
            onesblk = sb("onesblk", [128, 128], F32)
            m_su = sb("m_su", [128, 4, 128], BF16)
            m_u = sb("m_u", [128, 4, 128], BF16)
            m_sl = sb("m_sl", [128, 4, 128], BF16)
            rmask = sb("rmask", [128, TB], F32)
            g.memset("gpsimd", identf.v(), 0.0)
            g.aselect(identf.v(), identf.v(), [[-1, 128]], ALU.not_equal, 1.0, 0, 1)
            g.copy("vector", ident.v(), identf.v())
            for h in range(4):
                g.copy("vector", ident4.v(np.s_[:, h, :]), identf.v())
            g.memset("gpsimd", onesblk.v(), 0.0)
            g.memset("gpsimd", onesblk.v(np.s_[0:64, 0:64]), 1.0)
            g.memset("gpsimd", onesblk.v(np.s_[64:128, 64:128]), 1.0)
            for m, cm, pat, cmp in ((m_su, -1, 1, ALU.is_gt), (m_u, -1, 1, ALU.is_ge), (m_sl, 1, -1, ALU.is_gt)):
                g.memset("gpsimd", m.v(), 1.0)
                g.aselect(m.v(), m.v(), [[0, 4], [pat, 128]], cmp, 0.0, 0, cm)
            g.memset("gpsimd", rmask.v(), 1.0)
            g.memset("gpsimd", rmask.w(rmask.t[:].rearrange("p (c t) -> p c t", t=128)[:, :, 0:1]), 0.0)

            if stop == "const":
                dump(identf.v(), 128); dump(m_su.v(np.s_[:, 1, :]), 128); dump(m_sl.v(np.s_[:, 2, :]), 128); dump(m_u.v(np.s_[:, 3, :]), 128)
                dump(onesblk.v(), 128); dump(rmask.v(), 512)
            chk("const")
            gm = sb("gm", [128, 7, 8], F32)
            coefA = sb("coefA", [128, 6, 8], F32)
            coefB = sb("coefB", [128, 6, 8], F32)
            vec = sb("vec", [128, 7, 2], F32)
            WA = sb("WA", [128, 8, NCOL], BF16)
            WB = sb("WB", [128, 8, NCOL], BF16)
            w2b = sb("w2b", [64, 256], BF16)
            a2b = sb("a2b", [128, 256], BF16)
            g2a = sb("g2a", [128, 256], BF16)
            g2b = sb("g2b", [32, 256], BF16)
            xin = sb("xin", [128, 4, 1024], F32)
            g.dma_in("sync", gm.v(), gm_d[:, :, :])
            g.dma_in("sync", vec.v(), vec_d[:, :, :])
            g.dma_in("gpsimd", w2b.v(), w2_d[:, :])
            g.dma_in("gpsimd", a2b.v(np.s_[64:128, :]), a2_d[:, :])
            g.dma_in("gpsimd", g2a.v(), g2_d[0:128, :])
            g.dma_in("gpsimd", g2b.v(), g2_d[128:160, :])
            g0b = gm.w(gm.t[:, 0:1, :].to_broadcast([128, 6, 8]))
            g.tt("vector", coefB.v(), gm.v(np.s_[:, 1:7, :]), g0b, ALU.mult)
            g.tt("vector", coefA.v(), g0b, coefB.v(), ALU.subtract)
            stage = xin
            wv = wcat_d.rearrange("(kc p) n -> p kc n", p=128)
            XK = [("xin", j) for j in range(4)]
            for (nm, mi, ncol, off) in PROJ:
                sv = stage.t[:].rearrange("p a b -> p (a b)")[:, 0:8 * ncol].rearrange("p (k n) -> p k n", k=8)
                g.kb.op("sync", lambda e, sv=sv, off=off, ncol=ncol: e.dma_start(out=sv, in_=wv[:, :, off:off + ncol]), writes=XK, dma=True)
                ca = coefA.w(coefA.t[:, mi, :].unsqueeze(2).to_broadcast([128, 8, ncol]))
                cb = coefB.w(coefB.t[:, mi, :].unsqueeze(2).to_broadcast([128, 8, ncol]))
                g.tt("vector", WA.v(np.s_[:, :, off:off + ncol]), V(sv, XK), ca, ALU.mult)
                g.tt("gpsimd", WB.v(np.s_[:, :, off:off + ncol]), V(sv, XK), cb, ALU.mult)

            if stop == "weights":
                dump(WA.v(np.s_[:, 3, 0:512]), 512); dump(WB.v(np.s_[:, 7, 544:1056]), 512); dump(coefA.v(np.s_[:, 2, :]), 8)
            chk("weights")
            ms = sb("ms", [128, 4], F32)
            xn = [sb(f"xn{i}", [128, 1024], BF16) for i in range(2)]
            hT = sb("hT", [128, 8, TB + 1], BF16)
            T = {n: sb("t_" + n, [128, TB], F32) for n in
                 ("rT", "kraw", "aT", "sgw", "cs", "Er", "En", "kk", "kp", "beta", "t1", "t2", "EC")}
            junk = Buf(T["EC"].t[:].bitcast(BF16), "t_EC")
            tw = sb("tw", [128, TB], BF16)
            sg1a = sb("sg1a", [128, TB], BF16)
            sg1b = sb("sg1b", [32, TB], BF16)
            vT = sb("vT", [128, 2, TB], F32)
            gT2 = [sb(f"gT{i}", [128, 2, TB], F32) for i in range(2)]
            bonusT2 = [sb(f"bonusT{i}", [128, 2, TB], F32) for i in range(2)]
            PCs2 = [sb(f"PCs{i}", [128, 2, NCH], F32) for i in range(2)]
            rt2 = [sb(f"rt_{i}", [128, 2, TB], BF16) for i in range(2)]
            kt2 = [sb(f"kt_{i}", [128, 2, TB], BF16) for i in range(2)]
            at2 = [sb(f"at_{i}", [128, 2, TB], BF16) for i in range(2)]
            bt2 = [sb(f"bt_{i}", [128, 2, TB], BF16) for i in range(2)]
            T2 = {n: sb("t2_" + n, [128, TB], F32) for n in ("o1", "o2")}
            khT = sb("khT", [128, 2, TB], BF16)
            bhT = sb("bhT", [128, 2, TB], BF16)
            XR2 = [sb(f"XR{i}", [128, NCH, 4, 128], BF16) for i in range(2)]
            Kpad2 = [sb(f"Kpad{i}", [128, NCH, 4, 128], BF16) for i in range(2)]
            Bpad2 = [sb(f"Bpad{i}", [128, NCH, 4, 128], BF16) for i in range(2)]
            Vpad2 = [sb(f"Vpad{i}", [128, NCH, 4, 128], BF16) for i in range(2)]
            NSL = 2
            Pb = [[sb(f"P{s}{i}", [128, 4, 128], BF16) for i in range(2)] for s in range(NSL)]
            SUt = [sb(f"SU{s}", [128, 2, 4, 128], BF16) for s in range(NSL)]
            UUt = [sb(f"UU{s}", [128, 2, 4, 128], BF16) for s in range(NSL)]
            PTb = [[Buf(SUt[s].t[:, 0], f"PT{s}0"), sb(f"PT{s}1", [128, 4, 128], BF16)] for s in range(NSL)]
            NTb = [[sb(f"NT{s}{i}", [128, 4, 128], BF16) for i in range(2)] for s in range(NSL)]
            AakT = [Buf(SUt[s].t[:, 1], f"AakT{s}") for s in range(NSL)]
            ArbT = [Buf(UUt[s].t[:, 0], f"ArbT{s}") for s in range(NSL)]
            ArkT = [Buf(UUt[s].t[:, 1], f"ArkT{s}") for s in range(NSL)]
            Apad = [sb(f"Apad{s}", [128, 4, 128], BF16) for s in range(NSL)]
            Wpad = [sb(f"Wpad{s}", [128, 4, 128], BF16) for s in range(NSL)]
            TTbd = [sb(f"TTbd{s}", [128, 2, 128], BF16) for s in range(NSL)]
            RhT = [sb(f"RhT{s}", [128, 2, 128], BF16) for s in range(NSL)]
            Sbd = [sb(f"Sbd{i}", [128, 2, 128], BF16) for i in range(2)]
            yraw = sb("yraw", [128, 2, TB], F32)
            ygo = sb("ygo", [128, 2, TB], BF16)
            for b_ in Kpad2 + Bpad2 + Vpad2:
                g.memset("gpsimd", b_.v(), 0.0)
            for s in range(NSL):
                g.memset("gpsimd", Apad[s].v(), 0.0)
                g.memset("gpsimd", Wpad[s].v(), 0.0)
            g.memset("gpsimd", Sbd[0].v(), 0.0)
            g.memset("vector", hT.v(np.s_[:, :, 0:1]), 0.0)
            s_cur = 0

            def bf(bank):
                return bank.t[:].bitcast(BF16)

            def hsl(h, j):
                cc, q = divmod(h, 2)
                return np.s_[q * 64:(q + 1) * 64, cc, j * 128:(j + 1) * 128]

            def load_x(blk):
                t0 = blk * TB
                for j in range(4):
                    g.dma_in("sync", xin.v(np.s_[:, j, :], sub=j), x_d[t0 + j * 128:t0 + (j + 1) * 128, :])

            def front(blk):
                F = blk % 2
                rt_, kt_, at_, bt_ = rt2[F], kt2[F], at2[F], bt2[F]
                XR, Kpad, Bpad, Vpad, PCs, gT, bonusT = XR2[F], Kpad2[F], Bpad2[F], Vpad2[F], PCs2[F], gT2[F], bonusT2[F]
                for j in range(4):
                    g.act(junk.v(), xin.v(np.s_[:, j, :], sub=j), AF.Square, scale=1.0 / 32, accum=ms.v(np.s_[:, j:j + 1]))
                g.ts("vector", ms.v(), ms.v(), RMS_EPS, ALU.add)
                g.act(ms.v(), ms.v(), AF.Sqrt)
                g.recip(ms.v(), ms.v())
                if blk > 0:
                    g.copy("vector", hT.v(np.s_[:, :, 0:1]), hT.v(np.s_[:, :, TB:TB + 1]))
                yield
                for j in range(4):
                    xj = xn[j % 2]
                    g.act(xj.v(), xin.v(np.s_[:, j, :], sub=j), AF.Copy, scale=ms.v(np.s_[:, j:j + 1]))
                    bk = g.bank()
                    bv = bf(bk)
                    for kc in range(8):
                        g.tr(bk.w(bv[:, kc * 128:(kc + 1) * 128]), xj.v(np.s_[:, kc * 128:(kc + 1) * 128]), ident.v())
                    g.copy("vector" if j % 2 == 0 else "scalar", hT.v(np.s_[:, :, 1 + 128 * j:1 + 128 * (j + 1)]),
                           bk.w(bv.rearrange("p (k t) -> p k t", k=8)))
                    yield
                if blk + 1 < nblk:
                    load_x(blk + 1)

                def proj_fm(off, ncol):
                    bk = g.bank()
                    for kc in range(8):
                        g.mm(bk.v(np.s_[0:ncol, :]), WA.v(np.s_[:, kc, off:off + ncol]), hT.v(np.s_[:, kc, 1:TB + 1]),
                             start=(kc == 0), stop=False)
                        g.mm(bk.v(np.s_[0:ncol, :]), WB.v(np.s_[:, kc, off:off + ncol]), hT.v(np.s_[:, kc, 0:TB]),
                             start=False, stop=(kc == 7))
                    return bk

                bk = proj_fm(768, 128)
                g.act(tw.v(np.s_[0:64, :]), bk.v(np.s_[0:64, :]), AF.Tanh)
                g.copy("vector", tw.v(np.s_[64:128, :]), bk.v(np.s_[64:128, :]))
                yield
                bk = proj_fm(896, 128)
                g.act(sg1a.v(), bk.v(), AF.Sigmoid)
                yield
                bk = proj_fm(1024, 32)
                g.act(sg1b.v(), bk.v(np.s_[0:32, :]), AF.Sigmoid)
                yield
                for j in range(4):
                    bk = g.bank()
                    for kc in range(8):
                        g.mm(bk.v(np.s_[:, 0:256]), hT.v(np.s_[:, kc, 1 + 128 * j:1 + 128 * (j + 1)]), WA.v(np.s_[:, kc, 512:768]),
                             start=(kc == 0), stop=False)
                        g.mm(bk.v(np.s_[:, 0:256]), hT.v(np.s_[:, kc, 128 * j:128 * (j + 1)]), WB.v(np.s_[:, kc, 512:768]),
                             start=False, stop=(kc == 7))
                    pv = bk.t[:, 0:256].rearrange("p (h c) -> p h c", h=4)
                    g.copy("vector", Vpad.v(np.s_[:, j, 0::2, 0:64]), bk.w(pv[:, 0::2, :]))
                    g.copy("scalar", Vpad.v(np.s_[:, j, 1::2, 64:128]), bk.w(pv[:, 1::2, :]))
                    yield

                for cc in range(2):
                    bk = proj_fm(0 + cc * 128, 128)
                    g.copy("scalar", T["rT"].v(), bk.v())
                    yield
                    bk = proj_fm(256 + cc * 128, 128)
                    g.copy("vector", T["kraw"].v(), bk.v())
                    yield
                    bk = proj_fm(512 + cc * 128, 128)
                    g.copy("scalar", vT.v(np.s_[:, cc, :]), bk.v())
                    yield
                    bk = g.bank()
                    g.mm(bk.v(), w2b.v(np.s_[0:64, cc * 128:(cc + 1) * 128]), tw.v(np.s_[0:64, :]))
                    g.act(T["sgw"].v(), bk.v(), AF.Sigmoid, bias=vec.v(np.s_[:, 0, cc:cc + 1]))
                    bk = g.bank()
                    g.mm(bk.v(), a2b.v(np.s_[64:128, cc * 128:(cc + 1) * 128]), tw.v(np.s_[64:128, :]))
                    g.act(T["aT"].v(), bk.v(), AF.Sigmoid, bias=vec.v(np.s_[:, 1, cc:cc + 1]))
                    bk = g.bank()
                    g.mm(bk.v(), g2a.v(np.s_[:, cc * 128:(cc + 1) * 128]), sg1a.v(), start=True, stop=False)
                    g.mm(bk.v(), g2b.v(np.s_[0:32, cc * 128:(cc + 1) * 128]), sg1b.v(np.s_[0:32, :]), start=False, stop=True)
                    g.copy("vector", gT.v(np.s_[:, cc, :]), bk.v())
                    yield

                    kk, kp, beta, t1, t2 = T["kk"], T["kp"], T["beta"], T["t1"], T["t2"]
                    Er, En, EC, cs = T["Er"], T["En"], T["EC"], T["cs"]
                    g.ts("gpsimd", kk.v(), T["kraw"].v(), vec.v(np.s_[:, 2, cc:cc + 1]), ALU.mult)
                    g.tt("gpsimd", t1.v(), kk.v(), kk.v(), ALU.mult)
                    bk = g.bank()
                    g.mm(bk.v(), onesblk.v(), t1.v())
                    g.scan(cs.v(), rmask.v(), T["sgw"].v(), 0.0, ALU.mult, ALU.add)
                    yield
                    g.ts("vector", t2.v(), bk.v(), 1e-24, ALU.max)
                    g.act(t2.v(), t2.v(), AF.Sqrt)
                    g.act(Er.v(), cs.v(), AF.Exp, scale=NEG_C)
                    g.act(En.v(), cs.v(), AF.Exp, scale=-NEG_C)
                    g.recip(t2.v(), t2.v())
                    yield
                    g.tt("gpsimd", kk.v(), kk.v(), t2.v(), ALU.mult)
                    g.ts("vector", t1.v(), T["aT"].v(), -1.0, ALU.add, vec.v(np.s_[:, 3, cc:cc + 1]), ALU.mult)
                    g.tt("gpsimd", beta.v(), kk.v(), T["aT"].v(), ALU.mult)
                    g.stt("vector", kp.v(), t1.v(), 1.0, T["kraw"].v(), ALU.add, ALU.mult)
                    yield
                    Er3 = Er.t[:].rearrange("p (c t) -> p c t", t=128)
                    g.copy("vector", PCs.v(np.s_[:, cc, :]), Er.w(Er3[:, :, 127]))
                    g.tt("vector", EC.w(EC.t[:].rearrange("p (c t) -> p c t", t=128)),
                         En.w(En.t[:].rearrange("p (c t) -> p c t", t=128)),
                         Er.w(Er3[:, :, 127:128].to_broadcast([128, NCH, 128])), ALU.mult)
                    g.tt("gpsimd", kt_.v(np.s_[:, cc, :]), kp.v(), En.v(), ALU.mult)
                    g.tt("gpsimd", rt_.v(np.s_[:, cc, :]), T["rT"].v(), Er.v(), ALU.mult)
                    yield
                    g.tt("gpsimd", bt_.v(np.s_[:, cc, :]), beta.v(), En.v(), ALU.mult)
                    g.tt("gpsimd", khT.v(np.s_[:, cc, :]), kp.v(), EC.v(), ALU.mult)
                    g.tt("gpsimd", bhT.v(np.s_[:, cc, :]), beta.v(), EC.v(), ALU.mult)
                    kk3 = kk.t[:].rearrange("p (c t) -> p c t", t=128)
                    at3 = at_.t[:, cc, :].rearrange("p (c t) -> p c t", t=128)
                    g.stt("vector", at_.w(at3[:, :, 1:128]), kk.w(kk3[:, :, 1:128]), -1.0, Er.w(Er3[:, :, 0:127]), ALU.mult, ALU.mult)
                    g.ts("vector", at_.w(at3[:, :, 0:1]), kk.w(kk3[:, :, 0:1]), -1.0, ALU.mult)
                    yield
                    g.stt("vector", t1.v(), T["rT"].v(), vec.v(np.s_[:, 4, cc:cc + 1]), kp.v(), ALU.mult, ALU.mult)
                    bk = g.bank()
                    g.mm(bk.v(), onesblk.v(), t1.v())
                    g.tt("vector", bonusT.v(np.s_[:, cc, :]), vT.v(np.s_[:, cc, :]), bk.v(), ALU.mult)
                    yield

                for (src, kind) in ((at_, "A"), (khT, "K"), (bhT, "B")):
                    bk = g.bank()
                    bv = bf(bk)
                    for j in range(NCH):
                        for cc in range(2):
                            sl = (j * 2 + cc) * 128
                            g.tr(bk.w(bv[:, sl:sl + 128]), src.v(np.s_[:, cc, j * 128:(j + 1) * 128]), ident.v())
                    if kind == "A":
                        g.copy("vector", XR.v(np.s_[:, :, :, 0:64]),
                               bk.w(bv.rearrange("p (j h c) -> p j h c", j=NCH, h=4)))
                    else:
                        dst = Kpad if kind == "K" else Bpad
                        b5 = bv.rearrange("p (j c q d) -> p j c q d", j=NCH, c=2, q=2)
                        g.copy("vector", dst.v(np.s_[:, :, 0::2, 0:64]), bk.w(b5[:, :, :, 0, :]))
                        g.copy("scalar", dst.v(np.s_[:, :, 1::2, 64:128]), bk.w(b5[:, :, :, 1, :]))
                    yield

            def back(blk):
                nonlocal s_cur
                t0 = blk * TB
                F = blk % 2
                rt_, kt_, at_, bt_ = rt2[F], kt2[F], at2[F], bt2[F]
                XR, Kpad, Bpad, Vpad, PCs, gT, bonusT = XR2[F], Kpad2[F], Bpad2[F], Vpad2[F], PCs2[F], gT2[F], bonusT2[F]

                def pre_a(j, s):
                    def grp(specs, msk, dsts_t, dkeys):
                        bks = [g.bank(), g.bank()]
                        for si, (lh, rh) in enumerate(specs):
                            for h in range(4):
                                cc, q = divmod(h, 2)
                                sl = (si * 2 + cc) * 128
                                g.mm(bks[q].v(np.s_[:, sl:sl + 128]), lh.v(hsl(h, j)), rh.v(hsl(h, j)))
                        n = len(specs)
                        for q in range(2):
                            if n == 2:
                                src = bks[q].w(bks[q].t[:].rearrange("p (a c t) -> p a c t", a=2, c=2))
                                dst = V(dsts_t[:, :, q::2, :], dkeys)
                                mk = msk.w(msk.t[:].rearrange("p (a c) t -> p a c t", a=2))
                            else:
                                src = bks[q].w(bks[q].t[:, 0:256].rearrange("p (c t) -> p c t", c=2))
                                dst = V(dsts_t[:, q::2, :], dkeys)
                                mk = msk.v(np.s_[:, 0:2, :])
                            g.tt("vector", dst, src, mk, ALU.mult)
                    grp([(bt_, at_), (kt_, at_)], m_su, SUt[s].t, [(PTb[s][0].name, None), (AakT[s].name, None)])
                    yield
                    grp([(bt_, rt_), (kt_, rt_)], m_u, UUt[s].t, [(ArbT[s].name, None), (ArkT[s].name, None)])
                    yield
                    grp([(at_, bt_)], m_sl, Pb[s][0].t, [(Pb[s][0].name, None)])
                    g.tt("gpsimd", NTb[s][0].v(), PTb[s][0].v(), ident4.v(), ALU.add)
                    yield

                def pre_dbl(s, it):
                    cur, nxt = it % 2, (it + 1) % 2
                    bk = g.bank()
                    for h in range(4):
                        g.mm(bk.v(np.s_[:, h * 128:(h + 1) * 128]), PTb[s][cur].v(np.s_[:, h, :]), Pb[s][cur].v(np.s_[:, h, :]))
                    g.copy("scalar", Pb[s][nxt].v(), bk.w(bk.t[:].rearrange("p (h t) -> p h t", h=4)))
                    if it < 5:
                        bk = g.bank()
                        for h in range(4):
                            g.mm(bk.v(np.s_[:, h * 128:(h + 1) * 128]), Pb[s][cur].v(np.s_[:, h, :]), PTb[s][cur].v(np.s_[:, h, :]))
                        g.copy("vector", PTb[s][nxt].v(), bk.w(bk.t[:].rearrange("p (h t) -> p h t", h=4)))
                    yield
                    bk = g.bank()
                    for h in range(4):
                        o = bk.v(np.s_[:, h * 128:(h + 1) * 128])
                        g.mm(o, ident.v(), NTb[s][cur].v(np.s_[:, h, :]), start=True, stop=False)
                        g.mm(o, Pb[s][nxt].v(np.s_[:, h, :]), NTb[s][cur].v(np.s_[:, h, :]), start=False, stop=True)
                    eng = "scalar" if it % 2 == 0 else "vector"
                    g.copy(eng, NTb[s][nxt].v(), bk.w(bk.t[:].rearrange("p (h t) -> p h t", h=4)))
                    yield

                def pre_b(j, s):
                    NTf = NTb[s][0]
                    bk = g.bank()
                    for h in range(4):
                        q = h % 2
                        g.mm(bk.v(np.s_[:, h * 64:(h + 1) * 64]), AakT[s].v(np.s_[:, h, :]), Vpad.v(np.s_[:, j, h, q * 64:(q + 1) * 64]))
                    g.copy("vector", XR.v(np.s_[:, j, :, 64:128]), bk.w(bk.t[:, 0:256].rearrange("p (h c) -> p h c", h=4)))
                    yield
                    bk = g.bank()
                    for h in range(4):
                        g.mm(bk.v(np.s_[:, h * 128:(h + 1) * 128]), NTf.v(np.s_[:, h, :]), XR.v(np.s_[:, j, h, :]))
                    x4 = bk.t[:].rearrange("p (h c) -> p h c", h=4)
                    g.copy("vector", Apad[s].v(np.s_[:, 0::2, 0:64]), bk.w(x4[:, 0::2, 0:64]))
                    g.copy("scalar", Apad[s].v(np.s_[:, 1::2, 64:128]), bk.w(x4[:, 1::2, 0:64]))
                    g.copy("vector", Wpad[s].v(np.s_[:, 0::2, 0:64]), bk.w(x4[:, 0::2, 64:128]))
                    g.copy("scalar", Wpad[s].v(np.s_[:, 1::2, 64:128]), bk.w(x4[:, 1::2, 64:128]))
                    yield
                    bk = g.bank()
                    for cc in range(2):
                        o = bk.v(np.s_[:, cc * 128:(cc + 1) * 128])
                        for q in range(2):
                            h = 2 * cc + q
                            g.mm(o, Apad[s].v(np.s_[:, h, :]), Bpad.v(np.s_[:, j, h, :]), start=(q == 0), stop=(q == 1))
                    for cc in range(2):
                        g.stt("vector", TTbd[s].v(np.s_[:, cc, :]), identf.v(), PCs.v(np.s_[:, cc, j:j + 1]),
                              bk.v(np.s_[:, cc * 128:(cc + 1) * 128]), ALU.mult, ALU.add)
                    bk = g.bank()
                    for cc in range(2):
                        o = bk.v(np.s_[:, cc * 128:(cc + 1) * 128])
                        for q in range(2):
                            h = 2 * cc + q
                            g.mm(o, Apad[s].v(np.s_[:, h, :]), ArbT[s].v(np.s_[:, h, :]), start=(q == 0), stop=(q == 1))
                    g.tt("vector", RhT[s].v(), bk.w(bk.t[:, 0:256].rearrange("p (c t) -> p c t", c=2)),
                         rt_.v(np.s_[:, :, j * 128:(j + 1) * 128]), ALU.add)
                    yield

                def seq(j, s):
                    nonlocal s_cur
                    Sc, Sn = Sbd[s_cur], Sbd[1 - s_cur]
                    bk = g.bank()
                    for cc in range(2):
                        o = bk.v(np.s_[:, cc * 128:(cc + 1) * 128])
                        g.mm(o, TTbd[s].v(np.s_[:, cc, :]), Sc.v(np.s_[:, cc, :]), start=True, stop=False)
                        for q in range(2):
                            h = 2 * cc + q
                            g.mm(o, Bpad.v(np.s_[:, j, h, :]), Wpad[s].v(np.s_[:, h, :]), start=False, stop=False)
                            g.mm(o, Kpad.v(np.s_[:, j, h, :]), Vpad.v(np.s_[:, j, h, :]), start=False, stop=(q == 1))
                    g.copy("vector", Sn.v(), bk.w(bk.t[:, 0:256].rearrange("p (c t) -> p c t", c=2)))
                    bk = g.bank()
                    for cc in range(2):
                        o = bk.v(np.s_[:, cc * 128:(cc + 1) * 128])
                        g.mm(o, Sc.v(np.s_[:, cc, :]), RhT[s].v(np.s_[:, cc, :]), start=True, stop=False)
                        for q in range(2):
                            h = 2 * cc + q
                            g.mm(o, Wpad[s].v(np.s_[:, h, :]), ArbT[s].v(np.s_[:, h, :]), start=False, stop=False)
                            g.mm(o, Vpad.v(np.s_[:, j, h, :]), ArkT[s].v(np.s_[:, h, :]), start=False, stop=(q == 1))
                    g.copy("scalar", yraw.v(np.s_[:, :, j * 128:(j + 1) * 128]), bk.w(bk.t[:, 0:256].rearrange("p (c t) -> p c t", c=2)))
                    s_cur = 1 - s_cur
                    yield

                def rr(gens):
                    gens = list(gens)
                    while gens:
                        for gn in list(gens):
                            try:
                                next(gn)
                            except StopIteration:
                                gens.remove(gn)
                            yield

                for jp in range(0, NCH, NSL):
                    yield from rr([pre_a(jp + s, s) for s in range(NSL)])
                    for it in range(6):
                        yield from rr([pre_dbl(s, it) for s in range(NSL)])
                    yield from rr([pre_b(jp + s, s) for s in range(NSL)])
                    for s in range(NSL):
                        yield from seq(jp + s, s)

                for cc in range(2):
                    t1, t2 = T2["o1"], T2["o2"]
                    yr = yraw.v(np.s_[:, cc, :])
                    bk = g.bank()
                    g.mm(bk.v(), onesblk.v(), yr)
                    g.stt("vector", t1.v(), bk.v(), -1.0 / 64, yr, ALU.mult, ALU.add)
                    g.tt("gpsimd", t2.v(), t1.v(), t1.v(), ALU.mult)
                    yield
                    bk = g.bank()
                    g.mm(bk.v(), onesblk.v(), t2.v())
                    g.ts("vector", t2.v(), bk.v(), 1.0 / 64, ALU.mult, GN_EPS, ALU.add)
                    g.act(t2.v(), t2.v(), AF.Sqrt)
                    g.recip(t2.v(), t2.v())
                    yield
                    g.tt("gpsimd", t1.v(), t1.v(), t2.v(), ALU.mult)
                    g.ts("vector", t1.v(), t1.v(), vec.v(np.s_[:, 5, cc:cc + 1]), ALU.mult, vec.v(np.s_[:, 6, cc:cc + 1]), ALU.add)
                    g.tt("gpsimd", t1.v(), t1.v(), bonusT.v(np.s_[:, cc, :]), ALU.add)
                    g.tt("vector", ygo.v(np.s_[:, cc, :]), t1.v(), gT.v(np.s_[:, cc, :]), ALU.mult)
                    g.dma_out("sync", yg_d[cc * 128:(cc + 1) * 128, t0:t0 + TB], ygo.v(np.s_[:, cc, :]))
                    yield

            def drive(gens, weights=None):
                gens = list(gens)
                weights = list(weights or [1] * len(gens))
                while gens:
                    for gn, w in list(zip(gens, weights)):
                        for _ in range(w):
                            try:
                                next(gn)
                            except StopIteration:
                                k = gens.index(gn)
                                gens.pop(k)
                                weights.pop(k)
                                break

            load_x(0)
            drive([front(0)])
            for blk in range(nblk):
                gs = [back(blk)]
                if blk + 1 < nblk:
                    gs.append(front(blk + 1))
                drive(gs, [BACK_W, 1])
        try:
            emit_all()
        except Stop:
            pass
        if stop:
            g.dma_out("sync", dbg_d[:, :], dbgt.v())
        g.finish()
    return nc


def phase1_inputs(inp, core):
    b, hg = divmod(core, 4)
    cs = slice(hg * 256, (hg + 1) * 256)
    pm = lambda v: np.ascontiguousarray(v.reshape(-1, 128).T)
    gm = np.stack([pm(inp["norm_mix_g"][0])] + [pm(inp["rwkv_mu"][0, i]) for i in range(6)], axis=1)
    wcat = np.concatenate([inp["rwkv_w_r"][0][:, cs], inp["rwkv_w_k"][0][:, cs], inp["rwkv_w_v"][0][:, cs],
                           inp["rwkv_w1"][0], inp["rwkv_a1"][0], inp["rwkv_g1"][0]], axis=1)
    vecs = [inp["rwkv_w0"][0][cs], inp["rwkv_a0"][0][cs], inp["rwkv_k_k"][0][cs], inp["rwkv_k_a"][0][cs],
            inp["rwkv_r_k"][0].reshape(-1)[cs], inp["rwkv_ln_w"][0][cs], inp["rwkv_ln_b"][0][cs]]
    vec = np.stack([pm(v) for v in vecs], axis=1)
    return {
        "x": np.ascontiguousarray(inp["x"][b]),
        "gm": np.ascontiguousarray(gm, dtype=np.float32),
        "wcat": np.ascontiguousarray(wcat, dtype=np.float32),
        "w2": np.ascontiguousarray(inp["rwkv_w2"][0][:, cs]),
        "a2": np.ascontiguousarray(inp["rwkv_a2"][0][:, cs]),
        "g2": np.ascontiguousarray(inp["rwkv_g2"][0][:, cs]),
        "vec": np.ascontiguousarray(vec, dtype=np.float32),
    }


NT2 = 17
NTOK2 = NT2 * 128
DFF = 4096
HG = 512
NGRP = DFF // HG
NEG_BIG = -30000.0
TWO_PI = 6.283185307179586
PI = 3.141592653589793
C1_2PI = 6.28125
C2_2PI = TWO_PI - 6.28125


def build_phase2():
    nc = bass.Bass("TRN2", target_bir_lowering=False)
    dr = lambda n, s, d=F32: nc.dram_tensor(n, s, d, kind="ExternalInput").ap()
    x_d = dr("x", [NTOK2, D])
    yg_d = dr("ygT", [D, NTOK2], BF16)
    pos_d = dr("pos", [1, NTOK2], I32)
    wo0_d = dr("wo0", [D, D])
    win_d = [dr("win0", [D, DFF]), dr("win1", [D, DFF])]
    wout_d = [dr("wout0", [DFF, D]), dr("wout1", [DFF, D])]
    wqkv_d = dr("wqkv", [D, 1280])
    wo1_d = dr("wo1", [D, D])
    gains_d = dr("gains", [128, 3, 8])
    bq_d = dr("bq", [128, 10])
    rowv_d = dr("rowv", [1, 1024 + 1024 + 128 + 16])
    cst_d = dr("cst", [128, 4])
    out_d = nc.dram_tensor("out", [16 * 128, D], F32, kind="ExternalOutput").ap()

    with ExitStack() as st:
        g = G(nc, st)
        sb = g.sb
        identf = sb("identf", [128, 128], F32)
        ident = sb("ident", [128, 128], BF16)
        g.memset("gpsimd", identf.v(), 0.0)
        g.aselect(identf.v(), identf.v(), [[-1, 128]], ALU.not_equal, 1.0, 0, 1)
        g.copy("vector", ident.v(), identf.v())
        rotf = sb("rotf", [128, 128], F32)
        rot = sb("rot", [128, 128], BF16)
        g.memset("gpsimd", rotf.v(), 0.0)
        for blk in range(2):
            o = blk * 64
            sub = rotf.v(np.s_[:, o:o + 32])
            g.aselect(sub, sub, [[-1, 32]], ALU.not_equal, -1.0, -(o + 32), 1)
            sub = rotf.v(np.s_[:, o + 32:o + 64])
            g.aselect(sub, sub, [[-1, 32]], ALU.not_equal, 1.0, -(o + 32) + 32, 1)
        g.copy("vector", rot.v(), rotf.v())
        mscr = sb("scr", [128, 1024], F32)
        maskb = Buf(mscr.t[:, 0:256], "scr_m0")
        mask1 = Buf(mscr.t[:, 256:512], "scr_m1")
        g.memset("gpsimd", maskb.v(), 0.0)
        g.aselect(maskb.v(), maskb.v(), [[1, 256]], ALU.is_gt, NEG_BIG, 0, -1)
        g.aselect(maskb.v(), maskb.v(), [[-1, 256]], ALU.is_ge, NEG_BIG, 128, 1)
        cst = sb("cst", [128, 4], F32)
        g.dma_in("sync", cst.v(), cst_d[:, :])
        g.copy("vector", mask1.v(), maskb.v())
        g.ts("vector", mask1.v(np.s_[:, 0:128]), maskb.v(np.s_[:, 0:128]), cst.v(np.s_[:, 2:3]), ALU.add)
        maskbf = sb("maskbf", [128, 256], BF16)
        mask1bf = sb("mask1bf", [128, 256], BF16)
        g.ts("vector", maskbf.v(), maskb.v(), 8.0, ALU.mult)
        g.ts("vector", mask1bf.v(), mask1.v(), 8.0, ALU.mult)
        ones_row = sb("ones_row", [1, 128], F32)
        g.memset("gpsimd", ones_row.v(), 1.0)
        gains = sb("gains", [128, 3, 8], F32)
        g.dma_in("sync", gains.v(), gains_d[:, :, :])
        bq = sb("bq", [128, 10], F32)
        g.dma_in("sync", bq.v(), bq_d[:, :])
        rowv = sb("rowv", [1, 1024], F32)
        g.dma_in("sync", rowv.v(), rowv_d[0:1, 1024:2048])
        bvb = sb("bvb", [128, 128], F32)
        sinkb = sb("sinkb", [128, 16], F32)
        g.dma_in("sync", bvb.v(), rowv_d[0:1, 2048:2176].to_broadcast([128, 128]))
        g.dma_in("sync", sinkb.v(), rowv_d[0:1, 2176:2192].to_broadcast([128, 16]))

        xres = sb("xres", [128, NT2, 1024], F32)
        hT = sb("hT", [128, 8, NTOK2], BF16)
        uT = sb("uT", [128, 4, NTOK2], BF16)
        arena = sb("arena", [128, 18432], BF16)
        junk = sb("junk", [128, 1024], BF16)
        ms = sb("ms", [128, NT2], F32)
        xn = [sb("xn0", [128, 1024], BF16)] * 2
        scr = mscr
        NU = 4
        asc = sb("asc", [128, NU, 2, 256], F32)
        relu_s = [Buf(asc.t[:, i].rearrange("p a b -> p (a b)").bitcast(BF16)[:, 0:512], f"asc_relu{i}") for i in range(2)]

        def bf(bank):
            return bank.t[:].bitcast(BF16)

        NTILES = [(0, 512), (512, 512), (1024, 512), (1536, 512), (2048, 128)]
        NTILES_M = {0: NTILES, 1: [(128, 512), (640, 512), (1152, 512), (1664, 512)]}

        def load_x_and_yg():
            for i in range(NT2):
                g.dma_in("sync", xres.v(np.s_[:, i, :], sub=i), x_d[i * 128:(i + 1) * 128, :])
            for kc in range(8):
                g.dma_in("sync", hT.v(np.s_[:, kc, :]), yg_d[kc * 128:(kc + 1) * 128, :])

        def wload(dst_ap, key, src_ap):
            g.kb.op("gpsimd", lambda e: e.dma_start(out=dst_ap, in_=src_ap), writes=[key], dma=True)

        def rstd_all(tiles):
            allk = [("ms", i) for i in tiles]
            lo, hi = tiles[0], tiles[-1] + 1
            for i in tiles:
                g.act(junk.v(), xres.v(np.s_[:, i, :], sub=i), AF.Square, scale=1.0 / 32, accum=ms.v(np.s_[:, i:i + 1], sub=i))
            mv = V(ms.t[:, lo:hi], allk)
            g.ts("vector", mv, mv, RMS_EPS, ALU.add)
            g.act(mv, mv, AF.Sqrt)
            g.recip(mv, mv)

        def norm_to_hT(gi, tiles):
            tiles = list(tiles)
            rstd_all(tiles)
            for i in tiles:
                xj = xn[i % 2]
                g.act(xj.v(), xres.v(np.s_[:, i, :], sub=i), AF.Copy, scale=ms.v(np.s_[:, i:i + 1], sub=i))
                bk = g.bank()
                bv = bf(bk)
                for kc in range(8):
                    g.tr(bk.w(bv[:, kc * 128:(kc + 1) * 128]), xj.v(np.s_[:, kc * 128:(kc + 1) * 128]), ident.v())
                g.tt("vector", hT.v(np.s_[:, :, i * 128:(i + 1) * 128]), bk.w(bv.rearrange("p (k t) -> p k t", k=8)),
                     gains.w(gains.t[:, gi, :].unsqueeze(2).to_broadcast([128, 8, 128])), ALU.mult)

        def mlp(layer, first_tile=0):
            win, wout = win_d[layer], wout_d[layer]
            ntiles = [(n0, nn) for (n0, nn) in NTILES_M[first_tile]]
            winv = win.rearrange("(kc p) n -> p kc n", p=128)
            woutv = wout.rearrange("(m p) n -> p m n", p=128)
            def WI(s):
                return arena.t[:, s * 4096:(s + 1) * 4096].rearrange("p (k n) -> p k n", k=8)

            def WO(s):
                return arena.t[:, 8192 + s * 4096:8192 + (s + 1) * 4096].rearrange("p (m n) -> p m n", m=4)

            ri = 0
            for grp in range(NGRP):
                s = grp % 2
                kwi, kwo = ("arena", "wi%d" % s), ("arena", "wo%d" % s)
                wload(WI(s), kwi, winv[:, :, grp * HG:(grp + 1) * HG])
                wload(WO(s), kwo, woutv[:, grp * 4:(grp + 1) * 4, :])
                for m in range(4):
                    for (n0, nn) in ntiles:
                        bk = g.bank()
                        for kc in range(8):
                            g.mm(bk.v(np.s_[:, 0:nn]), V(WI(s)[:, kc, m * 128:(m + 1) * 128], [kwi]), hT.v(np.s_[:, kc, n0:n0 + nn]),
                                 start=(kc == 0), stop=(kc == 7))
                        r = relu_s[ri % 2]
                        ri += 1
                        g.act(V(r.t[:, 0:nn], r.v().keys + [("asc", 0), ("asc", 1)]), bk.v(np.s_[:, 0:nn]), AF.Relu)
                        g.tt("gpsimd", uT.v(np.s_[:, m, n0:n0 + nn]), r.v(np.s_[:, 0:nn]), r.v(np.s_[:, 0:nn]), ALU.mult)
                for i in range(first_tile, NT2):
                    for half in range(2):
                        bk = g.bank()
                        for m in range(4):
                            g.mm(bk.v(), uT.v(np.s_[:, m, i * 128:(i + 1) * 128]), V(WO(s)[:, m, half * 512:(half + 1) * 512], [kwo]),
                                 start=(m == 0), stop=(m == 3))
                        xs = xres.v(np.s_[:, i, half * 512:(half + 1) * 512], sub=i)
                        g.tt("vector", xs, xs, bk.v(), ALU.add)

        load_x_and_yg()
        wo_v = arena.t[:, 0:8192].rearrange("p (k n) -> p k n", k=8)
        g.kb.op("gpsimd", lambda e: e.dma_start(out=wo_v, in_=wo0_d.rearrange("(kc p) n -> p kc n", p=128)),
                writes=[("arena", "wi0"), ("arena", "wi1")], dma=True)
        for i in range(NT2):
            for half in range(2):
                bk = g.bank()
                for kc in range(8):
                    g.mm(bk.v(), hT.v(np.s_[:, kc, i * 128:(i + 1) * 128]),
                         V(wo_v[:, kc, half * 512:(half + 1) * 512], [("arena", "wi0"), ("arena", "wi1")]),
                         start=(kc == 0), stop=(kc == 7))
                xs = xres.v(np.s_[:, i, half * 512:(half + 1) * 512], sub=i)
                g.tt("vector", xs, xs, bk.v(), ALU.add)

        norm_to_hT(0, range(NT2))
        mlp(0)

        norm_to_hT(1, range(NT2))
        wq_v = arena.t[:, 0:10240].rearrange("p (k n) -> p k n", k=8)
        wo1_v = arena.t[:, 10240:18432].rearrange("p (k n) -> p k n", k=8)
        KQ = [("arena", "wi0"), ("arena", "wi1"), ("arena", "wo0")]
        KO = [("arena", "wo0"), ("arena", "wo1"), ("arena", "x")]
        g.kb.op("gpsimd", lambda e: e.dma_start(out=wq_v, in_=wqkv_d.rearrange("(kc p) n -> p kc n", p=128)), writes=KQ, dma=True)
        g.kb.op("gpsimd", lambda e: e.dma_start(out=wo1_v, in_=wo1_d.rearrange("(kc p) n -> p kc n", p=128)), writes=KO, dma=True)
        tabs = uT.t[:].rearrange("p a b -> p (a b)").bitcast(F32)
        cosT = uT.w(tabs[:, 0:NTOK2])
        sinT = uT.w(tabs[:, NTOK2:2 * NTOK2])
        posi = V(scr.t[:, 0:512].bitcast(I32), [("scr", "a"), ("scr_m0", None), ("scr_m1", None)])
        for (n0, nn) in NTILES:
            pch = V(posi.ap[:, 0:nn], posi.keys)
            ach = scr.v(np.s_[:, 512:512 + nn], sub="b")
            g.dma_in("sync", pch, pos_d[0:1, n0:n0 + nn].to_broadcast([128, nn]))
            g.copy("vector", ach, pch)
            g.ts("vector", ach, ach, cst.v(np.s_[:, 0:1]), ALU.mult)
            sch = V(sinT.ap[:, n0:n0 + nn], sinT.keys)
            cch = V(cosT.ap[:, n0:n0 + nn], cosT.keys)
            T1 = V(xn[0].t[:].bitcast(F32)[:, 0:nn], [("xn0", None)])
            A2 = V(junk.t[:].bitcast(F32)[:, 0:nn], [("junk", None)])
            TI = pch
            for (src, dst, shift) in ((ach, sch, 0.0), (ach, cch, 0.5 * PI)):
                if shift:
                    g.ts("vector", A2, src, shift, ALU.add)
                    src = A2
                g.ts("vector", T1, src, 1.0 / TWO_PI, ALU.mult)
                g.copy("vector", TI, T1)
                g.copy("vector", T1, TI)
                g.stt("vector", dst, T1, -C1_2PI, src, ALU.mult, ALU.add)
                g.stt("vector", dst, T1, -C2_2PI, dst, ALU.mult, ALU.add)
                g.ts("vector", dst, dst, -PI, ALU.max, PI, ALU.min)
                g.act(dst, dst, AF.Sin)

        kr = sb("kr", [128, 2, NTOK2], BF16)
        vpad = [sb(f"vpad{i}", [128, 2, 2, 128], BF16) for i in range(3)]
        for v_ in vpad:
            g.memset("gpsimd", v_.v(), 0.0)
        qb16 = sb("qb16", [128, 512], BF16)
        SCALE = 0.125
        negsink = sb("negsink", [128, 16], F32)
        esink = sb("esink", [128, 16], F32)
        g.ts("vector", negsink.v(), sinkb.v(), -1.0, ALU.mult)
        g.act(esink.v(), sinkb.v(), AF.Exp)

        def rope_evac(bk, nn, bias_col, n0, dst, qf, qb, r2):
            g.act(qf, bk.v(np.s_[:, 0:nn]), AF.Identity, bias=bias_col)
            g.copy("gpsimd", qb, qf)
            b2 = g.bank()
            g.mm(b2.v(np.s_[:, 0:nn]), rot.v(), qb)
            g.tt("vector", r2, b2.v(np.s_[:, 0:nn]), V(sinT.ap[:, n0:n0 + nn], sinT.keys), ALU.mult)
            g.tt("gpsimd", qf, qf, V(cosT.ap[:, n0:n0 + nn], cosT.keys), ALU.mult)
            g.tt("vector", dst, qf, r2, ALU.add)

        wkd_ap = asc.t[:].rearrange("p a b c -> p (a b c)")[:, 0:1024].bitcast(BF16).rearrange("p (k j c) -> p k j c", k=8, j=2)
        WK = [("asc", u) for u in range(NU)] + [("asc_relu0", None), ("asc_relu1", None)]
        for j in range(2):
            for dup in range(2):
                g.copy("vector", V(wkd_ap[:, :, j, dup * 64:(dup + 1) * 64], WK), V(wq_v[:, :, 1024 + j * 64:1024 + (j + 1) * 64], KQ))
        for j in range(2):
            for (n0, nn) in NTILES:
                bk = g.bank()
                for kc in range(8):
                    g.mm(bk.v(np.s_[:, 0:nn]), V(wkd_ap[:, kc, j, :], WK), hT.v(np.s_[:, kc, n0:n0 + nn]), start=(kc == 0), stop=(kc == 7))
                rope_evac(bk, nn, bq.v(np.s_[:, 8 + j:9 + j]), n0, kr.v(np.s_[:, j, n0:n0 + nn]),
                          scr.v(np.s_[:, 0:nn], sub="a"), qb16.v(np.s_[:, 0:nn]), scr.v(np.s_[:, 512:512 + nn], sub="b"))

        def make_vpad(i):
            vp = vpad[i % 3]
            bk = g.bank()
            for kc in range(8):
                g.mm(bk.v(np.s_[:, 0:128]), hT.v(np.s_[:, kc, i * 128:(i + 1) * 128]), V(wq_v[:, kc, 1152:1280], KQ), start=(kc == 0), stop=(kc == 7))
            for q2 in range(2):
                g.tt("vector", vp.v(np.s_[:, :, q2, q2 * 64:(q2 + 1) * 64]), bk.w(bk.t[:, 0:128].rearrange("p (j d) -> p j d", j=2)),
                     bvb.w(bvb.t[:].rearrange("p (j d) -> p j d", j=2)), ALU.add)
            return vp

        qr2 = [sb(f"qr{i}", [128, 8, 128], BF16) for i in range(2)]
        oT = sb("oT", [128, 8, 128], BF16)
        pn = [sb(f"pn{i}", [128, 2, 256], BF16) for i in range(NU)]
        pT = [sb(f"pT{i}", [128, 2, 2, 128], BF16) for i in range(NU)]
        stat = [sb(f"stat{i}", [128, 8], F32) for i in range(NU)]
        def prep(i):
            make_vpad(i)
            qr = qr2[i % 2]
            yield
            for hh in range(2):
                bk = g.bank()
                for a4 in range(4):
                    hp = hh * 4 + a4
                    for kc in range(8):
                        g.mm(bk.v(np.s_[:, a4 * 128:(a4 + 1) * 128]), V(wq_v[:, kc, hp * 128:(hp + 1) * 128], KQ),
                             hT.v(np.s_[:, kc, i * 128:(i + 1) * 128]), start=(kc == 0), stop=(kc == 7))
                qf = scr.v(np.s_[:, 0:512], sub="a")
                r2 = scr.v(np.s_[:, 512:1024], sub="b")
                qb = qb16.v()
                qf3 = V(scr.t[:, 0:512].rearrange("p (a t) -> p a t", a=4), [("scr", "a")])
                r23 = V(scr.t[:, 512:1024].rearrange("p (a t) -> p a t", a=4), [("scr", "b")])
                g.tt("vector", qf3, bk.w(bk.t[:].rearrange("p (a t) -> p a t", a=4)),
                     bq.w(bq.t[:, hh * 4:hh * 4 + 4].unsqueeze(2).to_broadcast([128, 4, 128])), ALU.add)
                g.copy("gpsimd", qb, qf)
                b2 = g.bank()
                g.mm(b2.v(), rot.v(), qb)
                sin_b = V(sinT.ap[:, i * 128:(i + 1) * 128].unsqueeze(1).to_broadcast([128, 4, 128]), sinT.keys)
                cos_b = V(cosT.ap[:, i * 128:(i + 1) * 128].unsqueeze(1).to_broadcast([128, 4, 128]), cosT.keys)
                g.tt("vector", r23, b2.w(b2.t[:].rearrange("p (a t) -> p a t", a=4)), sin_b, ALU.mult)
                g.tt("gpsimd", qf3, qf3, cos_b, ALU.mult)
                g.tt("vector", qr.v(np.s_[:, hh * 4:hh * 4 + 4, :]), qf3, r23, ALU.add)
                yield

        def attn(i):
            vps = [vpad[(i - 1) % 3], vpad[i % 3]]
            qr = qr2[i % 2]
            mk = mask1bf if i == 1 else maskbf
            for j in range(2):
                sbank = {}
                for gp in range(2):
                    hp0 = 4 * j + 2 * gp
                    bks = [g.bank(), g.bank()]
                    for a in range(2):
                        for q2 in range(2):
                            o = bks[q2].v(np.s_[:, a * 256:(a + 1) * 256])
                            g.mm(o, ident.v(), mk.v(), start=True, stop=False)
                            g.mm(o, qr.v(np.s_[q2 * 64:(q2 + 1) * 64, hp0 + a, :]),
                                 kr.v(np.s_[q2 * 64:(q2 + 1) * 64, j, (i - 1) * 128:(i + 1) * 128]), start=False, stop=True)
                    for q2 in range(2):
                        sbank[gp * 2 + q2] = bks[q2]
                UN = range(4)
                yield
                for u in UN:
                    gp, q2 = divmod(u, 2)
                    st_ = stat[u]
                    ps3 = sbank[u].w(sbank[u].t[:].rearrange("p (a k) -> p a k", a=2))
                    g.kb.op("vector", lambda e, st_=st_, ps3=ps3: e.tensor_reduce(out=st_.t[:, 0:2], in_=ps3.ap, axis=mybir.AxisListType.X, op=ALU.max),
                            reads=ps3.keys, writes=st_.v().keys)
                    h0 = 2 * (4 * j + 2 * gp) + q2
                    g.stt("vector", st_.v(np.s_[:, 2:4]), st_.v(np.s_[:, 0:2]), -SCALE, negsink.w(negsink.t[:, h0:h0 + 3:2]), ALU.mult, ALU.min)
                yield
                for u in UN:
                    st_ = stat[u]
                    for a in range(2):
                        g.act(V(asc.t[:, u, a, :], [("asc", u)]), sbank[u].v(np.s_[:, a * 256:(a + 1) * 256]), AF.Exp, scale=SCALE,
                              bias=st_.v(np.s_[:, 2 + a:3 + a]), accum=st_.v(np.s_[:, 4 + a:5 + a]))
                    g.act(st_.v(np.s_[:, 6:8]), st_.v(np.s_[:, 2:4]), AF.Exp)
                yield
                for u in UN:
                    gp, q2 = divmod(u, 2)
                    st_ = stat[u]
                    h0 = 2 * (4 * j + 2 * gp) + q2
                    g.tt("vector", st_.v(np.s_[:, 6:8]), st_.v(np.s_[:, 6:8]), esink.w(esink.t[:, h0:h0 + 3:2]), ALU.mult)
                    g.tt("vector", st_.v(np.s_[:, 4:6]), st_.v(np.s_[:, 4:6]), st_.v(np.s_[:, 6:8]), ALU.add)
                    g.recip(st_.v(np.s_[:, 4:6]), st_.v(np.s_[:, 4:6]))
                for u in UN:
                    st_ = stat[u]
                    g.tt("gpsimd", pn[u].v(), V(asc.t[:, u], [("asc", u)]), st_.w(st_.t[:, 4:6].unsqueeze(2).to_broadcast([128, 2, 256])), ALU.mult)
                yield
                tbs = {}
                for u in UN:
                    tb = g.bank()
                    tv = bf(tb)
                    for a in range(2):
                        for kb in range(2):
                            sl = (a * 2 + kb) * 128
                            g.tr(tb.w(tv[:, sl:sl + 128]), pn[u].v(np.s_[:, a, kb * 128:(kb + 1) * 128]), ident.v())
                    tbs[u] = (tb, tv)
                for u in UN:
                    tb, tv = tbs[u]
                    g.copy("scalar" if u % 2 == 0 else "vector", pT[u].v(), tb.w(tv[:, 0:512].rearrange("p (a k q) -> p a k q", a=2, k=2)))
                yield
                for gp in range(2):
                    hp0 = 4 * j + 2 * gp
                    obk = g.bank()
                    for a in range(2):
                        first = True
                        for q2 in range(2):
                            for kb in range(2):
                                g.mm(obk.v(np.s_[:, a * 128:(a + 1) * 128]), vps[kb].v(np.s_[:, j, q2, :]), pT[gp * 2 + q2].v(np.s_[:, a, kb, :]),
                                     start=first, stop=(q2 == 1 and kb == 1))
                                first = False
                    g.copy("scalar", oT.v(np.s_[:, hp0:hp0 + 2, :]), obk.w(obk.t[:, 0:256].rearrange("p (a q) -> p a q", a=2)))
            for half in range(2):
                bk = g.bank()
                for hp in range(8):
                    g.mm(bk.v(), oT.v(np.s_[:, hp, :]), V(wo1_v[:, hp, half * 512:(half + 1) * 512], KO), start=(hp == 0), stop=False)
                g.mm(bk.v(), ones_row.v(), rowv.v(np.s_[0:1, half * 512:(half + 1) * 512]), start=False, stop=True)
                xs = xres.v(np.s_[:, i, half * 512:(half + 1) * 512], sub=i)
                g.tt("vector", xs, xs, bk.v(), ALU.add)

            yield

        def drive2(gens):
            gens = list(gens)
            while gens:
                for gn in list(gens):
                    try:
                        next(gn)
                    except StopIteration:
                        gens.remove(gn)

        make_vpad(0)
        drive2([prep(1)])
        for i in range(1, NT2):
            gs = [attn(i)]
            if i + 1 < NT2:
                gs.append(prep(i + 1))
            drive2(gs)

        norm_to_hT(2, range(1, NT2))
        mlp(1, first_tile=1)

        gfb = V(asc.t[:].rearrange("p a b c -> p (a b c)")[:, 0:1024], [("asc", u) for u in range(NU)] + [("asc_relu0", None), ("asc_relu1", None)])
        g.dma_in("sync", gfb, rowv_d[0:1, 0:1024].to_broadcast([128, 1024]))
        rstd_all(list(range(1, NT2)))
        for i in range(1, NT2):
            o_ = V(scr.t[:], [("scr", "a"), ("scr", "b")])
            g.stt("vector", o_, xres.v(np.s_[:, i, :], sub=i), ms.v(np.s_[:, i:i + 1], sub=i), gfb, ALU.mult, ALU.mult)
            g.dma_out("sync", out_d[(i - 1) * 128:i * 128, :], o_)
        g.finish()
    return nc


def _pm(v):
    return np.ascontiguousarray(np.asarray(v).reshape(-1, 128).T)


def phase2_inputs(inp, ygT_full, core):
    b, tq = divmod(core, 4)
    t0 = tq * 2048
    x = np.zeros((NTOK2, D), np.float32)
    yg = np.zeros((D, NTOK2), ml_dtypes.bfloat16)
    pos = np.zeros((1, NTOK2), np.int32)
    lo = t0 - 128
    if tq > 0:
        x[:] = inp["x"][b, lo:lo + NTOK2]
        yg[:] = ygT_full[b][:, lo:lo + NTOK2]
        pos[0] = inp["positions"][b, lo:lo + NTOK2]
    else:
        x[128:] = inp["x"][b, 0:2048]
        yg[:, 128:] = ygT_full[b][:, 0:2048]
        pos[0, 128:] = inp["positions"][b, 0:2048]
    gains = np.stack([_pm(inp["norm_mlp_g"][0]), _pm(inp["norm_mix_g"][1]), _pm(inp["norm_mlp_g"][1])], axis=1)
    bqkv = inp["attn_b_qkv"][0]
    bq = np.zeros((128, 10), np.float32)
    bq[:, 0:8] = _pm(bqkv[0:1024])
    for j in range(2):
        bk = bqkv[1024 + j * 64:1024 + (j + 1) * 64]
        bq[:, 8 + j] = np.concatenate([bk, bk])
    rowv = np.concatenate([inp["norm_final_g"], inp["attn_b_o"][0], bqkv[1152:1280], inp["attn_sinks"][0]])[None, :]
    cst = np.zeros((128, 4), np.float32)
    p = np.arange(128)
    cst[:, 0] = (10000.0 ** (-(np.arange(0, 64, 2, dtype=np.float32)) / 64.0))[p % 32]
    cst[:, 1] = np.where((p % 64) < 32, 1.0, 1.0)
    cst[:, 2] = 0.0 if tq > 0 else NEG_BIG
    return {
        "x": x, "ygT": yg, "pos": pos,
        "wo0": np.ascontiguousarray(inp["rwkv_w_o"][0]),
        "win0": np.ascontiguousarray(inp["mlp_w_in"][0]), "win1": np.ascontiguousarray(inp["mlp_w_in"][1]),
        "wout0": np.ascontiguousarray(inp["mlp_w_out"][0]), "wout1": np.ascontiguousarray(inp["mlp_w_out"][1]),
        "wqkv": np.ascontiguousarray(inp["attn_w_qkv"][0]), "wo1": np.ascontiguousarray(inp["attn_w_o"][0]),
        "gains": np.ascontiguousarray(gains, dtype=np.float32), "bq": bq,
        "rowv": np.ascontiguousarray(rowv, dtype=np.float32), "cst": cst,
    }


_NC_CACHE = {}


def kernel(**inputs):
    inp = {k: np.asarray(v) for k, v in inputs.items()}
    if "p1" not in _NC_CACHE:
        _NC_CACHE["p1"] = build_phase1()
        _NC_CACHE["p2"] = build_phase2()
    r1 = run_bass_kernel_spmd(_NC_CACHE["p1"], [phase1_inputs(inp, c) for c in range(8)], core_ids=list(range(8)))
    ygT = np.zeros((2, D, S_LEN), ml_dtypes.bfloat16)
    for c in range(8):
        b, hg = divmod(c, 4)
        ygT[b, hg * 256:(hg + 1) * 256] = r1.results[c]["yg"]
    r2 = run_bass_kernel_spmd(_NC_CACHE["p2"], [phase2_inputs(inp, ygT, c) for c in range(8)], core_ids=list(range(8)))
    out = np.zeros((2, S_LEN, D), np.float32)
    for c in range(8):
        b, tq = divmod(c, 4)
        out[b, tq * 2048:(tq + 1) * 2048] = r2.results[c]["out"]
    return out
```

```python
import numpy as np
import ml_dtypes
from contextlib import ExitStack
import concourse.bass as bass
import concourse.mybir as mybir
from concourse.bass_utils import run_bass_kernel_spmd

F32 = mybir.dt.float32
BF16 = mybir.dt.bfloat16
I32 = mybir.dt.int32
AF = mybir.ActivationFunctionType
ALU = mybir.AluOpType

EPOCH = 4000
DMA_ROT = 8
NEG_C = -0.6065306597126334


class KB:
    ENGS = ("tensor", "vector", "scalar", "gpsimd", "sync")

    def __init__(self, nc, stack):
        self.nc = nc
        self.stack = stack
        self.streams = {e: [] for e in self.ENGS}
        self.count = {e: 0 for e in self.ENGS}
        self.sems = {e: [] for e in self.ENGS}
        self.dma_count = {e: 0 for e in self.ENGS}
        self.dma_sems = {e: [] for e in self.ENGS}
        self.waited = {e: {} for e in self.ENGS}
        self.last_w = {}
        self.readers = {}

    def _newsem(self, name):
        return self.stack.enter_context(self.nc.semaphore(name))

    def _compute_token(self, e):
        n = self.count[e]
        ep, idx = divmod(n, EPOCH)
        while len(self.sems[e]) <= ep:
            self.sems[e].append(self._newsem(f"s_{e}_{len(self.sems[e])}"))
        self.count[e] = n + 1
        return (self.sems[e][ep], idx + 1, 1, e)

    def _dma_token(self, e):
        n = self.dma_count[e]
        if not self.dma_sems[e]:
            self.dma_sems[e] = [self._newsem(f"d_{e}_{i}") for i in range(DMA_ROT)]
        self.dma_count[e] = n + 1
        return (self.dma_sems[e][n % DMA_ROT], 16 * (n // DMA_ROT + 1), 16, "dma_" + e)

    def op(self, e, fn, reads=(), writes=(), dma=False):
        deps = []
        for k in reads:
            t = self.last_w.get(k)
            if t is not None:
                deps.append((t, True))
        for k in writes:
            t = self.last_w.get(k)
            if t is not None:
                deps.append((t, False))
            for t in self.readers.get(k, ()):
                deps.append((t, False))
        wd = self.waited[e]
        ww = {}
        for (sem, val, _inc, src), is_raw in deps:
            if src == e and not dma and e == "tensor":
                continue
            sid = id(sem)
            if wd.get(sid, 0) >= val:
                continue
            if sid not in ww or ww[sid][1] < val:
                ww[sid] = (sem, val)
        for sid, (sem, val) in ww.items():
            wd[sid] = val
        if dma:
            n = self.dma_count[e]
            if n >= DMA_ROT:
                sem = self.dma_sems[e][n % DMA_ROT]
                val = 16 * (n // DMA_ROT)
                if wd.get(id(sem), 0) < val:
                    wd[id(sem)] = val
                    ww[id(sem)] = (sem, val)
        tok = self._dma_token(e) if dma else self._compute_token(e)
        self.streams[e].append((list(ww.values()), fn, tok))
        for k in reads:
            self.readers.setdefault(k, []).append(tok)
        for k in writes:
            self.last_w[k] = tok
            self.readers[k] = []
        return tok

    def wait_tokens(self, e, toks):
        wd = self.waited[e]
        waits = []
        for (sem, val, _i, _s) in toks:
            if wd.get(id(sem), 0) >= val:
                continue
            wd[id(sem)] = val
            waits.append((sem, val))
        self.streams[e].append((waits, None, None))

    def emit(self):
        nc = self.nc
        with nc.Block() as block:
            def mk(e):
                def body(eng):
                    for waits, fn, tok in self.streams[e]:
                        for sem, val in waits:
                            eng.wait_ge(sem, val)
                        if fn is not None:
                            ins = fn(eng)
                            ins.then_inc(tok[0], tok[2])
                return body
            for e in self.ENGS:
                if self.streams[e]:
                    getattr(block, e)(mk(e))


class V:
    __slots__ = ("ap", "keys")

    def __init__(self, ap, keys):
        self.ap = ap
        self.keys = keys


class Buf:
    def __init__(self, t, name):
        self.t = t
        self.name = name

    def v(self, idx=None, sub=None):
        ap = self.t[idx] if idx is not None else self.t[:]
        return V(ap, [(self.name, sub)])

    def w(self, ap, sub=None):
        return V(ap, [(self.name, sub)])


class G:
    def __init__(self, nc, st):
        self.nc = nc
        self.st = st
        self.kb = KB(nc, st)
        self.banks = [Buf(st.enter_context(nc.psum_tensor(f"psb{i}", [128, 512], F32)), f"ps{i}") for i in range(8)]
        self.bank_i = 0
        self.out_tokens = []

    def sb(self, name, shape, dt):
        return Buf(self.st.enter_context(self.nc.sbuf_tensor("sb_" + name, shape, dt)), name)

    def bank(self):
        b = self.banks[self.bank_i % 8]
        self.bank_i += 1
        return b

    @staticmethod
    def _k(vs):
        ks = []
        for v in vs:
            if isinstance(v, V):
                ks.extend(v.keys)
        return ks

    def mm(self, out, lhsT, rhs, start=True, stop=True):
        return self.kb.op("tensor", lambda e: e.matmul(out.ap, lhsT=lhsT.ap, rhs=rhs.ap, start=start, stop=stop),
                          reads=self._k([lhsT, rhs]), writes=out.keys)

    def tr(self, out, in_, ident):
        return self.kb.op("tensor", lambda e: e.transpose(out=out.ap, in_=in_.ap, identity=ident.ap),
                          reads=self._k([in_, ident]), writes=out.keys)

    def act(self, out, in_, func, bias=None, scale=1.0, accum=None, eng="scalar"):
        kw = {}
        if bias is not None:
            kw["bias"] = bias.ap if isinstance(bias, V) else bias
        if accum is not None:
            kw["accum_out"] = accum.ap
        sc = scale.ap if isinstance(scale, V) else scale
        return self.kb.op("scalar", lambda e: e.activation(out=out.ap, in_=in_.ap, func=func, scale=sc, **kw),
                          reads=self._k([in_, bias, scale]), writes=self._k([out, accum]))

    def tt(self, eng, out, a, b, op):
        return self.kb.op(eng, lambda e: e.tensor_tensor(out=out.ap, in0=a.ap, in1=b.ap, op=op),
                          reads=self._k([a, b]), writes=out.keys)

    def ts(self, eng, out, a, s1, op0, s2=None, op1=None):
        s1a = s1.ap if isinstance(s1, V) else s1
        s2a = s2.ap if isinstance(s2, V) else s2
        if op1 is None:
            fn = lambda e: e.tensor_scalar(out=out.ap, in0=a.ap, scalar1=s1a, scalar2=None, op0=op0)
        else:
            fn = lambda e: e.tensor_scalar(out=out.ap, in0=a.ap, scalar1=s1a, scalar2=s2a, op0=op0, op1=op1)
        return self.kb.op(eng, fn, reads=self._k([a, s1, s2]), writes=out.keys)

    def stt(self, eng, out, in0, scalar, in1, op0, op1):
        sa = scalar.ap if isinstance(scalar, V) else scalar
        return self.kb.op(eng, lambda e: e.scalar_tensor_tensor(out=out.ap, in0=in0.ap, scalar=sa, in1=in1.ap, op0=op0, op1=op1),
                          reads=self._k([in0, scalar, in1]), writes=out.keys)

    def copy(self, eng, out, in_):
        if eng == "scalar":
            return self.act(out, in_, AF.Copy)
        return self.kb.op(eng, lambda e: e.tensor_copy(out=out.ap, in_=in_.ap), reads=in_.keys, writes=out.keys)

    def memset(self, eng, out, val):
        return self.kb.op(eng, lambda e: e.memset(out.ap, val), writes=out.keys)

    def recip(self, out, in_):
        return self.kb.op("vector", lambda e: e.reciprocal(out=out.ap, in_=in_.ap), reads=in_.keys, writes=out.keys)

    def scan(self, out, d0, d1, init, op0, op1):
        return self.kb.op("vector", lambda e: e.tensor_tensor_scan(out=out.ap, data0=d0.ap, data1=d1.ap, initial=init, op0=op0, op1=op1),
                          reads=self._k([d0, d1]), writes=out.keys)

    def aselect(self, out, in_, pattern, cmp, fill, base, cm):
        return self.kb.op("gpsimd", lambda e: e.affine_select(out=out.ap, in_=in_.ap, pattern=pattern, compare_op=cmp,
                                                               fill=fill, base=base, channel_multiplier=cm),
                          reads=in_.keys, writes=out.keys)

    def dma_in(self, eng, out, in_ap, **kw):
        return self.kb.op(eng, lambda e: e.dma_start(out=out.ap, in_=in_ap, **kw), writes=out.keys, dma=True)

    def dma_out(self, eng, out_ap, in_, final=True):
        t = self.kb.op(eng, lambda e: e.dma_start(out=out_ap, in_=in_.ap), reads=in_.keys, dma=True)
        if final:
            self.out_tokens.append(t)
        return t

    def finish(self):
        self.kb.wait_tokens("sync", self.out_tokens)
        self.kb.emit()


S_LEN = 8192
D = 1024
TB = 512
NCH = TB // 128
GN_EPS = 64e-5
RMS_EPS = 1e-5
PROJ = [("r", 0, 256, 0), ("k", 2, 256, 256), ("v", 3, 256, 512), ("w1", 1, 64, 768), ("a1", 4, 64, 832), ("g1", 5, 160, 896)]
NCOL = 1056


BACK_W = 4


def build_phase1(n_tok=S_LEN, stop=None, stopargs=()):
    nc = bass.Bass("TRN2", target_bir_lowering=False)
    dr = lambda n, s, d=F32: nc.dram_tensor(n, s, d, kind="ExternalInput").ap()
    x_d = dr("x", [n_tok, D])
    gm_d = dr("gm", [128, 7, 8])
    wcat_d = dr("wcat", [D, NCOL])
    w2_d = dr("w2", [64, 256])
    a2_d = dr("a2", [64, 256])
    g2_d = dr("g2", [160, 256])
    vec_d = dr("vec", [128, 7, 2])
    yg_d = nc.dram_tensor("yg", [256, n_tok], BF16, kind="ExternalOutput").ap()
    dbg_d = nc.dram_tensor("dbg", [128, 2560], F32, kind="ExternalOutput").ap() if stop else None
    nblk = n_tok // TB

    class Stop(Exception):
        pass

    with ExitStack() as st:
        g = G(nc, st)
        sb = g.sb
        dbgt = sb("dbgt", [128, 2560], F32) if stop else None
        dbg_off = [0]

        def dump(v, n):
            o = dbg_off[0]
            p = v.ap.shape[0]
            g.copy("vector", dbgt.w(dbgt.t[0:p, o:o + n]), v)
            dbg_off[0] = o + n

        def chk(name):
            if stop == name:
                raise Stop()
        def emit_all():
            identf = sb("identf", [128, 128], F32)
            ident = sb("ident", [128, 128], BF16)
            ident4 = sb("ident4", [128, 4, 128], BF16)
            onesblk = sb("onesblk", [128, 128], F32)
            m_su = sb("m_su", [128, 4, 128], BF16)
            m_u = sb("m_u", [128, 4, 128], BF16)
            m_sl = sb("m_sl", [128, 4, 128], BF16)
            rmask = sb("rmask", [128, TB], F32)
            g.memset("gpsimd", identf.v(), 0.0)
            g.aselect(identf.v(), identf.v(), [[-1, 128]], ALU.not_equal, 1.0, 0, 1)
            g.copy("vector", ident.v(), identf.v())
            for h in range(4):
                g.copy("vector", ident4.v(np.s_[:, h, :]), identf.v())
            g.memset("gpsimd", onesblk.v(), 0.0)
            g.memset("gpsimd", onesblk.v(np.s_[0:64, 0:64]), 1.0)
            g.memset("gpsimd", onesblk.v(np.s_[64:128, 64:128]), 1.0)
            for m, cm, pat, cmp in ((m_su, -1, 1, ALU.is_gt), (m_u, -1, 1, ALU.is_ge), (m_sl, 1, -1, ALU.is_gt)):
                g.memset("gpsimd", m.v(), 1.0)
                g.aselect(m.v(), m.v(), [[0, 4], [pat, 128]], cmp, 0.0, 0, cm)
            g.memset("gpsimd", rmask.v(), 1.0)
            g.memset("gpsimd", rmask.w(rmask.t[:].rearrange("p (c t) -> p c t", t=128)[:, :, 0:1]), 0.0)

            if stop == "const":
                dump(identf.v(), 128); dump(m_su.v(np.s_[:, 1, :]), 128); dump(m_sl.v(np.s_[:, 2, :]), 128); dump(m_u.v(np.s_[:, 3, :]), 128)
                dump(onesblk.v(), 128); dump(rmask.v(), 512)
            chk("const")
            gm = sb("gm", [128, 7, 8], F32)
            coefA = sb("coefA", [128, 6, 8], F32)
            coefB = sb("coefB", [128, 6, 8], F32)
            vec = sb("vec", [128, 7, 2], F32)
            WA = sb("WA", [128, 8, NCOL], BF16)
            WB = sb("WB", [128, 8, NCOL], BF16)
            w2b = sb("w2b", [64, 256], BF16)
            a2b = sb("a2b", [128, 256], BF16)
            g2a = sb("g2a", [128, 256], BF16)
            g2b = sb("g2b", [32, 256], BF16)
            xin = sb("xin", [128, 4, 1024], F32)
            g.dma_in("sync", gm.v(), gm_d[:, :, :])
            g.dma_in("sync", vec.v(), vec_d[:, :, :])
            g.dma_in("gpsimd", w2b.v(), w2_d[:, :])
            g.dma_in("gpsimd", a2b.v(np.s_[64:128, :]), a2_d[:, :])
            g.dma_in("gpsimd", g2a.v(), g2_d[0:128, :])
            g.dma_in("gpsimd", g2b.v(), g2_d[128:160, :])
            g0b = gm.w(gm.t[:, 0:1, :].to_broadcast([128, 6, 8]))
            g.tt("vector", coefB.v(), gm.v(np.s_[:, 1:7, :]), g0b, ALU.mult)
            g.tt("vector", coefA.v(), g0b, coefB.v(), ALU.subtract)
            stage = xin
            wv = wcat_d.rearrange("(kc p) n -> p kc n", p=128)
            XK = [("xin", j) for j in range(4)]
            for (nm, mi, ncol, off) in PROJ:
                sv = stage.t[:].rearrange("p a b -> p (a b)")[:, 0:8 * ncol].rearrange("p (k n) -> p k n", k=8)
                g.kb.op("sync", lambda e, sv=sv, off=off, ncol=ncol: e.dma_start(out=sv, in_=wv[:, :, off:off + ncol]), writes=XK, dma=True)
                ca = coefA.w(coefA.t[:, mi, :].unsqueeze(2).to_broadcast([128, 8, ncol]))
                cb = coefB.w(coefB.t[:, mi, :].unsqueeze(2).to_broadcast([128, 8, ncol]))
                g.tt("vector", WA.v(np.s_[:, :, off:off + ncol]), V(sv, XK), ca, ALU.mult)
                g.tt("gpsimd", WB.v(np.s_[:, :, off:off + ncol]), V(sv, XK), cb, ALU.mult)

            if stop == "weights":
                dump(WA.v(np.s_[:, 3, 0:512]), 512); dump(WB.v(np.s_[:, 7, 544:1056]), 512); dump(coefA.v(np.s_[:, 2, :]), 8)
            chk("weights")
            ms = sb("ms", [128, 4], F32)
            xn = [sb(f"xn{i}", [128, 1024], BF16) for i in range(2)]
            hT = sb("hT", [128, 8, TB + 1], BF16)
            T = {n: sb("t_" + n, [128, TB], F32) for n in
                 ("rT", "kraw", "aT", "sgw", "cs", "Er", "En", "kk", "kp", "beta", "t1", "t2", "EC")}
            junk = Buf(T["EC"].t[:].bitcast(BF16), "t_EC")
            tw = sb("tw", [128, TB], BF16)
            sg1a = sb("sg1a", [128, TB], BF16)
            sg1b = sb("sg1b", [32, TB], BF16)
            vT = sb("vT", [128, 2, TB], F32)
            gT2 = [sb(f"gT{i}", [128, 2, TB], F32) for i in range(2)]
            bonusT2 = [sb(f"bonusT{i}", [128, 2, TB], F32) for i in range(2)]
            PCs2 = [sb(f"PCs{i}", [128, 2, NCH], F32) for i in range(2)]
            rt2 = [sb(f"rt_{i}", [128, 2, TB], BF16) for i in range(2)]
            kt2 = [sb(f"kt_{i}", [128, 2, TB], BF16) for i in range(2)]
            at2 = [sb(f"at_{i}", [128, 2, TB], BF16) for i in range(2)]
            bt2 = [sb(f"bt_{i}", [128, 2, TB], BF16) for i in range(2)]
            T2 = {n: sb("t2_" + n, [128, TB], F32) for n in ("o1", "o2")}
            khT = sb("khT", [128, 2, TB], BF16)
            bhT = sb("bhT", [128, 2, TB], BF16)
            XR2 = [sb(f"XR{i}", [128, NCH, 4, 128], BF16) for i in range(2)]
            Kpad2 = [sb(f"Kpad{i}", [128, NCH, 4, 128], BF16) for i in range(2)]
            Bpad2 = [sb(f"Bpad{i}", [128, NCH, 4, 128], BF16) for i in range(2)]
            Vpad2 = [sb(f"Vpad{i}", [128, NCH, 4, 128], BF16) for i in range(2)]
            NSL = 2
            Pb = [[sb(f"P{s}{i}", [128, 4, 128], BF16) for i in range(2)] for s in range(NSL)]
            SUt = [sb(f"SU{s}", [128, 2, 4, 128], BF16) for s in range(NSL)]
            UUt = [sb(f"UU{s}", [128, 2, 4, 128], BF16) for s in range(NSL)]
            PTb = [[Buf(SUt[s].t[:, 0], f"PT{s}0"), sb(f"PT{s}1", [128, 4, 128], BF16)] for s in range(NSL)]
            NTb = [[sb(f"NT{s}{i}", [128, 4, 128], BF16) for i in range(2)] for s in range(NSL)]
            AakT = [Buf(SUt[s].t[:, 1], f"AakT{s}") for s in range(NSL)]
            ArbT = [Buf(UUt[s].t[:, 0], f"ArbT{s}") for s in range(NSL)]
            ArkT = [Buf(UUt[s].t[:, 1], f"ArkT{s}") for s in range(NSL)]
            Apad = [sb(f"Apad{s}", [128, 4, 128], BF16) for s in range(NSL)]
            Wpad = [sb(f"Wpad{s}", [128, 4, 128], BF16) for s in range(NSL)]
            TTbd = [sb(f"TTbd{s}", [128, 2, 128], BF16) for s in range(NSL)]
            RhT = [sb(f"RhT{s}", [128, 2, 128], BF16) for s in range(NSL)]
            Sbd = [sb(f"Sbd{i}", [128, 2, 128], BF16) for i in range(2)]
            yraw = sb("yraw", [128, 2, TB], F32)
            ygo = sb("ygo", [128, 2, TB], BF16)
            for b_ in Kpad2 + Bpad2 + Vpad2:
                g.memset("gpsimd", b_.v(), 0.0)
            for s in range(NSL):
                g.memset("gpsimd", Apad[s].v(), 0.0)
                g.memset("gpsimd", Wpad[s].v(), 0.0)
            g.memset("gpsimd", Sbd[0].v(), 0.0)
            g.memset("vector", hT.v(np.s_[:, :, 0:1]), 0.0)
            s_cur = 0

            def bf(bank):
                return bank.t[:].bitcast(BF16)

            def hsl(h, j):
                cc, q = divmod(h, 2)
                return np.s_[q * 64:(q + 1) * 64, cc, j * 128:(j + 1) * 128]

            def load_x(blk):
                t0 = blk * TB
                for j in range(4):
                    g.dma_in("sync", xin.v(np.s_[:, j, :], sub=j), x_d[t0 + j * 128:t0 + (j + 1) * 128, :])

            def front(blk):
                F = blk % 2
                rt_, kt_, at_, bt_ = rt2[F], kt2[F], at2[F], bt2[F]
                XR, Kpad, Bpad, Vpad, PCs, gT, bonusT = XR2[F], Kpad2[F], Bpad2[F], Vpad2[F], PCs2[F], gT2[F], bonusT2[F]
                for j in range(4):
                    g.act(junk.v(), xin.v(np.s_[:, j, :], sub=j), AF.Square, scale=1.0 / 32, accum=ms.v(np.s_[:, j:j + 1]))
                g.ts("vector", ms.v(), ms.v(), RMS_EPS, ALU.add)
                g.act(ms.v(), ms.v(), AF.Sqrt)
                g.recip(ms.v(), ms.v())
                if blk > 0:
                    g.copy("vector", hT.v(np.s_[:, :, 0:1]), hT.v(np.s_[:, :, TB:TB + 1]))
                yield
                for j in range(4):
                    xj = xn[j % 2]
                    g.act(xj.v(), xin.v(np.s_[:, j, :], sub=j), AF.Copy, scale=ms.v(np.s_[:, j:j + 1]))
                    bk = g.bank()
                    bv = bf(bk)
                    for kc in range(8):
                        g.tr(bk.w(bv[:, kc * 128:(kc + 1) * 128]), xj.v(np.s_[:, kc * 128:(kc + 1) * 128]), ident.v())
                    g.copy("vector" if j % 2 == 0 else "scalar", hT.v(np.s_[:, :, 1 + 128 * j:1 + 128 * (j + 1)]),
                           bk.w(bv.rearrange("p (k t) -> p k t", k=8)))
                    yield
                if blk + 1 < nblk:
                    load_x(blk + 1)

                def proj_fm(off, ncol):
                    bk = g.bank()
                    for kc in range(8):
                        g.mm(bk.v(np.s_[0:ncol, :]), WA.v(np.s_[:, kc, off:off + ncol]), hT.v(np.s_[:, kc, 1:TB + 1]),
                             start=(kc == 0), stop=False)
                        g.mm(bk.v(np.s_[0:ncol, :]), WB.v(np.s_[:, kc, off:off + ncol]), hT.v(np.s_[:, kc, 0:TB]),
                             start=False, stop=(kc == 7))
                    return bk

                bk = proj_fm(768, 128)
                g.act(tw.v(np.s_[0:64, :]), bk.v(np.s_[0:64, :]), AF.Tanh)
                g.copy("vector", tw.v(np.s_[64:128, :]), bk.v(np.s_[64:128, :]))
                yield
                bk = proj_fm(896, 128)
                g.act(sg1a.v(), bk.v(), AF.Sigmoid)
                yield
                bk = proj_fm(1024, 32)
                g.act(sg1b.v(), bk.v(np.s_[0:32, :]), AF.Sigmoid)
                yield
                for j in range(4):
                    bk = g.bank()
                    for kc in range(8):
                        g.mm(bk.v(np.s_[:, 0:256]), hT.v(np.s_[:, kc, 1 + 128 * j:1 + 128 * (j + 1)]), WA.v(np.s_[:, kc, 512:768]),
                             start=(kc == 0), stop=False)
                        g.mm(bk.v(np.s_[:, 0:256]), hT.v(np.s_[:, kc, 128 * j:128 * (j + 1)]), WB.v(np.s_[:, kc, 512:768]),
                             start=False, stop=(kc == 7))
                    pv = bk.t[:, 0:256].rearrange("p (h c) -> p h c", h=4)
                    g.copy("vector", Vpad.v(np.s_[:, j, 0::2, 0:64]), bk.w(pv[:, 0::2, :]))
                    g.copy("scalar", Vpad.v(np.s_[:, j, 1::2, 64:128]), bk.w(pv[:, 1::2, :]))
                    yield

                for cc in range(2):
                    bk = proj_fm(0 + cc * 128, 128)
                    g.copy("scalar", T["rT"].v(), bk.v())
                    yield
                    bk = proj_fm(256 + cc * 128, 128)
                    g.copy("vector", T["kraw"].v(), bk.v())
                    yield
                    bk = proj_fm(512 + cc * 128, 128)
                    g.copy("scalar", vT.v(np.s_[:, cc, :]), bk.v())
                    yield
                    bk = g.bank()
                    g.mm(bk.v(), w2b.v(np.s_[0:64, cc * 128:(cc + 1) * 128]), tw.v(np.s_[0:64, :]))
                    g.act(T["sgw"].v(), bk.v(), AF.Sigmoid, bias=vec.v(np.s_[:, 0, cc:cc + 1]))
                    bk = g.bank()
                    g.mm(bk.v(), a2b.v(np.s_[64:128, cc * 128:(cc + 1) * 128]), tw.v(np.s_[64:128, :]))
                    g.act(T["aT"].v(), bk.v(), AF.Sigmoid, bias=vec.v(np.s_[:, 1, cc:cc + 1]))
                    bk = g.bank()
                    g.mm(bk.v(), g2a.v(np.s_[:, cc * 128:(cc + 1) * 128]), sg1a.v(), start=True, stop=False)
                    g.mm(bk.v(), g2b.v(np.s_[0:32, cc * 128:(cc + 1) * 128]), sg1b.v(np.s_[0:32, :]), start=False, stop=True)
                    g.copy("vector", gT.v(np.s_[:, cc, :]), bk.v())
                    yield

                    kk, kp, beta, t1, t2 = T["kk"], T["kp"], T["beta"], T["t1"], T["t2"]
                    Er, En, EC, cs = T["Er"], T["En"], T["EC"], T["cs"]
                    g.ts("gpsimd", kk.v(), T["kraw"].v(), vec.v(np.s_[:, 2, cc:cc + 1]), ALU.mult)
                    g.tt("gpsimd", t1.v(), kk.v(), kk.v(), ALU.mult)
                    bk = g.bank()
                    g.mm(bk.v(), onesblk.v(), t1.v())
                    g.scan(cs.v(), rmask.v(), T["sgw"].v(), 0.0, ALU.mult, ALU.add)
                    yield
                    g.ts("vector", t2.v(), bk.v(), 1e-24, ALU.max)
                    g.act(t2.v(), t2.v(), AF.Sqrt)
                    g.act(Er.v(), cs.v(), AF.Exp, scale=NEG_C)
                    g.act(En.v(), cs.v(), AF.Exp, scale=-NEG_C)
                    g.recip(t2.v(), t2.v())
                    yield
                    g.tt("gpsimd", kk.v(), kk.v(), t2.v(), ALU.mult)
                    g.ts("vector", t1.v(), T["aT"].v(), -1.0, ALU.add, vec.v(np.s_[:, 3, cc:cc + 1]), ALU.mult)
                    g.tt("gpsimd", beta.v(), kk.v(), T["aT"].v(), ALU.mult)
                    g.stt("vector", kp.v(), t1.v(), 1.0, T["kraw"].v(), ALU.add, ALU.mult)
                    yield
                    Er3 = Er.t[:].rearrange("p (c t) -> p c t", t=128)
                    g.copy("vector", PCs.v(np.s_[:, cc, :]), Er.w(Er3[:, :, 127]))
                    g.tt("vector", EC.w(EC.t[:].rearrange("p (c t) -> p c t", t=128)),
                         En.w(En.t[:].rearrange("p (c t) -> p c t", t=128)),
                         Er.w(Er3[:, :, 127:128].to_broadcast([128, NCH, 128])), ALU.mult)
                    g.tt("gpsimd", kt_.v(np.s_[:, cc, :]), kp.v(), En.v(), ALU.mult)
                    g.tt("gpsimd", rt_.v(np.s_[:, cc, :]), T["rT"].v(), Er.v(), ALU.mult)
                    yield
                    g.tt("gpsimd", bt_.v(np.s_[:, cc, :]), beta.v(), En.v(), ALU.mult)
                    g.tt("gpsimd", khT.v(np.s_[:, cc, :]), kp.v(), EC.v(), ALU.mult)
                    g.tt("gpsimd", bhT.v(np.s_[:, cc, :]), beta.v(), EC.v(), ALU.mult)
                    kk3 = kk.t[:].rearrange("p (c t) -> p c t", t=128)
                    at3 = at_.t[:, cc, :].rearrange("p (c t) -> p c t", t=128)
                    g.stt("vector", at_.w(at3[:, :, 1:128]), kk.w(kk3[:, :, 1:128]), -1.0, Er.w(Er3[:, :, 0:127]), ALU.mult, ALU.mult)
                    g.ts("vector", at_.w(at3[:, :, 0:1]), kk.w(kk3[:, :, 0:1]), -1.0, ALU.mult)
                    yield
                    g.stt("vector", t1.v(), T["rT"].v(), vec.v(np.s_[:, 4, cc:cc + 1]), kp.v(), ALU.mult, ALU.mult)
                    bk = g.bank()
                    g.mm(bk.v(), onesblk.v(), t1.v())
                    g.tt("vector", bonusT.v(np.s_[:, cc, :]), vT.v(np.s_[:, cc, :]), bk.v(), ALU.mult)
                    yield

                for (src, kind) in ((at_, "A"), (khT, "K"), (bhT, "B")):
                    bk = g.bank()
                    bv = bf(bk)
                    for j in range(NCH):
                        for cc in range(2):
                            sl = (j * 2 + cc) * 128
                            g.tr(bk.w(bv[:, sl:sl + 128]), src.v(np.s_[:, cc, j * 128:(j + 1) * 128]), ident.v())
                    if kind == "A":
                        g.copy("vector", XR.v(np.s_[:, :, :, 0:64]),
                               bk.w(bv.rearrange("p (j h c) -> p j h c", j=NCH, h=4)))
                    else:
                        dst = Kpad if kind == "K" else Bpad
                        b5 = bv.rearrange("p (j c q d) -> p j c q d", j=NCH, c=2, q=2)
                        g.copy("vector", dst.v(np.s_[:, :, 0::2, 0:64]), bk.w(b5[:, :, :, 0, :]))
                        g.copy("scalar", dst.v(np.s_[:, :, 1::2, 64:128]), bk.w(b5[:, :, :, 1, :]))
                    yield

            def back(blk):
                nonlocal s_cur
                t0 = blk * TB
                F = blk % 2
                rt_, kt_, at_, bt_ = rt2[F], kt2[F], at2[F], bt2[F]
                XR, Kpad, Bpad, Vpad, PCs, gT, bonusT = XR2[F], Kpad2[F], Bpad2[F], Vpad2[F], PCs2[F], gT2[F], bonusT2[F]

                def pre_a(j, s):
                    def grp(specs, msk, dsts_t, dkeys):
                        bks = [g.bank(), g.bank()]
                        for si, (lh, rh) in enumerate(specs):
                            for h in range(4):
                                cc, q = divmod(h, 2)
                                sl = (si * 2 + cc) * 128
                                g.mm(bks[q].v(np.s_[:, sl:sl + 128]), lh.v(hsl(h, j)), rh.v(hsl(h, j)))
                        n = len(specs)
                        for q in range(2):
                            if n == 2:
                                src = bks[q].w(bks[q].t[:].rearrange("p (a c t) -> p a c t", a=2, c=2))
                                dst = V(dsts_t[:, :, q::2, :], dkeys)
                                mk = msk.w(msk.t[:].rearrange("p (a c) t -> p a c t", a=2))
                            else:
                                src = bks[q].w(bks[q].t[:, 0:256].rearrange("p (c t) -> p c t", c=2))
                                dst = V(dsts_t[:, q::2, :], dkeys)
                                mk = msk.v(np.s_[:, 0:2, :])
                            g.tt("vector", dst, src, mk, ALU.mult)
                    grp([(bt_, at_), (kt_, at_)], m_su, SUt[s].t, [(PTb[s][0].name, None), (AakT[s].name, None)])
                    yield
                    grp([(bt_, rt_), (kt_, rt_)], m_u, UUt[s].t, [(ArbT[s].name, None), (ArkT[s].name, None)])
                    yield
                    grp([(at_, bt_)], m_sl, Pb[s][0].t, [(Pb[s][0].name, None)])
                    g.tt("gpsimd", NTb[s][0].v(), PTb[s][0].v(), ident4.v(), ALU.add)
                    yield

                def pre_dbl(s, it):
                    cur, nxt = it % 2, (it + 1) % 2
                    bk = g.bank()
                    for h in range(4):
                        g.mm(bk.v(np.s_[:, h * 128:(h + 1) * 128]), PTb[s][cur].v(np.s_[:, h, :]), Pb[s][cur].v(np.s_[:, h, :]))
                    g.copy("scalar", Pb[s][nxt].v(), bk.w(bk.t[:].rearrange("p (h t) -> p h t", h=4)))
                    if it < 5:
                        bk = g.bank()
                        for h in range(4):
                            g.mm(bk.v(np.s_[:, h * 128:(h + 1) * 128]), Pb[s][cur].v(np.s_[:, h, :]), PTb[s][cur].v(np.s_[:, h, :]))
                        g.copy("vector", PTb[s][nxt].v(), bk.w(bk.t[:].rearrange("p (h t) -> p h t", h=4)))
                    yield
                    bk = g.bank()
                    for h in range(4):
                        o = bk.v(np.s_[:, h * 128:(h + 1) * 128])
                        g.mm(o, ident.v(), NTb[s][cur].v(np.s_[:, h, :]), start=True, stop=False)
                        g.mm(o, Pb[s][nxt].v(np.s_[:, h, :]), NTb[s][cur].v(np.s_[:, h, :]), start=False, stop=True)
                    eng = "scalar" if it % 2 == 0 else "vector"
                    g.copy(eng, NTb[s][nxt].v(), bk.w(bk.t[:].rearrange("p (h t) -> p h t", h=4)))
                    yield

                def pre_b(j, s):
                    NTf = NTb[s][0]
                    bk = g.bank()
                    for h in range(4):
                        q = h % 2
                        g.mm(bk.v(np.s_[:, h * 64:(h + 1) * 64]), AakT[s].v(np.s_[:, h, :]), Vpad.v(np.s_[:, j, h, q * 64:(q + 1) * 64]))
                    g.copy("vector", XR.v(np.s_[:, j, :, 64:128]), bk.w(bk.t[:, 0:256].rearrange("p (h c) -> p h c", h=4)))
                    yield
                    bk = g.bank()
                    for h in range(4):
                        g.mm(bk.v(np.s_[:, h * 128:(h + 1) * 128]), NTf.v(np.s_[:, h, :]), XR.v(np.s_[:, j, h, :]))
                    x4 = bk.t[:].rearrange("p (h c) -> p h c", h=4)
                    g.copy("vector", Apad[s].v(np.s_[:, 0::2, 0:64]), bk.w(x4[:, 0::2, 0:64]))
                    g.copy("scalar", Apad[s].v(np.s_[:, 1::2, 64:128]), bk.w(x4[:, 1::2, 0:64]))
                    g.copy("vector", Wpad[s].v(np.s_[:, 0::2, 0:64]), bk.w(x4[:, 0::2, 64:128]))
                    g.copy("scalar", Wpad[s].v(np.s_[:, 1::2, 64:128]), bk.w(x4[:, 1::2, 64:128]))
                    yield
                    bk = g.bank()
                    for cc in range(2):
                        o = bk.v(np.s_[:, cc * 128:(cc + 1) * 128])
                        for q in range(2):
                            h = 2 * cc + q
                            g.mm(o, Apad[s].v(np.s_[:, h, :]), Bpad.v(np.s_[:, j, h, :]), start=(q == 0), stop=(q == 1))
                    for cc in range(2):
                        g.stt("vector", TTbd[s].v(np.s_[:, cc, :]), identf.v(), PCs.v(np.s_[:, cc, j:j + 1]),
                              bk.v(np.s_[:, cc * 128:(cc + 1) * 128]), ALU.mult, ALU.add)
                    bk = g.bank()
                    for cc in range(2):
                        o = bk.v(np.s_[:, cc * 128:(cc + 1) * 128])
                        for q in range(2):
                            h = 2 * cc + q
                            g.mm(o, Apad[s].v(np.s_[:, h, :]), ArbT[s].v(np.s_[:, h, :]), start=(q == 0), stop=(q == 1))
                    g.tt("vector", RhT[s].v(), bk.w(bk.t[:, 0:256].rearrange("p (c t) -> p c t", c=2)),
                         rt_.v(np.s_[:, :, j * 128:(j + 1) * 128]), ALU.add)
                    yield

                def seq(j, s):
                    nonlocal s_cur
                    Sc, Sn = Sbd[s_cur], Sbd[1 - s_cur]
                    bk = g.bank()
                    for cc in range(2):
                        o = bk.v(np.s_[:, cc * 128:(cc + 1) * 128])
                        g.mm(o, TTbd[s].v(np.s_[:, cc, :]), Sc.v(np.s_[:, cc, :]), start=True, stop=False)
                        for q in range(2):
                            h = 2 * cc + q
                            g.mm(o, Bpad.v(np.s_[:, j, h, :]), Wpad[s].v(np.s_[:, h, :]), start=False, stop=False)
                            g.mm(o, Kpad.v(np.s_[:, j, h, :]), Vpad.v(np.s_[:, j, h, :]), start=False, stop=(q == 1))
                    g.copy("vector", Sn.v(), bk.w(bk.t[:, 0:256].rearrange("p (c t) -> p c t", c=2)))
                    bk = g.bank()
                    for cc in range(2):
                        o = bk.v(np.s_[:, cc * 128:(cc + 1) * 128])
                        g.mm(o, Sc.v(np.s_[:, cc, :]), RhT[s].v(np.s_[:, cc, :]), start=True, stop=False)
                        for q in range(2):
                            h = 2 * cc + q
                            g.mm(o, Wpad[s].v(np.s_[:, h, :]), ArbT[s].v(np.s_[:, h, :]), start=False, stop=False)
                            g.mm(o, Vpad.v(np.s_[:, j, h, :]), ArkT[s].v(np.s_[:, h, :]), start=False, stop=(q == 1))
                    g.copy("scalar", yraw.v(np.s_[:, :, j * 128:(j + 1) * 128]), bk.w(bk.t[:, 0:256].rearrange("p (c t) -> p c t", c=2)))
                    s_cur = 1 - s_cur
                    yield

                def rr(gens):
                    gens = list(gens)
                    while gens:
                        for gn in list(gens):
                            try:
                                next(gn)
                            except StopIteration:
                                gens.remove(gn)
                            yield

                for jp in range(0, NCH, NSL):
                    yield from rr([pre_a(jp + s, s) for s in range(NSL)])
                    for it in range(6):
                        yield from rr([pre_dbl(s, it) for s in range(NSL)])
                    yield from rr([pre_b(jp + s, s) for s in range(NSL)])
                    for s in range(NSL):
                        yield from seq(jp + s, s)

                for cc in range(2):
                    t1, t2 = T2["o1"], T2["o2"]
                    yr = yraw.v(np.s_[:, cc, :])
                    bk = g.bank()
                    g.mm(bk.v(), onesblk.v(), yr)
                    g.stt("vector", t1.v(), bk.v(), -1.0 / 64, yr, ALU.mult, ALU.add)
                    g.tt("gpsimd", t2.v(), t1.v(), t1.v(), ALU.mult)
                    yield
                    bk = g.bank()
                    g.mm(bk.v(), onesblk.v(), t2.v())
                    g.ts("vector", t2.v(), bk.v(), 1.0 / 64, ALU.mult, GN_EPS, ALU.add)
                    g.act(t2.v(), t2.v(), AF.Sqrt)
                    g.recip(t2.v(), t2.v())
                    yield
                    g.tt("gpsimd", t1.v(), t1.v(), t2.v(), ALU.mult)
                    g.ts("vector", t1.v(), t1.v(), vec.v(np.s_[:, 5, cc:cc + 1]), ALU.mult, vec.v(np.s_[:, 6, cc:cc + 1]), ALU.add)
                    g.tt("gpsimd", t1.v(), t1.v(), bonusT.v(np.s_[:, cc, :]), ALU.add)
                    g.tt("vector", ygo.v(np.s_[:, cc, :]), t1.v(), gT.v(np.s_[:, cc, :]), ALU.mult)
                    g.dma_out("sync", yg_d[cc * 128:(cc + 1) * 128, t0:t0 + TB], ygo.v(np.s_[:, cc, :]))
                    yield

            def drive(gens, weights=None):
                gens = list(gens)
                weights = list(weights or [1] * len(gens))
                while gens:
                    for gn, w in list(zip(gens, weights)):
                        for _ in range(w):
                            try:
                                next(gn)
                            except StopIteration:
                                k = gens.index(gn)
                                gens.pop(k)
                                weights.pop(k)
                                break

            load_x(0)
            drive([front(0)])
            for blk in range(nblk):
                gs = [back(blk)]
                if blk + 1 < nblk:
                    gs.append(front(blk + 1))
                drive(gs, [BACK_W, 1])
        try:
            emit_all()
        except Stop:
            pass
        if stop:
            g.dma_out("sync", dbg_d[:, :], dbgt.v())
        g.finish()
    return nc


def phase1_inputs(inp, core):
    b, hg = divmod(core, 4)
    cs = slice(hg * 256, (hg + 1) * 256)
    pm = lambda v: np.ascontiguousarray(v.reshape(-1, 128).T)
    gm = np.stack([pm(inp["norm_mix_g"][0])] + [pm(inp["rwkv_mu"][0, i]) for i in range(6)], axis=1)
    wcat = np.concatenate([inp["rwkv_w_r"][0][:, cs], inp["rwkv_w_k"][0][:, cs], inp["rwkv_w_v"][0][:, cs],
                           inp["rwkv_w1"][0], inp["rwkv_a1"][0], inp["rwkv_g1"][0]], axis=1)
    vecs = [inp["rwkv_w0"][0][cs], inp["rwkv_a0"][0][cs], inp["rwkv_k_k"][0][cs], inp["rwkv_k_a"][0][cs],
            inp["rwkv_r_k"][0].reshape(-1)[cs], inp["rwkv_ln_w"][0][cs], inp["rwkv_ln_b"][0][cs]]
    vec = np.stack([pm(v) for v in vecs], axis=1)
    return {
        "x": np.ascontiguousarray(inp["x"][b]),
        "gm": np.ascontiguousarray(gm, dtype=np.float32),
        "wcat": np.ascontiguousarray(wcat, dtype=np.float32),
        "w2": np.ascontiguousarray(inp["rwkv_w2"][0][:, cs]),
        "a2": np.ascontiguousarray(inp["rwkv_a2"][0][:, cs]),
        "g2": np.ascontiguousarray(inp["rwkv_g2"][0][:, cs]),
        "vec": np.ascontiguousarray(vec, dtype=np.float32),
    }


NT2 = 17
NTOK2 = NT2 * 128
DFF = 4096
HG = 512
NGRP = DFF // HG
NEG_BIG = -30000.0
TWO_PI = 6.283185307179586
PI = 3.141592653589793
C1_2PI = 6.28125
C2_2PI = TWO_PI - 6.28125


def build_phase2():
    nc = bass.Bass("TRN2", target_bir_lowering=False)
    dr = lambda n, s, d=F32: nc.dram_tensor(n, s, d, kind="ExternalInput").ap()
    x_d = dr("x", [NTOK2, D])
    yg_d = dr("ygT", [D, NTOK2], BF16)
    pos_d = dr("pos", [1, NTOK2], I32)
    wo0_d = dr("wo0", [D, D])
    win_d = [dr("win0", [D, DFF]), dr("win1", [D, DFF])]
    wout_d = [dr("wout0", [DFF, D]), dr("wout1", [DFF, D])]
    wqkv_d = dr("wqkv", [D, 1280])
    wo1_d = dr("wo1", [D, D])
    gains_d = dr("gains", [128, 3, 8])
    bq_d = dr("bq", [128, 10])
    rowv_d = dr("rowv", [1, 1024 + 1024 + 128 + 16])
    cst_d = dr("cst", [128, 4])
    out_d = nc.dram_tensor("out", [16 * 128, D], F32, kind="ExternalOutput").ap()

    with ExitStack() as st:
        g = G(nc, st)
        sb = g.sb
        identf = sb("identf", [128, 128], F32)
        ident = sb("ident", [128, 128], BF16)
        g.memset("gpsimd", identf.v(), 0.0)
        g.aselect(identf.v(), identf.v(), [[-1, 128]], ALU.not_equal, 1.0, 0, 1)
        g.copy("vector", ident.v(), identf.v())
        rotf = sb("rotf", [128, 128], F32)
        rot = sb("rot", [128, 128], BF16)
        g.memset("gpsimd", rotf.v(), 0.0)
        for blk in range(2):
            o = blk * 64
            sub = rotf.v(np.s_[:, o:o + 32])
            g.aselect(sub, sub, [[-1, 32]], ALU.not_equal, -1.0, -(o + 32), 1)
            sub = rotf.v(np.s_[:, o + 32:o + 64])
            g.aselect(sub, sub, [[-1, 32]], ALU.not_equal, 1.0, -(o + 32) + 32, 1)
        g.copy("vector", rot.v(), rotf.v())
        mscr = sb("scr", [128, 1024], F32)
        maskb = Buf(mscr.t[:, 0:256], "scr_m0")
        mask1 = Buf(mscr.t[:, 256:512], "scr_m1")
        g.memset("gpsimd", maskb.v(), 0.0)
        g.aselect(maskb.v(), maskb.v(), [[1, 256]], ALU.is_gt, NEG_BIG, 0, -1)
        g.aselect(maskb.v(), maskb.v(), [[-1, 256]], ALU.is_ge, NEG_BIG, 128, 1)
        cst = sb("cst", [128, 4], F32)
        g.dma_in("sync", cst.v(), cst_d[:, :])
        g.copy("vector", mask1.v(), maskb.v())
        g.ts("vector", mask1.v(np.s_[:, 0:128]), maskb.v(np.s_[:, 0:128]), cst.v(np.s_[:, 2:3]), ALU.add)
        maskbf = sb("maskbf", [128, 256], BF16)
        mask1bf = sb("mask1bf", [128, 256], BF16)
        g.ts("vector", maskbf.v(), maskb.v(), 8.0, ALU.mult)
        g.ts("vector", mask1bf.v(), mask1.v(), 8.0, ALU.mult)
        ones_row = sb("ones_row", [1, 128], F32)
        g.memset("gpsimd", ones_row.v(), 1.0)
        gains = sb("gains", [128, 3, 8], F32)
        g.dma_in("sync", gains.v(), gains_d[:, :, :])
        bq = sb("bq", [128, 10], F32)
        g.dma_in("sync", bq.v(), bq_d[:, :])
        rowv = sb("rowv", [1, 1024], F32)
        g.dma_in("sync", rowv.v(), rowv_d[0:1, 1024:2048])
        bvb = sb("bvb", [128, 128], F32)
        sinkb = sb("sinkb", [128, 16], F32)
        g.dma_in("sync", bvb.v(), rowv_d[0:1, 2048:2176].to_broadcast([128, 128]))
        g.dma_in("sync", sinkb.v(), rowv_d[0:1, 2176:2192].to_broadcast([128, 16]))

        xres = sb("xres", [128, NT2, 1024], F32)
        hT = sb("hT", [128, 8, NTOK2], BF16)
        uT = sb("uT", [128, 4, NTOK2], BF16)
        arena = sb("arena", [128, 18432], BF16)
        junk = sb("junk", [128, 1024], BF16)
        ms = sb("ms", [128, NT2], F32)
        xn = [sb("xn0", [128, 1024], BF16)] * 2
        scr = mscr
        NU = 4
        asc = sb("asc", [128, NU, 2, 256], F32)
        relu_s = [Buf(asc.t[:, i].rearrange("p a b -> p (a b)").bitcast(BF16)[:, 0:512], f"asc_relu{i}") for i in range(2)]

        def bf(bank):
            return bank.t[:].bitcast(BF16)

        NTILES = [(0, 512), (512, 512), (1024, 512), (1536, 512), (2048, 128)]
        NTILES_M = {0: NTILES, 1: [(128, 512), (640, 512), (1152, 512), (1664, 512)]}

        def load_x_and_yg():
            for kc in range(8):
                g.dma_in("sync", hT.v(np.s_[:, kc, :]), yg_d[kc * 128:(kc + 1) * 128, :])
            for i in range(NT2):
                g.dma_in("sync", xres.v(np.s_[:, i, :], sub=i), x_d[i * 128:(i + 1) * 128, :])

        def wload(dst_ap, key, src_ap):
            g.kb.op("gpsimd", lambda e: e.dma_start(out=dst_ap, in_=src_ap), writes=[key], dma=True)

        def rstd_all(tiles):
            allk = [("ms", i) for i in tiles]
            lo, hi = tiles[0], tiles[-1] + 1
            for i in tiles:
                g.act(junk.v(), xres.v(np.s_[:, i, :], sub=i), AF.Square, scale=1.0 / 32, accum=ms.v(np.s_[:, i:i + 1], sub=i))
            mv = V(ms.t[:, lo:hi], allk)
            g.ts("vector", mv, mv, RMS_EPS, ALU.add)
            g.act(mv, mv, AF.Sqrt)
            g.recip(mv, mv)

        def norm_to_hT(gi, tiles):
            tiles = list(tiles)
            rstd_all(tiles)
            for i in tiles:
                xj = xn[i % 2]
                g.act(xj.v(), xres.v(np.s_[:, i, :], sub=i), AF.Copy, scale=ms.v(np.s_[:, i:i + 1], sub=i))
                bk = g.bank()
                bv = bf(bk)
                for kc in range(8):
                    g.tr(bk.w(bv[:, kc * 128:(kc + 1) * 128]), xj.v(np.s_[:, kc * 128:(kc + 1) * 128]), ident.v())
                g.tt("vector", hT.v(np.s_[:, :, i * 128:(i + 1) * 128]), bk.w(bv.rearrange("p (k t) -> p k t", k=8)),
                     gains.w(gains.t[:, gi, :].unsqueeze(2).to_broadcast([128, 8, 128])), ALU.mult)

        def mlp(layer, first_tile=0):
            win, wout = win_d[layer], wout_d[layer]
            ntiles = [(n0, nn) for (n0, nn) in NTILES_M[first_tile]]
            winv = win.rearrange("(kc p) n -> p kc n", p=128)
            woutv = wout.rearrange("(m p) n -> p m n", p=128)
            def WI(s):
                return arena.t[:, s * 4096:(s + 1) * 4096].rearrange("p (k n) -> p k n", k=8)

            def WO(s):
                return arena.t[:, 8192 + s * 4096:8192 + (s + 1) * 4096].rearrange("p (m n) -> p m n", m=4)

            ri = 0
            for grp in range(NGRP):
                s = grp % 2
                kwi, kwo = ("arena", "wi%d" % s), ("arena", "wo%d" % s)
                wload(WI(s), kwi, winv[:, :, grp * HG:(grp + 1) * HG])
                wload(WO(s), kwo, woutv[:, grp * 4:(grp + 1) * 4, :])
                for m in range(4):
                    for (n0, nn) in ntiles:
                        bk = g.bank()
                        for kc in range(8):
                            g.mm(bk.v(np.s_[:, 0:nn]), V(WI(s)[:, kc, m * 128:(m + 1) * 128], [kwi]), hT.v(np.s_[:, kc, n0:n0 + nn]),
                                 start=(kc == 0), stop=(kc == 7))
                        r = relu_s[ri % 2]
                        ri += 1
                        g.act(V(r.t[:, 0:nn], r.v().keys + [("asc", 0), ("asc", 1)]), bk.v(np.s_[:, 0:nn]), AF.Relu)
                        g.tt("gpsimd", uT.v(np.s_[:, m, n0:n0 + nn]), r.v(np.s_[:, 0:nn]), r.v(np.s_[:, 0:nn]), ALU.mult)
                for i in range(first_tile, NT2):
                    for half in range(2):
                        bk = g.bank()
                        for m in range(4):
                            g.mm(bk.v(), uT.v(np.s_[:, m, i * 128:(i + 1) * 128]), V(WO(s)[:, m, half * 512:(half + 1) * 512], [kwo]),
                                 start=(m == 0), stop=(m == 3))
                        xs = xres.v(np.s_[:, i, half * 512:(half + 1) * 512], sub=i)
                        g.tt("vector", xs, xs, bk.v(), ALU.add)

        load_x_and_yg()
        wo_v = arena.t[:, 0:8192].rearrange("p (k n) -> p k n", k=8)
        g.kb.op("gpsimd", lambda e: e.dma_start(out=wo_v, in_=wo0_d.rearrange("(kc p) n -> p kc n", p=128)),
                writes=[("arena", "wi0"), ("arena", "wi1")], dma=True)
        for i in range(NT2):
            for half in range(2):
                bk = g.bank()
                for kc in range(8):
                    g.mm(bk.v(), hT.v(np.s_[:, kc, i * 128:(i + 1) * 128]),
                         V(wo_v[:, kc, half * 512:(half + 1) * 512], [("arena", "wi0"), ("arena", "wi1")]),
                         start=(kc == 0), stop=(kc == 7))
                xs = xres.v(np.s_[:, i, half * 512:(half + 1) * 512], sub=i)
                g.tt("vector", xs, xs, bk.v(), ALU.add)

        norm_to_hT(0, range(NT2))
        mlp(0)

        norm_to_hT(1, range(NT2))
        wq_v = arena.t[:, 0:10240].rearrange("p (k n) -> p k n", k=8)
        wo1_v = arena.t[:, 10240:18432].rearrange("p (k n) -> p k n", k=8)
        KQ = [("arena", "wi0"), ("arena", "wi1"), ("arena", "wo0")]
        KO = [("arena", "wo0"), ("arena", "wo1"), ("arena", "x")]
        g.kb.op("gpsimd", lambda e: e.dma_start(out=wq_v, in_=wqkv_d.rearrange("(kc p) n -> p kc n", p=128)), writes=KQ, dma=True)
        g.kb.op("gpsimd", lambda e: e.dma_start(out=wo1_v, in_=wo1_d.rearrange("(kc p) n -> p kc n", p=128)), writes=KO, dma=True)
        tabs = uT.t[:].rearrange("p a b -> p (a b)").bitcast(F32)
        cosT = uT.w(tabs[:, 0:NTOK2])
        sinT = uT.w(tabs[:, NTOK2:2 * NTOK2])
        posi = V(scr.t[:, 0:512].bitcast(I32), [("scr", "a"), ("scr_m0", None), ("scr_m1", None)])
        for (n0, nn) in NTILES:
            pch = V(posi.ap[:, 0:nn], posi.keys)
            ach = scr.v(np.s_[:, 512:512 + nn], sub="b")
            g.dma_in("sync", pch, pos_d[0:1, n0:n0 + nn].to_broadcast([128, nn]))
            g.copy("vector", ach, pch)
            g.ts("vector", ach, ach, cst.v(np.s_[:, 0:1]), ALU.mult)
            sch = V(sinT.ap[:, n0:n0 + nn], sinT.keys)
            cch = V(cosT.ap[:, n0:n0 + nn], cosT.keys)
            T1 = V(xn[0].t[:].bitcast(F32)[:, 0:nn], [("xn0", None)])
            A2 = V(junk.t[:].bitcast(F32)[:, 0:nn], [("junk", None)])
            TI = pch
            for (src, dst, shift) in ((ach, sch, 0.0), (ach, cch, 0.5 * PI)):
                if shift:
                    g.ts("vector", A2, src, shift, ALU.add)
                    src = A2
                g.ts("vector", T1, src, 1.0 / TWO_PI, ALU.mult)
                g.copy("vector", TI, T1)
                g.copy("vector", T1, TI)
                g.stt("vector", dst, T1, -C1_2PI, src, ALU.mult, ALU.add)
                g.stt("vector", dst, T1, -C2_2PI, dst, ALU.mult, ALU.add)
                g.ts("vector", dst, dst, -PI, ALU.max, PI, ALU.min)
                g.act(dst, dst, AF.Sin)

        kr = sb("kr", [128, 2, NTOK2], BF16)
        vpad = [sb(f"vpad{i}", [128, 2, 2, 128], BF16) for i in range(3)]
        for v_ in vpad:
            g.memset("gpsimd", v_.v(), 0.0)
        qb16 = sb("qb16", [128, 512], BF16)
        SCALE = 0.125
        negsink = sb("negsink", [128, 16], F32)
        esink = sb("esink", [128, 16], F32)
        g.ts("vector", negsink.v(), sinkb.v(), -1.0, ALU.mult)
        g.act(esink.v(), sinkb.v(), AF.Exp)

        def rope_evac(bk, nn, bias_col, n0, dst, qf, qb, r2):
            g.act(qf, bk.v(np.s_[:, 0:nn]), AF.Identity, bias=bias_col)
            g.copy("gpsimd", qb, qf)
            b2 = g.bank()
            g.mm(b2.v(np.s_[:, 0:nn]), rot.v(), qb)
            g.tt("vector", r2, b2.v(np.s_[:, 0:nn]), V(sinT.ap[:, n0:n0 + nn], sinT.keys), ALU.mult)
            g.tt("gpsimd", qf, qf, V(cosT.ap[:, n0:n0 + nn], cosT.keys), ALU.mult)
            g.tt("vector", dst, qf, r2, ALU.add)

        wkd_ap = asc.t[:].rearrange("p a b c -> p (a b c)")[:, 0:1024].bitcast(BF16).rearrange("p (k j c) -> p k j c", k=8, j=2)
        WK = [("asc", u) for u in range(NU)] + [("asc_relu0", None), ("asc_relu1", None)]
        for j in range(2):
            for dup in range(2):
                g.copy("vector", V(wkd_ap[:, :, j, dup * 64:(dup + 1) * 64], WK), V(wq_v[:, :, 1024 + j * 64:1024 + (j + 1) * 64], KQ))
        for j in range(2):
            for (n0, nn) in NTILES:
                bk = g.bank()
                for kc in range(8):
                    g.mm(bk.v(np.s_[:, 0:nn]), V(wkd_ap[:, kc, j, :], WK), hT.v(np.s_[:, kc, n0:n0 + nn]), start=(kc == 0), stop=(kc == 7))
                rope_evac(bk, nn, bq.v(np.s_[:, 8 + j:9 + j]), n0, kr.v(np.s_[:, j, n0:n0 + nn]),
                          scr.v(np.s_[:, 0:nn], sub="a"), qb16.v(np.s_[:, 0:nn]), scr.v(np.s_[:, 512:512 + nn], sub="b"))

        def make_vpad(i):
            vp = vpad[i % 3]
            bk = g.bank()
            for kc in range(8):
                g.mm(bk.v(np.s_[:, 0:128]), hT.v(np.s_[:, kc, i * 128:(i + 1) * 128]), V(wq_v[:, kc, 1152:1280], KQ), start=(kc == 0), stop=(kc == 7))
            for q2 in range(2):
                g.tt("vector", vp.v(np.s_[:, :, q2, q2 * 64:(q2 + 1) * 64]), bk.w(bk.t[:, 0:128].rearrange("p (j d) -> p j d", j=2)),
                     bvb.w(bvb.t[:].rearrange("p (j d) -> p j d", j=2)), ALU.add)
            return vp

        qr2 = [sb(f"qr{i}", [128, 8, 128], BF16) for i in range(2)]
        oT = sb("oT", [128, 8, 128], BF16)
        pn = [sb(f"pn{i}", [128, 2, 256], BF16) for i in range(NU)]
        pT = [sb(f"pT{i}", [128, 2, 2, 128], BF16) for i in range(NU)]
        stat = [sb(f"stat{i}", [128, 8], F32) for i in range(NU)]
        def prep(i):
            make_vpad(i)
            qr = qr2[i % 2]
            yield
            for hh in range(2):
                bk = g.bank()
                for a4 in range(4):
                    hp = hh * 4 + a4
                    for kc in range(8):
                        g.mm(bk.v(np.s_[:, a4 * 128:(a4 + 1) * 128]), V(wq_v[:, kc, hp * 128:(hp + 1) * 128], KQ),
                             hT.v(np.s_[:, kc, i * 128:(i + 1) * 128]), start=(kc == 0), stop=(kc == 7))
                qf = scr.v(np.s_[:, 0:512], sub="a")
                r2 = scr.v(np.s_[:, 512:1024], sub="b")
                qb = qb16.v()
                qf3 = V(scr.t[:, 0:512].rearrange("p (a t) -> p a t", a=4), [("scr", "a")])
                r23 = V(scr.t[:, 512:1024].rearrange("p (a t) -> p a t", a=4), [("scr", "b")])
                g.tt("vector", qf3, bk.w(bk.t[:].rearrange("p (a t) -> p a t", a=4)),
                     bq.w(bq.t[:, hh * 4:hh * 4 + 4].unsqueeze(2).to_broadcast([128, 4, 128])), ALU.add)
                g.copy("gpsimd", qb, qf)
                b2 = g.bank()
                g.mm(b2.v(), rot.v(), qb)
                sin_b = V(sinT.ap[:, i * 128:(i + 1) * 128].unsqueeze(1).to_broadcast([128, 4, 128]), sinT.keys)
                cos_b = V(cosT.ap[:, i * 128:(i + 1) * 128].unsqueeze(1).to_broadcast([128, 4, 128]), cosT.keys)
                g.tt("vector", r23, b2.w(b2.t[:].rearrange("p (a t) -> p a t", a=4)), sin_b, ALU.mult)
                g.tt("gpsimd", qf3, qf3, cos_b, ALU.mult)
                g.tt("vector", qr.v(np.s_[:, hh * 4:hh * 4 + 4, :]), qf3, r23, ALU.add)
                yield

        def attn(i):
            vps = [vpad[(i - 1) % 3], vpad[i % 3]]
            qr = qr2[i % 2]
            mk = mask1bf if i == 1 else maskbf
            for j in range(2):
                sbank = {}
                for gp in range(2):
                    hp0 = 4 * j + 2 * gp
                    bks = [g.bank(), g.bank()]
                    for a in range(2):
                        for q2 in range(2):
                            o = bks[q2].v(np.s_[:, a * 256:(a + 1) * 256])
                            g.mm(o, ident.v(), mk.v(), start=True, stop=False)
                            g.mm(o, qr.v(np.s_[q2 * 64:(q2 + 1) * 64, hp0 + a, :]),
                                 kr.v(np.s_[q2 * 64:(q2 + 1) * 64, j, (i - 1) * 128:(i + 1) * 128]), start=False, stop=True)
                    for q2 in range(2):
                        sbank[gp * 2 + q2] = bks[q2]
                UN = range(4)
                yield
                for u in UN:
                    gp, q2 = divmod(u, 2)
                    st_ = stat[u]
                    ps3 = sbank[u].w(sbank[u].t[:].rearrange("p (a k) -> p a k", a=2))
                    g.kb.op("vector", lambda e, st_=st_, ps3=ps3: e.tensor_reduce(out=st_.t[:, 0:2], in_=ps3.ap, axis=mybir.AxisListType.X, op=ALU.max),
                            reads=ps3.keys, writes=st_.v().keys)
                    h0 = 2 * (4 * j + 2 * gp) + q2
                    g.stt("vector", st_.v(np.s_[:, 2:4]), st_.v(np.s_[:, 0:2]), -SCALE, negsink.w(negsink.t[:, h0:h0 + 3:2]), ALU.mult, ALU.min)
                yield
                for u in UN:
                    st_ = stat[u]
                    for a in range(2):
                        g.act(V(asc.t[:, u, a, :], [("asc", u)]), sbank[u].v(np.s_[:, a * 256:(a + 1) * 256]), AF.Exp, scale=SCALE,
                              bias=st_.v(np.s_[:, 2 + a:3 + a]), accum=st_.v(np.s_[:, 4 + a:5 + a]))
                    g.act(st_.v(np.s_[:, 6:8]), st_.v(np.s_[:, 2:4]), AF.Exp)
                yield
                for u in UN:
                    gp, q2 = divmod(u, 2)
                    st_ = stat[u]
                    h0 = 2 * (4 * j + 2 * gp) + q2
                    g.tt("vector", st_.v(np.s_[:, 6:8]), st_.v(np.s_[:, 6:8]), esink.w(esink.t[:, h0:h0 + 3:2]), ALU.mult)
                    g.tt("vector", st_.v(np.s_[:, 4:6]), st_.v(np.s_[:, 4:6]), st_.v(np.s_[:, 6:8]), ALU.add)
                    g.recip(st_.v(np.s_[:, 4:6]), st_.v(np.s_[:, 4:6]))
                for u in UN:
                    st_ = stat[u]
                    g.tt("gpsimd", pn[u].v(), V(asc.t[:, u], [("asc", u)]), st_.w(st_.t[:, 4:6].unsqueeze(2).to_broadcast([128, 2, 256])), ALU.mult)
                yield
                tbs = {}
                for u in UN:
                    tb = g.bank()
                    tv = bf(tb)
                    for a in range(2):
                        for kb in range(2):
                            sl = (a * 2 + kb) * 128
                            g.tr(tb.w(tv[:, sl:sl + 128]), pn[u].v(np.s_[:, a, kb * 128:(kb + 1) * 128]), ident.v())
                    tbs[u] = (tb, tv)
                for u in UN:
                    tb, tv = tbs[u]
                    g.copy("scalar" if u % 2 == 0 else "vector", pT[u].v(), tb.w(tv[:, 0:512].rearrange("p (a k q) -> p a k q", a=2, k=2)))
                yield
                for gp in range(2):
                    hp0 = 4 * j + 2 * gp
                    obk = g.bank()
                    for a in range(2):
                        first = True
                        for q2 in range(2):
                            for kb in range(2):
                                g.mm(obk.v(np.s_[:, a * 128:(a + 1) * 128]), vps[kb].v(np.s_[:, j, q2, :]), pT[gp * 2 + q2].v(np.s_[:, a, kb, :]),
                                     start=first, stop=(q2 == 1 and kb == 1))
                                first = False
                    g.copy("scalar", oT.v(np.s_[:, hp0:hp0 + 2, :]), obk.w(obk.t[:, 0:256].rearrange("p (a q) -> p a q", a=2)))
            for half in range(2):
                bk = g.bank()
                for hp in range(8):
                    g.mm(bk.v(), oT.v(np.s_[:, hp, :]), V(wo1_v[:, hp, half * 512:(half + 1) * 512], KO), start=(hp == 0), stop=False)
                g.mm(bk.v(), ones_row.v(), rowv.v(np.s_[0:1, half * 512:(half + 1) * 512]), start=False, stop=True)
                xs = xres.v(np.s_[:, i, half * 512:(half + 1) * 512], sub=i)
                g.tt("vector", xs, xs, bk.v(), ALU.add)

            yield

        def drive2(gens):
            gens = list(gens)
            while gens:
                for gn in list(gens):
                    try:
                        next(gn)
                    except StopIteration:
                        gens.remove(gn)

        make_vpad(0)
        drive2([prep(1)])
        for i in range(1, NT2):
            gs = [attn(i)]
            if i + 1 < NT2:
                gs.append(prep(i + 1))
            drive2(gs)

        norm_to_hT(2, range(1, NT2))
        mlp(1, first_tile=1)

        gfb = V(asc.t[:].rearrange("p a b c -> p (a b c)")[:, 0:1024], [("asc", u) for u in range(NU)] + [("asc_relu0", None), ("asc_relu1", None)])
        g.dma_in("sync", gfb, rowv_d[0:1, 0:1024].to_broadcast([128, 1024]))
        rstd_all(list(range(1, NT2)))
        for i in range(1, NT2):
            o_ = V(scr.t[:], [("scr", "a"), ("scr", "b")])
            g.stt("vector", o_, xres.v(np.s_[:, i, :], sub=i), ms.v(np.s_[:, i:i + 1], sub=i), gfb, ALU.mult, ALU.mult)
            g.dma_out("sync", out_d[(i - 1) * 128:i * 128, :], o_)
        g.finish()
    return nc


def _pm(v):
    return np.ascontiguousarray(np.asarray(v).reshape(-1, 128).T)


def phase2_inputs(inp, ygT_full, core):
    b, tq = divmod(core, 4)
    t0 = tq * 2048
    x = np.zeros((NTOK2, D), np.float32)
    yg = np.zeros((D, NTOK2), ml_dtypes.bfloat16)
    pos = np.zeros((1, NTOK2), np.int32)
    lo = t0 - 128
    if tq > 0:
        x[:] = inp["x"][b, lo:lo + NTOK2]
        yg[:] = ygT_full[b][:, lo:lo + NTOK2]
        pos[0] = inp["positions"][b, lo:lo + NTOK2]
    else:
        x[128:] = inp["x"][b, 0:2048]
        yg[:, 128:] = ygT_full[b][:, 0:2048]
        pos[0, 128:] = inp["positions"][b, 0:2048]
    gains = np.stack([_pm(inp["norm_mlp_g"][0]), _pm(inp["norm_mix_g"][1]), _pm(inp["norm_mlp_g"][1])], axis=1)
    bqkv = inp["attn_b_qkv"][0]
    bq = np.zeros((128, 10), np.float32)
    bq[:, 0:8] = _pm(bqkv[0:1024])
    for j in range(2):
        bk = bqkv[1024 + j * 64:1024 + (j + 1) * 64]
        bq[:, 8 + j] = np.concatenate([bk, bk])
    rowv = np.concatenate([inp["norm_final_g"], inp["attn_b_o"][0], bqkv[1152:1280], inp["attn_sinks"][0]])[None, :]
    cst = np.zeros((128, 4), np.float32)
    p = np.arange(128)
    cst[:, 0] = (10000.0 ** (-(np.arange(0, 64, 2, dtype=np.float32)) / 64.0))[p % 32]
    cst[:, 1] = np.where((p % 64) < 32, 1.0, 1.0)
    cst[:, 2] = 0.0 if tq > 0 else NEG_BIG
    return {
        "x": x, "ygT": yg, "pos": pos,
        "wo0": np.ascontiguousarray(inp["rwkv_w_o"][0]),
        "win0": np.ascontiguousarray(inp["mlp_w_in"][0]), "win1": np.ascontiguousarray(inp["mlp_w_in"][1]),
        "wout0": np.ascontiguousarray(inp["mlp_w_out"][0]), "wout1": np.ascontiguousarray(inp["mlp_w_out"][1]),
        "wqkv": np.ascontiguousarray(inp["attn_w_qkv"][0]), "wo1": np.ascontiguousarray(inp["attn_w_o"][0]),
        "gains": np.ascontiguousarray(gains, dtype=np.float32), "bq": bq,
        "rowv": np.ascontiguousarray(rowv, dtype=np.float32), "cst": cst,
    }


_NC_CACHE = {}


def kernel(**inputs):
    inp = {k: np.asarray(v) for k, v in inputs.items()}
    if "p1" not in _NC_CACHE:
        _NC_CACHE["p1"] = build_phase1()
        _NC_CACHE["p2"] = build_phase2()
    r1 = run_bass_kernel_spmd(_NC_CACHE["p1"], [phase1_inputs(inp, c) for c in range(8)], core_ids=list(range(8)))
    ygT = np.zeros((2, D, S_LEN), ml_dtypes.bfloat16)
    for c in range(8):
        b, hg = divmod(c, 4)
        ygT[b, hg * 256:(hg + 1) * 256] = r1.results[c]["yg"]
    r2 = run_bass_kernel_spmd(_NC_CACHE["p2"], [phase2_inputs(inp, ygT, c) for c in range(8)], core_ids=list(range(8)))
    out = np.zeros((2, S_LEN, D), np.float32)
    for c in range(8):
        b, tq = divmod(c, 4)
        out[b, tq * 2048:(tq + 1) * 2048] = r2.results[c]["out"]
    return out
```

```python
import numpy as np
import ml_dtypes
from contextlib import ExitStack
import concourse.bass as bass
import concourse.mybir as mybir
from concourse.bass_utils import run_bass_kernel_spmd

F32 = mybir.dt.float32
BF16 = mybir.dt.bfloat16
I32 = mybir.dt.int32
AF = mybir.ActivationFunctionType
ALU = mybir.AluOpType

EPOCH = 4000
DMA_ROT = 8
NEG_C = -0.6065306597126334


class KB:
    ENGS = ("tensor", "vector", "scalar", "gpsimd", "sync")

    def __init__(self, nc, stack):
        self.nc = nc
        self.stack = stack
        self.streams = {e: [] for e in self.ENGS}
        self.count = {e: 0 for e in self.ENGS}
        self.sems = {e: [] for e in self.ENGS}
        self.dma_count = {e: 0 for e in self.ENGS}
        self.dma_sems = {e: [] for e in self.ENGS}
        self.waited = {e: {} for e in self.ENGS}
        self.last_w = {}
        self.readers = {}

    def _newsem(self, name):
        return self.stack.enter_context(self.nc.semaphore(name))

    def _compute_token(self, e):
        n = self.count[e]
        ep, idx = divmod(n, EPOCH)
        while len(self.sems[e]) <= ep:
            self.sems[e].append(self._newsem(f"s_{e}_{len(self.sems[e])}"))
        self.count[e] = n + 1
        return (self.sems[e][ep], idx + 1, 1, e)

    def _dma_token(self, e):
        n = self.dma_count[e]
        if not self.dma_sems[e]:
            self.dma_sems[e] = [self._newsem(f"d_{e}_{i}") for i in range(DMA_ROT)]
        self.dma_count[e] = n + 1
        return (self.dma_sems[e][n % DMA_ROT], 16 * (n // DMA_ROT + 1), 16, "dma_" + e)

    def op(self, e, fn, reads=(), writes=(), dma=False):
        deps = []
        for k in reads:
            t = self.last_w.get(k)
            if t is not None:
                deps.append((t, True))
        for k in writes:
            t = self.last_w.get(k)
            if t is not None:
                deps.append((t, False))
            for t in self.readers.get(k, ()):
                deps.append((t, False))
        wd = self.waited[e]
        ww = {}
        for (sem, val, _inc, src), is_raw in deps:
            if src == e and not dma and e == "tensor":
                continue
            sid = id(sem)
            if wd.get(sid, 0) >= val:
                continue
            if sid not in ww or ww[sid][1] < val:
                ww[sid] = (sem, val)
        for sid, (sem, val) in ww.items():
            wd[sid] = val
        if dma:
            n = self.dma_count[e]
            if n >= DMA_ROT:
                sem = self.dma_sems[e][n % DMA_ROT]
                val = 16 * (n // DMA_ROT)
                if wd.get(id(sem), 0) < val:
                    wd[id(sem)] = val
                    ww[id(sem)] = (sem, val)
        tok = self._dma_token(e) if dma else self._compute_token(e)
        self.streams[e].append((list(ww.values()), fn, tok))
        for k in reads:
            self.readers.setdefault(k, []).append(tok)
        for k in writes:
            self.last_w[k] = tok
            self.readers[k] = []
        return tok

    def wait_tokens(self, e, toks):
        wd = self.waited[e]
        waits = []
        for (sem, val, _i, _s) in toks:
            if wd.get(id(sem), 0) >= val:
                continue
            wd[id(sem)] = val
            waits.append((sem, val))
        self.streams[e].append((waits, None, None))

    def emit(self):
        nc = self.nc
        with nc.Block() as block:
            def mk(e):
                def body(eng):
                    for waits, fn, tok in self.streams[e]:
                        for sem, val in waits:
                            eng.wait_ge(sem, val)
                        if fn is not None:
                            ins = fn(eng)
                            ins.then_inc(tok[0], tok[2])
                return body
            for e in self.ENGS:
                if self.streams[e]:
                    getattr(block, e)(mk(e))


class V:
    __slots__ = ("ap", "keys")

    def __init__(self, ap, keys):
        self.ap = ap
        self.keys = keys


class Buf:
    def __init__(self, t, name):
        self.t = t
        self.name = name

    def v(self, idx=None, sub=None):
        ap = self.t[idx] if idx is not None else self.t[:]
        return V(ap, [(self.name, sub)])

    def w(self, ap, sub=None):
        return V(ap, [(self.name, sub)])


class G:
    def __init__(self, nc, st):
        self.nc = nc
        self.st = st
        self.kb = KB(nc, st)
        self.banks = [Buf(st.enter_context(nc.psum_tensor(f"psb{i}", [128, 512], F32)), f"ps{i}") for i in range(8)]
        self.bank_i = 0
        self.out_tokens = []

    def sb(self, name, shape, dt):
        return Buf(self.st.enter_context(self.nc.sbuf_tensor("sb_" + name, shape, dt)), name)

    def bank(self):
        b = self.banks[self.bank_i % 8]
        self.bank_i += 1
        return b

    @staticmethod
    def _k(vs):
        ks = []
        for v in vs:
            if isinstance(v, V):
                ks.extend(v.keys)
        return ks

    def mm(self, out, lhsT, rhs, start=True, stop=True):
        return self.kb.op("tensor", lambda e: e.matmul(out.ap, lhsT=lhsT.ap, rhs=rhs.ap, start=start, stop=stop),
                          reads=self._k([lhsT, rhs]), writes=out.keys)

    def tr(self, out, in_, ident):
        return self.kb.op("tensor", lambda e: e.transpose(out=out.ap, in_=in_.ap, identity=ident.ap),
                          reads=self._k([in_, ident]), writes=out.keys)

    def act(self, out, in_, func, bias=None, scale=1.0, accum=None, eng="scalar"):
        kw = {}
        if bias is not None:
            kw["bias"] = bias.ap if isinstance(bias, V) else bias
        if accum is not None:
            kw["accum_out"] = accum.ap
        sc = scale.ap if isinstance(scale, V) else scale
        return self.kb.op("scalar", lambda e: e.activation(out=out.ap, in_=in_.ap, func=func, scale=sc, **kw),
                          reads=self._k([in_, bias, scale]), writes=self._k([out, accum]))

    def tt(self, eng, out, a, b, op):
        return self.kb.op(eng, lambda e: e.tensor_tensor(out=out.ap, in0=a.ap, in1=b.ap, op=op),
                          reads=self._k([a, b]), writes=out.keys)

    def ts(self, eng, out, a, s1, op0, s2=None, op1=None):
        s1a = s1.ap if isinstance(s1, V) else s1
        s2a = s2.ap if isinstance(s2, V) else s2
        if op1 is None:
            fn = lambda e: e.tensor_scalar(out=out.ap, in0=a.ap, scalar1=s1a, scalar2=None, op0=op0)
        else:
            fn = lambda e: e.tensor_scalar(out=out.ap, in0=a.ap, scalar1=s1a, scalar2=s2a, op0=op0, op1=op1)
        return self.kb.op(eng, fn, reads=self._k([a, s1, s2]), writes=out.keys)

    def stt(self, eng, out, in0, scalar, in1, op0, op1):
        sa = scalar.ap if isinstance(scalar, V) else scalar
        return self.kb.op(eng, lambda e: e.scalar_tensor_tensor(out=out.ap, in0=in0.ap, scalar=sa, in1=in1.ap, op0=op0, op1=op1),
                          reads=self._k([in0, scalar, in1]), writes=out.keys)

    def copy(self, eng, out, in_):
        if eng == "scalar":
            return self.act(out, in_, AF.Copy)
        return self.kb.op(eng, lambda e: e.tensor_copy(out=out.ap, in_=in_.ap), reads=in_.keys, writes=out.keys)

    def memset(self, eng, out, val):
        return self.kb.op(eng, lambda e: e.memset(out.ap, val), writes=out.keys)

    def recip(self, out, in_):
        return self.kb.op("vector", lambda e: e.reciprocal(out=out.ap, in_=in_.ap), reads=in_.keys, writes=out.keys)

    def scan(self, out, d0, d1, init, op0, op1):
        return self.kb.op("vector", lambda e: e.tensor_tensor_scan(out=out.ap, data0=d0.ap, data1=d1.ap, initial=init, op0=op0, op1=op1),
                          reads=self._k([d0, d1]), writes=out.keys)

    def aselect(self, out, in_, pattern, cmp, fill, base, cm):
        return self.kb.op("gpsimd", lambda e: e.affine_select(out=out.ap, in_=in_.ap, pattern=pattern, compare_op=cmp,
                                                               fill=fill, base=base, channel_multiplier=cm),
                          reads=in_.keys, writes=out.keys)

    def dma_in(self, eng, out, in_ap, **kw):
        return self.kb.op(eng, lambda e: e.dma_start(out=out.ap, in_=in_ap, **kw), writes=out.keys, dma=True)

    def dma_out(self, eng, out_ap, in_, final=True):
        t = self.kb.op(eng, lambda e: e.dma_start(out=out_ap, in_=in_.ap), reads=in_.keys, dma=True)
        if final:
            self.out_tokens.append(t)
        return t

    def finish(self):
        self.kb.wait_tokens("sync", self.out_tokens)
        self.kb.emit()


S_LEN = 8192
D = 1024
TB = 512
NCH = TB // 128
GN_EPS = 64e-5
RMS_EPS = 1e-5
PROJ = [("r", 0, 256, 0), ("k", 2, 256, 256), ("v", 3, 256, 512), ("w1", 1, 64, 768), ("a1", 4, 64, 832), ("g1", 5, 160, 896)]
NCOL = 1056


BACK_W = 4


def build_phase1(n_tok=S_LEN, stop=None, stopargs=()):
    nc = bass.Bass("TRN2", target_bir_lowering=False)
    dr = lambda n, s, d=F32: nc.dram_tensor(n, s, d, kind="ExternalInput").ap()
    x_d = dr("x", [n_tok, D])
    gm_d = dr("gm", [128, 7, 8])
    wcat_d = dr("wcat", [D, NCOL])
    w2_d = dr("w2", [64, 256])
    a2_d = dr("a2", [64, 256])
    g2_d = dr("g2", [160, 256])
    vec_d = dr("vec", [128, 7, 2])
    yg_d = nc.dram_tensor("yg", [256, n_tok], BF16, kind="ExternalOutput").ap()
    dbg_d = nc.dram_tensor("dbg", [128, 2560], F32, kind="ExternalOutput").ap() if stop else None
    nblk = n_tok // TB

    class Stop(Exception):
        pass

    with ExitStack() as st:
        g = G(nc, st)
        sb = g.sb
        dbgt = sb("dbgt", [128, 2560], F32) if stop else None
        dbg_off = [0]

        def dump(v, n):
            o = dbg_off[0]
            p = v.ap.shape[0]
            g.copy("vector", dbgt.w(dbgt.t[0:p, o:o + n]), v)
            dbg_off[0] = o + n

        def chk(name):
            if stop == name:
                raise Stop()
        def emit_all():
            identf = sb("identf", [128, 128], F32)
            ident = sb("ident", [128, 128], BF16)
            ident4 = sb("ident4", [128, 4, 128], BF16)
            onesblk = sb("onesblk", [128, 128], F32)
            m_su = sb("m_su", [128, 4, 128], BF16)
            m_u = sb("m_u", [128, 4, 128], BF16)
            m_sl = sb("m_sl", [128, 4, 128], BF16)
            rmask = sb("rmask", [128, TB], F32)
            g.memset("gpsimd", identf.v(), 0.0)
            g.aselect(identf.v(), identf.v(), [[-1, 128]], ALU.not_equal, 1.0, 0, 1)
            g.copy("vector", ident.v(), identf.v())
            for h in range(4):
                g.copy("vector", ident4.v(np.s_[:, h, :]), identf.v())
            g.memset("gpsimd", onesblk.v(), 0.0)
            g.memset("gpsimd", onesblk.v(np.s_[0:64, 0:64]), 1.0)
            g.memset("gpsimd", onesblk.v(np.s_[64:128, 64:128]), 1.0)
            for m, cm, pat, cmp in ((m_su, -1, 1, ALU.is_gt), (m_u, -1, 1, ALU.is_ge), (m_sl, 1, -1, ALU.is_gt)):
                g.memset("gpsimd", m.v(), 1.0)
                g.aselect(m.v(), m.v(), [[0, 4], [pat, 128]], cmp, 0.0, 0, cm)
            g.memset("gpsimd", rmask.v(), 1.0)
            g.memset("gpsimd", rmask.w(rmask.t[:].rearrange("p (c t) -> p c t", t=128)[:, :, 0:1]), 0.0)

            if stop == "const":
                dump(identf.v(), 128); dump(m_su.v(np.s_[:, 1, :]), 128); dump(m_sl.v(np.s_[:, 2, :]), 128); dump(m_u.v(np.s_[:, 3, :]), 128)
                dump(onesblk.v(), 128); dump(rmask.v(), 512)
            chk("const")
            gm = sb("gm", [128, 7, 8], F32)
            coefA = sb("coefA", [128, 6, 8], F32)
            coefB = sb("coefB", [128, 6, 8], F32)
            vec = sb("vec", [128, 7, 2], F32)
            WA = sb("WA", [128, 8, NCOL], BF16)
            WB = sb("WB", [128, 8, NCOL], BF16)
            w2b = sb("w2b", [64, 256], BF16)
            a2b = sb("a2b", [128, 256], BF16)
            g2a = sb("g2a", [128, 256], BF16)
            g2b = sb("g2b", [32, 256], BF16)
            xin = sb("xin", [128, 4, 1024], F32)
            g.dma_in("sync", gm.v(), gm_d[:, :, :])
            g.dma_in("sync", vec.v(), vec_d[:, :, :])
            g.dma_in("gpsimd", w2b.v(), w2_d[:, :])
            g.dma_in("gpsimd", a2b.v(np.s_[64:128, :]), a2_d[:, :])
            g.dma_in("gpsimd", g2a.v(), g2_d[0:128, :])
            g.dma_in("gpsimd", g2b.v(), g2_d[128:160, :])
            g0b = gm.w(gm.t[:, 0:1, :].to_broadcast([128, 6, 8]))
            g.tt("vector", coefB.v(), gm.v(np.s_[:, 1:7, :]), g0b, ALU.mult)
            g.tt("vector", coefA.v(), g0b, coefB.v(), ALU.subtract)
            stage = xin
            wv = wcat_d.rearrange("(kc p) n -> p kc n", p=128)
            XK = [("xin", j) for j in range(4)]
            for (nm, mi, ncol, off) in PROJ:
                sv = stage.t[:].rearrange("p a b -> p (a b)")[:, 0:8 * ncol].rearrange("p (k n) -> p k n", k=8)
                g.kb.op("sync", lambda e, sv=sv, off=off, ncol=ncol: e.dma_start(out=sv, in_=wv[:, :, off:off + ncol]), writes=XK, dma=True)
                ca = coefA.w(coefA.t[:, mi, :].unsqueeze(2).to_broadcast([128, 8, ncol]))
                cb = coefB.w(coefB.t[:, mi, :].unsqueeze(2).to_broadcast([128, 8, ncol]))
                g.tt("vector", WA.v(np.s_[:, :, off:off + ncol]), V(sv, XK), ca, ALU.mult)
                g.tt("gpsimd", WB.v(np.s_[:, :, off:off + ncol]), V(sv, XK), cb, ALU.mult)

            if stop == "weights":
                dump(WA.v(np.s_[:, 3, 0:512]), 512); dump(WB.v(np.s_[:, 7, 544:1056]), 512); dump(coefA.v(np.s_[:, 2, :]), 8)
            chk("weights")
            ms = sb("ms", [128, 4], F32)
            xn = [sb(f"xn{i}", [128, 1024], BF16) for i in range(2)]
            hT = sb("hT", [128, 8, TB + 1], BF16)
            T = {n: sb("t_" + n, [128, TB], F32) for n in
                 ("rT", "kraw", "aT", "sgw", "cs", "Er", "En", "kk", "kp", "beta", "t1", "t2", "EC")}
            junk = Buf(T["EC"].t[:].bitcast(BF16), "t_EC")
            tw = sb("tw", [128, TB], BF16)
            sg1a = sb("sg1a", [128, TB], BF16)
            sg1b = sb("sg1b", [32, TB], BF16)
            vT = sb("vT", [128, 2, TB], F32)
            gT2 = [sb(f"gT{i}", [128, 2, TB], F32) for i in range(2)]
            bonusT2 = [sb(f"bonusT{i}", [128, 2, TB], F32) for i in range(2)]
            PCs2 = [sb(f"PCs{i}", [128, 2, NCH], F32) for i in range(2)]
            rt2 = [sb(f"rt_{i}", [128, 2, TB], BF16) for i in range(2)]
            kt2 = [sb(f"kt_{i}", [128, 2, TB], BF16) for i in range(2)]
            at2 = [sb(f"at_{i}", [128, 2, TB], BF16) for i in range(2)]
            bt2 = [sb(f"bt_{i}", [128, 2, TB], BF16) for i in range(2)]
            T2 = {n: sb("t2_" + n, [128, TB], F32) for n in ("o1", "o2")}
            khT = sb("khT", [128, 2, TB], BF16)
            bhT = sb("bhT", [128, 2, TB], BF16)
            XR2 = [sb(f"XR{i}", [128, NCH, 4, 128], BF16) for i in range(2)]
            Kpad2 = [sb(f"Kpad{i}", [128, NCH, 4, 128], BF16) for i in range(2)]
            Bpad2 = [sb(f"Bpad{i}", [128, NCH, 4, 128], BF16) for i in range(2)]
            Vpad2 = [sb(f"Vpad{i}", [128, NCH, 4, 128], BF16) for i in range(2)]
            NSL = 2
            Pb = [[sb(f"P{s}{i}", [128, 4, 128], BF16) for i in range(2)] for s in range(NSL)]
            SUt = [sb(f"SU{s}", [128, 2, 4, 128], BF16) for s in range(NSL)]
            UUt = [sb(f"UU{s}", [128, 2, 4, 128], BF16) for s in range(NSL)]
            PTb = [[Buf(SUt[s].t[:, 0], f"PT{s}0"), sb(f"PT{s}1", [128, 4, 128], BF16)] for s in range(NSL)]
            NTb = [[sb(f"NT{s}{i}", [128, 4, 128], BF16) for i in range(2)] for s in range(NSL)]
            AakT = [Buf(SUt[s].t[:, 1], f"AakT{s}") for s in range(NSL)]
            ArbT = [Buf(UUt[s].t[:, 0], f"ArbT{s}") for s in range(NSL)]
            ArkT = [Buf(UUt[s].t[:, 1], f"ArkT{s}") for s in range(NSL)]
            Apad = [sb(f"Apad{s}", [128, 4, 128], BF16) for s in range(NSL)]
            Wpad = [sb(f"Wpad{s}", [128, 4, 128], BF16) for s in range(NSL)]
            TTbd = [sb(f"TTbd{s}", [128, 2, 128], BF16) for s in range(NSL)]
            RhT = [sb(f"RhT{s}", [128, 2, 128], BF16) for s in range(NSL)]
            Sbd = [sb(f"Sbd{i}", [128, 2, 128], BF16) for i in range(2)]
            yraw = sb("yraw", [128, 2, TB], F32)
            ygo = sb("ygo", [128, 2, TB], BF16)
            for b_ in Kpad2 + Bpad2 + Vpad2:
                g.memset("gpsimd", b_.v(), 0.0)
            for s in range(NSL):
                g.memset("gpsimd", Apad[s].v(), 0.0)
                g.memset("gpsimd", Wpad[s].v(), 0.0)
            g.memset("gpsimd", Sbd[0].v(), 0.0)
            g.memset("vector", hT.v(np.s_[:, :, 0:1]), 0.0)
            s_cur = 0

            def bf(bank):
                return bank.t[:].bitcast(BF16)

            def hsl(h, j):
                cc, q = divmod(h, 2)
                return np.s_[q * 64:(q + 1) * 64, cc, j * 128:(j + 1) * 128]

            def load_x(blk):
                t0 = blk * TB
                for j in range(4):
                    g.dma_in("sync", xin.v(np.s_[:, j, :], sub=j), x_d[t0 + j * 128:t0 + (j + 1) * 128, :])

            def front(blk):
                F = blk % 2
                rt_, kt_, at_, bt_ = rt2[F], kt2[F], at2[F], bt2[F]
                XR, Kpad, Bpad, Vpad, PCs, gT, bonusT = XR2[F], Kpad2[F], Bpad2[F], Vpad2[F], PCs2[F], gT2[F], bonusT2[F]
                for j in range(4):
                    g.act(junk.v(), xin.v(np.s_[:, j, :], sub=j), AF.Square, scale=1.0 / 32, accum=ms.v(np.s_[:, j:j + 1]))
                g.ts("vector", ms.v(), ms.v(), RMS_EPS, ALU.add)
                g.act(ms.v(), ms.v(), AF.Sqrt)
                g.recip(ms.v(), ms.v())
                if blk > 0:
                    g.copy("vector", hT.v(np.s_[:, :, 0:1]), hT.v(np.s_[:, :, TB:TB + 1]))
                yield
                for j in range(4):
                    xj = xn[j % 2]
                    g.act(xj.v(), xin.v(np.s_[:, j, :], sub=j), AF.Copy, scale=ms.v(np.s_[:, j:j + 1]))
                    bk = g.bank()
                    bv = bf(bk)
                    for kc in range(8):
                        g.tr(bk.w(bv[:, kc * 128:(kc + 1) * 128]), xj.v(np.s_[:, kc * 128:(kc + 1) * 128]), ident.v())
                    g.copy("vector" if j % 2 == 0 else "scalar", hT.v(np.s_[:, :, 1 + 128 * j:1 + 128 * (j + 1)]),
                           bk.w(bv.rearrange("p (k t) -> p k t", k=8)))
                    yield
                if blk + 1 < nblk:
                    load_x(blk + 1)

                def proj_fm(off, ncol):
                    bk = g.bank()
                    for kc in range(8):
                        g.mm(bk.v(np.s_[0:ncol, :]), WA.v(np.s_[:, kc, off:off + ncol]), hT.v(np.s_[:, kc, 1:TB + 1]),
                             start=(kc == 0), stop=False)
                        g.mm(bk.v(np.s_[0:ncol, :]), WB.v(np.s_[:, kc, off:off + ncol]), hT.v(np.s_[:, kc, 0:TB]),
                             start=False, stop=(kc == 7))
                    return bk

                bk = proj_fm(768, 128)
                g.act(tw.v(np.s_[0:64, :]), bk.v(np.s_[0:64, :]), AF.Tanh)
                g.copy("vector", tw.v(np.s_[64:128, :]), bk.v(np.s_[64:128, :]))
                yield
                bk = proj_fm(896, 128)
                g.act(sg1a.v(), bk.v(), AF.Sigmoid)
                yield
                bk = proj_fm(1024, 32)
                g.act(sg1b.v(), bk.v(np.s_[0:32, :]), AF.Sigmoid)
                yield
                for j in range(4):
                    bk = g.bank()
                    for kc in range(8):
                        g.mm(bk.v(np.s_[:, 0:256]), hT.v(np.s_[:, kc, 1 + 128 * j:1 + 128 * (j + 1)]), WA.v(np.s_[:, kc, 512:768]),
                             start=(kc == 0), stop=False)
                        g.mm(bk.v(np.s_[:, 0:256]), hT.v(np.s_[:, kc, 128 * j:128 * (j + 1)]), WB.v(np.s_[:, kc, 512:768]),
                             start=False, stop=(kc == 7))
                    pv = bk.t[:, 0:256].rearrange("p (h c) -> p h c", h=4)
                    g.copy("vector", Vpad.v(np.s_[:, j, 0::2, 0:64]), bk.w(pv[:, 0::2, :]))
                    g.copy("scalar", Vpad.v(np.s_[:, j, 1::2, 64:128]), bk.w(pv[:, 1::2, :]))
                    yield

                for cc in range(2):
                    bk = proj_fm(0 + cc * 128, 128)
                    g.copy("scalar", T["rT"].v(), bk.v())
                    yield
                    bk = proj_fm(256 + cc * 128, 128)
                    g.copy("vector", T["kraw"].v(), bk.v())
                    yield
                    bk = proj_fm(512 + cc * 128, 128)
                    g.copy("scalar", vT.v(np.s_[:, cc, :]), bk.v())
                    yield
                    bk = g.bank()
                    g.mm(bk.v(), w2b.v(np.s_[0:64, cc * 128:(cc + 1) * 128]), tw.v(np.s_[0:64, :]))
                    g.act(T["sgw"].v(), bk.v(), AF.Sigmoid, bias=vec.v(np.s_[:, 0, cc:cc + 1]))
                    bk = g.bank()
                    g.mm(bk.v(), a2b.v(np.s_[64:128, cc * 128:(cc + 1) * 128]), tw.v(np.s_[64:128, :]))
                    g.act(T["aT"].v(), bk.v(), AF.Sigmoid, bias=vec.v(np.s_[:, 1, cc:cc + 1]))
                    bk = g.bank()
                    g.mm(bk.v(), g2a.v(np.s_[:, cc * 128:(cc + 1) * 128]), sg1a.v(), start=True, stop=False)
                    g.mm(bk.v(), g2b.v(np.s_[0:32, cc * 128:(cc + 1) * 128]), sg1b.v(np.s_[0:32, :]), start=False, stop=True)
                    g.copy("vector", gT.v(np.s_[:, cc, :]), bk.v())
                    yield

                    kk, kp, beta, t1, t2 = T["kk"], T["kp"], T["beta"], T["t1"], T["t2"]
                    Er, En, EC, cs = T["Er"], T["En"], T["EC"], T["cs"]
                    g.ts("gpsimd", kk.v(), T["kraw"].v(), vec.v(np.s_[:, 2, cc:cc + 1]), ALU.mult)
                    g.tt("gpsimd", t1.v(), kk.v(), kk.v(), ALU.mult)
                    bk = g.bank()
                    g.mm(bk.v(), onesblk.v(), t1.v())
                    g.scan(cs.v(), rmask.v(), T["sgw"].v(), 0.0, ALU.mult, ALU.add)
                    yield
                    g.ts("vector", t2.v(), bk.v(), 1e-24, ALU.max)
                    g.act(t2.v(), t2.v(), AF.Sqrt)
                    g.act(Er.v(), cs.v(), AF.Exp, scale=NEG_C)
                    g.act(En.v(), cs.v(), AF.Exp, scale=-NEG_C)
                    g.recip(t2.v(), t2.v())
                    yield
                    g.tt("gpsimd", kk.v(), kk.v(), t2.v(), ALU.mult)
                    g.ts("vector", t1.v(), T["aT"].v(), -1.0, ALU.add, vec.v(np.s_[:, 3, cc:cc + 1]), ALU.mult)
                    g.tt("gpsimd", beta.v(), kk.v(), T["aT"].v(), ALU.mult)
                    g.stt("vector", kp.v(), t1.v(), 1.0, T["kraw"].v(), ALU.add, ALU.mult)
                    yield
                    Er3 = Er.t[:].rearrange("p (c t) -> p c t", t=128)
                    g.copy("vector", PCs.v(np.s_[:, cc, :]), Er.w(Er3[:, :, 127]))
                    g.tt("vector", EC.w(EC.t[:].rearrange("p (c t) -> p c t", t=128)),
                         En.w(En.t[:].rearrange("p (c t) -> p c t", t=128)),
                         Er.w(Er3[:, :, 127:128].to_broadcast([128, NCH, 128])), ALU.mult)
                    g.tt("gpsimd", kt_.v(np.s_[:, cc, :]), kp.v(), En.v(), ALU.mult)
                    g.tt("gpsimd", rt_.v(np.s_[:, cc, :]), T["rT"].v(), Er.v(), ALU.mult)
                    yield
                    g.tt("gpsimd", bt_.v(np.s_[:, cc, :]), beta.v(), En.v(), ALU.mult)
                    g.tt("gpsimd", khT.v(np.s_[:, cc, :]), kp.v(), EC.v(), ALU.mult)
                    g.tt("gpsimd", bhT.v(np.s_[:, cc, :]), beta.v(), EC.v(), ALU.mult)
                    kk3 = kk.t[:].rearrange("p (c t) -> p c t", t=128)
                    at3 = at_.t[:, cc, :].rearrange("p (c t) -> p c t", t=128)
                    g.stt("vector", at_.w(at3[:, :, 1:128]), kk.w(kk3[:, :, 1:128]), -1.0, Er.w(Er3[:, :, 0:127]), ALU.mult, ALU.mult)
                    g.ts("vector", at_.w(at3[:, :, 0:1]), kk.w(kk3[:, :, 0:1]), -1.0, ALU.mult)
                    yield
                    g.stt("vector", t1.v(), T["rT"].v(), vec.v(np.s_[:, 4, cc:cc + 1]), kp.v(), ALU.mult, ALU.mult)
                    bk = g.bank()
                    g.mm(bk.v(), onesblk.v(), t1.v())
                    g.tt("vector", bonusT.v(np.s_[:, cc, :]), vT.v(np.s_[:, cc, :]), bk.v(), ALU.mult)
                    yield

                for (src, kind) in ((at_, "A"), (khT, "K"), (bhT, "B")):
                    bk = g.bank()
                    bv = bf(bk)
                    for j in range(NCH):
                        for cc in range(2):
                            sl = (j * 2 + cc) * 128
                            g.tr(bk.w(bv[:, sl:sl + 128]), src.v(np.s_[:, cc, j * 128:(j + 1) * 128]), ident.v())
                    if kind == "A":
                        g.copy("vector", XR.v(np.s_[:, :, :, 0:64]),
                               bk.w(bv.rearrange("p (j h c) -> p j h c", j=NCH, h=4)))
                    else:
                        dst = Kpad if kind == "K" else Bpad
                        b5 = bv.rearrange("p (j c q d) -> p j c q d", j=NCH, c=2, q=2)
                        g.copy("vector", dst.v(np.s_[:, :, 0::2, 0:64]), bk.w(b5[:, :, :, 0, :]))
                        g.copy("scalar", dst.v(np.s_[:, :, 1::2, 64:128]), bk.w(b5[:, :, :, 1, :]))
                    yield

            def back(blk):
                nonlocal s_cur
                t0 = blk * TB
                F = blk % 2
                rt_, kt_, at_, bt_ = rt2[F], kt2[F], at2[F], bt2[F]
                XR, Kpad, Bpad, Vpad, PCs, gT, bonusT = XR2[F], Kpad2[F], Bpad2[F], Vpad2[F], PCs2[F], gT2[F], bonusT2[F]

                def pre_a(j, s):
                    def grp(specs, msk, dsts_t, dkeys):
                        bks = [g.bank(), g.bank()]
                        for si, (lh, rh) in enumerate(specs):
                            for h in range(4):
                                cc, q = divmod(h, 2)
                                sl = (si * 2 + cc) * 128
                                g.mm(bks[q].v(np.s_[:, sl:sl + 128]), lh.v(hsl(h, j)), rh.v(hsl(h, j)))
                        n = len(specs)
                        for q in range(2):
                            if n == 2:
                                src = bks[q].w(bks[q].t[:].rearrange("p (a c t) -> p a c t", a=2, c=2))
                                dst = V(dsts_t[:, :, q::2, :], dkeys)
                                mk = msk.w(msk.t[:].rearrange("p (a c) t -> p a c t", a=2))
                            else:
                                src = bks[q].w(bks[q].t[:, 0:256].rearrange("p (c t) -> p c t", c=2))
                                dst = V(dsts_t[:, q::2, :], dkeys)
                                mk = msk.v(np.s_[:, 0:2, :])
                            g.tt("vector", dst, src, mk, ALU.mult)
                    grp([(bt_, at_), (kt_, at_)], m_su, SUt[s].t, [(PTb[s][0].name, None), (AakT[s].name, None)])
                    yield
                    grp([(bt_, rt_), (kt_, rt_)], m_u, UUt[s].t, [(ArbT[s].name, None), (ArkT[s].name, None)])
                    yield
                    grp([(at_, bt_)], m_sl, Pb[s][0].t, [(Pb[s][0].name, None)])
                    g.tt("gpsimd", NTb[s][0].v(), PTb[s][0].v(), ident4.v(), ALU.add)
                    yield

                def pre_dbl(s, it):
                    cur, nxt = it % 2, (it + 1) % 2
                    bk = g.bank()
                    for h in range(4):
                        g.mm(bk.v(np.s_[:, h * 128:(h + 1) * 128]), PTb[s][cur].v(np.s_[:, h, :]), Pb[s][cur].v(np.s_[:, h, :]))
                    g.copy("scalar", Pb[s][nxt].v(), bk.w(bk.t[:].rearrange("p (h t) -> p h t", h=4)))
                    if it < 5:
                        bk = g.bank()
                        for h in range(4):
                            g.mm(bk.v(np.s_[:, h * 128:(h + 1) * 128]), Pb[s][cur].v(np.s_[:, h, :]), PTb[s][cur].v(np.s_[:, h, :]))
                        g.copy("vector", PTb[s][nxt].v(), bk.w(bk.t[:].rearrange("p (h t) -> p h t", h=4)))
                    yield
                    bk = g.bank()
                    for h in range(4):
                        o = bk.v(np.s_[:, h * 128:(h + 1) * 128])
                        g.mm(o, ident.v(), NTb[s][cur].v(np.s_[:, h, :]), start=True, stop=False)
                        g.mm(o, Pb[s][nxt].v(np.s_[:, h, :]), NTb[s][cur].v(np.s_[:, h, :]), start=False, stop=True)
                    eng = "scalar" if it % 2 == 0 else "vector"
                    g.copy(eng, NTb[s][nxt].v(), bk.w(bk.t[:].rearrange("p (h t) -> p h t", h=4)))
                    yield

                def pre_b(j, s):
                    NTf = NTb[s][0]
                    bk = g.bank()
                    for h in range(4):
                        q = h % 2
                        g.mm(bk.v(np.s_[:, h * 64:(h + 1) * 64]), AakT[s].v(np.s_[:, h, :]), Vpad.v(np.s_[:, j, h, q * 64:(q + 1) * 64]))
                    g.copy("vector", XR.v(np.s_[:, j, :, 64:128]), bk.w(bk.t[:, 0:256].rearrange("p (h c) -> p h c", h=4)))
                    yield
                    bk = g.bank()
                    for h in range(4):
                        g.mm(bk.v(np.s_[:, h * 128:(h + 1) * 128]), NTf.v(np.s_[:, h, :]), XR.v(np.s_[:, j, h, :]))
                    x4 = bk.t[:].rearrange("p (h c) -> p h c", h=4)
                    g.copy("vector", Apad[s].v(np.s_[:, 0::2, 0:64]), bk.w(x4[:, 0::2, 0:64]))
                    g.copy("scalar", Apad[s].v(np.s_[:, 1::2, 64:128]), bk.w(x4[:, 1::2, 0:64]))
                    g.copy("vector", Wpad[s].v(np.s_[:, 0::2, 0:64]), bk.w(x4[:, 0::2, 64:128]))
                    g.copy("scalar", Wpad[s].v(np.s_[:, 1::2, 64:128]), bk.w(x4[:, 1::2, 64:128]))
                    yield
                    bk = g.bank()
                    for cc in range(2):
                        o = bk.v(np.s_[:, cc * 128:(cc + 1) * 128])
                        for q in range(2):
                            h = 2 * cc + q
                            g.mm(o, Apad[s].v(np.s_[:, h, :]), Bpad.v(np.s_[:, j, h, :]), start=(q == 0), stop=(q == 1))
                    for cc in range(2):
                        g.stt("vector", TTbd[s].v(np.s_[:, cc, :]), identf.v(), PCs.v(np.s_[:, cc, j:j + 1]),
                              bk.v(np.s_[:, cc * 128:(cc + 1) * 128]), ALU.mult, ALU.add)
                    bk = g.bank()
                    for cc in range(2):
                        o = bk.v(np.s_[:, cc * 128:(cc + 1) * 128])
                        for q in range(2):
                            h = 2 * cc + q
                            g.mm(o, Apad[s].v(np.s_[:, h, :]), ArbT[s].v(np.s_[:, h, :]), start=(q == 0), stop=(q == 1))
                    g.tt("vector", RhT[s].v(), bk.w(bk.t[:, 0:256].rearrange("p (c t) -> p c t", c=2)),
                         rt_.v(np.s_[:, :, j * 128:(j + 1) * 128]), ALU.add)
                    yield

                def seq(j, s):
                    nonlocal s_cur
                    Sc, Sn = Sbd[s_cur], Sbd[1 - s_cur]
                    bk = g.bank()
                    for cc in range(2):
                        o = bk.v(np.s_[:, cc * 128:(cc + 1) * 128])
                        g.mm(o, TTbd[s].v(np.s_[:, cc, :]), Sc.v(np.s_[:, cc, :]), start=True, stop=False)
                        for q in range(2):
                            h = 2 * cc + q
                            g.mm(o, Bpad.v(np.s_[:, j, h, :]), Wpad[s].v(np.s_[:, h, :]), start=False, stop=False)
                            g.mm(o, Kpad.v(np.s_[:, j, h, :]), Vpad.v(np.s_[:, j, h, :]), start=False, stop=(q == 1))
                    g.copy("vector", Sn.v(), bk.w(bk.t[:, 0:256].rearrange("p (c t) -> p c t", c=2)))
                    bk = g.bank()
                    for cc in range(2):
                        o = bk.v(np.s_[:, cc * 128:(cc + 1) * 128])
                        g.mm(o, Sc.v(np.s_[:, cc, :]), RhT[s].v(np.s_[:, cc, :]), start=True, stop=False)
                        for q in range(2):
                            h = 2 * cc + q
                            g.mm(o, Wpad[s].v(np.s_[:, h, :]), ArbT[s].v(np.s_[:, h, :]), start=False, stop=False)
                            g.mm(o, Vpad.v(np.s_[:, j, h, :]), ArkT[s].v(np.s_[:, h, :]), start=False, stop=(q == 1))
                    g.copy("scalar", yraw.v(np.s_[:, :, j * 128:(j + 1) * 128]), bk.w(bk.t[:, 0:256].rearrange("p (c t) -> p c t", c=2)))
                    s_cur = 1 - s_cur
                    yield

                def rr(gens):
                    gens = list(gens)
                    while gens:
                        for gn in list(gens):
                            try:
                                next(gn)
                            except StopIteration:
                                gens.remove(gn)
                            yield

                for jp in range(0, NCH, NSL):
                    yield from rr([pre_a(jp + s, s) for s in range(NSL)])
                    for it in range(6):
                        yield from rr([pre_dbl(s, it) for s in range(NSL)])
                    yield from rr([pre_b(jp + s, s) for s in range(NSL)])
                    for s in range(NSL):
                        yield from seq(jp + s, s)

                for cc in range(2):
                    t1, t2 = T2["o1"], T2["o2"]
                    yr = yraw.v(np.s_[:, cc, :])
                    bk = g.bank()
                    g.mm(bk.v(), onesblk.v(), yr)
                    g.stt("vector", t1.v(), bk.v(), -1.0 / 64, yr, ALU.mult, ALU.add)
                    g.tt("gpsimd", t2.v(), t1.v(), t1.v(), ALU.mult)
                    yield
                    bk = g.bank()
                    g.mm(bk.v(), onesblk.v(), t2.v())
                    g.ts("vector", t2.v(), bk.v(), 1.0 / 64, ALU.mult, GN_EPS, ALU.add)
                    g.act(t2.v(), t2.v(), AF.Sqrt)
                    g.recip(t2.v(), t2.v())
                    yield
                    g.tt("gpsimd", t1.v(), t1.v(), t2.v(), ALU.mult)
                    g.ts("vector", t1.v(), t1.v(), vec.v(np.s_[:, 5, cc:cc + 1]), ALU.mult, vec.v(np.s_[:, 6, cc:cc + 1]), ALU.add)
                    g.tt("gpsimd", t1.v(), t1.v(), bonusT.v(np.s_[:, cc, :]), ALU.add)
                    g.tt("vector", ygo.v(np.s_[:, cc, :]), t1.v(), gT.v(np.s_[:, cc, :]), ALU.mult)
                    g.dma_out("sync", yg_d[cc * 128:(cc + 1) * 128, t0:t0 + TB], ygo.v(np.s_[:, cc, :]))
                    yield

            def drive(gens, weights=None):
                gens = list(gens)
                weights = list(weights or [1] * len(gens))
                while gens:
                    for gn, w in list(zip(gens, weights)):
                        for _ in range(w):
                            try:
                                next(gn)
                            except StopIteration:
                                k = gens.index(gn)
                                gens.pop(k)
                                weights.pop(k)
                                break

            load_x(0)
            drive([front(0)])
            for blk in range(nblk):
                gs = [back(blk)]
                if blk + 1 < nblk:
                    gs.append(front(blk + 1))
                drive(gs, [BACK_W, 1])
        try:
            emit_all()
        except Stop:
            pass
        if stop:
            g.dma_out("sync", dbg_d[:, :], dbgt.v())
        g.finish()
    return nc


def phase1_inputs(inp, core):
    b, hg = divmod(core, 4)
    cs = slice(hg * 256, (hg + 1) * 256)
    pm = lambda v: np.ascontiguousarray(v.reshape(-1, 128).T)
    gm = np.stack([pm(inp["norm_mix_g"][0])] + [pm(inp["rwkv_mu"][0, i]) for i in range(6)], axis=1)
    wcat = np.concatenate([inp["rwkv_w_r"][0][:, cs], inp["rwkv_w_k"][0][:, cs], inp["rwkv_w_v"][0][:, cs],
                           inp["rwkv_w1"][0], inp["rwkv_a1"][0], inp["rwkv_g1"][0]], axis=1)
    vecs = [inp["rwkv_w0"][0][cs], inp["rwkv_a0"][0][cs], inp["rwkv_k_k"][0][cs], inp["rwkv_k_a"][0][cs],
            inp["rwkv_r_k"][0].reshape(-1)[cs], inp["rwkv_ln_w"][0][cs], inp["rwkv_ln_b"][0][cs]]
    vec = np.stack([pm(v) for v in vecs], axis=1)
    return {
        "x": np.ascontiguousarray(inp["x"][b]),
        "gm": np.ascontiguousarray(gm, dtype=np.float32),
        "wcat": np.ascontiguousarray(wcat, dtype=np.float32),
        "w2": np.ascontiguousarray(inp["rwkv_w2"][0][:, cs]),
        "a2": np.ascontiguousarray(inp["rwkv_a2"][0][:, cs]),
        "g2": np.ascontiguousarray(inp["rwkv_g2"][0][:, cs]),
        "vec": np.ascontiguousarray(vec, dtype=np.float32),
    }


NT2 = 17
NTOK2 = NT2 * 128
DFF = 4096
HG = 512
NGRP = DFF // HG
NEG_BIG = -30000.0
TWO_PI = 6.283185307179586
PI = 3.141592653589793
C1_2PI = 6.28125
C2_2PI = TWO_PI - 6.28125


def build_phase2():
    nc = bass.Bass("TRN2", target_bir_lowering=False)
    dr = lambda n, s, d=F32: nc.dram_tensor(n, s, d, kind="ExternalInput").ap()
    x_d = dr("x", [NTOK2, D])
    yg_d = dr("ygT", [D, NTOK2], BF16)
    pos_d = dr("pos", [1, NTOK2], I32)
    wo0_d = dr("wo0", [D, D])
    win_d = [dr("win0", [D, DFF]), dr("win1", [D, DFF])]
    wout_d = [dr("wout0", [DFF, D]), dr("wout1", [DFF, D])]
    wqkv_d = dr("wqkv", [D, 1280])
    wo1_d = dr("wo1", [D, D])
    gains_d = dr("gains", [128, 3, 8])
    bq_d = dr("bq", [128, 10])
    rowv_d = dr("rowv", [1, 1024 + 1024 + 128 + 16])
    cst_d = dr("cst", [128, 4])
    out_d = nc.dram_tensor("out", [16 * 128, D], F32, kind="ExternalOutput").ap()

    with ExitStack() as st:
        g = G(nc, st)
        sb = g.sb
        identf = sb("identf", [128, 128], F32)
        ident = sb("ident", [128, 128], BF16)
        g.memset("gpsimd", identf.v(), 0.0)
        g.aselect(identf.v(), identf.v(), [[-1, 128]], ALU.not_equal, 1.0, 0, 1)
        g.copy("vector", ident.v(), identf.v())
        rotf = sb("rotf", [128, 128], F32)
        rot = sb("rot", [128, 128], BF16)
        g.memset("gpsimd", rotf.v(), 0.0)
        for blk in range(2):
            o = blk * 64
            sub = rotf.v(np.s_[:, o:o + 32])
            g.aselect(sub, sub, [[-1, 32]], ALU.not_equal, -1.0, -(o + 32), 1)
            sub = rotf.v(np.s_[:, o + 32:o + 64])
            g.aselect(sub, sub, [[-1, 32]], ALU.not_equal, 1.0, -(o + 32) + 32, 1)
        g.copy("vector", rot.v(), rotf.v())
        mscr = sb("scr", [128, 1024], F32)
        maskb = Buf(mscr.t[:, 0:256], "scr_m0")
        mask1 = Buf(mscr.t[:, 256:512], "scr_m1")
        g.memset("gpsimd", maskb.v(), 0.0)
        g.aselect(maskb.v(), maskb.v(), [[1, 256]], ALU.is_gt, NEG_BIG, 0, -1)
        g.aselect(maskb.v(), maskb.v(), [[-1, 256]], ALU.is_ge, NEG_BIG, 128, 1)
        cst = sb("cst", [128, 4], F32)
        g.dma_in("sync", cst.v(), cst_d[:, :])
        g.copy("vector", mask1.v(), maskb.v())
        g.ts("vector", mask1.v(np.s_[:, 0:128]), maskb.v(np.s_[:, 0:128]), cst.v(np.s_[:, 2:3]), ALU.add)
        maskbf = sb("maskbf", [128, 256], BF16)
        mask1bf = sb("mask1bf", [128, 256], BF16)
        g.ts("vector", maskbf.v(), maskb.v(), 8.0, ALU.mult)
        g.ts("vector", mask1bf.v(), mask1.v(), 8.0, ALU.mult)
        ones_row = sb("ones_row", [1, 128], F32)
        g.memset("gpsimd", ones_row.v(), 1.0)
        gains = sb("gains", [128, 3, 8], F32)
        g.dma_in("sync", gains.v(), gains_d[:, :, :])
        bq = sb("bq", [128, 10], F32)
        g.dma_in("sync", bq.v(), bq_d[:, :])
        rowv = sb("rowv", [1, 1024], F32)
        g.dma_in("sync", rowv.v(), rowv_d[0:1, 1024:2048])
        bvb = sb("bvb", [128, 128], F32)
        sinkb = sb("sinkb", [128, 16], F32)
        g.dma_in("sync", bvb.v(), rowv_d[0:1, 2048:2176].to_broadcast([128, 128]))
        g.dma_in("sync", sinkb.v(), rowv_d[0:1, 2176:2192].to_broadcast([128, 16]))

        xres = sb("xres", [128, NT2, 1024], F32)
        hT = sb("hT", [128, 8, NTOK2], BF16)
        uT = sb("uT", [128, 4, NTOK2], BF16)
        arena = sb("arena", [128, 18432], BF16)
        junk = sb("junk", [128, 1024], BF16)
        ms = sb("ms", [128, NT2], F32)
        xn = [sb("xn0", [128, 1024], BF16)] * 2
        scr = mscr
        NU = 4
        asc = sb("asc", [128, NU, 2, 256], F32)
        relu_s = [Buf(asc.t[:, i].rearrange("p a b -> p (a b)").bitcast(BF16)[:, 0:512], f"asc_relu{i}") for i in range(2)]

        def bf(bank):
            return bank.t[:].bitcast(BF16)

        NTILES = [(0, 512), (512, 512), (1024, 512), (1536, 512), (2048, 128)]
        NTILES_M = {0: NTILES, 1: [(128, 512), (640, 512), (1152, 512), (1664, 512)]}

        def load_x_and_yg():
            for kc in range(8):
                g.dma_in("sync", hT.v(np.s_[:, kc, :]), yg_d[kc * 128:(kc + 1) * 128, :])
            for i in range(NT2):
                g.dma_in("sync", xres.v(np.s_[:, i, :], sub=i), x_d[i * 128:(i + 1) * 128, :])

        def wload(dst_ap, key, src_ap):
            g.kb.op("gpsimd", lambda e: e.dma_start(out=dst_ap, in_=src_ap), writes=[key], dma=True)

        def rstd_all(tiles):
            allk = [("ms", i) for i in tiles]
            lo, hi = tiles[0], tiles[-1] + 1
            for i in tiles:
                g.act(junk.v(), xres.v(np.s_[:, i, :], sub=i), AF.Square, scale=1.0 / 32, accum=ms.v(np.s_[:, i:i + 1], sub=i))
            mv = V(ms.t[:, lo:hi], allk)
            g.ts("vector", mv, mv, RMS_EPS, ALU.add)
            g.act(mv, mv, AF.Sqrt)
            g.recip(mv, mv)

        def norm_to_hT(gi, tiles):
            tiles = list(tiles)
            rstd_all(tiles)
            for i in tiles:
                xj = xn[i % 2]
                g.act(xj.v(), xres.v(np.s_[:, i, :], sub=i), AF.Copy, scale=ms.v(np.s_[:, i:i + 1], sub=i))
                bk = g.bank()
                bv = bf(bk)
                for kc in range(8):
                    g.tr(bk.w(bv[:, kc * 128:(kc + 1) * 128]), xj.v(np.s_[:, kc * 128:(kc + 1) * 128]), ident.v())
                g.tt("vector", hT.v(np.s_[:, :, i * 128:(i + 1) * 128]), bk.w(bv.rearrange("p (k t) -> p k t", k=8)),
                     gains.w(gains.t[:, gi, :].unsqueeze(2).to_broadcast([128, 8, 128])), ALU.mult)

        def mlp(layer, first_tile=0):
            win, wout = win_d[layer], wout_d[layer]
            ntiles = [(n0, nn) for (n0, nn) in NTILES_M[first_tile]]
            winv = win.rearrange("(kc p) n -> p kc n", p=128)
            woutv = wout.rearrange("(m p) n -> p m n", p=128)
            def WI(s):
                return arena.t[:, s * 4096:(s + 1) * 4096].rearrange("p (k n) -> p k n", k=8)

            def WO(s):
                return arena.t[:, 8192 + s * 4096:8192 + (s + 1) * 4096].rearrange("p (m n) -> p m n", m=4)

            ri = 0
            for grp in range(NGRP):
                s = grp % 2
                kwi, kwo = ("arena", "wi%d" % s), ("arena", "wo%d" % s)
                wload(WI(s), kwi, winv[:, :, grp * HG:(grp + 1) * HG])
                wload(WO(s), kwo, woutv[:, grp * 4:(grp + 1) * 4, :])
                for m in range(4):
                    for (n0, nn) in ntiles:
                        bk = g.bank()
                        for kc in range(8):
                            g.mm(bk.v(np.s_[:, 0:nn]), V(WI(s)[:, kc, m * 128:(m + 1) * 128], [kwi]), hT.v(np.s_[:, kc, n0:n0 + nn]),
                                 start=(kc == 0), stop=(kc == 7))
                        r = relu_s[ri % 2]
                        ri += 1
                        g.act(V(r.t[:, 0:nn], r.v().keys + [("asc", 0), ("asc", 1)]), bk.v(np.s_[:, 0:nn]), AF.Relu)
                        g.tt("gpsimd", uT.v(np.s_[:, m, n0:n0 + nn]), r.v(np.s_[:, 0:nn]), r.v(np.s_[:, 0:nn]), ALU.mult)
                for i in range(first_tile, NT2):
                    for half in range(2):
                        bk = g.bank()
                        for m in range(4):
                            g.mm(bk.v(), uT.v(np.s_[:, m, i * 128:(i + 1) * 128]), V(WO(s)[:, m, half * 512:(half + 1) * 512], [kwo]),
                                 start=(m == 0), stop=(m == 3))
                        xs = xres.v(np.s_[:, i, half * 512:(half + 1) * 512], sub=i)
                        g.tt("vector", xs, xs, bk.v(), ALU.add)

        load_x_and_yg()
        wo_v = arena.t[:, 0:8192].rearrange("p (k n) -> p k n", k=8)
        g.kb.op("gpsimd", lambda e: e.dma_start(out=wo_v, in_=wo0_d.rearrange("(kc p) n -> p kc n", p=128)),
                writes=[("arena", "wi0"), ("arena", "wi1")], dma=True)
        for i in range(NT2):
            for half in range(2):
                bk = g.bank()
                for kc in range(8):
                    g.mm(bk.v(), hT.v(np.s_[:, kc, i * 128:(i + 1) * 128]),
                         V(wo_v[:, kc, half * 512:(half + 1) * 512], [("arena", "wi0"), ("arena", "wi1")]),
                         start=(kc == 0), stop=(kc == 7))
                xs = xres.v(np.s_[:, i, half * 512:(half + 1) * 512], sub=i)
                g.tt("vector", xs, xs, bk.v(), ALU.add)

        norm_to_hT(0, range(NT2))
        mlp(0)

        norm_to_hT(1, range(NT2))
        wq_v = arena.t[:, 0:10240].rearrange("p (k n) -> p k n", k=8)
        wo1_v = arena.t[:, 10240:18432].rearrange("p (k n) -> p k n", k=8)
        KQ = [("arena", "wi0"), ("arena", "wi1"), ("arena", "wo0")]
        KO = [("arena", "wo0"), ("arena", "wo1"), ("arena", "x")]
        g.kb.op("gpsimd", lambda e: e.dma_start(out=wq_v, in_=wqkv_d.rearrange("(kc p) n -> p kc n", p=128)), writes=KQ, dma=True)
        g.kb.op("gpsimd", lambda e: e.dma_start(out=wo1_v, in_=wo1_d.rearrange("(kc p) n -> p kc n", p=128)), writes=KO, dma=True)
        tabs = uT.t[:].rearrange("p a b -> p (a b)").bitcast(F32)
        cosT = uT.w(tabs[:, 0:NTOK2])
        sinT = uT.w(tabs[:, NTOK2:2 * NTOK2])
        posi = V(scr.t[:, 0:512].bitcast(I32), [("scr", "a"), ("scr_m0", None), ("scr_m1", None)])
        for (n0, nn) in NTILES:
            pch = V(posi.ap[:, 0:nn], posi.keys)
            ach = scr.v(np.s_[:, 512:512 + nn], sub="b")
            g.dma_in("sync", pch, pos_d[0:1, n0:n0 + nn].to_broadcast([128, nn]))
            g.copy("vector", ach, pch)
            g.ts("vector", ach, ach, cst.v(np.s_[:, 0:1]), ALU.mult)
            sch = V(sinT.ap[:, n0:n0 + nn], sinT.keys)
            cch = V(cosT.ap[:, n0:n0 + nn], cosT.keys)
            T1 = V(xn[0].t[:].bitcast(F32)[:, 0:nn], [("xn0", None)])
            A2 = V(junk.t[:].bitcast(F32)[:, 0:nn], [("junk", None)])
            TI = pch
            for (src, dst, shift) in ((ach, sch, 0.0), (ach, cch, 0.5 * PI)):
                if shift:
                    g.ts("vector", A2, src, shift, ALU.add)
                    src = A2
                g.ts("vector", T1, src, 1.0 / TWO_PI, ALU.mult)
                g.copy("vector", TI, T1)
                g.copy("vector", T1, TI)
                g.stt("vector", dst, T1, -C1_2PI, src, ALU.mult, ALU.add)
                g.stt("vector", dst, T1, -C2_2PI, dst, ALU.mult, ALU.add)
                g.ts("vector", dst, dst, -PI, ALU.max, PI, ALU.min)
                g.act(dst, dst, AF.Sin)

        kr = sb("kr", [128, 2, NTOK2], BF16)
        vpad = [sb(f"vpad{i}", [128, 2, 2, 128], BF16) for i in range(3)]
        for v_ in vpad:
            g.memset("gpsimd", v_.v(), 0.0)
        qb16 = sb("qb16", [128, 512], BF16)
        SCALE = 0.125
        negsink = sb("negsink", [128, 16], F32)
        esink = sb("esink", [128, 16], F32)
        g.ts("vector", negsink.v(), sinkb.v(), -1.0, ALU.mult)
        g.act(esink.v(), sinkb.v(), AF.Exp)

        def rope_evac(bk, nn, bias_col, n0, dst, qf, qb, r2):
            g.act(qf, bk.v(np.s_[:, 0:nn]), AF.Identity, bias=bias_col)
            g.copy("gpsimd", qb, qf)
            b2 = g.bank()
            g.mm(b2.v(np.s_[:, 0:nn]), rot.v(), qb)
            g.tt("vector", r2, b2.v(np.s_[:, 0:nn]), V(sinT.ap[:, n0:n0 + nn], sinT.keys), ALU.mult)
            g.tt("gpsimd", qf, qf, V(cosT.ap[:, n0:n0 + nn], cosT.keys), ALU.mult)
            g.tt("vector", dst, qf, r2, ALU.add)

        wkd_ap = asc.t[:].rearrange("p a b c -> p (a b c)")[:, 0:1024].bitcast(BF16).rearrange("p (k j c) -> p k j c", k=8, j=2)
        WK = [("asc", u) for u in range(NU)] + [("asc_relu0", None), ("asc_relu1", None)]
        for j in range(2):
            for dup in range(2):
                g.copy("vector", V(wkd_ap[:, :, j, dup * 64:(dup + 1) * 64], WK), V(wq_v[:, :, 1024 + j * 64:1024 + (j + 1) * 64], KQ))
        for j in range(2):
            for (n0, nn) in NTILES:
                bk = g.bank()
                for kc in range(8):
                    g.mm(bk.v(np.s_[:, 0:nn]), V(wkd_ap[:, kc, j, :], WK), hT.v(np.s_[:, kc, n0:n0 + nn]), start=(kc == 0), stop=(kc == 7))
                rope_evac(bk, nn, bq.v(np.s_[:, 8 + j:9 + j]), n0, kr.v(np.s_[:, j, n0:n0 + nn]),
                          scr.v(np.s_[:, 0:nn], sub="a"), qb16.v(np.s_[:, 0:nn]), scr.v(np.s_[:, 512:512 + nn], sub="b"))

        def make_vpad(i):
            vp = vpad[i % 3]
            bk = g.bank()
            for kc in range(8):
                g.mm(bk.v(np.s_[:, 0:128]), hT.v(np.s_[:, kc, i * 128:(i + 1) * 128]), V(wq_v[:, kc, 1152:1280], KQ), start=(kc == 0), stop=(kc == 7))
            for q2 in range(2):
                g.tt("vector", vp.v(np.s_[:, :, q2, q2 * 64:(q2 + 1) * 64]), bk.w(bk.t[:, 0:128].rearrange("p (j d) -> p j d", j=2)),
                     bvb.w(bvb.t[:].rearrange("p (j d) -> p j d", j=2)), ALU.add)
            return vp

        qr2 = [sb(f"qr{i}", [128, 8, 128], BF16) for i in range(2)]
        oT = sb("oT", [128, 8, 128], BF16)
        pn = [sb(f"pn{i}", [128, 2, 256], BF16) for i in range(NU)]
        pT = [sb(f"pT{i}", [128, 2, 2, 128], BF16) for i in range(NU)]
        stat = [sb(f"stat{i}", [128, 8], F32) for i in range(NU)]
        def prep(i):
            make_vpad(i)
            qr = qr2[i % 2]
            yield
            for hh in range(2):
                bk = g.bank()
                for a4 in range(4):
                    hp = hh * 4 + a4
                    for kc in range(8):
                        g.mm(bk.v(np.s_[:, a4 * 128:(a4 + 1) * 128]), V(wq_v[:, kc, hp * 128:(hp + 1) * 128], KQ),
                             hT.v(np.s_[:, kc, i * 128:(i + 1) * 128]), start=(kc == 0), stop=(kc == 7))
                qf = scr.v(np.s_[:, 0:512], sub="a")
                r2 = scr.v(np.s_[:, 512:1024], sub="b")
                qb = qb16.v()
                qf3 = V(scr.t[:, 0:512].rearrange("p (a t) -> p a t", a=4), [("scr", "a")])
                r23 = V(scr.t[:, 512:1024].rearrange("p (a t) -> p a t", a=4), [("scr", "b")])
                g.tt("vector", qf3, bk.w(bk.t[:].rearrange("p (a t) -> p a t", a=4)),
                     bq.w(bq.t[:, hh * 4:hh * 4 + 4].unsqueeze(2).to_broadcast([128, 4, 128])), ALU.add)
                g.copy("gpsimd", qb, qf)
                b2 = g.bank()
                g.mm(b2.v(), rot.v(), qb)
                sin_b = V(sinT.ap[:, i * 128:(i + 1) * 128].unsqueeze(1).to_broadcast([128, 4, 128]), sinT.keys)
                cos_b = V(cosT.ap[:, i * 128:(i + 1) * 128].unsqueeze(1).to_broadcast([128, 4, 128]), cosT.keys)
                g.tt("vector", r23, b2.w(b2.t[:].rearrange("p (a t) -> p a t", a=4)), sin_b, ALU.mult)
                g.tt("gpsimd", qf3, qf3, cos_b, ALU.mult)
                g.tt("vector", qr.v(np.s_[:, hh * 4:hh * 4 + 4, :]), qf3, r23, ALU.add)
                yield

        def attn(i):
            vps = [vpad[(i - 1) % 3], vpad[i % 3]]
            qr = qr2[i % 2]
            mk = mask1bf if i == 1 else maskbf
            for j in range(2):
                sbank = {}
                for gp in range(2):
                    hp0 = 4 * j + 2 * gp
                    bks = [g.bank(), g.bank()]
                    mk2 = mk.w(mk.t[:].unsqueeze(1).to_broadcast([128, 2, 256]))
                    for q2 in range(2):
                        g.mm(bks[q2].w(bks[q2].t[:].rearrange("p (a k) -> p a k", a=2)), ident.v(), mk2, start=True, stop=False)
                    for a in range(2):
                        for q2 in range(2):
                            o = bks[q2].v(np.s_[:, a * 256:(a + 1) * 256])
                            g.mm(o, qr.v(np.s_[q2 * 64:(q2 + 1) * 64, hp0 + a, :]),
                                 kr.v(np.s_[q2 * 64:(q2 + 1) * 64, j, (i - 1) * 128:(i + 1) * 128]), start=False, stop=True)
                    for q2 in range(2):
                        sbank[gp * 2 + q2] = bks[q2]
                UN = range(4)
                yield
                for u in UN:
                    gp, q2 = divmod(u, 2)
                    st_ = stat[u]
                    ps3 = sbank[u].w(sbank[u].t[:].rearrange("p (a k) -> p a k", a=2))
                    g.kb.op("vector", lambda e, st_=st_, ps3=ps3: e.tensor_reduce(out=st_.t[:, 0:2], in_=ps3.ap, axis=mybir.AxisListType.X, op=ALU.max),
                            reads=ps3.keys, writes=st_.v().keys)
                    h0 = 2 * (4 * j + 2 * gp) + q2
                    g.stt("vector", st_.v(np.s_[:, 2:4]), st_.v(np.s_[:, 0:2]), -SCALE, negsink.w(negsink.t[:, h0:h0 + 3:2]), ALU.mult, ALU.min)
                yield
                for u in UN:
                    st_ = stat[u]
                    for a in range(2):
                        g.act(V(asc.t[:, u, a, :], [("asc", u)]), sbank[u].v(np.s_[:, a * 256:(a + 1) * 256]), AF.Exp, scale=SCALE,
                              bias=st_.v(np.s_[:, 2 + a:3 + a]), accum=st_.v(np.s_[:, 4 + a:5 + a]))
                    g.act(st_.v(np.s_[:, 6:8]), st_.v(np.s_[:, 2:4]), AF.Exp)
                yield
                for u in UN:
                    gp, q2 = divmod(u, 2)
                    st_ = stat[u]
                    h0 = 2 * (4 * j + 2 * gp) + q2
                    g.tt("vector", st_.v(np.s_[:, 6:8]), st_.v(np.s_[:, 6:8]), esink.w(esink.t[:, h0:h0 + 3:2]), ALU.mult)
                    g.tt("vector", st_.v(np.s_[:, 4:6]), st_.v(np.s_[:, 4:6]), st_.v(np.s_[:, 6:8]), ALU.add)
                    g.recip(st_.v(np.s_[:, 4:6]), st_.v(np.s_[:, 4:6]))
                for u in UN:
                    st_ = stat[u]
                    g.tt("gpsimd", pn[u].v(), V(asc.t[:, u], [("asc", u)]), st_.w(st_.t[:, 4:6].unsqueeze(2).to_broadcast([128, 2, 256])), ALU.mult)
                yield
                tbs = {}
                for u in UN:
                    tb = g.bank()
                    tv = bf(tb)
                    for a in range(2):
                        for kb in range(2):
                            sl = (a * 2 + kb) * 128
                            g.tr(tb.w(tv[:, sl:sl + 128]), pn[u].v(np.s_[:, a, kb * 128:(kb + 1) * 128]), ident.v())
                    tbs[u] = (tb, tv)
                for u in UN:
                    tb, tv = tbs[u]
                    g.copy("scalar" if u % 2 == 0 else "vector", pT[u].v(), tb.w(tv[:, 0:512].rearrange("p (a k q) -> p a k q", a=2, k=2)))
                yield
                for gp in range(2):
                    hp0 = 4 * j + 2 * gp
                    obk = g.bank()
                    first = True
                    for q2 in range(2):
                        for kb in range(2):
                            g.mm(obk.w(obk.t[:, 0:256].rearrange("p (a q) -> p a q", a=2)), vps[kb].v(np.s_[:, j, q2, :]),
                                 pT[gp * 2 + q2].v(np.s_[:, :, kb, :]), start=first, stop=(q2 == 1 and kb == 1))
                            first = False
                    g.copy("scalar", oT.v(np.s_[:, hp0:hp0 + 2, :]), obk.w(obk.t[:, 0:256].rearrange("p (a q) -> p a q", a=2)))
            for half in range(2):
                bk = g.bank()
                for hp in range(8):
                    g.mm(bk.v(), oT.v(np.s_[:, hp, :]), V(wo1_v[:, hp, half * 512:(half + 1) * 512], KO), start=(hp == 0), stop=False)
                g.mm(bk.v(), ones_row.v(), rowv.v(np.s_[0:1, half * 512:(half + 1) * 512]), start=False, stop=True)
                xs = xres.v(np.s_[:, i, half * 512:(half + 1) * 512], sub=i)
                g.tt("vector", xs, xs, bk.v(), ALU.add)

            yield

        def drive2(gens):
            gens = list(gens)
            while gens:
                for gn in list(gens):
                    try:
                        next(gn)
                    except StopIteration:
                        gens.remove(gn)

        make_vpad(0)
        drive2([prep(1)])
        for i in range(1, NT2):
            gs = [attn(i)]
            if i + 1 < NT2:
                gs.append(prep(i + 1))
            drive2(gs)

        norm_to_hT(2, range(1, NT2))
        mlp(1, first_tile=1)

        gfb = V(asc.t[:].rearrange("p a b c -> p (a b c)")[:, 0:1024], [("asc", u) for u in range(NU)] + [("asc_relu0", None), ("asc_relu1", None)])
        g.dma_in("sync", gfb, rowv_d[0:1, 0:1024].to_broadcast([128, 1024]))
        rstd_all(list(range(1, NT2)))
        for i in range(1, NT2):
            o_ = V(scr.t[:], [("scr", "a"), ("scr", "b")])
            g.stt("vector", o_, xres.v(np.s_[:, i, :], sub=i), ms.v(np.s_[:, i:i + 1], sub=i), gfb, ALU.mult, ALU.mult)
            g.dma_out("sync", out_d[(i - 1) * 128:i * 128, :], o_)
        g.finish()
    return nc


def _pm(v):
    return np.ascontiguousarray(np.asarray(v).reshape(-1, 128).T)


def phase2_inputs(inp, ygT_full, core):
    b, tq = divmod(core, 4)
    t0 = tq * 2048
    x = np.zeros((NTOK2, D), np.float32)
    yg = np.zeros((D, NTOK2), ml_dtypes.bfloat16)
    pos = np.zeros((1, NTOK2), np.int32)
    lo = t0 - 128
    if tq > 0:
        x[:] = inp["x"][b, lo:lo + NTOK2]
        yg[:] = ygT_full[b][:, lo:lo + NTOK2]
        pos[0] = inp["positions"][b, lo:lo + NTOK2]
    else:
        x[128:] = inp["x"][b, 0:2048]
        yg[:, 128:] = ygT_full[b][:, 0:2048]
        pos[0, 128:] = inp["positions"][b, 0:2048]
    gains = np.stack([_pm(inp["norm_mlp_g"][0]), _pm(inp["norm_mix_g"][1]), _pm(inp["norm_mlp_g"][1])], axis=1)
    bqkv = inp["attn_b_qkv"][0]
    bq = np.zeros((128, 10), np.float32)
    bq[:, 0:8] = _pm(bqkv[0:1024])
    for j in range(2):
        bk = bqkv[1024 + j * 64:1024 + (j + 1) * 64]
        bq[:, 8 + j] = np.concatenate([bk, bk])
    rowv = np.concatenate([inp["norm_final_g"], inp["attn_b_o"][0], bqkv[1152:1280], inp["attn_sinks"][0]])[None, :]
    cst = np.zeros((128, 4), np.float32)
    p = np.arange(128)
    cst[:, 0] = (10000.0 ** (-(np.arange(0, 64, 2, dtype=np.float32)) / 64.0))[p % 32]
    cst[:, 1] = np.where((p % 64) < 32, 1.0, 1.0)
    cst[:, 2] = 0.0 if tq > 0 else NEG_BIG
    return {
        "x": x, "ygT": yg, "pos": pos,
        "wo0": np.ascontiguousarray(inp["rwkv_w_o"][0]),
        "win0": np.ascontiguousarray(inp["mlp_w_in"][0]), "win1": np.ascontiguousarray(inp["mlp_w_in"][1]),
        "wout0": np.ascontiguousarray(inp["mlp_w_out"][0]), "wout1": np.ascontiguousarray(inp["mlp_w_out"][1]),
        "wqkv": np.ascontiguousarray(inp["attn_w_qkv"][0]), "wo1": np.ascontiguousarray(inp["attn_w_o"][0]),
        "gains": np.ascontiguousarray(gains, dtype=np.float32), "bq": bq,
        "rowv": np.ascontiguousarray(rowv, dtype=np.float32), "cst": cst,
    }


_NC_CACHE = {}


def kernel(**inputs):
    inp = {k: np.asarray(v) for k, v in inputs.items()}
    if "p1" not in _NC_CACHE:
        _NC_CACHE["p1"] = build_phase1()
        _NC_CACHE["p2"] = build_phase2()
    r1 = run_bass_kernel_spmd(_NC_CACHE["p1"], [phase1_inputs(inp, c) for c in range(8)], core_ids=list(range(8)))
    ygT = np.zeros((2, D, S_LEN), ml_dtypes.bfloat16)
    for c in range(8):
        b, hg = divmod(c, 4)
        ygT[b, hg * 256:(hg + 1) * 256] = r1.results[c]["yg"]
    r2 = run_bass_kernel_spmd(_NC_CACHE["p2"], [phase2_inputs(inp, ygT, c) for c in range(8)], core_ids=list(range(8)))
    out = np.zeros((2, S_LEN, D), np.float32)
    for c in range(8):
        b, tq = divmod(c, 4)
        out[b, tq * 2048:(tq + 1) * 2048] = r2.results[c]["out"]
    return out
```

```python
import numpy as np
import ml_dtypes
from contextlib import ExitStack
import concourse.bass as bass
import concourse.mybir as mybir
from concourse.bass_utils import run_bass_kernel_spmd

F32 = mybir.dt.float32
BF16 = mybir.dt.bfloat16
I32 = mybir.dt.int32
AF = mybir.ActivationFunctionType
ALU = mybir.AluOpType

EPOCH = 4000
FUSE_WAIT = True
DMA_ROT = 8
NEG_C = -0.6065306597126334


class KB:
    ENGS = ("tensor", "vector", "scalar", "gpsimd", "sync")

    def __init__(self, nc, stack):
        self.nc = nc
        self.stack = stack
        self.streams = {e: [] for e in self.ENGS}
        self.count = {e: 0 for e in self.ENGS}
        self.sems = {e: [] for e in self.ENGS}
        self.dma_count = {e: 0 for e in self.ENGS}
        self.dma_sems = {e: [] for e in self.ENGS}
        self.waited = {e: {} for e in self.ENGS}
        self.last_w = {}
        self.readers = {}

    def _newsem(self, name):
        return self.stack.enter_context(self.nc.semaphore(name))

    def _compute_token(self, e):
        n = self.count[e]
        ep, idx = divmod(n, EPOCH)
        while len(self.sems[e]) <= ep:
            self.sems[e].append(self._newsem(f"s_{e}_{len(self.sems[e])}"))
        self.count[e] = n + 1
        return (self.sems[e][ep], idx + 1, 1, e)

    def _dma_token(self, e):
        n = self.dma_count[e]
        if not self.dma_sems[e]:
            self.dma_sems[e] = [self._newsem(f"d_{e}_{i}") for i in range(DMA_ROT)]
        self.dma_count[e] = n + 1
        return (self.dma_sems[e][n % DMA_ROT], 16 * (n // DMA_ROT + 1), 16, "dma_" + e)

    def op(self, e, fn, reads=(), writes=(), dma=False):
        deps = []
        for k in reads:
            t = self.last_w.get(k)
            if t is not None:
                deps.append((t, True))
        for k in writes:
            t = self.last_w.get(k)
            if t is not None:
                deps.append((t, False))
            for t in self.readers.get(k, ()):
                deps.append((t, False))
        wd = self.waited[e]
        ww = {}
        for (sem, val, _inc, src), is_raw in deps:
            if src == e and not dma and e == "tensor":
                continue
            sid = id(sem)
            if wd.get(sid, 0) >= val:
                continue
            if sid not in ww or ww[sid][1] < val:
                ww[sid] = (sem, val)
        for sid, (sem, val) in ww.items():
            wd[sid] = val
        if dma:
            n = self.dma_count[e]
            if n >= DMA_ROT:
                sem = self.dma_sems[e][n % DMA_ROT]
                val = 16 * (n // DMA_ROT)
                if wd.get(id(sem), 0) < val:
                    wd[id(sem)] = val
                    ww[id(sem)] = (sem, val)
        tok = self._dma_token(e) if dma else self._compute_token(e)
        self.streams[e].append((list(ww.values()), fn, tok))
        for k in reads:
            self.readers.setdefault(k, []).append(tok)
        for k in writes:
            self.last_w[k] = tok
            self.readers[k] = []
        return tok

    def wait_tokens(self, e, toks):
        wd = self.waited[e]
        waits = []
        for (sem, val, _i, _s) in toks:
            if wd.get(id(sem), 0) >= val:
                continue
            wd[id(sem)] = val
            waits.append((sem, val))
        self.streams[e].append((waits, None, None))

    def emit(self):
        nc = self.nc
        with nc.Block() as block:
            def mk(e):
                def body(eng):
                    for waits, fn, tok in self.streams[e]:
                        if fn is None or not FUSE_WAIT or not waits:
                            for sem, val in waits:
                                eng.wait_ge(sem, val)
                            if fn is not None:
                                fn(eng).then_inc(tok[0], tok[2])
                        else:
                            for sem, val in waits[:-1]:
                                eng.wait_ge(sem, val)
                            ins = fn(eng)
                            ins._wait_ge(waits[-1][0], waits[-1][1])
                            ins.then_inc(tok[0], tok[2])
                return body
            for e in self.ENGS:
                if self.streams[e]:
                    getattr(block, e)(mk(e))


class V:
    __slots__ = ("ap", "keys")

    def __init__(self, ap, keys):
        self.ap = ap
        self.keys = keys


class Buf:
    def __init__(self, t, name):
        self.t = t
        self.name = name

    def v(self, idx=None, sub=None):
        ap = self.t[idx] if idx is not None else self.t[:]
        return V(ap, [(self.name, sub)])

    def w(self, ap, sub=None):
        return V(ap, [(self.name, sub)])


class G:
    def __init__(self, nc, st):
        self.nc = nc
        self.st = st
        self.kb = KB(nc, st)
        self.banks = [Buf(st.enter_context(nc.psum_tensor(f"psb{i}", [128, 512], F32)), f"ps{i}") for i in range(8)]
        self.bank_i = 0
        self.out_tokens = []

    def sb(self, name, shape, dt):
        return Buf(self.st.enter_context(self.nc.sbuf_tensor("sb_" + name, shape, dt)), name)

    def bank(self):
        b = self.banks[self.bank_i % 8]
        self.bank_i += 1
        return b

    @staticmethod
    def _k(vs):
        ks = []
        for v in vs:
            if isinstance(v, V):
                ks.extend(v.keys)
        return ks

    def mm(self, out, lhsT, rhs, start=True, stop=True):
        return self.kb.op("tensor", lambda e: e.matmul(out.ap, lhsT=lhsT.ap, rhs=rhs.ap, start=start, stop=stop),
                          reads=self._k([lhsT, rhs]), writes=out.keys)

    def tr(self, out, in_, ident):
        return self.kb.op("tensor", lambda e: e.transpose(out=out.ap, in_=in_.ap, identity=ident.ap),
                          reads=self._k([in_, ident]), writes=out.keys)

    def act(self, out, in_, func, bias=None, scale=1.0, accum=None, eng="scalar"):
        kw = {}
        if bias is not None:
            kw["bias"] = bias.ap if isinstance(bias, V) else bias
        if accum is not None:
            kw["accum_out"] = accum.ap
        sc = scale.ap if isinstance(scale, V) else scale
        return self.kb.op("scalar", lambda e: e.activation(out=out.ap, in_=in_.ap, func=func, scale=sc, **kw),
                          reads=self._k([in_, bias, scale]), writes=self._k([out, accum]))

    def tt(self, eng, out, a, b, op):
        return self.kb.op(eng, lambda e: e.tensor_tensor(out=out.ap, in0=a.ap, in1=b.ap, op=op),
                          reads=self._k([a, b]), writes=out.keys)

    def ts(self, eng, out, a, s1, op0, s2=None, op1=None):
        s1a = s1.ap if isinstance(s1, V) else s1
        s2a = s2.ap if isinstance(s2, V) else s2
        if op1 is None:
            fn = lambda e: e.tensor_scalar(out=out.ap, in0=a.ap, scalar1=s1a, scalar2=None, op0=op0)
        else:
            fn = lambda e: e.tensor_scalar(out=out.ap, in0=a.ap, scalar1=s1a, scalar2=s2a, op0=op0, op1=op1)
        return self.kb.op(eng, fn, reads=self._k([a, s1, s2]), writes=out.keys)

    def stt(self, eng, out, in0, scalar, in1, op0, op1):
        sa = scalar.ap if isinstance(scalar, V) else scalar
        return self.kb.op(eng, lambda e: e.scalar_tensor_tensor(out=out.ap, in0=in0.ap, scalar=sa, in1=in1.ap, op0=op0, op1=op1),
                          reads=self._k([in0, scalar, in1]), writes=out.keys)

    def copy(self, eng, out, in_):
        if eng == "scalar":
            return self.act(out, in_, AF.Copy)
        return self.kb.op(eng, lambda e: e.tensor_copy(out=out.ap, in_=in_.ap), reads=in_.keys, writes=out.keys)

    def memset(self, eng, out, val):
        return self.kb.op(eng, lambda e: e.memset(out.ap, val), writes=out.keys)

    def recip(self, out, in_):
        return self.kb.op("vector", lambda e: e.reciprocal(out=out.ap, in_=in_.ap), reads=in_.keys, writes=out.keys)

    def scan(self, out, d0, d1, init, op0, op1):
        return self.kb.op("vector", lambda e: e.tensor_tensor_scan(out=out.ap, data0=d0.ap, data1=d1.ap, initial=init, op0=op0, op1=op1),
                          reads=self._k([d0, d1]), writes=out.keys)

    def aselect(self, out, in_, pattern, cmp, fill, base, cm):
        return self.kb.op("gpsimd", lambda e: e.affine_select(out=out.ap, in_=in_.ap, pattern=pattern, compare_op=cmp,
                                                               fill=fill, base=base, channel_multiplier=cm),
                          reads=in_.keys, writes=out.keys)

    def dma_in(self, eng, out, in_ap, **kw):
        return self.kb.op(eng, lambda e: e.dma_start(out=out.ap, in_=in_ap, **kw), writes=out.keys, dma=True)

    def dma_out(self, eng, out_ap, in_, final=True):
        t = self.kb.op(eng, lambda e: e.dma_start(out=out_ap, in_=in_.ap), reads=in_.keys, dma=True)
        if final:
            self.out_tokens.append(t)
        return t

    def finish(self):
        self.kb.wait_tokens("sync", self.out_tokens)
        self.kb.emit()


S_LEN = 8192
D = 1024
TB = 512
NCH = TB // 128
GN_EPS = 64e-5
RMS_EPS = 1e-5
PROJ = [("r", 0, 256, 0), ("k", 2, 256, 256), ("v", 3, 256, 512), ("w1", 1, 64, 768), ("a1", 4, 64, 832), ("g1", 5, 160, 896)]
NCOL = 1056


BACK_W = 4


def build_phase1(n_tok=S_LEN, stop=None, stopargs=()):
    nc = bass.Bass("TRN2", target_bir_lowering=False)
    dr = lambda n, s, d=F32: nc.dram_tensor(n, s, d, kind="ExternalInput").ap()
    x_d = dr("x", [n_tok, D])
    gm_d = dr("gm", [128, 7, 8])
    wcat_d = dr("wcat", [D, NCOL])
    w2_d = dr("w2", [64, 256])
    a2_d = dr("a2", [64, 256])
    g2_d = dr("g2", [160, 256])
    vec_d = dr("vec", [128, 7, 2])
    yg_d = nc.dram_tensor("yg", [256, n_tok], BF16, kind="ExternalOutput").ap()
    dbg_d = nc.dram_tensor("dbg", [128, 2560], F32, kind="ExternalOutput").ap() if stop else None
    nblk = n_tok // TB

    class Stop(Exception):
        pass

    with ExitStack() as st:
        g = G(nc, st)
        sb = g.sb
        dbgt = sb("dbgt", [128, 2560], F32) if stop else None
        dbg_off = [0]

        def dump(v, n):
            o = dbg_off[0]
            p = v.ap.shape[0]
            g.copy("vector", dbgt.w(dbgt.t[0:p, o:o + n]), v)
            dbg_off[0] = o + n

        def chk(name):
            if stop == name:
                raise Stop()
        def emit_all():
            identf = sb("identf", [128, 128], F32)
            ident = sb("ident", [128, 128], BF16)
            ident4 = sb("ident4", [128, 4, 128], BF16)
            onesblk = sb("onesblk", [128, 128], F32)
            m_su = sb("m_su", [128, 4, 128], BF16)
            m_u = sb("m_u", [128, 4, 128], BF16)
            m_sl = sb("m_sl", [128, 4, 128], BF16)
            rmask = sb("rmask", [128, TB], F32)
            g.memset("gpsimd", identf.v(), 0.0)
            g.aselect(identf.v(), identf.v(), [[-1, 128]], ALU.not_equal, 1.0, 0, 1)
            g.copy("vector", ident.v(), identf.v())
            for h in range(4):
                g.copy("vector", ident4.v(np.s_[:, h, :]), identf.v())
            g.memset("gpsimd", onesblk.v(), 0.0)
            g.memset("gpsimd", onesblk.v(np.s_[0:64, 0:64]), 1.0)
            g.memset("gpsimd", onesblk.v(np.s_[64:128, 64:128]), 1.0)
            for m, cm, pat, cmp in ((m_su, -1, 1, ALU.is_gt), (m_u, -1, 1, ALU.is_ge), (m_sl, 1, -1, ALU.is_gt)):
                g.memset("gpsimd", m.v(), 1.0)
                g.aselect(m.v(), m.v(), [[0, 4], [pat, 128]], cmp, 0.0, 0, cm)
            g.memset("gpsimd", rmask.v(), 1.0)
            g.memset("gpsimd", rmask.w(rmask.t[:].rearrange("p (c t) -> p c t", t=128)[:, :, 0:1]), 0.0)

            if stop == "const":
                dump(identf.v(), 128); dump(m_su.v(np.s_[:, 1, :]), 128); dump(m_sl.v(np.s_[:, 2, :]), 128); dump(m_u.v(np.s_[:, 3, :]), 128)
                dump(onesblk.v(), 128); dump(rmask.v(), 512)
            chk("const")
            gm = sb("gm", [128, 7, 8], F32)
            coefA = sb("coefA", [128, 6, 8], F32)
            coefB = sb("coefB", [128, 6, 8], F32)
            vec = sb("vec", [128, 7, 2], F32)
            WA = sb("WA", [128, 8, NCOL], BF16)
            WB = sb("WB", [128, 8, NCOL], BF16)
            w2b = sb("w2b", [64, 256], BF16)
            a2b = sb("a2b", [128, 256], BF16)
            g2a = sb("g2a", [128, 256], BF16)
            g2b = sb("g2b", [32, 256], BF16)
            xin = sb("xin", [128, 4, 1024], F32)
            g.dma_in("sync", gm.v(), gm_d[:, :, :])
            g.dma_in("sync", vec.v(), vec_d[:, :, :])
            g.dma_in("gpsimd", w2b.v(), w2_d[:, :])
            g.dma_in("gpsimd", a2b.v(np.s_[64:128, :]), a2_d[:, :])
            g.dma_in("gpsimd", g2a.v(), g2_d[0:128, :])
            g.dma_in("gpsimd", g2b.v(), g2_d[128:160, :])
            g0b = gm.w(gm.t[:, 0:1, :].to_broadcast([128, 6, 8]))
            g.tt("vector", coefB.v(), gm.v(np.s_[:, 1:7, :]), g0b, ALU.mult)
            g.tt("vector", coefA.v(), g0b, coefB.v(), ALU.subtract)
            stage = xin
            wv = wcat_d.rearrange("(kc p) n -> p kc n", p=128)
            XK = [("xin", j) for j in range(4)]
            for (nm, mi, ncol, off) in PROJ:
                sv = stage.t[:].rearrange("p a b -> p (a b)")[:, 0:8 * ncol].rearrange("p (k n) -> p k n", k=8)
                g.kb.op("sync", lambda e, sv=sv, off=off, ncol=ncol: e.dma_start(out=sv, in_=wv[:, :, off:off + ncol]), writes=XK, dma=True)
                ca = coefA.w(coefA.t[:, mi, :].unsqueeze(2).to_broadcast([128, 8, ncol]))
                cb = coefB.w(coefB.t[:, mi, :].unsqueeze(2).to_broadcast([128, 8, ncol]))
                g.tt("vector", WA.v(np.s_[:, :, off:off + ncol]), V(sv, XK), ca, ALU.mult)
                g.tt("gpsimd", WB.v(np.s_[:, :, off:off + ncol]), V(sv, XK), cb, ALU.mult)

            if stop == "weights":
                dump(WA.v(np.s_[:, 3, 0:512]), 512); dump(WB.v(np.s_[:, 7, 544:1056]), 512); dump(coefA.v(np.s_[:, 2, :]), 8)
            chk("weights")
            ms = sb("ms", [128, 4], F32)
            xn = [sb(f"xn{i}", [128, 1024], BF16) for i in range(2)]
            hT = sb("hT", [128, 8, TB + 1], BF16)
            T = {n: sb("t_" + n, [128, TB], F32) for n in
                 ("rT", "kraw", "aT", "sgw", "cs", "Er", "En", "kk", "kp", "beta", "t1", "t2", "EC")}
            junk = Buf(T["EC"].t[:].bitcast(BF16), "t_EC")
            tw = sb("tw", [128, TB], BF16)
            sg1a = sb("sg1a", [128, TB], BF16)
            sg1b = sb("sg1b", [32, TB], BF16)
            vT = sb("vT", [128, 2, TB], F32)
            gT2 = [sb(f"gT{i}", [128, 2, TB], F32) for i in range(2)]
            bonusT2 = [sb(f"bonusT{i}", [128, 2, TB], F32) for i in range(2)]
            PCs2 = [sb(f"PCs{i}", [128, 2, NCH], F32) for i in range(2)]
            rt2 = [sb(f"rt_{i}", [128, 2, TB], BF16) for i in range(2)]
            kt2 = [sb(f"kt_{i}", [128, 2, TB], BF16) for i in range(2)]
            at2 = [sb(f"at_{i}", [128, 2, TB], BF16) for i in range(2)]
            bt2 = [sb(f"bt_{i}", [128, 2, TB], BF16) for i in range(2)]
            T2 = {n: sb("t2_" + n, [128, TB], F32) for n in ("o1", "o2")}
            khT = sb("khT", [128, 2, TB], BF16)
            bhT = sb("bhT", [128, 2, TB], BF16)
            XR2 = [sb(f"XR{i}", [128, NCH, 4, 128], BF16) for i in range(2)]
            Kpad2 = [sb(f"Kpad{i}", [128, NCH, 4, 128], BF16) for i in range(2)]
            Bpad2 = [sb(f"Bpad{i}", [128, NCH, 4, 128], BF16) for i in range(2)]
            Vpad2 = [sb(f"Vpad{i}", [128, NCH, 4, 128], BF16) for i in range(2)]
            NSL = 2
            Pb = [[sb(f"P{s}{i}", [128, 4, 128], BF16) for i in range(2)] for s in range(NSL)]
            SUt = [sb(f"SU{s}", [128, 2, 4, 128], BF16) for s in range(NSL)]
            UUt = [sb(f"UU{s}", [128, 2, 4, 128], BF16) for s in range(NSL)]
            PTb = [[Buf(SUt[s].t[:, 0], f"PT{s}0"), sb(f"PT{s}1", [128, 4, 128], BF16)] for s in range(NSL)]
            NTb = [[sb(f"NT{s}{i}", [128, 4, 128], BF16) for i in range(2)] for s in range(NSL)]
            AakT = [Buf(SUt[s].t[:, 1], f"AakT{s}") for s in range(NSL)]
            ArbT = [Buf(UUt[s].t[:, 0], f"ArbT{s}") for s in range(NSL)]
            ArkT = [Buf(UUt[s].t[:, 1], f"ArkT{s}") for s in range(NSL)]
            Apad = [sb(f"Apad{s}", [128, 4, 128], BF16) for s in range(NSL)]
            Wpad = [sb(f"Wpad{s}", [128, 4, 128], BF16) for s in range(NSL)]
            TTbd = [sb(f"TTbd{s}", [128, 2, 128], BF16) for s in range(NSL)]
            RhT = [sb(f"RhT{s}", [128, 2, 128], BF16) for s in range(NSL)]
            Sbd = [sb(f"Sbd{i}", [128, 2, 128], BF16) for i in range(2)]
            yraw = sb("yraw", [128, 2, TB], F32)
            ygo = sb("ygo", [128, 2, TB], BF16)
            for b_ in Kpad2 + Bpad2 + Vpad2:
                g.memset("gpsimd", b_.v(), 0.0)
            for s in range(NSL):
                g.memset("gpsimd", Apad[s].v(), 0.0)
                g.memset("gpsimd", Wpad[s].v(), 0.0)
            g.memset("gpsimd", Sbd[0].v(), 0.0)
            g.memset("vector", hT.v(np.s_[:, :, 0:1]), 0.0)
            s_cur = 0

            def bf(bank):
                return bank.t[:].bitcast(BF16)

            def hsl(h, j):
                cc, q = divmod(h, 2)
                return np.s_[q * 64:(q + 1) * 64, cc, j * 128:(j + 1) * 128]

            def load_x(blk):
                t0 = blk * TB
                for j in range(4):
                    g.dma_in("sync", xin.v(np.s_[:, j, :], sub=j), x_d[t0 + j * 128:t0 + (j + 1) * 128, :])

            def front(blk):
                F = blk % 2
                rt_, kt_, at_, bt_ = rt2[F], kt2[F], at2[F], bt2[F]
                XR, Kpad, Bpad, Vpad, PCs, gT, bonusT = XR2[F], Kpad2[F], Bpad2[F], Vpad2[F], PCs2[F], gT2[F], bonusT2[F]
                for j in range(4):
                    g.act(junk.v(), xin.v(np.s_[:, j, :], sub=j), AF.Square, scale=1.0 / 32, accum=ms.v(np.s_[:, j:j + 1]))
                g.ts("vector", ms.v(), ms.v(), RMS_EPS, ALU.add)
                g.act(ms.v(), ms.v(), AF.Sqrt)
                g.recip(ms.v(), ms.v())
                if blk > 0:
                    g.copy("vector", hT.v(np.s_[:, :, 0:1]), hT.v(np.s_[:, :, TB:TB + 1]))
                yield
                for j in range(4):
                    xj = xn[j % 2]
                    g.act(xj.v(), xin.v(np.s_[:, j, :], sub=j), AF.Copy, scale=ms.v(np.s_[:, j:j + 1]))
                    bk = g.bank()
                    bv = bf(bk)
                    for kc in range(8):
                        g.tr(bk.w(bv[:, kc * 128:(kc + 1) * 128]), xj.v(np.s_[:, kc * 128:(kc + 1) * 128]), ident.v())
                    g.copy("vector" if j % 2 == 0 else "scalar", hT.v(np.s_[:, :, 1 + 128 * j:1 + 128 * (j + 1)]),
                           bk.w(bv.rearrange("p (k t) -> p k t", k=8)))
                    yield
                if blk + 1 < nblk:
                    load_x(blk + 1)

                def proj_fm(off, ncol):
                    bk = g.bank()
                    for kc in range(8):
                        g.mm(bk.v(np.s_[0:ncol, :]), WA.v(np.s_[:, kc, off:off + ncol]), hT.v(np.s_[:, kc, 1:TB + 1]),
                             start=(kc == 0), stop=False)
                        g.mm(bk.v(np.s_[0:ncol, :]), WB.v(np.s_[:, kc, off:off + ncol]), hT.v(np.s_[:, kc, 0:TB]),
                             start=False, stop=(kc == 7))
                    return bk

                bk = proj_fm(768, 128)
                g.act(tw.v(np.s_[0:64, :]), bk.v(np.s_[0:64, :]), AF.Tanh)
                g.copy("vector", tw.v(np.s_[64:128, :]), bk.v(np.s_[64:128, :]))
                yield
                bk = proj_fm(896, 128)
                g.act(sg1a.v(), bk.v(), AF.Sigmoid)
                yield
                bk = proj_fm(1024, 32)
                g.act(sg1b.v(), bk.v(np.s_[0:32, :]), AF.Sigmoid)
                yield
                for j in range(4):
                    bk = g.bank()
                    for kc in range(8):
                        g.mm(bk.v(np.s_[:, 0:256]), hT.v(np.s_[:, kc, 1 + 128 * j:1 + 128 * (j + 1)]), WA.v(np.s_[:, kc, 512:768]),
                             start=(kc == 0), stop=False)
                        g.mm(bk.v(np.s_[:, 0:256]), hT.v(np.s_[:, kc, 128 * j:128 * (j + 1)]), WB.v(np.s_[:, kc, 512:768]),
                             start=False, stop=(kc == 7))
                    pv = bk.t[:, 0:256].rearrange("p (h c) -> p h c", h=4)
                    g.copy("vector", Vpad.v(np.s_[:, j, 0::2, 0:64]), bk.w(pv[:, 0::2, :]))
                    g.copy("scalar", Vpad.v(np.s_[:, j, 1::2, 64:128]), bk.w(pv[:, 1::2, :]))
                    yield

                for cc in range(2):
                    bk = proj_fm(0 + cc * 128, 128)
                    g.copy("scalar", T["rT"].v(), bk.v())
                    yield
                    bk = proj_fm(256 + cc * 128, 128)
                    g.copy("vector", T["kraw"].v(), bk.v())
                    yield
                    bk = proj_fm(512 + cc * 128, 128)
                    g.copy("scalar", vT.v(np.s_[:, cc, :]), bk.v())
                    yield
                    bk = g.bank()
                    g.mm(bk.v(), w2b.v(np.s_[0:64, cc * 128:(cc + 1) * 128]), tw.v(np.s_[0:64, :]))
                    g.act(T["sgw"].v(), bk.v(), AF.Sigmoid, bias=vec.v(np.s_[:, 0, cc:cc + 1]))
                    bk = g.bank()
                    g.mm(bk.v(), a2b.v(np.s_[64:128, cc * 128:(cc + 1) * 128]), tw.v(np.s_[64:128, :]))
                    g.act(T["aT"].v(), bk.v(), AF.Sigmoid, bias=vec.v(np.s_[:, 1, cc:cc + 1]))
                    bk = g.bank()
                    g.mm(bk.v(), g2a.v(np.s_[:, cc * 128:(cc + 1) * 128]), sg1a.v(), start=True, stop=False)
                    g.mm(bk.v(), g2b.v(np.s_[0:32, cc * 128:(cc + 1) * 128]), sg1b.v(np.s_[0:32, :]), start=False, stop=True)
                    g.copy("vector", gT.v(np.s_[:, cc, :]), bk.v())
                    yield

                    kk, kp, beta, t1, t2 = T["kk"], T["kp"], T["beta"], T["t1"], T["t2"]
                    Er, En, EC, cs = T["Er"], T["En"], T["EC"], T["cs"]
                    g.ts("gpsimd", kk.v(), T["kraw"].v(), vec.v(np.s_[:, 2, cc:cc + 1]), ALU.mult)
                    g.tt("gpsimd", t1.v(), kk.v(), kk.v(), ALU.mult)
                    bk = g.bank()
                    g.mm(bk.v(), onesblk.v(), t1.v())
                    g.scan(cs.v(), rmask.v(), T["sgw"].v(), 0.0, ALU.mult, ALU.add)
                    yield
                    g.ts("vector", t2.v(), bk.v(), 1e-24, ALU.max)
                    g.act(t2.v(), t2.v(), AF.Sqrt)
                    g.act(Er.v(), cs.v(), AF.Exp, scale=NEG_C)
                    g.act(En.v(), cs.v(), AF.Exp, scale=-NEG_C)
                    g.recip(t2.v(), t2.v())
                    yield
                    g.tt("gpsimd", kk.v(), kk.v(), t2.v(), ALU.mult)
                    g.ts("vector", t1.v(), T["aT"].v(), -1.0, ALU.add, vec.v(np.s_[:, 3, cc:cc + 1]), ALU.mult)
                    g.tt("gpsimd", beta.v(), kk.v(), T["aT"].v(), ALU.mult)
                    g.stt("vector", kp.v(), t1.v(), 1.0, T["kraw"].v(), ALU.add, ALU.mult)
                    yield
                    Er3 = Er.t[:].rearrange("p (c t) -> p c t", t=128)
                    g.copy("vector", PCs.v(np.s_[:, cc, :]), Er.w(Er3[:, :, 127]))
                    g.tt("vector", EC.w(EC.t[:].rearrange("p (c t) -> p c t", t=128)),
                         En.w(En.t[:].rearrange("p (c t) -> p c t", t=128)),
                         Er.w(Er3[:, :, 127:128].to_broadcast([128, NCH, 128])), ALU.mult)
                    g.tt("gpsimd", kt_.v(np.s_[:, cc, :]), kp.v(), En.v(), ALU.mult)
                    g.tt("gpsimd", rt_.v(np.s_[:, cc, :]), T["rT"].v(), Er.v(), ALU.mult)
                    yield
                    g.tt("gpsimd", bt_.v(np.s_[:, cc, :]), beta.v(), En.v(), ALU.mult)
                    g.tt("gpsimd", khT.v(np.s_[:, cc, :]), kp.v(), EC.v(), ALU.mult)
                    g.tt("gpsimd", bhT.v(np.s_[:, cc, :]), beta.v(), EC.v(), ALU.mult)
                    kk3 = kk.t[:].rearrange("p (c t) -> p c t", t=128)
                    at3 = at_.t[:, cc, :].rearrange("p (c t) -> p c t", t=128)
                    g.stt("vector", at_.w(at3[:, :, 1:128]), kk.w(kk3[:, :, 1:128]), -1.0, Er.w(Er3[:, :, 0:127]), ALU.mult, ALU.mult)
                    g.ts("vector", at_.w(at3[:, :, 0:1]), kk.w(kk3[:, :, 0:1]), -1.0, ALU.mult)
                    yield
                    g.stt("vector", t1.v(), T["rT"].v(), vec.v(np.s_[:, 4, cc:cc + 1]), kp.v(), ALU.mult, ALU.mult)
                    bk = g.bank()
                    g.mm(bk.v(), onesblk.v(), t1.v())
                    g.tt("vector", bonusT.v(np.s_[:, cc, :]), vT.v(np.s_[:, cc, :]), bk.v(), ALU.mult)
                    yield

                for (src, kind) in ((at_, "A"), (khT, "K"), (bhT, "B")):
                    bk = g.bank()
                    bv = bf(bk)
                    for j in range(NCH):
                        for cc in range(2):
                            sl = (j * 2 + cc) * 128
                            g.tr(bk.w(bv[:, sl:sl + 128]), src.v(np.s_[:, cc, j * 128:(j + 1) * 128]), ident.v())
                    if kind == "A":
                        g.copy("vector", XR.v(np.s_[:, :, :, 0:64]),
                               bk.w(bv.rearrange("p (j h c) -> p j h c", j=NCH, h=4)))
                    else:
                        dst = Kpad if kind == "K" else Bpad
                        b5 = bv.rearrange("p (j c q d) -> p j c q d", j=NCH, c=2, q=2)
                        g.copy("vector", dst.v(np.s_[:, :, 0::2, 0:64]), bk.w(b5[:, :, :, 0, :]))
                        g.copy("scalar", dst.v(np.s_[:, :, 1::2, 64:128]), bk.w(b5[:, :, :, 1, :]))
                    yield

            def back(blk):
                nonlocal s_cur
                t0 = blk * TB
                F = blk % 2
                rt_, kt_, at_, bt_ = rt2[F], kt2[F], at2[F], bt2[F]
                XR, Kpad, Bpad, Vpad, PCs, gT, bonusT = XR2[F], Kpad2[F], Bpad2[F], Vpad2[F], PCs2[F], gT2[F], bonusT2[F]

                def pre_a(j, s):
                    def grp(specs, msk, dsts_t, dkeys):
                        bks = [g.bank(), g.bank()]
                        for si, (lh, rh) in enumerate(specs):
                            for h in range(4):
                                cc, q = divmod(h, 2)
                                sl = (si * 2 + cc) * 128
                                g.mm(bks[q].v(np.s_[:, sl:sl + 128]), lh.v(hsl(h, j)), rh.v(hsl(h, j)))
                        n = len(specs)
                        for q in range(2):
                            if n == 2:
                                src = bks[q].w(bks[q].t[:].rearrange("p (a c t) -> p a c t", a=2, c=2))
                                dst = V(dsts_t[:, :, q::2, :], dkeys)
                                mk = msk.w(msk.t[:].rearrange("p (a c) t -> p a c t", a=2))
                            else:
                                src = bks[q].w(bks[q].t[:, 0:256].rearrange("p (c t) -> p c t", c=2))
                                dst = V(dsts_t[:, q::2, :], dkeys)
                                mk = msk.v(np.s_[:, 0:2, :])
                            g.tt("vector", dst, src, mk, ALU.mult)
                    grp([(bt_, at_), (kt_, at_)], m_su, SUt[s].t, [(PTb[s][0].name, None), (AakT[s].name, None)])
                    yield
                    grp([(bt_, rt_), (kt_, rt_)], m_u, UUt[s].t, [(ArbT[s].name, None), (ArkT[s].name, None)])
                    yield
                    grp([(at_, bt_)], m_sl, Pb[s][0].t, [(Pb[s][0].name, None)])
                    g.tt("gpsimd", NTb[s][0].v(), PTb[s][0].v(), ident4.v(), ALU.add)
                    yield

                def pre_dbl(s, it):
                    cur, nxt = it % 2, (it + 1) % 2
                    bk = g.bank()
                    for h in range(4):
                        g.mm(bk.v(np.s_[:, h * 128:(h + 1) * 128]), PTb[s][cur].v(np.s_[:, h, :]), Pb[s][cur].v(np.s_[:, h, :]))
                    g.copy("scalar", Pb[s][nxt].v(), bk.w(bk.t[:].rearrange("p (h t) -> p h t", h=4)))
                    if it < 5:
                        bk = g.bank()
                        for h in range(4):
                            g.mm(bk.v(np.s_[:, h * 128:(h + 1) * 128]), Pb[s][cur].v(np.s_[:, h, :]), PTb[s][cur].v(np.s_[:, h, :]))
                        g.copy("vector", PTb[s][nxt].v(), bk.w(bk.t[:].rearrange("p (h t) -> p h t", h=4)))
                    yield
                    bk = g.bank()
                    for h in range(4):
                        o = bk.v(np.s_[:, h * 128:(h + 1) * 128])
                        g.mm(o, ident.v(), NTb[s][cur].v(np.s_[:, h, :]), start=True, stop=False)
                        g.mm(o, Pb[s][nxt].v(np.s_[:, h, :]), NTb[s][cur].v(np.s_[:, h, :]), start=False, stop=True)
                    eng = "scalar" if it % 2 == 0 else "vector"
                    g.copy(eng, NTb[s][nxt].v(), bk.w(bk.t[:].rearrange("p (h t) -> p h t", h=4)))
                    yield

                def pre_b(j, s):
                    NTf = NTb[s][0]
                    bk = g.bank()
                    for h in range(4):
                        q = h % 2
                        g.mm(bk.v(np.s_[:, h * 64:(h + 1) * 64]), AakT[s].v(np.s_[:, h, :]), Vpad.v(np.s_[:, j, h, q * 64:(q + 1) * 64]))
                    g.copy("vector", XR.v(np.s_[:, j, :, 64:128]), bk.w(bk.t[:, 0:256].rearrange("p (h c) -> p h c", h=4)))
                    yield
                    bk = g.bank()
                    for h in range(4):
                        g.mm(bk.v(np.s_[:, h * 128:(h + 1) * 128]), NTf.v(np.s_[:, h, :]), XR.v(np.s_[:, j, h, :]))
                    x4 = bk.t[:].rearrange("p (h c) -> p h c", h=4)
                    g.copy("vector", Apad[s].v(np.s_[:, 0::2, 0:64]), bk.w(x4[:, 0::2, 0:64]))
                    g.copy("scalar", Apad[s].v(np.s_[:, 1::2, 64:128]), bk.w(x4[:, 1::2, 0:64]))
                    g.copy("vector", Wpad[s].v(np.s_[:, 0::2, 0:64]), bk.w(x4[:, 0::2, 64:128]))
                    g.copy("scalar", Wpad[s].v(np.s_[:, 1::2, 64:128]), bk.w(x4[:, 1::2, 64:128]))
                    yield
                    bk = g.bank()
                    for cc in range(2):
                        o = bk.v(np.s_[:, cc * 128:(cc + 1) * 128])
                        for q in range(2):
                            h = 2 * cc + q
                            g.mm(o, Apad[s].v(np.s_[:, h, :]), Bpad.v(np.s_[:, j, h, :]), start=(q == 0), stop=(q == 1))
                    for cc in range(2):
                        g.stt("vector", TTbd[s].v(np.s_[:, cc, :]), identf.v(), PCs.v(np.s_[:, cc, j:j + 1]),
                              bk.v(np.s_[:, cc * 128:(cc + 1) * 128]), ALU.mult, ALU.add)
                    bk = g.bank()
                    for cc in range(2):
                        o = bk.v(np.s_[:, cc * 128:(cc + 1) * 128])
                        for q in range(2):
                            h = 2 * cc + q
                            g.mm(o, Apad[s].v(np.s_[:, h, :]), ArbT[s].v(np.s_[:, h, :]), start=(q == 0), stop=(q == 1))
                    g.tt("vector", RhT[s].v(), bk.w(bk.t[:, 0:256].rearrange("p (c t) -> p c t", c=2)),
                         rt_.v(np.s_[:, :, j * 128:(j + 1) * 128]), ALU.add)
                    yield

                def seq(j, s):
                    nonlocal s_cur
                    Sc, Sn = Sbd[s_cur], Sbd[1 - s_cur]
                    bk = g.bank()
                    for cc in range(2):
                        o = bk.v(np.s_[:, cc * 128:(cc + 1) * 128])
                        g.mm(o, TTbd[s].v(np.s_[:, cc, :]), Sc.v(np.s_[:, cc, :]), start=True, stop=False)
                        for q in range(2):
                            h = 2 * cc + q
                            g.mm(o, Bpad.v(np.s_[:, j, h, :]), Wpad[s].v(np.s_[:, h, :]), start=False, stop=False)
                            g.mm(o, Kpad.v(np.s_[:, j, h, :]), Vpad.v(np.s_[:, j, h, :]), start=False, stop=(q == 1))
                    g.copy("vector", Sn.v(), bk.w(bk.t[:, 0:256].rearrange("p (c t) -> p c t", c=2)))
                    bk = g.bank()
                    for cc in range(2):
                        o = bk.v(np.s_[:, cc * 128:(cc + 1) * 128])
                        g.mm(o, Sc.v(np.s_[:, cc, :]), RhT[s].v(np.s_[:, cc, :]), start=True, stop=False)
                        for q in range(2):
                            h = 2 * cc + q
                            g.mm(o, Wpad[s].v(np.s_[:, h, :]), ArbT[s].v(np.s_[:, h, :]), start=False, stop=False)
                            g.mm(o, Vpad.v(np.s_[:, j, h, :]), ArkT[s].v(np.s_[:, h, :]), start=False, stop=(q == 1))
                    g.copy("scalar", yraw.v(np.s_[:, :, j * 128:(j + 1) * 128]), bk.w(bk.t[:, 0:256].rearrange("p (c t) -> p c t", c=2)))
                    s_cur = 1 - s_cur
                    yield

                def rr(gens):
                    gens = list(gens)
                    while gens:
                        for gn in list(gens):
                            try:
                                next(gn)
                            except StopIteration:
                                gens.remove(gn)
                            yield

                for jp in range(0, NCH, NSL):
                    yield from rr([pre_a(jp + s, s) for s in range(NSL)])
                    for it in range(6):
                        yield from rr([pre_dbl(s, it) for s in range(NSL)])
                    yield from rr([pre_b(jp + s, s) for s in range(NSL)])
                    for s in range(NSL):
                        yield from seq(jp + s, s)

                for cc in range(2):
                    t1, t2 = T2["o1"], T2["o2"]
                    yr = yraw.v(np.s_[:, cc, :])
                    bk = g.bank()
                    g.mm(bk.v(), onesblk.v(), yr)
                    g.stt("vector", t1.v(), bk.v(), -1.0 / 64, yr, ALU.mult, ALU.add)
                    g.tt("gpsimd", t2.v(), t1.v(), t1.v(), ALU.mult)
                    yield
                    bk = g.bank()
                    g.mm(bk.v(), onesblk.v(), t2.v())
                    g.ts("vector", t2.v(), bk.v(), 1.0 / 64, ALU.mult, GN_EPS, ALU.add)
                    g.act(t2.v(), t2.v(), AF.Sqrt)
                    g.recip(t2.v(), t2.v())
                    yield
                    g.tt("gpsimd", t1.v(), t1.v(), t2.v(), ALU.mult)
                    g.ts("vector", t1.v(), t1.v(), vec.v(np.s_[:, 5, cc:cc + 1]), ALU.mult, vec.v(np.s_[:, 6, cc:cc + 1]), ALU.add)
                    g.tt("gpsimd", t1.v(), t1.v(), bonusT.v(np.s_[:, cc, :]), ALU.add)
                    g.tt("vector", ygo.v(np.s_[:, cc, :]), t1.v(), gT.v(np.s_[:, cc, :]), ALU.mult)
                    g.dma_out("sync", yg_d[cc * 128:(cc + 1) * 128, t0:t0 + TB], ygo.v(np.s_[:, cc, :]))
                    yield

            def drive(gens, weights=None):
                gens = list(gens)
                weights = list(weights or [1] * len(gens))
                while gens:
                    for gn, w in list(zip(gens, weights)):
                        for _ in range(w):
                            try:
                                next(gn)
                            except StopIteration:
                                k = gens.index(gn)
                                gens.pop(k)
                                weights.pop(k)
                                break

            load_x(0)
            drive([front(0)])
            for blk in range(nblk):
                gs = [back(blk)]
                if blk + 1 < nblk:
                    gs.append(front(blk + 1))
                drive(gs, [BACK_W, 1])
        try:
            emit_all()
        except Stop:
            pass
        if stop:
            g.dma_out("sync", dbg_d[:, :], dbgt.v())
        g.finish()
    return nc


def phase1_inputs(inp, core):
    b, hg = divmod(core, 4)
    cs = slice(hg * 256, (hg + 1) * 256)
    pm = lambda v: np.ascontiguousarray(v.reshape(-1, 128).T)
    gm = np.stack([pm(inp["norm_mix_g"][0])] + [pm(inp["rwkv_mu"][0, i]) for i in range(6)], axis=1)
    wcat = np.concatenate([inp["rwkv_w_r"][0][:, cs], inp["rwkv_w_k"][0][:, cs], inp["rwkv_w_v"][0][:, cs],
                           inp["rwkv_w1"][0], inp["rwkv_a1"][0], inp["rwkv_g1"][0]], axis=1)
    vecs = [inp["rwkv_w0"][0][cs], inp["rwkv_a0"][0][cs], inp["rwkv_k_k"][0][cs], inp["rwkv_k_a"][0][cs],
            inp["rwkv_r_k"][0].reshape(-1)[cs], inp["rwkv_ln_w"][0][cs], inp["rwkv_ln_b"][0][cs]]
    vec = np.stack([pm(v) for v in vecs], axis=1)
    return {
        "x": np.ascontiguousarray(inp["x"][b]),
        "gm": np.ascontiguousarray(gm, dtype=np.float32),
        "wcat": np.ascontiguousarray(wcat, dtype=np.float32),
        "w2": np.ascontiguousarray(inp["rwkv_w2"][0][:, cs]),
        "a2": np.ascontiguousarray(inp["rwkv_a2"][0][:, cs]),
        "g2": np.ascontiguousarray(inp["rwkv_g2"][0][:, cs]),
        "vec": np.ascontiguousarray(vec, dtype=np.float32),
    }


NT2 = 17
NTOK2 = NT2 * 128
DFF = 4096
HG = 512
NGRP = DFF // HG
NEG_BIG = -30000.0
TWO_PI = 6.283185307179586
PI = 3.141592653589793
C1_2PI = 6.28125
C2_2PI = TWO_PI - 6.28125


def build_phase2():
    nc = bass.Bass("TRN2", target_bir_lowering=False)
    dr = lambda n, s, d=F32: nc.dram_tensor(n, s, d, kind="ExternalInput").ap()
    x_d = dr("x", [NTOK2, D])
    yg_d = dr("ygT", [D, NTOK2], BF16)
    pos_d = dr("pos", [1, NTOK2], I32)
    wo0_d = dr("wo0", [D, D])
    win_d = [dr("win0", [D, DFF]), dr("win1", [D, DFF])]
    wout_d = [dr("wout0", [DFF, D]), dr("wout1", [DFF, D])]
    wqkv_d = dr("wqkv", [D, 1280])
    wo1_d = dr("wo1", [D, D])
    gains_d = dr("gains", [128, 3, 8])
    bq_d = dr("bq", [128, 10])
    rowv_d = dr("rowv", [1, 1024 + 1024 + 128 + 16])
    cst_d = dr("cst", [128, 4])
    out_d = nc.dram_tensor("out", [16 * 128, D], F32, kind="ExternalOutput").ap()

    with ExitStack() as st:
        g = G(nc, st)
        sb = g.sb
        identf = sb("identf", [128, 128], F32)
        ident = sb("ident", [128, 128], BF16)
        g.memset("gpsimd", identf.v(), 0.0)
        g.aselect(identf.v(), identf.v(), [[-1, 128]], ALU.not_equal, 1.0, 0, 1)
        g.copy("vector", ident.v(), identf.v())
        rotf = sb("rotf", [128, 128], F32)
        rot = sb("rot", [128, 128], BF16)
        g.memset("gpsimd", rotf.v(), 0.0)
        for blk in range(2):
            o = blk * 64
            sub = rotf.v(np.s_[:, o:o + 32])
            g.aselect(sub, sub, [[-1, 32]], ALU.not_equal, -1.0, -(o + 32), 1)
            sub = rotf.v(np.s_[:, o + 32:o + 64])
            g.aselect(sub, sub, [[-1, 32]], ALU.not_equal, 1.0, -(o + 32) + 32, 1)
        g.copy("vector", rot.v(), rotf.v())
        mscr = sb("scr", [128, 1024], F32)
        maskb = Buf(mscr.t[:, 0:256], "scr_m0")
        mask1 = Buf(mscr.t[:, 256:512], "scr_m1")
        g.memset("gpsimd", maskb.v(), 0.0)
        g.aselect(maskb.v(), maskb.v(), [[1, 256]], ALU.is_gt, NEG_BIG, 0, -1)
        g.aselect(maskb.v(), maskb.v(), [[-1, 256]], ALU.is_ge, NEG_BIG, 128, 1)
        cst = sb("cst", [128, 4], F32)
        g.dma_in("sync", cst.v(), cst_d[:, :])
        g.copy("vector", mask1.v(), maskb.v())
        g.ts("vector", mask1.v(np.s_[:, 0:128]), maskb.v(np.s_[:, 0:128]), cst.v(np.s_[:, 2:3]), ALU.add)
        maskbf = sb("maskbf", [128, 256], BF16)
        mask1bf = sb("mask1bf", [128, 256], BF16)
        g.ts("vector", maskbf.v(), maskb.v(), 8.0, ALU.mult)
        g.ts("vector", mask1bf.v(), mask1.v(), 8.0, ALU.mult)
        ones_row = sb("ones_row", [1, 128], F32)
        g.memset("gpsimd", ones_row.v(), 1.0)
        gains = sb("gains", [128, 3, 8], F32)
        g.dma_in("sync", gains.v(), gains_d[:, :, :])
        bq = sb("bq", [128, 10], F32)
        g.dma_in("sync", bq.v(), bq_d[:, :])
        rowv = sb("rowv", [1, 1024], F32)
        g.dma_in("sync", rowv.v(), rowv_d[0:1, 1024:2048])
        bvb = sb("bvb", [128, 128], F32)
        sinkb = sb("sinkb", [128, 16], F32)
        g.dma_in("sync", bvb.v(), rowv_d[0:1, 2048:2176].to_broadcast([128, 128]))
        g.dma_in("sync", sinkb.v(), rowv_d[0:1, 2176:2192].to_broadcast([128, 16]))

        xres = sb("xres", [128, NT2, 1024], F32)
        hT = sb("hT", [128, 8, NTOK2], BF16)
        uT = sb("uT", [128, 4, NTOK2], BF16)
        arena = sb("arena", [128, 18432], BF16)
        junk = sb("junk", [128, 1024], BF16)
        ms = sb("ms", [128, NT2], F32)
        xn = [sb("xn0", [128, 1024], BF16)] * 2
        scr = mscr
        NU = 4
        asc = sb("asc", [128, NU, 2, 256], F32)
        relu_s = [Buf(asc.t[:, i].rearrange("p a b -> p (a b)").bitcast(BF16)[:, 0:512], f"asc_relu{i}") for i in range(2)]

        def bf(bank):
            return bank.t[:].bitcast(BF16)

        NTILES = [(0, 512), (512, 512), (1024, 512), (1536, 512), (2048, 128)]
        NTILES_M = {0: NTILES, 1: [(128, 512), (640, 512), (1152, 512), (1664, 512)]}

        def load_x_and_yg():
            for kc in range(8):
                g.dma_in("sync", hT.v(np.s_[:, kc, :]), yg_d[kc * 128:(kc + 1) * 128, :])
            for i in range(NT2):
                g.dma_in("sync", xres.v(np.s_[:, i, :], sub=i), x_d[i * 128:(i + 1) * 128, :])

        def wload(dst_ap, key, src_ap):
            g.kb.op("gpsimd", lambda e: e.dma_start(out=dst_ap, in_=src_ap), writes=[key], dma=True)

        def rstd_all(tiles):
            allk = [("ms", i) for i in tiles]
            lo, hi = tiles[0], tiles[-1] + 1
            for i in tiles:
                g.act(junk.v(), xres.v(np.s_[:, i, :], sub=i), AF.Square, scale=1.0 / 32, accum=ms.v(np.s_[:, i:i + 1], sub=i))
            mv = V(ms.t[:, lo:hi], allk)
            g.ts("vector", mv, mv, RMS_EPS, ALU.add)
            g.act(mv, mv, AF.Sqrt)
            g.recip(mv, mv)

        def norm_to_hT(gi, tiles):
            tiles = list(tiles)
            rstd_all(tiles)
            for i in tiles:
                xj = xn[i % 2]
                g.act(xj.v(), xres.v(np.s_[:, i, :], sub=i), AF.Copy, scale=ms.v(np.s_[:, i:i + 1], sub=i))
                bk = g.bank()
                bv = bf(bk)
                for kc in range(8):
                    g.tr(bk.w(bv[:, kc * 128:(kc + 1) * 128]), xj.v(np.s_[:, kc * 128:(kc + 1) * 128]), ident.v())
                g.tt("vector", hT.v(np.s_[:, :, i * 128:(i + 1) * 128]), bk.w(bv.rearrange("p (k t) -> p k t", k=8)),
                     gains.w(gains.t[:, gi, :].unsqueeze(2).to_broadcast([128, 8, 128])), ALU.mult)

        def mlp(layer, first_tile=0):
            win, wout = win_d[layer], wout_d[layer]
            ntiles = [(n0, nn) for (n0, nn) in NTILES_M[first_tile]]
            winv = win.rearrange("(kc p) n -> p kc n", p=128)
            woutv = wout.rearrange("(m p) n -> p m n", p=128)
            def WI(s):
                return arena.t[:, s * 4096:(s + 1) * 4096].rearrange("p (k n) -> p k n", k=8)

            def WO(s):
                return arena.t[:, 8192 + s * 4096:8192 + (s + 1) * 4096].rearrange("p (m n) -> p m n", m=4)

            ri = 0
            for grp in range(NGRP):
                s = grp % 2
                kwi, kwo = ("arena", "wi%d" % s), ("arena", "wo%d" % s)
                wload(WI(s), kwi, winv[:, :, grp * HG:(grp + 1) * HG])
                wload(WO(s), kwo, woutv[:, grp * 4:(grp + 1) * 4, :])
                for m in range(4):
                    for (n0, nn) in ntiles:
                        bk = g.bank()
                        for kc in range(8):
                            g.mm(bk.v(np.s_[:, 0:nn]), V(WI(s)[:, kc, m * 128:(m + 1) * 128], [kwi]), hT.v(np.s_[:, kc, n0:n0 + nn]),
                                 start=(kc == 0), stop=(kc == 7))
                        r = relu_s[ri % 2]
                        ri += 1
                        g.act(V(r.t[:, 0:nn], r.v().keys + [("asc", 0), ("asc", 1)]), bk.v(np.s_[:, 0:nn]), AF.Relu)
                        g.tt("gpsimd", uT.v(np.s_[:, m, n0:n0 + nn]), r.v(np.s_[:, 0:nn]), r.v(np.s_[:, 0:nn]), ALU.mult)
                for i in range(first_tile, NT2):
                    for half in range(2):
                        bk = g.bank()
                        for m in range(4):
                            g.mm(bk.v(), uT.v(np.s_[:, m, i * 128:(i + 1) * 128]), V(WO(s)[:, m, half * 512:(half + 1) * 512], [kwo]),
                                 start=(m == 0), stop=(m == 3))
                        xs = xres.v(np.s_[:, i, half * 512:(half + 1) * 512], sub=i)
                        g.tt("vector", xs, xs, bk.v(), ALU.add)

        load_x_and_yg()
        wo_v = arena.t[:, 0:8192].rearrange("p (k n) -> p k n", k=8)
        g.kb.op("gpsimd", lambda e: e.dma_start(out=wo_v, in_=wo0_d.rearrange("(kc p) n -> p kc n", p=128)),
                writes=[("arena", "wi0"), ("arena", "wi1")], dma=True)
        for i in range(NT2):
            for half in range(2):
                bk = g.bank()
                for kc in range(8):
                    g.mm(bk.v(), hT.v(np.s_[:, kc, i * 128:(i + 1) * 128]),
                         V(wo_v[:, kc, half * 512:(half + 1) * 512], [("arena", "wi0"), ("arena", "wi1")]),
                         start=(kc == 0), stop=(kc == 7))
                xs = xres.v(np.s_[:, i, half * 512:(half + 1) * 512], sub=i)
                g.tt("vector", xs, xs, bk.v(), ALU.add)

        norm_to_hT(0, range(NT2))
        mlp(0)

        norm_to_hT(1, range(NT2))
        wq_v = arena.t[:, 0:10240].rearrange("p (k n) -> p k n", k=8)
        wo1_v = arena.t[:, 10240:18432].rearrange("p (k n) -> p k n", k=8)
        KQ = [("arena", "wi0"), ("arena", "wi1"), ("arena", "wo0")]
        KO = [("arena", "wo0"), ("arena", "wo1"), ("arena", "x")]
        g.kb.op("gpsimd", lambda e: e.dma_start(out=wq_v, in_=wqkv_d.rearrange("(kc p) n -> p kc n", p=128)), writes=KQ, dma=True)
        g.kb.op("gpsimd", lambda e: e.dma_start(out=wo1_v, in_=wo1_d.rearrange("(kc p) n -> p kc n", p=128)), writes=KO, dma=True)
        tabs = uT.t[:].rearrange("p a b -> p (a b)").bitcast(F32)
        cosT = uT.w(tabs[:, 0:NTOK2])
        sinT = uT.w(tabs[:, NTOK2:2 * NTOK2])
        posi = V(scr.t[:, 0:512].bitcast(I32), [("scr", "a"), ("scr_m0", None), ("scr_m1", None)])
        for (n0, nn) in NTILES:
            pch = V(posi.ap[:, 0:nn], posi.keys)
            ach = scr.v(np.s_[:, 512:512 + nn], sub="b")
            g.dma_in("sync", pch, pos_d[0:1, n0:n0 + nn].to_broadcast([128, nn]))
            g.copy("vector", ach, pch)
            g.ts("vector", ach, ach, cst.v(np.s_[:, 0:1]), ALU.mult)
            sch = V(sinT.ap[:, n0:n0 + nn], sinT.keys)
            cch = V(cosT.ap[:, n0:n0 + nn], cosT.keys)
            T1 = V(xn[0].t[:].bitcast(F32)[:, 0:nn], [("xn0", None)])
            A2 = V(junk.t[:].bitcast(F32)[:, 0:nn], [("junk", None)])
            TI = pch
            for (src, dst, shift) in ((ach, sch, 0.0), (ach, cch, 0.5 * PI)):
                if shift:
                    g.ts("vector", A2, src, shift, ALU.add)
                    src = A2
                g.ts("vector", T1, src, 1.0 / TWO_PI, ALU.mult)
                g.copy("vector", TI, T1)
                g.copy("vector", T1, TI)
                g.stt("vector", dst, T1, -C1_2PI, src, ALU.mult, ALU.add)
                g.stt("vector", dst, T1, -C2_2PI, dst, ALU.mult, ALU.add)
                g.ts("vector", dst, dst, -PI, ALU.max, PI, ALU.min)
                g.act(dst, dst, AF.Sin)

        kr = sb("kr", [128, 2, NTOK2], BF16)
        vpad = [sb(f"vpad{i}", [128, 2, 2, 128], BF16) for i in range(3)]
        for v_ in vpad:
            g.memset("gpsimd", v_.v(), 0.0)
        qb16 = sb("qb16", [128, 512], BF16)
        SCALE = 0.125
        negsink = sb("negsink", [128, 16], F32)
        esink = sb("esink", [128, 16], F32)
        g.ts("vector", negsink.v(), sinkb.v(), -1.0, ALU.mult)
        g.act(esink.v(), sinkb.v(), AF.Exp)

        def rope_evac(bk, nn, bias_col, n0, dst, qf, qb, r2):
            g.act(qf, bk.v(np.s_[:, 0:nn]), AF.Identity, bias=bias_col)
            g.copy("gpsimd", qb, qf)
            b2 = g.bank()
            g.mm(b2.v(np.s_[:, 0:nn]), rot.v(), qb)
            g.tt("vector", r2, b2.v(np.s_[:, 0:nn]), V(sinT.ap[:, n0:n0 + nn], sinT.keys), ALU.mult)
            g.tt("gpsimd", qf, qf, V(cosT.ap[:, n0:n0 + nn], cosT.keys), ALU.mult)
            g.tt("vector", dst, qf, r2, ALU.add)

        wkd_ap = asc.t[:].rearrange("p a b c -> p (a b c)")[:, 0:1024].bitcast(BF16).rearrange("p (k j c) -> p k j c", k=8, j=2)
        WK = [("asc", u) for u in range(NU)] + [("asc_relu0", None), ("asc_relu1", None)]
        for j in range(2):
            for dup in range(2):
                g.copy("vector", V(wkd_ap[:, :, j, dup * 64:(dup + 1) * 64], WK), V(wq_v[:, :, 1024 + j * 64:1024 + (j + 1) * 64], KQ))
        for j in range(2):
            for (n0, nn) in NTILES:
                bk = g.bank()
                for kc in range(8):
                    g.mm(bk.v(np.s_[:, 0:nn]), V(wkd_ap[:, kc, j, :], WK), hT.v(np.s_[:, kc, n0:n0 + nn]), start=(kc == 0), stop=(kc == 7))
                rope_evac(bk, nn, bq.v(np.s_[:, 8 + j:9 + j]), n0, kr.v(np.s_[:, j, n0:n0 + nn]),
                          scr.v(np.s_[:, 0:nn], sub="a"), qb16.v(np.s_[:, 0:nn]), scr.v(np.s_[:, 512:512 + nn], sub="b"))

        def make_vpad(i):
            vp = vpad[i % 3]
            bk = g.bank()
            for kc in range(8):
                g.mm(bk.v(np.s_[:, 0:128]), hT.v(np.s_[:, kc, i * 128:(i + 1) * 128]), V(wq_v[:, kc, 1152:1280], KQ), start=(kc == 0), stop=(kc == 7))
            for q2 in range(2):
                g.tt("vector", vp.v(np.s_[:, :, q2, q2 * 64:(q2 + 1) * 64]), bk.w(bk.t[:, 0:128].rearrange("p (j d) -> p j d", j=2)),
                     bvb.w(bvb.t[:].rearrange("p (j d) -> p j d", j=2)), ALU.add)
            return vp

        qr2 = [sb(f"qr{i}", [128, 8, 128], BF16) for i in range(2)]
        oT = sb("oT", [128, 8, 128], BF16)
        pn = [sb(f"pn{i}", [128, 2, 256], BF16) for i in range(NU)]
        pT = [sb(f"pT{i}", [128, 2, 2, 128], BF16) for i in range(NU)]
        stat = [sb(f"stat{i}", [128, 8], F32) for i in range(NU)]
        def prep(i):
            make_vpad(i)
            qr = qr2[i % 2]
            yield
            for hh in range(2):
                bk = g.bank()
                for a4 in range(4):
                    hp = hh * 4 + a4
                    for kc in range(8):
                        g.mm(bk.v(np.s_[:, a4 * 128:(a4 + 1) * 128]), V(wq_v[:, kc, hp * 128:(hp + 1) * 128], KQ),
                             hT.v(np.s_[:, kc, i * 128:(i + 1) * 128]), start=(kc == 0), stop=(kc == 7))
                qf = scr.v(np.s_[:, 0:512], sub="a")
                r2 = scr.v(np.s_[:, 512:1024], sub="b")
                qb = qb16.v()
                qf3 = V(scr.t[:, 0:512].rearrange("p (a t) -> p a t", a=4), [("scr", "a")])
                r23 = V(scr.t[:, 512:1024].rearrange("p (a t) -> p a t", a=4), [("scr", "b")])
                g.tt("vector", qf3, bk.w(bk.t[:].rearrange("p (a t) -> p a t", a=4)),
                     bq.w(bq.t[:, hh * 4:hh * 4 + 4].unsqueeze(2).to_broadcast([128, 4, 128])), ALU.add)
                g.copy("gpsimd", qb, qf)
                b2 = g.bank()
                g.mm(b2.v(), rot.v(), qb)
                sin_b = V(sinT.ap[:, i * 128:(i + 1) * 128].unsqueeze(1).to_broadcast([128, 4, 128]), sinT.keys)
                cos_b = V(cosT.ap[:, i * 128:(i + 1) * 128].unsqueeze(1).to_broadcast([128, 4, 128]), cosT.keys)
                g.tt("vector", r23, b2.w(b2.t[:].rearrange("p (a t) -> p a t", a=4)), sin_b, ALU.mult)
                g.tt("gpsimd", qf3, qf3, cos_b, ALU.mult)
                g.tt("vector", qr.v(np.s_[:, hh * 4:hh * 4 + 4, :]), qf3, r23, ALU.add)
                yield

        def attn(i):
            vps = [vpad[(i - 1) % 3], vpad[i % 3]]
            qr = qr2[i % 2]
            mk = mask1bf if i == 1 else maskbf
            for j in range(2):
                sbank = {}
                for gp in range(2):
                    hp0 = 4 * j + 2 * gp
                    bks = [g.bank(), g.bank()]
                    mk2 = mk.w(mk.t[:].unsqueeze(1).to_broadcast([128, 2, 256]))
                    for q2 in range(2):
                        g.mm(bks[q2].w(bks[q2].t[:].rearrange("p (a k) -> p a k", a=2)), ident.v(), mk2, start=True, stop=False)
                    for a in range(2):
                        for q2 in range(2):
                            o = bks[q2].v(np.s_[:, a * 256:(a + 1) * 256])
                            g.mm(o, qr.v(np.s_[q2 * 64:(q2 + 1) * 64, hp0 + a, :]),
                                 kr.v(np.s_[q2 * 64:(q2 + 1) * 64, j, (i - 1) * 128:(i + 1) * 128]), start=False, stop=True)
                    for q2 in range(2):
                        sbank[gp * 2 + q2] = bks[q2]
                UN = range(4)
                yield
                for u in UN:
                    gp, q2 = divmod(u, 2)
                    st_ = stat[u]
                    ps3 = sbank[u].w(sbank[u].t[:].rearrange("p (a k) -> p a k", a=2))
                    g.kb.op("vector", lambda e, st_=st_, ps3=ps3: e.tensor_reduce(out=st_.t[:, 0:2], in_=ps3.ap, axis=mybir.AxisListType.X, op=ALU.max),
                            reads=ps3.keys, writes=st_.v().keys)
                    h0 = 2 * (4 * j + 2 * gp) + q2
                    g.stt("vector", st_.v(np.s_[:, 2:4]), st_.v(np.s_[:, 0:2]), -SCALE, negsink.w(negsink.t[:, h0:h0 + 3:2]), ALU.mult, ALU.min)
                yield
                for u in UN:
                    st_ = stat[u]
                    for a in range(2):
                        g.act(V(asc.t[:, u, a, :], [("asc", u)]), sbank[u].v(np.s_[:, a * 256:(a + 1) * 256]), AF.Exp, scale=SCALE,
                              bias=st_.v(np.s_[:, 2 + a:3 + a]), accum=st_.v(np.s_[:, 4 + a:5 + a]))
                    g.act(st_.v(np.s_[:, 6:8]), st_.v(np.s_[:, 2:4]), AF.Exp)
                yield
                for u in UN:
                    gp, q2 = divmod(u, 2)
                    st_ = stat[u]
                    h0 = 2 * (4 * j + 2 * gp) + q2
                    g.tt("vector", st_.v(np.s_[:, 6:8]), st_.v(np.s_[:, 6:8]), esink.w(esink.t[:, h0:h0 + 3:2]), ALU.mult)
                    g.tt("vector", st_.v(np.s_[:, 4:6]), st_.v(np.s_[:, 4:6]), st_.v(np.s_[:, 6:8]), ALU.add)
                    g.recip(st_.v(np.s_[:, 4:6]), st_.v(np.s_[:, 4:6]))
                for u in UN:
                    st_ = stat[u]
                    g.tt("gpsimd", pn[u].v(), V(asc.t[:, u], [("asc", u)]), st_.w(st_.t[:, 4:6].unsqueeze(2).to_broadcast([128, 2, 256])), ALU.mult)
                yield
                tbs = {}
                for u in UN:
                    tb = g.bank()
                    tv = bf(tb)
                    for a in range(2):
                        for kb in range(2):
                            sl = (a * 2 + kb) * 128
                            g.tr(tb.w(tv[:, sl:sl + 128]), pn[u].v(np.s_[:, a, kb * 128:(kb + 1) * 128]), ident.v())
                    tbs[u] = (tb, tv)
                for u in UN:
                    tb, tv = tbs[u]
                    g.copy("scalar" if u % 2 == 0 else "vector", pT[u].v(), tb.w(tv[:, 0:512].rearrange("p (a k q) -> p a k q", a=2, k=2)))
                yield
                for gp in range(2):
                    hp0 = 4 * j + 2 * gp
                    obk = g.bank()
                    first = True
                    for q2 in range(2):
                        for kb in range(2):
                            g.mm(obk.w(obk.t[:, 0:256].rearrange("p (a q) -> p a q", a=2)), vps[kb].v(np.s_[:, j, q2, :]),
                                 pT[gp * 2 + q2].v(np.s_[:, :, kb, :]), start=first, stop=(q2 == 1 and kb == 1))
                            first = False
                    g.copy("scalar", oT.v(np.s_[:, hp0:hp0 + 2, :]), obk.w(obk.t[:, 0:256].rearrange("p (a q) -> p a q", a=2)))
            for half in range(2):
                bk = g.bank()
                for hp in range(8):
                    g.mm(bk.v(), oT.v(np.s_[:, hp, :]), V(wo1_v[:, hp, half * 512:(half + 1) * 512], KO), start=(hp == 0), stop=False)
                g.mm(bk.v(), ones_row.v(), rowv.v(np.s_[0:1, half * 512:(half + 1) * 512]), start=False, stop=True)
                xs = xres.v(np.s_[:, i, half * 512:(half + 1) * 512], sub=i)
                g.tt("vector", xs, xs, bk.v(), ALU.add)

            yield

        def drive2(gens):
            gens = list(gens)
            while gens:
                for gn in list(gens):
                    try:
                        next(gn)
                    except StopIteration:
                        gens.remove(gn)

        make_vpad(0)
        drive2([prep(1)])
        for i in range(1, NT2):
            gs = [attn(i)]
            if i + 1 < NT2:
                gs.append(prep(i + 1))
            drive2(gs)

        norm_to_hT(2, range(1, NT2))
        mlp(1, first_tile=1)

        gfb = V(asc.t[:].rearrange("p a b c -> p (a b c)")[:, 0:1024], [("asc", u) for u in range(NU)] + [("asc_relu0", None), ("asc_relu1", None)])
        g.dma_in("sync", gfb, rowv_d[0:1, 0:1024].to_broadcast([128, 1024]))
        rstd_all(list(range(1, NT2)))
        for i in range(1, NT2):
            o_ = V(scr.t[:], [("scr", "a"), ("scr", "b")])
            g.stt("vector", o_, xres.v(np.s_[:, i, :], sub=i), ms.v(np.s_[:, i:i + 1], sub=i), gfb, ALU.mult, ALU.mult)
            g.dma_out("sync", out_d[(i - 1) * 128:i * 128, :], o_)
        g.finish()
    return nc


def _pm(v):
    return np.ascontiguousarray(np.asarray(v).reshape(-1, 128).T)


def phase2_inputs(inp, ygT_full, core):
    b, tq = divmod(core, 4)
    t0 = tq * 2048
    x = np.zeros((NTOK2, D), np.float32)
    yg = np.zeros((D, NTOK2), ml_dtypes.bfloat16)
    pos = np.zeros((1, NTOK2), np.int32)
    lo = t0 - 128
    if tq > 0:
        x[:] = inp["x"][b, lo:lo + NTOK2]
        yg[:] = ygT_full[b][:, lo:lo + NTOK2]
        pos[0] = inp["positions"][b, lo:lo + NTOK2]
    else:
        x[128:] = inp["x"][b, 0:2048]
        yg[:, 128:] = ygT_full[b][:, 0:2048]
        pos[0, 128:] = inp["positions"][b, 0:2048]
    gains = np.stack([_pm(inp["norm_mlp_g"][0]), _pm(inp["norm_mix_g"][1]), _pm(inp["norm_mlp_g"][1])], axis=1)
    bqkv = inp["attn_b_qkv"][0]
    bq = np.zeros((128, 10), np.float32)
    bq[:, 0:8] = _pm(bqkv[0:1024])
    for j in range(2):
        bk = bqkv[1024 + j * 64:1024 + (j + 1) * 64]
        bq[:, 8 + j] = np.concatenate([bk, bk])
    rowv = np.concatenate([inp["norm_final_g"], inp["attn_b_o"][0], bqkv[1152:1280], inp["attn_sinks"][0]])[None, :]
    cst = np.zeros((128, 4), np.float32)
    p = np.arange(128)
    cst[:, 0] = (10000.0 ** (-(np.arange(0, 64, 2, dtype=np.float32)) / 64.0))[p % 32]
    cst[:, 1] = np.where((p % 64) < 32, 1.0, 1.0)
    cst[:, 2] = 0.0 if tq > 0 else NEG_BIG
    return {
        "x": x, "ygT": yg, "pos": pos,
        "wo0": np.ascontiguousarray(inp["rwkv_w_o"][0]),
        "win0": np.ascontiguousarray(inp["mlp_w_in"][0]), "win1": np.ascontiguousarray(inp["mlp_w_in"][1]),
        "wout0": np.ascontiguousarray(inp["mlp_w_out"][0]), "wout1": np.ascontiguousarray(inp["mlp_w_out"][1]),
        "wqkv": np.ascontiguousarray(inp["attn_w_qkv"][0]), "wo1": np.ascontiguousarray(inp["attn_w_o"][0]),
        "gains": np.ascontiguousarray(gains, dtype=np.float32), "bq": bq,
        "rowv": np.ascontiguousarray(rowv, dtype=np.float32), "cst": cst,
    }


_NC_CACHE = {}


def kernel(**inputs):
    inp = {k: np.asarray(v) for k, v in inputs.items()}
    if "p1" not in _NC_CACHE:
        _NC_CACHE["p1"] = build_phase1()
        _NC_CACHE["p2"] = build_phase2()
    r1 = run_bass_kernel_spmd(_NC_CACHE["p1"], [phase1_inputs(inp, c) for c in range(8)], core_ids=list(range(8)))
    ygT = np.zeros((2, D, S_LEN), ml_dtypes.bfloat16)
    for c in range(8):
        b, hg = divmod(c, 4)
        ygT[b, hg * 256:(hg + 1) * 256] = r1.results[c]["yg"]
    r2 = run_bass_kernel_spmd(_NC_CACHE["p2"], [phase2_inputs(inp, ygT, c) for c in range(8)], core_ids=list(range(8)))
    out = np.zeros((2, S_LEN, D), np.float32)
    for c in range(8):
        b, tq = divmod(c, 4)
        out[b, tq * 2048:(tq + 1) * 2048] = r2.results[c]["out"]
    return out
```

```python
import numpy as np
import ml_dtypes
from contextlib import ExitStack
import concourse.bass as bass
import concourse.mybir as mybir
from concourse.bass_utils import run_bass_kernel_spmd

F32 = mybir.dt.float32
BF16 = mybir.dt.bfloat16
I32 = mybir.dt.int32
AF = mybir.ActivationFunctionType
ALU = mybir.AluOpType

EPOCH = 4000
FUSE_WAIT = True
DMA_ROT = 8
NEG_C = -0.6065306597126334


class KB:
    ENGS = ("tensor", "vector", "scalar", "gpsimd", "sync")

    def __init__(self, nc, stack):
        self.nc = nc
        self.stack = stack
        self.streams = {e: [] for e in self.ENGS}
        self.count = {e: 0 for e in self.ENGS}
        self.sems = {e: [] for e in self.ENGS}
        self.dma_count = {e: 0 for e in self.ENGS}
        self.dma_sems = {e: [] for e in self.ENGS}
        self.waited = {e: {} for e in self.ENGS}
        self.last_w = {}
        self.readers = {}
        self.order = 0

    def _newsem(self, name):
        return self.stack.enter_context(self.nc.semaphore(name))

    def _compute_token(self, e):
        n = self.count[e]
        ep, idx = divmod(n, EPOCH)
        while len(self.sems[e]) <= ep:
            self.sems[e].append(self._newsem(f"s_{e}_{len(self.sems[e])}"))
        self.count[e] = n + 1
        return (self.sems[e][ep], idx + 1, 1, e)

    def _dma_token(self, e):
        n = self.dma_count[e]
        if not self.dma_sems[e]:
            self.dma_sems[e] = [self._newsem(f"d_{e}_{i}") for i in range(DMA_ROT)]
        self.dma_count[e] = n + 1
        return (self.dma_sems[e][n % DMA_ROT], 16 * (n // DMA_ROT + 1), 16, "dma_" + e)

    def op(self, e, fn, reads=(), writes=(), dma=False):
        deps = []
        for k in reads:
            t = self.last_w.get(k)
            if t is not None:
                deps.append(t)
        for k in writes:
            t = self.last_w.get(k)
            if t is not None:
                deps.append(t)
            deps.extend(self.readers.get(k, ()))
        know = self.waited[e]
        cand = {}
        for t in deps:
            sem, val, _inc, src, _vc, _ord = t
            if src == e and not dma and e == "tensor":
                continue
            sid = id(sem)
            if know.get(sid, 0) >= val:
                continue
            if sid not in cand or cand[sid][1] < val:
                cand[sid] = t
        ww = []
        for t in sorted(cand.values(), key=lambda t: -t[5]):
            sem, val, _inc, _src, vc, _ord = t
            sid = id(sem)
            if know.get(sid, 0) >= val:
                continue
            ww.append((sem, val))
            know[sid] = val
            for s2, v2 in vc.items():
                if know.get(s2, 0) < v2:
                    know[s2] = v2
        if dma:
            n = self.dma_count[e]
            if n >= DMA_ROT:
                sem = self.dma_sems[e][n % DMA_ROT]
                val = 16 * (n // DMA_ROT)
                if know.get(id(sem), 0) < val:
                    know[id(sem)] = val
                    ww.append((sem, val))
        base = self._dma_token(e) if dma else self._compute_token(e)
        self.order += 1
        vc = dict(know)
        tok = base + (vc, self.order)
        if not dma:
            vc[id(base[0])] = max(vc.get(id(base[0]), 0), base[1])
        self.streams[e].append((ww, fn, tok))
        for k in reads:
            self.readers.setdefault(k, []).append(tok)
        for k in writes:
            self.last_w[k] = tok
            self.readers[k] = []
        return tok

    def wait_tokens(self, e, toks):
        wd = self.waited[e]
        waits = []
        for (sem, val, *_rest) in toks:
            if wd.get(id(sem), 0) >= val:
                continue
            wd[id(sem)] = val
            waits.append((sem, val))
        self.streams[e].append((waits, None, None))

    def emit(self):
        nc = self.nc
        with nc.Block() as block:
            def mk(e):
                def body(eng):
                    for waits, fn, tok in self.streams[e]:
                        if fn is None or not FUSE_WAIT or not waits:
                            for sem, val in waits:
                                eng.wait_ge(sem, val)
                            if fn is not None:
                                fn(eng).then_inc(tok[0], tok[2])
                        else:
                            for sem, val in waits[:-1]:
                                eng.wait_ge(sem, val)
                            ins = fn(eng)
                            ins._wait_ge(waits[-1][0], waits[-1][1])
                            ins.then_inc(tok[0], tok[2])
                return body
            for e in self.ENGS:
                if self.streams[e]:
                    getattr(block, e)(mk(e))


class V:
    __slots__ = ("ap", "keys")

    def __init__(self, ap, keys):
        self.ap = ap
        self.keys = keys


class Buf:
    def __init__(self, t, name):
        self.t = t
        self.name = name

    def v(self, idx=None, sub=None):
        ap = self.t[idx] if idx is not None else self.t[:]
        return V(ap, [(self.name, sub)])

    def w(self, ap, sub=None):
        return V(ap, [(self.name, sub)])


class G:
    def __init__(self, nc, st):
        self.nc = nc
        self.st = st
        self.kb = KB(nc, st)
        self.banks = [Buf(st.enter_context(nc.psum_tensor(f"psb{i}", [128, 512], F32)), f"ps{i}") for i in range(8)]
        self.bank_i = 0
        self.out_tokens = []

    def sb(self, name, shape, dt):
        return Buf(self.st.enter_context(self.nc.sbuf_tensor("sb_" + name, shape, dt)), name)

    def bank(self):
        b = self.banks[self.bank_i % 8]
        self.bank_i += 1
        return b

    @staticmethod
    def _k(vs):
        ks = []
        for v in vs:
            if isinstance(v, V):
                ks.extend(v.keys)
        return ks

    def mm(self, out, lhsT, rhs, start=True, stop=True):
        return self.kb.op("tensor", lambda e: e.matmul(out.ap, lhsT=lhsT.ap, rhs=rhs.ap, start=start, stop=stop),
                          reads=self._k([lhsT, rhs]), writes=out.keys)

    def tr(self, out, in_, ident):
        return self.kb.op("tensor", lambda e: e.transpose(out=out.ap, in_=in_.ap, identity=ident.ap),
                          reads=self._k([in_, ident]), writes=out.keys)

    def act(self, out, in_, func, bias=None, scale=1.0, accum=None, eng="scalar"):
        kw = {}
        if bias is not None:
            kw["bias"] = bias.ap if isinstance(bias, V) else bias
        if accum is not None:
            kw["accum_out"] = accum.ap
        sc = scale.ap if isinstance(scale, V) else scale
        return self.kb.op("scalar", lambda e: e.activation(out=out.ap, in_=in_.ap, func=func, scale=sc, **kw),
                          reads=self._k([in_, bias, scale]), writes=self._k([out, accum]))

    def tt(self, eng, out, a, b, op):
        return self.kb.op(eng, lambda e: e.tensor_tensor(out=out.ap, in0=a.ap, in1=b.ap, op=op),
                          reads=self._k([a, b]), writes=out.keys)

    def ts(self, eng, out, a, s1, op0, s2=None, op1=None):
        s1a = s1.ap if isinstance(s1, V) else s1
        s2a = s2.ap if isinstance(s2, V) else s2
        if op1 is None:
            fn = lambda e: e.tensor_scalar(out=out.ap, in0=a.ap, scalar1=s1a, scalar2=None, op0=op0)
        else:
            fn = lambda e: e.tensor_scalar(out=out.ap, in0=a.ap, scalar1=s1a, scalar2=s2a, op0=op0, op1=op1)
        return self.kb.op(eng, fn, reads=self._k([a, s1, s2]), writes=out.keys)

    def stt(self, eng, out, in0, scalar, in1, op0, op1):
        sa = scalar.ap if isinstance(scalar, V) else scalar
        return self.kb.op(eng, lambda e: e.scalar_tensor_tensor(out=out.ap, in0=in0.ap, scalar=sa, in1=in1.ap, op0=op0, op1=op1),
                          reads=self._k([in0, scalar, in1]), writes=out.keys)

    def copy(self, eng, out, in_):
        if eng == "scalar":
            return self.act(out, in_, AF.Copy)
        return self.kb.op(eng, lambda e: e.tensor_copy(out=out.ap, in_=in_.ap), reads=in_.keys, writes=out.keys)

    def memset(self, eng, out, val):
        return self.kb.op(eng, lambda e: e.memset(out.ap, val), writes=out.keys)

    def recip(self, out, in_):
        return self.kb.op("vector", lambda e: e.reciprocal(out=out.ap, in_=in_.ap), reads=in_.keys, writes=out.keys)

    def scan(self, out, d0, d1, init, op0, op1):
        return self.kb.op("vector", lambda e: e.tensor_tensor_scan(out=out.ap, data0=d0.ap, data1=d1.ap, initial=init, op0=op0, op1=op1),
                          reads=self._k([d0, d1]), writes=out.keys)

    def aselect(self, out, in_, pattern, cmp, fill, base, cm):
        return self.kb.op("gpsimd", lambda e: e.affine_select(out=out.ap, in_=in_.ap, pattern=pattern, compare_op=cmp,
                                                               fill=fill, base=base, channel_multiplier=cm),
                          reads=in_.keys, writes=out.keys)

    def dma_in(self, eng, out, in_ap, **kw):
        return self.kb.op(eng, lambda e: e.dma_start(out=out.ap, in_=in_ap, **kw), writes=out.keys, dma=True)

    def dma_out(self, eng, out_ap, in_, final=True):
        t = self.kb.op(eng, lambda e: e.dma_start(out=out_ap, in_=in_.ap), reads=in_.keys, dma=True)
        if final:
            self.out_tokens.append(t)
        return t

    def finish(self):
        self.kb.wait_tokens("sync", self.out_tokens)
        self.kb.emit()


S_LEN = 8192
D = 1024
TB = 512
NCH = TB // 128
GN_EPS = 64e-5
RMS_EPS = 1e-5
PROJ = [("r", 0, 256, 0), ("k", 2, 256, 256), ("v", 3, 256, 512), ("w1", 1, 64, 768), ("a1", 4, 64, 832), ("g1", 5, 160, 896)]
NCOL = 1056


BACK_W = 4


def build_phase1(n_tok=S_LEN, stop=None, stopargs=()):
    nc = bass.Bass("TRN2", target_bir_lowering=False)
    dr = lambda n, s, d=F32: nc.dram_tensor(n, s, d, kind="ExternalInput").ap()
    x_d = dr("x", [n_tok, D])
    gm_d = dr("gm", [128, 7, 8])
    wcat_d = dr("wcat", [D, NCOL])
    w2_d = dr("w2", [64, 256])
    a2_d = dr("a2", [64, 256])
    g2_d = dr("g2", [160, 256])
    vec_d = dr("vec", [128, 7, 2])
    yg_d = nc.dram_tensor("yg", [256, n_tok], BF16, kind="ExternalOutput").ap()
    dbg_d = nc.dram_tensor("dbg", [128, 2560], F32, kind="ExternalOutput").ap() if stop else None
    nblk = n_tok // TB

    class Stop(Exception):
        pass

    with ExitStack() as st:
        g = G(nc, st)
        sb = g.sb
        dbgt = sb("dbgt", [128, 2560], F32) if stop else None
        dbg_off = [0]

        def dump(v, n):
            o = dbg_off[0]
            p = v.ap.shape[0]
            g.copy("vector", dbgt.w(dbgt.t[0:p, o:o + n]), v)
            dbg_off[0] = o + n

        def chk(name):
            if stop == name:
                raise Stop()
        def emit_all():
            identf = sb("identf", [128, 128], F32)
            ident = sb("ident", [128, 128], BF16)
            ident4 = sb("ident4", [128, 4, 128], BF16)
            onesblk = sb("onesblk", [128, 128], F32)
            m_su = sb("m_su", [128, 4, 128], BF16)
            m_u = sb("m_u", [128, 4, 128], BF16)
            m_sl = sb("m_sl", [128, 4, 128], BF16)
            rmask = sb("rmask", [128, TB], F32)
            g.memset("gpsimd", identf.v(), 0.0)
            g.aselect(identf.v(), identf.v(), [[-1, 128]], ALU.not_equal, 1.0, 0, 1)
            g.copy("vector", ident.v(), identf.v())
            for h in range(4):
                g.copy("vector", ident4.v(np.s_[:, h, :]), identf.v())
            g.memset("gpsimd", onesblk.v(), 0.0)
            g.memset("gpsimd", onesblk.v(np.s_[0:64, 0:64]), 1.0)
            g.memset("gpsimd", onesblk.v(np.s_[64:128, 64:128]), 1.0)
            for m, cm, pat, cmp in ((m_su, -1, 1, ALU.is_gt), (m_u, -1, 1, ALU.is_ge), (m_sl, 1, -1, ALU.is_gt)):
                g.memset("gpsimd", m.v(), 1.0)
                g.aselect(m.v(), m.v(), [[0, 4], [pat, 128]], cmp, 0.0, 0, cm)
            g.memset("gpsimd", rmask.v(), 1.0)
            g.memset("gpsimd", rmask.w(rmask.t[:].rearrange("p (c t) -> p c t", t=128)[:, :, 0:1]), 0.0)

            if stop == "const":
                dump(identf.v(), 128); dump(m_su.v(np.s_[:, 1, :]), 128); dump(m_sl.v(np.s_[:, 2, :]), 128); dump(m_u.v(np.s_[:, 3, :]), 128)
                dump(onesblk.v(), 128); dump(rmask.v(), 512)
            chk("const")
            gm = sb("gm", [128, 7, 8], F32)
            coefA = sb("coefA", [128, 6, 8], F32)
            coefB = sb("coefB", [128, 6, 8], F32)
            vec = sb("vec", [128, 7, 2], F32)
            WA = sb("WA", [128, 8, NCOL], BF16)
            WB = sb("WB", [128, 8, NCOL], BF16)
            w2b = sb("w2b", [64, 256], BF16)
            a2b = sb("a2b", [128, 256], BF16)
            g2a = sb("g2a", [128, 256], BF16)
            g2b = sb("g2b", [32, 256], BF16)
            xin = sb("xin", [128, 4, 1024], F32)
            g.dma_in("sync", gm.v(), gm_d[:, :, :])
            g.dma_in("sync", vec.v(), vec_d[:, :, :])
            g.dma_in("gpsimd", w2b.v(), w2_d[:, :])
            g.dma_in("gpsimd", a2b.v(np.s_[64:128, :]), a2_d[:, :])
            g.dma_in("gpsimd", g2a.v(), g2_d[0:128, :])
            g.dma_in("gpsimd", g2b.v(), g2_d[128:160, :])
            g0b = gm.w(gm.t[:, 0:1, :].to_broadcast([128, 6, 8]))
            g.tt("vector", coefB.v(), gm.v(np.s_[:, 1:7, :]), g0b, ALU.mult)
            g.tt("vector", coefA.v(), g0b, coefB.v(), ALU.subtract)
            stage = xin
            wv = wcat_d.rearrange("(kc p) n -> p kc n", p=128)
            XK = [("xin", j) for j in range(4)]
            for (nm, mi, ncol, off) in PROJ:
                sv = stage.t[:].rearrange("p a b -> p (a b)")[:, 0:8 * ncol].rearrange("p (k n) -> p k n", k=8)
                g.kb.op("sync", lambda e, sv=sv, off=off, ncol=ncol: e.dma_start(out=sv, in_=wv[:, :, off:off + ncol]), writes=XK, dma=True)
                ca = coefA.w(coefA.t[:, mi, :].unsqueeze(2).to_broadcast([128, 8, ncol]))
                cb = coefB.w(coefB.t[:, mi, :].unsqueeze(2).to_broadcast([128, 8, ncol]))
                g.tt("vector", WA.v(np.s_[:, :, off:off + ncol]), V(sv, XK), ca, ALU.mult)
                g.tt("gpsimd", WB.v(np.s_[:, :, off:off + ncol]), V(sv, XK), cb, ALU.mult)

            if stop == "weights":
                dump(WA.v(np.s_[:, 3, 0:512]), 512); dump(WB.v(np.s_[:, 7, 544:1056]), 512); dump(coefA.v(np.s_[:, 2, :]), 8)
            chk("weights")
            ms = sb("ms", [128, 4], F32)
            xn = [sb(f"xn{i}", [128, 1024], BF16) for i in range(2)]
            hT = sb("hT", [128, 8, TB + 1], BF16)
            T = {n: sb("t_" + n, [128, TB], F32) for n in
                 ("rT", "kraw", "aT", "sgw", "cs", "Er", "En", "kk", "kp", "beta", "t1", "t2", "EC")}
            junk = Buf(T["EC"].t[:].bitcast(BF16), "t_EC")
            tw = sb("tw", [128, TB], BF16)
            sg1a = sb("sg1a", [128, TB], BF16)
            sg1b = sb("sg1b", [32, TB], BF16)
            vT = sb("vT", [128, 2, TB], F32)
            gT2 = [sb(f"gT{i}", [128, 2, TB], F32) for i in range(2)]
            bonusT2 = [sb(f"bonusT{i}", [128, 2, TB], F32) for i in range(2)]
            PCs2 = [sb(f"PCs{i}", [128, 2, NCH], F32) for i in range(2)]
            rt2 = [sb(f"rt_{i}", [128, 2, TB], BF16) for i in range(2)]
            kt2 = [sb(f"kt_{i}", [128, 2, TB], BF16) for i in range(2)]
            at2 = [sb(f"at_{i}", [128, 2, TB], BF16) for i in range(2)]
            bt2 = [sb(f"bt_{i}", [128, 2, TB], BF16) for i in range(2)]
            T2 = {n: sb("t2_" + n, [128, TB], F32) for n in ("o1", "o2")}
            khT = sb("khT", [128, 2, TB], BF16)
            bhT = sb("bhT", [128, 2, TB], BF16)
            XR2 = [sb(f"XR{i}", [128, NCH, 4, 128], BF16) for i in range(2)]
            Kpad2 = [sb(f"Kpad{i}", [128, NCH, 4, 128], BF16) for i in range(2)]
            Bpad2 = [sb(f"Bpad{i}", [128, NCH, 4, 128], BF16) for i in range(2)]
            Vpad2 = [sb(f"Vpad{i}", [128, NCH, 4, 128], BF16) for i in range(2)]
            NSL = 2
            Pb = [[sb(f"P{s}{i}", [128, 4, 128], BF16) for i in range(2)] for s in range(NSL)]
            SUt = [sb(f"SU{s}", [128, 2, 4, 128], BF16) for s in range(NSL)]
            UUt = [sb(f"UU{s}", [128, 2, 4, 128], BF16) for s in range(NSL)]
            PTb = [[Buf(SUt[s].t[:, 0], f"PT{s}0"), sb(f"PT{s}1", [128, 4, 128], BF16)] for s in range(NSL)]
            NTb = [[sb(f"NT{s}{i}", [128, 4, 128], BF16) for i in range(2)] for s in range(NSL)]
            AakT = [Buf(SUt[s].t[:, 1], f"AakT{s}") for s in range(NSL)]
            ArbT = [Buf(UUt[s].t[:, 0], f"ArbT{s}") for s in range(NSL)]
            ArkT = [Buf(UUt[s].t[:, 1], f"ArkT{s}") for s in range(NSL)]
            Apad = [sb(f"Apad{s}", [128, 4, 128], BF16) for s in range(NSL)]
            Wpad = [sb(f"Wpad{s}", [128, 4, 128], BF16) for s in range(NSL)]
            TTbd = [sb(f"TTbd{s}", [128, 2, 128], BF16) for s in range(NSL)]
            RhT = [sb(f"RhT{s}", [128, 2, 128], BF16) for s in range(NSL)]
            Sbd = [sb(f"Sbd{i}", [128, 2, 128], BF16) for i in range(2)]
            yraw = sb("yraw", [128, 2, TB], F32)
            ygo = sb("ygo", [128, 2, TB], BF16)
            for b_ in Kpad2 + Bpad2 + Vpad2:
                g.memset("gpsimd", b_.v(), 0.0)
            for s in range(NSL):
                g.memset("gpsimd", Apad[s].v(), 0.0)
                g.memset("gpsimd", Wpad[s].v(), 0.0)
            g.memset("gpsimd", Sbd[0].v(), 0.0)
            g.memset("vector", hT.v(np.s_[:, :, 0:1]), 0.0)
            s_cur = 0

            def bf(bank):
                return bank.t[:].bitcast(BF16)

            def hsl(h, j):
                cc, q = divmod(h, 2)
                return np.s_[q * 64:(q + 1) * 64, cc, j * 128:(j + 1) * 128]

            def load_x(blk):
                t0 = blk * TB
                for j in range(4):
                    g.dma_in("sync", xin.v(np.s_[:, j, :], sub=j), x_d[t0 + j * 128:t0 + (j + 1) * 128, :])

            def front(blk):
                F = blk % 2
                rt_, kt_, at_, bt_ = rt2[F], kt2[F], at2[F], bt2[F]
                XR, Kpad, Bpad, Vpad, PCs, gT, bonusT = XR2[F], Kpad2[F], Bpad2[F], Vpad2[F], PCs2[F], gT2[F], bonusT2[F]
                for j in range(4):
                    g.act(junk.v(), xin.v(np.s_[:, j, :], sub=j), AF.Square, scale=1.0 / 32, accum=ms.v(np.s_[:, j:j + 1]))
                g.ts("vector", ms.v(), ms.v(), RMS_EPS, ALU.add)
                g.act(ms.v(), ms.v(), AF.Sqrt)
                g.recip(ms.v(), ms.v())
                if blk > 0:
                    g.copy("vector", hT.v(np.s_[:, :, 0:1]), hT.v(np.s_[:, :, TB:TB + 1]))
                yield
                for j in range(4):
                    xj = xn[j % 2]
                    g.act(xj.v(), xin.v(np.s_[:, j, :], sub=j), AF.Copy, scale=ms.v(np.s_[:, j:j + 1]))
                    bk = g.bank()
                    bv = bf(bk)
                    for kc in range(8):
                        g.tr(bk.w(bv[:, kc * 128:(kc + 1) * 128]), xj.v(np.s_[:, kc * 128:(kc + 1) * 128]), ident.v())
                    g.copy("vector" if j % 2 == 0 else "scalar", hT.v(np.s_[:, :, 1 + 128 * j:1 + 128 * (j + 1)]),
                           bk.w(bv.rearrange("p (k t) -> p k t", k=8)))
                    yield
                if blk + 1 < nblk:
                    load_x(blk + 1)

                def proj_fm(off, ncol):
                    bk = g.bank()
                    for kc in range(8):
                        g.mm(bk.v(np.s_[0:ncol, :]), WA.v(np.s_[:, kc, off:off + ncol]), hT.v(np.s_[:, kc, 1:TB + 1]),
                             start=(kc == 0), stop=False)
                        g.mm(bk.v(np.s_[0:ncol, :]), WB.v(np.s_[:, kc, off:off + ncol]), hT.v(np.s_[:, kc, 0:TB]),
                             start=False, stop=(kc == 7))
                    return bk

                bk = proj_fm(768, 128)
                g.act(tw.v(np.s_[0:64, :]), bk.v(np.s_[0:64, :]), AF.Tanh)
                g.copy("vector", tw.v(np.s_[64:128, :]), bk.v(np.s_[64:128, :]))
                yield
                bk = proj_fm(896, 128)
                g.act(sg1a.v(), bk.v(), AF.Sigmoid)
                yield
                bk = proj_fm(1024, 32)
                g.act(sg1b.v(), bk.v(np.s_[0:32, :]), AF.Sigmoid)
                yield
                for j in range(4):
                    bk = g.bank()
                    for kc in range(8):
                        g.mm(bk.v(np.s_[:, 0:256]), hT.v(np.s_[:, kc, 1 + 128 * j:1 + 128 * (j + 1)]), WA.v(np.s_[:, kc, 512:768]),
                             start=(kc == 0), stop=False)
                        g.mm(bk.v(np.s_[:, 0:256]), hT.v(np.s_[:, kc, 128 * j:128 * (j + 1)]), WB.v(np.s_[:, kc, 512:768]),
                             start=False, stop=(kc == 7))
                    pv = bk.t[:, 0:256].rearrange("p (h c) -> p h c", h=4)
                    g.copy("vector", Vpad.v(np.s_[:, j, 0::2, 0:64]), bk.w(pv[:, 0::2, :]))
                    g.copy("scalar", Vpad.v(np.s_[:, j, 1::2, 64:128]), bk.w(pv[:, 1::2, :]))
                    yield

                for cc in range(2):
                    bk = proj_fm(0 + cc * 128, 128)
                    g.copy("scalar", T["rT"].v(), bk.v())
                    yield
                    bk = proj_fm(256 + cc * 128, 128)
                    g.copy("vector", T["kraw"].v(), bk.v())
                    yield
                    bk = proj_fm(512 + cc * 128, 128)
                    g.copy("scalar", vT.v(np.s_[:, cc, :]), bk.v())
                    yield
                    bk = g.bank()
                    g.mm(bk.v(), w2b.v(np.s_[0:64, cc * 128:(cc + 1) * 128]), tw.v(np.s_[0:64, :]))
                    g.act(T["sgw"].v(), bk.v(), AF.Sigmoid, bias=vec.v(np.s_[:, 0, cc:cc + 1]))
                    bk = g.bank()
                    g.mm(bk.v(), a2b.v(np.s_[64:128, cc * 128:(cc + 1) * 128]), tw.v(np.s_[64:128, :]))
                    g.act(T["aT"].v(), bk.v(), AF.Sigmoid, bias=vec.v(np.s_[:, 1, cc:cc + 1]))
                    bk = g.bank()
                    g.mm(bk.v(), g2a.v(np.s_[:, cc * 128:(cc + 1) * 128]), sg1a.v(), start=True, stop=False)
                    g.mm(bk.v(), g2b.v(np.s_[0:32, cc * 128:(cc + 1) * 128]), sg1b.v(np.s_[0:32, :]), start=False, stop=True)
                    g.copy("vector", gT.v(np.s_[:, cc, :]), bk.v())
                    yield

                    kk, kp, beta, t1, t2 = T["kk"], T["kp"], T["beta"], T["t1"], T["t2"]
                    Er, En, EC, cs = T["Er"], T["En"], T["EC"], T["cs"]
                    g.ts("gpsimd", kk.v(), T["kraw"].v(), vec.v(np.s_[:, 2, cc:cc + 1]), ALU.mult)
                    g.tt("gpsimd", t1.v(), kk.v(), kk.v(), ALU.mult)
                    bk = g.bank()
                    g.mm(bk.v(), onesblk.v(), t1.v())
                    g.scan(cs.v(), rmask.v(), T["sgw"].v(), 0.0, ALU.mult, ALU.add)
                    yield
                    g.ts("vector", t2.v(), bk.v(), 1e-24, ALU.max)
                    g.act(t2.v(), t2.v(), AF.Sqrt)
                    g.act(Er.v(), cs.v(), AF.Exp, scale=NEG_C)
                    g.act(En.v(), cs.v(), AF.Exp, scale=-NEG_C)
                    g.recip(t2.v(), t2.v())
                    yield
                    g.tt("gpsimd", kk.v(), kk.v(), t2.v(), ALU.mult)
                    g.ts("vector", t1.v(), T["aT"].v(), -1.0, ALU.add, vec.v(np.s_[:, 3, cc:cc + 1]), ALU.mult)
                    g.tt("gpsimd", beta.v(), kk.v(), T["aT"].v(), ALU.mult)
                    g.stt("vector", kp.v(), t1.v(), 1.0, T["kraw"].v(), ALU.add, ALU.mult)
                    yield
                    Er3 = Er.t[:].rearrange("p (c t) -> p c t", t=128)
                    g.copy("vector", PCs.v(np.s_[:, cc, :]), Er.w(Er3[:, :, 127]))
                    g.tt("vector", EC.w(EC.t[:].rearrange("p (c t) -> p c t", t=128)),
                         En.w(En.t[:].rearrange("p (c t) -> p c t", t=128)),
                         Er.w(Er3[:, :, 127:128].to_broadcast([128, NCH, 128])), ALU.mult)
                    g.tt("gpsimd", kt_.v(np.s_[:, cc, :]), kp.v(), En.v(), ALU.mult)
                    g.tt("gpsimd", rt_.v(np.s_[:, cc, :]), T["rT"].v(), Er.v(), ALU.mult)
                    yield
                    g.tt("gpsimd", bt_.v(np.s_[:, cc, :]), beta.v(), En.v(), ALU.mult)
                    g.tt("gpsimd", khT.v(np.s_[:, cc, :]), kp.v(), EC.v(), ALU.mult)
                    g.tt("gpsimd", bhT.v(np.s_[:, cc, :]), beta.v(), EC.v(), ALU.mult)
                    kk3 = kk.t[:].rearrange("p (c t) -> p c t", t=128)
                    at3 = at_.t[:, cc, :].rearrange("p (c t) -> p c t", t=128)
                    g.stt("vector", at_.w(at3[:, :, 1:128]), kk.w(kk3[:, :, 1:128]), -1.0, Er.w(Er3[:, :, 0:127]), ALU.mult, ALU.mult)
                    g.ts("vector", at_.w(at3[:, :, 0:1]), kk.w(kk3[:, :, 0:1]), -1.0, ALU.mult)
                    yield
                    g.stt("vector", t1.v(), T["rT"].v(), vec.v(np.s_[:, 4, cc:cc + 1]), kp.v(), ALU.mult, ALU.mult)
                    bk = g.bank()
                    g.mm(bk.v(), onesblk.v(), t1.v())
                    g.tt("vector", bonusT.v(np.s_[:, cc, :]), vT.v(np.s_[:, cc, :]), bk.v(), ALU.mult)
                    yield

                for (src, kind) in ((at_, "A"), (khT, "K"), (bhT, "B")):
                    bk = g.bank()
                    bv = bf(bk)
                    for j in range(NCH):
                        for cc in range(2):
                            sl = (j * 2 + cc) * 128
                            g.tr(bk.w(bv[:, sl:sl + 128]), src.v(np.s_[:, cc, j * 128:(j + 1) * 128]), ident.v())
                    if kind == "A":
                        g.copy("vector", XR.v(np.s_[:, :, :, 0:64]),
                               bk.w(bv.rearrange("p (j h c) -> p j h c", j=NCH, h=4)))
                    else:
                        dst = Kpad if kind == "K" else Bpad
                        b5 = bv.rearrange("p (j c q d) -> p j c q d", j=NCH, c=2, q=2)
                        g.copy("vector", dst.v(np.s_[:, :, 0::2, 0:64]), bk.w(b5[:, :, :, 0, :]))
                        g.copy("scalar", dst.v(np.s_[:, :, 1::2, 64:128]), bk.w(b5[:, :, :, 1, :]))
                    yield

            def back(blk):
                nonlocal s_cur
                t0 = blk * TB
                F = blk % 2
                rt_, kt_, at_, bt_ = rt2[F], kt2[F], at2[F], bt2[F]
                XR, Kpad, Bpad, Vpad, PCs, gT, bonusT = XR2[F], Kpad2[F], Bpad2[F], Vpad2[F], PCs2[F], gT2[F], bonusT2[F]

                def pre_a(j, s):
                    def grp(specs, msk, dsts_t, dkeys):
                        bks = [g.bank(), g.bank()]
                        for si, (lh, rh) in enumerate(specs):
                            for h in range(4):
                                cc, q = divmod(h, 2)
                                sl = (si * 2 + cc) * 128
                                g.mm(bks[q].v(np.s_[:, sl:sl + 128]), lh.v(hsl(h, j)), rh.v(hsl(h, j)))
                        n = len(specs)
                        for q in range(2):
                            if n == 2:
                                src = bks[q].w(bks[q].t[:].rearrange("p (a c t) -> p a c t", a=2, c=2))
                                dst = V(dsts_t[:, :, q::2, :], dkeys)
                                mk = msk.w(msk.t[:].rearrange("p (a c) t -> p a c t", a=2))
                            else:
                                src = bks[q].w(bks[q].t[:, 0:256].rearrange("p (c t) -> p c t", c=2))
                                dst = V(dsts_t[:, q::2, :], dkeys)
                                mk = msk.v(np.s_[:, 0:2, :])
                            g.tt("vector", dst, src, mk, ALU.mult)
                    grp([(bt_, at_), (kt_, at_)], m_su, SUt[s].t, [(PTb[s][0].name, None), (AakT[s].name, None)])
                    yield
                    grp([(bt_, rt_), (kt_, rt_)], m_u, UUt[s].t, [(ArbT[s].name, None), (ArkT[s].name, None)])
                    yield
                    grp([(at_, bt_)], m_sl, Pb[s][0].t, [(Pb[s][0].name, None)])
                    g.tt("gpsimd", NTb[s][0].v(), PTb[s][0].v(), ident4.v(), ALU.add)
                    yield

                def pre_dbl(s, it):
                    cur, nxt = it % 2, (it + 1) % 2
                    bk = g.bank()
                    for h in range(4):
                        g.mm(bk.v(np.s_[:, h * 128:(h + 1) * 128]), PTb[s][cur].v(np.s_[:, h, :]), Pb[s][cur].v(np.s_[:, h, :]))
                    g.copy("scalar", Pb[s][nxt].v(), bk.w(bk.t[:].rearrange("p (h t) -> p h t", h=4)))
                    if it < 5:
                        bk = g.bank()
                        for h in range(4):
                            g.mm(bk.v(np.s_[:, h * 128:(h + 1) * 128]), Pb[s][cur].v(np.s_[:, h, :]), PTb[s][cur].v(np.s_[:, h, :]))
                        g.copy("vector", PTb[s][nxt].v(), bk.w(bk.t[:].rearrange("p (h t) -> p h t", h=4)))
                    yield
                    bk = g.bank()
                    for h in range(4):
                        o = bk.v(np.s_[:, h * 128:(h + 1) * 128])
                        g.mm(o, ident.v(), NTb[s][cur].v(np.s_[:, h, :]), start=True, stop=False)
                        g.mm(o, Pb[s][nxt].v(np.s_[:, h, :]), NTb[s][cur].v(np.s_[:, h, :]), start=False, stop=True)
                    eng = "scalar" if it % 2 == 0 else "vector"
                    g.copy(eng, NTb[s][nxt].v(), bk.w(bk.t[:].rearrange("p (h t) -> p h t", h=4)))
                    yield

                def pre_b(j, s):
                    NTf = NTb[s][0]
                    bk = g.bank()
                    for h in range(4):
                        q = h % 2
                        g.mm(bk.v(np.s_[:, h * 64:(h + 1) * 64]), AakT[s].v(np.s_[:, h, :]), Vpad.v(np.s_[:, j, h, q * 64:(q + 1) * 64]))
                    g.copy("vector", XR.v(np.s_[:, j, :, 64:128]), bk.w(bk.t[:, 0:256].rearrange("p (h c) -> p h c", h=4)))
                    yield
                    bk = g.bank()
                    for h in range(4):
                        g.mm(bk.v(np.s_[:, h * 128:(h + 1) * 128]), NTf.v(np.s_[:, h, :]), XR.v(np.s_[:, j, h, :]))
                    x4 = bk.t[:].rearrange("p (h c) -> p h c", h=4)
                    g.copy("vector", Apad[s].v(np.s_[:, 0::2, 0:64]), bk.w(x4[:, 0::2, 0:64]))
                    g.copy("scalar", Apad[s].v(np.s_[:, 1::2, 64:128]), bk.w(x4[:, 1::2, 0:64]))
                    g.copy("vector", Wpad[s].v(np.s_[:, 0::2, 0:64]), bk.w(x4[:, 0::2, 64:128]))
                    g.copy("scalar", Wpad[s].v(np.s_[:, 1::2, 64:128]), bk.w(x4[:, 1::2, 64:128]))
                    yield
                    bk = g.bank()
                    for cc in range(2):
                        o = bk.v(np.s_[:, cc * 128:(cc + 1) * 128])
                        for q in range(2):
                            h = 2 * cc + q
                            g.mm(o, Apad[s].v(np.s_[:, h, :]), Bpad.v(np.s_[:, j, h, :]), start=(q == 0), stop=(q == 1))
                    for cc in range(2):
                        g.stt("vector", TTbd[s].v(np.s_[:, cc, :]), identf.v(), PCs.v(np.s_[:, cc, j:j + 1]),
                              bk.v(np.s_[:, cc * 128:(cc + 1) * 128]), ALU.mult, ALU.add)
                    bk = g.bank()
                    for cc in range(2):
                        o = bk.v(np.s_[:, cc * 128:(cc + 1) * 128])
                        for q in range(2):
                            h = 2 * cc + q
                            g.mm(o, Apad[s].v(np.s_[:, h, :]), ArbT[s].v(np.s_[:, h, :]), start=(q == 0), stop=(q == 1))
                    g.tt("vector", RhT[s].v(), bk.w(bk.t[:, 0:256].rearrange("p (c t) -> p c t", c=2)),
                         rt_.v(np.s_[:, :, j * 128:(j + 1) * 128]), ALU.add)
                    yield

                def seq(j, s):
                    nonlocal s_cur
                    Sc, Sn = Sbd[s_cur], Sbd[1 - s_cur]
                    bk = g.bank()
                    for cc in range(2):
                        o = bk.v(np.s_[:, cc * 128:(cc + 1) * 128])
                        g.mm(o, TTbd[s].v(np.s_[:, cc, :]), Sc.v(np.s_[:, cc, :]), start=True, stop=False)
                        for q in range(2):
                            h = 2 * cc + q
                            g.mm(o, Bpad.v(np.s_[:, j, h, :]), Wpad[s].v(np.s_[:, h, :]), start=False, stop=False)
                            g.mm(o, Kpad.v(np.s_[:, j, h, :]), Vpad.v(np.s_[:, j, h, :]), start=False, stop=(q == 1))
                    g.copy("vector", Sn.v(), bk.w(bk.t[:, 0:256].rearrange("p (c t) -> p c t", c=2)))
                    bk = g.bank()
                    for cc in range(2):
                        o = bk.v(np.s_[:, cc * 128:(cc + 1) * 128])
                        g.mm(o, Sc.v(np.s_[:, cc, :]), RhT[s].v(np.s_[:, cc, :]), start=True, stop=False)
                        for q in range(2):
                            h = 2 * cc + q
                            g.mm(o, Wpad[s].v(np.s_[:, h, :]), ArbT[s].v(np.s_[:, h, :]), start=False, stop=False)
                            g.mm(o, Vpad.v(np.s_[:, j, h, :]), ArkT[s].v(np.s_[:, h, :]), start=False, stop=(q == 1))
                    g.copy("scalar", yraw.v(np.s_[:, :, j * 128:(j + 1) * 128]), bk.w(bk.t[:, 0:256].rearrange("p (c t) -> p c t", c=2)))
                    s_cur = 1 - s_cur
                    yield

                def rr(gens):
                    gens = list(gens)
                    while gens:
                        for gn in list(gens):
                            try:
                                next(gn)
                            except StopIteration:
                                gens.remove(gn)
                            yield

                for jp in range(0, NCH, NSL):
                    yield from rr([pre_a(jp + s, s) for s in range(NSL)])
                    for it in range(6):
                        yield from rr([pre_dbl(s, it) for s in range(NSL)])
                    yield from rr([pre_b(jp + s, s) for s in range(NSL)])
                    for s in range(NSL):
                        yield from seq(jp + s, s)

                for cc in range(2):
                    t1, t2 = T2["o1"], T2["o2"]
                    yr = yraw.v(np.s_[:, cc, :])
                    bk = g.bank()
                    g.mm(bk.v(), onesblk.v(), yr)
                    g.stt("vector", t1.v(), bk.v(), -1.0 / 64, yr, ALU.mult, ALU.add)
                    g.tt("gpsimd", t2.v(), t1.v(), t1.v(), ALU.mult)
                    yield
                    bk = g.bank()
                    g.mm(bk.v(), onesblk.v(), t2.v())
                    g.ts("vector", t2.v(), bk.v(), 1.0 / 64, ALU.mult, GN_EPS, ALU.add)
                    g.act(t2.v(), t2.v(), AF.Sqrt)
                    g.recip(t2.v(), t2.v())
                    yield
                    g.tt("gpsimd", t1.v(), t1.v(), t2.v(), ALU.mult)
                    g.ts("vector", t1.v(), t1.v(), vec.v(np.s_[:, 5, cc:cc + 1]), ALU.mult, vec.v(np.s_[:, 6, cc:cc + 1]), ALU.add)
                    g.tt("gpsimd", t1.v(), t1.v(), bonusT.v(np.s_[:, cc, :]), ALU.add)
                    g.tt("vector", ygo.v(np.s_[:, cc, :]), t1.v(), gT.v(np.s_[:, cc, :]), ALU.mult)
                    g.dma_out("sync", yg_d[cc * 128:(cc + 1) * 128, t0:t0 + TB], ygo.v(np.s_[:, cc, :]))
                    yield

            def drive(gens, weights=None):
                gens = list(gens)
                weights = list(weights or [1] * len(gens))
                while gens:
                    for gn, w in list(zip(gens, weights)):
                        for _ in range(w):
                            try:
                                next(gn)
                            except StopIteration:
                                k = gens.index(gn)
                                gens.pop(k)
                                weights.pop(k)
                                break

            load_x(0)
            drive([front(0)])
            for blk in range(nblk):
                gs = [back(blk)]
                if blk + 1 < nblk:
                    gs.append(front(blk + 1))
                drive(gs, [BACK_W, 1])
        try:
            emit_all()
        except Stop:
            pass
        if stop:
            g.dma_out("sync", dbg_d[:, :], dbgt.v())
        g.finish()
    return nc


def phase1_inputs(inp, core):
    b, hg = divmod(core, 4)
    cs = slice(hg * 256, (hg + 1) * 256)
    pm = lambda v: np.ascontiguousarray(v.reshape(-1, 128).T)
    gm = np.stack([pm(inp["norm_mix_g"][0])] + [pm(inp["rwkv_mu"][0, i]) for i in range(6)], axis=1)
    wcat = np.concatenate([inp["rwkv_w_r"][0][:, cs], inp["rwkv_w_k"][0][:, cs], inp["rwkv_w_v"][0][:, cs],
                           inp["rwkv_w1"][0], inp["rwkv_a1"][0], inp["rwkv_g1"][0]], axis=1)
    vecs = [inp["rwkv_w0"][0][cs], inp["rwkv_a0"][0][cs], inp["rwkv_k_k"][0][cs], inp["rwkv_k_a"][0][cs],
            inp["rwkv_r_k"][0].reshape(-1)[cs], inp["rwkv_ln_w"][0][cs], inp["rwkv_ln_b"][0][cs]]
    vec = np.stack([pm(v) for v in vecs], axis=1)
    return {
        "x": np.ascontiguousarray(inp["x"][b]),
        "gm": np.ascontiguousarray(gm, dtype=np.float32),
        "wcat": np.ascontiguousarray(wcat, dtype=np.float32),
        "w2": np.ascontiguousarray(inp["rwkv_w2"][0][:, cs]),
        "a2": np.ascontiguousarray(inp["rwkv_a2"][0][:, cs]),
        "g2": np.ascontiguousarray(inp["rwkv_g2"][0][:, cs]),
        "vec": np.ascontiguousarray(vec, dtype=np.float32),
    }


NT2 = 17
NTOK2 = NT2 * 128
DFF = 4096
HG = 512
NGRP = DFF // HG
NEG_BIG = -30000.0
TWO_PI = 6.283185307179586
PI = 3.141592653589793
C1_2PI = 6.28125
C2_2PI = TWO_PI - 6.28125


def build_phase2():
    nc = bass.Bass("TRN2", target_bir_lowering=False)
    dr = lambda n, s, d=F32: nc.dram_tensor(n, s, d, kind="ExternalInput").ap()
    x_d = dr("x", [NTOK2, D])
    yg_d = dr("ygT", [D, NTOK2], BF16)
    pos_d = dr("pos", [1, NTOK2], I32)
    wo0_d = dr("wo0", [D, D])
    win_d = [dr("win0", [D, DFF]), dr("win1", [D, DFF])]
    wout_d = [dr("wout0", [DFF, D]), dr("wout1", [DFF, D])]
    wqkv_d = dr("wqkv", [D, 1280])
    wo1_d = dr("wo1", [D, D])
    gains_d = dr("gains", [128, 3, 8])
    bq_d = dr("bq", [128, 10])
    rowv_d = dr("rowv", [1, 1024 + 1024 + 128 + 16])
    cst_d = dr("cst", [128, 4])
    out_d = nc.dram_tensor("out", [16 * 128, D], F32, kind="ExternalOutput").ap()

    with ExitStack() as st:
        g = G(nc, st)
        sb = g.sb
        identf = sb("identf", [128, 128], F32)
        ident = sb("ident", [128, 128], BF16)
        g.memset("gpsimd", identf.v(), 0.0)
        g.aselect(identf.v(), identf.v(), [[-1, 128]], ALU.not_equal, 1.0, 0, 1)
        g.copy("vector", ident.v(), identf.v())
        rotf = sb("rotf", [128, 128], F32)
        rot = sb("rot", [128, 128], BF16)
        g.memset("gpsimd", rotf.v(), 0.0)
        for blk in range(2):
            o = blk * 64
            sub = rotf.v(np.s_[:, o:o + 32])
            g.aselect(sub, sub, [[-1, 32]], ALU.not_equal, -1.0, -(o + 32), 1)
            sub = rotf.v(np.s_[:, o + 32:o + 64])
            g.aselect(sub, sub, [[-1, 32]], ALU.not_equal, 1.0, -(o + 32) + 32, 1)
        g.copy("vector", rot.v(), rotf.v())
        mscr = sb("scr", [128, 1024], F32)
        maskb = Buf(mscr.t[:, 0:256], "scr_m0")
        mask1 = Buf(mscr.t[:, 256:512], "scr_m1")
        g.memset("gpsimd", maskb.v(), 0.0)
        g.aselect(maskb.v(), maskb.v(), [[1, 256]], ALU.is_gt, NEG_BIG, 0, -1)
        g.aselect(maskb.v(), maskb.v(), [[-1, 256]], ALU.is_ge, NEG_BIG, 128, 1)
        cst = sb("cst", [128, 4], F32)
        g.dma_in("sync", cst.v(), cst_d[:, :])
        g.copy("vector", mask1.v(), maskb.v())
        g.ts("vector", mask1.v(np.s_[:, 0:128]), maskb.v(np.s_[:, 0:128]), cst.v(np.s_[:, 2:3]), ALU.add)
        maskbf = sb("maskbf", [128, 256], BF16)
        mask1bf = sb("mask1bf", [128, 256], BF16)
        g.ts("vector", maskbf.v(), maskb.v(), 8.0, ALU.mult)
        g.ts("vector", mask1bf.v(), mask1.v(), 8.0, ALU.mult)
        ones_row = sb("ones_row", [1, 128], F32)
        g.memset("gpsimd", ones_row.v(), 1.0)
        gains = sb("gains", [128, 3, 8], F32)
        g.dma_in("sync", gains.v(), gains_d[:, :, :])
        bq = sb("bq", [128, 10], F32)
        g.dma_in("sync", bq.v(), bq_d[:, :])
        rowv = sb("rowv", [1, 1024], F32)
        g.dma_in("sync", rowv.v(), rowv_d[0:1, 1024:2048])
        bvb = sb("bvb", [128, 128], F32)
        sinkb = sb("sinkb", [128, 16], F32)
        g.dma_in("sync", bvb.v(), rowv_d[0:1, 2048:2176].to_broadcast([128, 128]))
        g.dma_in("sync", sinkb.v(), rowv_d[0:1, 2176:2192].to_broadcast([128, 16]))

        xres = sb("xres", [128, NT2, 1024], F32)
        hT = sb("hT", [128, 8, NTOK2], BF16)
        uT = sb("uT", [128, 4, NTOK2], BF16)
        arena = sb("arena", [128, 18432], BF16)
        junk = sb("junk", [128, 1024], BF16)
        ms = sb("ms", [128, NT2], F32)
        xn = [sb("xn0", [128, 1024], BF16)] * 2
        scr = mscr
        NU = 4
        asc = sb("asc", [128, NU, 2, 256], F32)
        relu_s = [Buf(asc.t[:, i].rearrange("p a b -> p (a b)").bitcast(BF16)[:, 0:512], f"asc_relu{i}") for i in range(2)]

        def bf(bank):
            return bank.t[:].bitcast(BF16)

        NTILES = [(0, 512), (512, 512), (1024, 512), (1536, 512), (2048, 128)]
        NTILES_M = {0: NTILES, 1: [(128, 512), (640, 512), (1152, 512), (1664, 512)]}

        def load_x_and_yg():
            for kc in range(8):
                g.dma_in("sync", hT.v(np.s_[:, kc, :]), yg_d[kc * 128:(kc + 1) * 128, :])
            for i in range(NT2):
                g.dma_in("sync", xres.v(np.s_[:, i, :], sub=i), x_d[i * 128:(i + 1) * 128, :])

        def wload(dst_ap, key, src_ap):
            g.kb.op("gpsimd", lambda e: e.dma_start(out=dst_ap, in_=src_ap), writes=[key], dma=True)

        def rstd_all(tiles):
            allk = [("ms", i) for i in tiles]
            lo, hi = tiles[0], tiles[-1] + 1
            for i in tiles:
                g.act(junk.v(), xres.v(np.s_[:, i, :], sub=i), AF.Square, scale=1.0 / 32, accum=ms.v(np.s_[:, i:i + 1], sub=i))
            mv = V(ms.t[:, lo:hi], allk)
            g.ts("vector", mv, mv, RMS_EPS, ALU.add)
            g.act(mv, mv, AF.Sqrt)
            g.recip(mv, mv)

        def norm_to_hT(gi, tiles):
            tiles = list(tiles)
            rstd_all(tiles)
            for i in tiles:
                xj = xn[i % 2]
                g.act(xj.v(), xres.v(np.s_[:, i, :], sub=i), AF.Copy, scale=ms.v(np.s_[:, i:i + 1], sub=i))
                bk = g.bank()
                bv = bf(bk)
                for kc in range(8):
                    g.tr(bk.w(bv[:, kc * 128:(kc + 1) * 128]), xj.v(np.s_[:, kc * 128:(kc + 1) * 128]), ident.v())
                g.tt("vector", hT.v(np.s_[:, :, i * 128:(i + 1) * 128]), bk.w(bv.rearrange("p (k t) -> p k t", k=8)),
                     gains.w(gains.t[:, gi, :].unsqueeze(2).to_broadcast([128, 8, 128])), ALU.mult)

        def mlp(layer, first_tile=0):
            win, wout = win_d[layer], wout_d[layer]
            ntiles = [(n0, nn) for (n0, nn) in NTILES_M[first_tile]]
            winv = win.rearrange("(kc p) n -> p kc n", p=128)
            woutv = wout.rearrange("(m p) n -> p m n", p=128)
            def WI(s):
                return arena.t[:, s * 4096:(s + 1) * 4096].rearrange("p (k n) -> p k n", k=8)

            def WO(s):
                return arena.t[:, 8192 + s * 4096:8192 + (s + 1) * 4096].rearrange("p (m n) -> p m n", m=4)

            ri = 0
            for grp in range(NGRP):
                s = grp % 2
                kwi, kwo = ("arena", "wi%d" % s), ("arena", "wo%d" % s)
                wload(WI(s), kwi, winv[:, :, grp * HG:(grp + 1) * HG])
                wload(WO(s), kwo, woutv[:, grp * 4:(grp + 1) * 4, :])
                for m in range(4):
                    for (n0, nn) in ntiles:
                        bk = g.bank()
                        for kc in range(8):
                            g.mm(bk.v(np.s_[:, 0:nn]), V(WI(s)[:, kc, m * 128:(m + 1) * 128], [kwi]), hT.v(np.s_[:, kc, n0:n0 + nn]),
                                 start=(kc == 0), stop=(kc == 7))
                        r = relu_s[ri % 2]
                        ri += 1
                        g.act(V(r.t[:, 0:nn], r.v().keys + [("asc", 0), ("asc", 1)]), bk.v(np.s_[:, 0:nn]), AF.Relu)
                        g.tt("gpsimd", uT.v(np.s_[:, m, n0:n0 + nn]), r.v(np.s_[:, 0:nn]), r.v(np.s_[:, 0:nn]), ALU.mult)
                for i in range(first_tile, NT2):
                    for half in range(2):
                        bk = g.bank()
                        for m in range(4):
                            g.mm(bk.v(), uT.v(np.s_[:, m, i * 128:(i + 1) * 128]), V(WO(s)[:, m, half * 512:(half + 1) * 512], [kwo]),
                                 start=(m == 0), stop=(m == 3))
                        xs = xres.v(np.s_[:, i, half * 512:(half + 1) * 512], sub=i)
                        g.tt("vector", xs, xs, bk.v(), ALU.add)

        load_x_and_yg()
        wo_v = arena.t[:, 0:8192].rearrange("p (k n) -> p k n", k=8)
        g.kb.op("gpsimd", lambda e: e.dma_start(out=wo_v, in_=wo0_d.rearrange("(kc p) n -> p kc n", p=128)),
                writes=[("arena", "wi0"), ("arena", "wi1")], dma=True)
        for i in range(NT2):
            for half in range(2):
                bk = g.bank()
                for kc in range(8):
                    g.mm(bk.v(), hT.v(np.s_[:, kc, i * 128:(i + 1) * 128]),
                         V(wo_v[:, kc, half * 512:(half + 1) * 512], [("arena", "wi0"), ("arena", "wi1")]),
                         start=(kc == 0), stop=(kc == 7))
                xs = xres.v(np.s_[:, i, half * 512:(half + 1) * 512], sub=i)
                g.tt("vector", xs, xs, bk.v(), ALU.add)

        norm_to_hT(0, range(NT2))
        mlp(0)

        norm_to_hT(1, range(NT2))
        wq_v = arena.t[:, 0:10240].rearrange("p (k n) -> p k n", k=8)
        wo1_v = arena.t[:, 10240:18432].rearrange("p (k n) -> p k n", k=8)
        KQ = [("arena", "wi0"), ("arena", "wi1"), ("arena", "wo0")]
        KO = [("arena", "wo0"), ("arena", "wo1"), ("arena", "x")]
        g.kb.op("gpsimd", lambda e: e.dma_start(out=wq_v, in_=wqkv_d.rearrange("(kc p) n -> p kc n", p=128)), writes=KQ, dma=True)
        g.kb.op("gpsimd", lambda e: e.dma_start(out=wo1_v, in_=wo1_d.rearrange("(kc p) n -> p kc n", p=128)), writes=KO, dma=True)
        tabs = uT.t[:].rearrange("p a b -> p (a b)").bitcast(F32)
        cosT = uT.w(tabs[:, 0:NTOK2])
        sinT = uT.w(tabs[:, NTOK2:2 * NTOK2])
        posi = V(scr.t[:, 0:512].bitcast(I32), [("scr", "a"), ("scr_m0", None), ("scr_m1", None)])
        for (n0, nn) in NTILES:
            pch = V(posi.ap[:, 0:nn], posi.keys)
            ach = scr.v(np.s_[:, 512:512 + nn], sub="b")
            g.dma_in("sync", pch, pos_d[0:1, n0:n0 + nn].to_broadcast([128, nn]))
            g.copy("vector", ach, pch)
            g.ts("vector", ach, ach, cst.v(np.s_[:, 0:1]), ALU.mult)
            sch = V(sinT.ap[:, n0:n0 + nn], sinT.keys)
            cch = V(cosT.ap[:, n0:n0 + nn], cosT.keys)
            T1 = V(xn[0].t[:].bitcast(F32)[:, 0:nn], [("xn0", None)])
            A2 = V(junk.t[:].bitcast(F32)[:, 0:nn], [("junk", None)])
            TI = pch
            for (src, dst, shift) in ((ach, sch, 0.0), (ach, cch, 0.5 * PI)):
                if shift:
                    g.ts("vector", A2, src, shift, ALU.add)
                    src = A2
                g.ts("vector", T1, src, 1.0 / TWO_PI, ALU.mult)
                g.copy("vector", TI, T1)
                g.copy("vector", T1, TI)
                g.stt("vector", dst, T1, -C1_2PI, src, ALU.mult, ALU.add)
                g.stt("vector", dst, T1, -C2_2PI, dst, ALU.mult, ALU.add)
                g.ts("vector", dst, dst, -PI, ALU.max, PI, ALU.min)
                g.act(dst, dst, AF.Sin)

        kr = sb("kr", [128, 2, NTOK2], BF16)
        vpad = [sb(f"vpad{i}", [128, 2, 2, 128], BF16) for i in range(3)]
        for v_ in vpad:
            g.memset("gpsimd", v_.v(), 0.0)
        qb16 = sb("qb16", [128, 512], BF16)
        SCALE = 0.125
        negsink = sb("negsink", [128, 16], F32)
        esink = sb("esink", [128, 16], F32)
        g.ts("vector", negsink.v(), sinkb.v(), -1.0, ALU.mult)
        g.act(esink.v(), sinkb.v(), AF.Exp)

        def rope_evac(bk, nn, bias_col, n0, dst, qf, qb, r2):
            g.act(qf, bk.v(np.s_[:, 0:nn]), AF.Identity, bias=bias_col)
            g.copy("gpsimd", qb, qf)
            b2 = g.bank()
            g.mm(b2.v(np.s_[:, 0:nn]), rot.v(), qb)
            g.tt("vector", r2, b2.v(np.s_[:, 0:nn]), V(sinT.ap[:, n0:n0 + nn], sinT.keys), ALU.mult)
            g.tt("gpsimd", qf, qf, V(cosT.ap[:, n0:n0 + nn], cosT.keys), ALU.mult)
            g.tt("vector", dst, qf, r2, ALU.add)

        wkd_ap = asc.t[:].rearrange("p a b c -> p (a b c)")[:, 0:1024].bitcast(BF16).rearrange("p (k j c) -> p k j c", k=8, j=2)
        WK = [("asc", u) for u in range(NU)] + [("asc_relu0", None), ("asc_relu1", None)]
        for j in range(2):
            for dup in range(2):
                g.copy("vector", V(wkd_ap[:, :, j, dup * 64:(dup + 1) * 64], WK), V(wq_v[:, :, 1024 + j * 64:1024 + (j + 1) * 64], KQ))
        for j in range(2):
            for (n0, nn) in NTILES:
                bk = g.bank()
                for kc in range(8):
                    g.mm(bk.v(np.s_[:, 0:nn]), V(wkd_ap[:, kc, j, :], WK), hT.v(np.s_[:, kc, n0:n0 + nn]), start=(kc == 0), stop=(kc == 7))
                rope_evac(bk, nn, bq.v(np.s_[:, 8 + j:9 + j]), n0, kr.v(np.s_[:, j, n0:n0 + nn]),
                          scr.v(np.s_[:, 0:nn], sub="a"), qb16.v(np.s_[:, 0:nn]), scr.v(np.s_[:, 512:512 + nn], sub="b"))

        def make_vpad(i):
            vp = vpad[i % 3]
            bk = g.bank()
            for kc in range(8):
                g.mm(bk.v(np.s_[:, 0:128]), hT.v(np.s_[:, kc, i * 128:(i + 1) * 128]), V(wq_v[:, kc, 1152:1280], KQ), start=(kc == 0), stop=(kc == 7))
            for q2 in range(2):
                g.tt("vector", vp.v(np.s_[:, :, q2, q2 * 64:(q2 + 1) * 64]), bk.w(bk.t[:, 0:128].rearrange("p (j d) -> p j d", j=2)),
                     bvb.w(bvb.t[:].rearrange("p (j d) -> p j d", j=2)), ALU.add)
            return vp

        qr2 = [sb(f"qr{i}", [128, 8, 128], BF16) for i in range(2)]
        oT = sb("oT", [128, 8, 128], BF16)
        pn = [sb(f"pn{i}", [128, 2, 256], BF16) for i in range(NU)]
        pT = [sb(f"pT{i}", [128, 2, 2, 128], BF16) for i in range(NU)]
        stat = [sb(f"stat{i}", [128, 8], F32) for i in range(NU)]
        def prep(i):
            make_vpad(i)
            qr = qr2[i % 2]
            yield
            for hh in range(2):
                bk = g.bank()
                for a4 in range(4):
                    hp = hh * 4 + a4
                    for kc in range(8):
                        g.mm(bk.v(np.s_[:, a4 * 128:(a4 + 1) * 128]), V(wq_v[:, kc, hp * 128:(hp + 1) * 128], KQ),
                             hT.v(np.s_[:, kc, i * 128:(i + 1) * 128]), start=(kc == 0), stop=(kc == 7))
                qf = scr.v(np.s_[:, 0:512], sub="a")
                r2 = scr.v(np.s_[:, 512:1024], sub="b")
                qb = qb16.v()
                qf3 = V(scr.t[:, 0:512].rearrange("p (a t) -> p a t", a=4), [("scr", "a")])
                r23 = V(scr.t[:, 512:1024].rearrange("p (a t) -> p a t", a=4), [("scr", "b")])
                g.tt("vector", qf3, bk.w(bk.t[:].rearrange("p (a t) -> p a t", a=4)),
                     bq.w(bq.t[:, hh * 4:hh * 4 + 4].unsqueeze(2).to_broadcast([128, 4, 128])), ALU.add)
                g.copy("gpsimd", qb, qf)
                b2 = g.bank()
                g.mm(b2.v(), rot.v(), qb)
                sin_b = V(sinT.ap[:, i * 128:(i + 1) * 128].unsqueeze(1).to_broadcast([128, 4, 128]), sinT.keys)
                cos_b = V(cosT.ap[:, i * 128:(i + 1) * 128].unsqueeze(1).to_broadcast([128, 4, 128]), cosT.keys)
                g.tt("vector", r23, b2.w(b2.t[:].rearrange("p (a t) -> p a t", a=4)), sin_b, ALU.mult)
                g.tt("gpsimd", qf3, qf3, cos_b, ALU.mult)
                g.tt("vector", qr.v(np.s_[:, hh * 4:hh * 4 + 4, :]), qf3, r23, ALU.add)
                yield

        def attn(i):
            vps = [vpad[(i - 1) % 3], vpad[i % 3]]
            qr = qr2[i % 2]
            mk = mask1bf if i == 1 else maskbf
            for j in range(2):
                sbank = {}
                for gp in range(2):
                    hp0 = 4 * j + 2 * gp
                    bks = [g.bank(), g.bank()]
                    mk2 = mk.w(mk.t[:].unsqueeze(1).to_broadcast([128, 2, 256]))
                    for q2 in range(2):
                        g.mm(bks[q2].w(bks[q2].t[:].rearrange("p (a k) -> p a k", a=2)), ident.v(), mk2, start=True, stop=False)
                    for a in range(2):
                        for q2 in range(2):
                            o = bks[q2].v(np.s_[:, a * 256:(a + 1) * 256])
                            g.mm(o, qr.v(np.s_[q2 * 64:(q2 + 1) * 64, hp0 + a, :]),
                                 kr.v(np.s_[q2 * 64:(q2 + 1) * 64, j, (i - 1) * 128:(i + 1) * 128]), start=False, stop=True)
                    for q2 in range(2):
                        sbank[gp * 2 + q2] = bks[q2]
                UN = range(4)
                yield
                for u in UN:
                    gp, q2 = divmod(u, 2)
                    st_ = stat[u]
                    ps3 = sbank[u].w(sbank[u].t[:].rearrange("p (a k) -> p a k", a=2))
                    g.kb.op("vector", lambda e, st_=st_, ps3=ps3: e.tensor_reduce(out=st_.t[:, 0:2], in_=ps3.ap, axis=mybir.AxisListType.X, op=ALU.max),
                            reads=ps3.keys, writes=st_.v().keys)
                    h0 = 2 * (4 * j + 2 * gp) + q2
                    g.stt("vector", st_.v(np.s_[:, 2:4]), st_.v(np.s_[:, 0:2]), -SCALE, negsink.w(negsink.t[:, h0:h0 + 3:2]), ALU.mult, ALU.min)
                yield
                for u in UN:
                    st_ = stat[u]
                    for a in range(2):
                        g.act(V(asc.t[:, u, a, :], [("asc", u)]), sbank[u].v(np.s_[:, a * 256:(a + 1) * 256]), AF.Exp, scale=SCALE,
                              bias=st_.v(np.s_[:, 2 + a:3 + a]), accum=st_.v(np.s_[:, 4 + a:5 + a]))
                    g.act(st_.v(np.s_[:, 6:8]), st_.v(np.s_[:, 2:4]), AF.Exp)
                yield
                for u in UN:
                    gp, q2 = divmod(u, 2)
                    st_ = stat[u]
                    h0 = 2 * (4 * j + 2 * gp) + q2
                    g.tt("vector", st_.v(np.s_[:, 6:8]), st_.v(np.s_[:, 6:8]), esink.w(esink.t[:, h0:h0 + 3:2]), ALU.mult)
                    g.tt("vector", st_.v(np.s_[:, 4:6]), st_.v(np.s_[:, 4:6]), st_.v(np.s_[:, 6:8]), ALU.add)
                    g.recip(st_.v(np.s_[:, 4:6]), st_.v(np.s_[:, 4:6]))
                for u in UN:
                    st_ = stat[u]
                    g.tt("gpsimd", pn[u].v(), V(asc.t[:, u], [("asc", u)]), st_.w(st_.t[:, 4:6].unsqueeze(2).to_broadcast([128, 2, 256])), ALU.mult)
                yield
                tbs = {}
                for u in UN:
                    tb = g.bank()
                    tv = bf(tb)
                    for a in range(2):
                        for kb in range(2):
                            sl = (a * 2 + kb) * 128
                            g.tr(tb.w(tv[:, sl:sl + 128]), pn[u].v(np.s_[:, a, kb * 128:(kb + 1) * 128]), ident.v())
                    tbs[u] = (tb, tv)
                for u in UN:
                    tb, tv = tbs[u]
                    g.copy("scalar" if u % 2 == 0 else "vector", pT[u].v(), tb.w(tv[:, 0:512].rearrange("p (a k q) -> p a k q", a=2, k=2)))
                yield
                for gp in range(2):
                    hp0 = 4 * j + 2 * gp
                    obk = g.bank()
                    first = True
                    for q2 in range(2):
                        for kb in range(2):
                            g.mm(obk.w(obk.t[:, 0:256].rearrange("p (a q) -> p a q", a=2)), vps[kb].v(np.s_[:, j, q2, :]),
                                 pT[gp * 2 + q2].v(np.s_[:, :, kb, :]), start=first, stop=(q2 == 1 and kb == 1))
                            first = False
                    g.copy("scalar", oT.v(np.s_[:, hp0:hp0 + 2, :]), obk.w(obk.t[:, 0:256].rearrange("p (a q) -> p a q", a=2)))
            for half in range(2):
                bk = g.bank()
                for hp in range(8):
                    g.mm(bk.v(), oT.v(np.s_[:, hp, :]), V(wo1_v[:, hp, half * 512:(half + 1) * 512], KO), start=(hp == 0), stop=False)
                g.mm(bk.v(), ones_row.v(), rowv.v(np.s_[0:1, half * 512:(half + 1) * 512]), start=False, stop=True)
                xs = xres.v(np.s_[:, i, half * 512:(half + 1) * 512], sub=i)
                g.tt("vector", xs, xs, bk.v(), ALU.add)

            yield

        def drive2(gens):
            gens = list(gens)
            while gens:
                for gn in list(gens):
                    try:
                        next(gn)
                    except StopIteration:
                        gens.remove(gn)

        make_vpad(0)
        drive2([prep(1)])
        for i in range(1, NT2):
            gs = [attn(i)]
            if i + 1 < NT2:
                gs.append(prep(i + 1))
            drive2(gs)

        norm_to_hT(2, range(1, NT2))
        mlp(1, first_tile=1)

        gfb = V(asc.t[:].rearrange("p a b c -> p (a b c)")[:, 0:1024], [("asc", u) for u in range(NU)] + [("asc_relu0", None), ("asc_relu1", None)])
        g.dma_in("sync", gfb, rowv_d[0:1, 0:1024].to_broadcast([128, 1024]))
        rstd_all(list(range(1, NT2)))
        for i in range(1, NT2):
            o_ = V(scr.t[:], [("scr", "a"), ("scr", "b")])
            g.stt("vector", o_, xres.v(np.s_[:, i, :], sub=i), ms.v(np.s_[:, i:i + 1], sub=i), gfb, ALU.mult, ALU.mult)
            g.dma_out("sync", out_d[(i - 1) * 128:i * 128, :], o_)
        g.finish()
    return nc


def _pm(v):
    return np.ascontiguousarray(np.asarray(v).reshape(-1, 128).T)


def phase2_inputs(inp, ygT_full, core):
    b, tq = divmod(core, 4)
    t0 = tq * 2048
    x = np.zeros((NTOK2, D), np.float32)
    yg = np.zeros((D, NTOK2), ml_dtypes.bfloat16)
    pos = np.zeros((1, NTOK2), np.int32)
    lo = t0 - 128
    if tq > 0:
        x[:] = inp["x"][b, lo:lo + NTOK2]
        yg[:] = ygT_full[b][:, lo:lo + NTOK2]
        pos[0] = inp["positions"][b, lo:lo + NTOK2]
    else:
        x[128:] = inp["x"][b, 0:2048]
        yg[:, 128:] = ygT_full[b][:, 0:2048]
        pos[0, 128:] = inp["positions"][b, 0:2048]
    gains = np.stack([_pm(inp["norm_mlp_g"][0]), _pm(inp["norm_mix_g"][1]), _pm(inp["norm_mlp_g"][1])], axis=1)
    bqkv = inp["attn_b_qkv"][0]
    bq = np.zeros((128, 10), np.float32)
    bq[:, 0:8] = _pm(bqkv[0:1024])
    for j in range(2):
        bk = bqkv[1024 + j * 64:1024 + (j + 1) * 64]
        bq[:, 8 + j] = np.concatenate([bk, bk])
    rowv = np.concatenate([inp["norm_final_g"], inp["attn_b_o"][0], bqkv[1152:1280], inp["attn_sinks"][0]])[None, :]
    cst = np.zeros((128, 4), np.float32)
    p = np.arange(128)
    cst[:, 0] = (10000.0 ** (-(np.arange(0, 64, 2, dtype=np.float32)) / 64.0))[p % 32]
    cst[:, 1] = np.where((p % 64) < 32, 1.0, 1.0)
    cst[:, 2] = 0.0 if tq > 0 else NEG_BIG
    return {
        "x": x, "ygT": yg, "pos": pos,
        "wo0": np.ascontiguousarray(inp["rwkv_w_o"][0]),
        "win0": np.ascontiguousarray(inp["mlp_w_in"][0]), "win1": np.ascontiguousarray(inp["mlp_w_in"][1]),
        "wout0": np.ascontiguousarray(inp["mlp_w_out"][0]), "wout1": np.ascontiguousarray(inp["mlp_w_out"][1]),
        "wqkv": np.ascontiguousarray(inp["attn_w_qkv"][0]), "wo1": np.ascontiguousarray(inp["attn_w_o"][0]),
        "gains": np.ascontiguousarray(gains, dtype=np.float32), "bq": bq,
        "rowv": np.ascontiguousarray(rowv, dtype=np.float32), "cst": cst,
    }


_NC_CACHE = {}


def kernel(**inputs):
    inp = {k: np.asarray(v) for k, v in inputs.items()}
    if "p1" not in _NC_CACHE:
        _NC_CACHE["p1"] = build_phase1()
        _NC_CACHE["p2"] = build_phase2()
    r1 = run_bass_kernel_spmd(_NC_CACHE["p1"], [phase1_inputs(inp, c) for c in range(8)], core_ids=list(range(8)))
    ygT = np.zeros((2, D, S_LEN), ml_dtypes.bfloat16)
    for c in range(8):
        b, hg = divmod(c, 4)
        ygT[b, hg * 256:(hg + 1) * 256] = r1.results[c]["yg"]
    r2 = run_bass_kernel_spmd(_NC_CACHE["p2"], [phase2_inputs(inp, ygT, c) for c in range(8)], core_ids=list(range(8)))
    out = np.zeros((2, S_LEN, D), np.float32)
    for c in range(8):
        b, tq = divmod(c, 4)
        out[b, tq * 2048:(tq + 1) * 2048] = r2.results[c]["out"]
    return out
```

```python
import numpy as np
import ml_dtypes
from contextlib import ExitStack
import concourse.bass as bass
import concourse.mybir as mybir
from concourse.bass_utils import run_bass_kernel_spmd

F32 = mybir.dt.float32
BF16 = mybir.dt.bfloat16
I32 = mybir.dt.int32
AF = mybir.ActivationFunctionType
ALU = mybir.AluOpType

EPOCH = 4000
FUSE_WAIT = True
DMA_ROT = 8
NEG_C = -0.6065306597126334


class KB:
    ENGS = ("tensor", "vector", "scalar", "gpsimd", "sync")

    def __init__(self, nc, stack):
        self.nc = nc
        self.stack = stack
        self.streams = {e: [] for e in self.ENGS}
        self.count = {e: 0 for e in self.ENGS}
        self.sems = {e: [] for e in self.ENGS}
        self.dma_count = {e: 0 for e in self.ENGS}
        self.dma_sems = {e: [] for e in self.ENGS}
        self.waited = {e: {} for e in self.ENGS}
        self.last_w = {}
        self.readers = {}
        self.order = 0

    def _newsem(self, name):
        return self.stack.enter_context(self.nc.semaphore(name))

    def _compute_token(self, e):
        n = self.count[e]
        ep, idx = divmod(n, EPOCH)
        while len(self.sems[e]) <= ep:
            self.sems[e].append(self._newsem(f"s_{e}_{len(self.sems[e])}"))
        self.count[e] = n + 1
        return (self.sems[e][ep], idx + 1, 1, e)

    def _dma_token(self, e):
        n = self.dma_count[e]
        if not self.dma_sems[e]:
            self.dma_sems[e] = [self._newsem(f"d_{e}_{i}") for i in range(DMA_ROT)]
        self.dma_count[e] = n + 1
        return (self.dma_sems[e][n % DMA_ROT], 16 * (n // DMA_ROT + 1), 16, "dma_" + e)

    def op(self, e, fn, reads=(), writes=(), dma=False):
        deps = []
        for k in reads:
            t = self.last_w.get(k)
            if t is not None:
                deps.append(t)
        for k in writes:
            t = self.last_w.get(k)
            if t is not None:
                deps.append(t)
            deps.extend(self.readers.get(k, ()))
        know = self.waited[e]
        cand = {}
        for t in deps:
            sem, val, _inc, src, _vc, _ord = t
            if src == e and not dma and e == "tensor":
                continue
            sid = id(sem)
            if know.get(sid, 0) >= val:
                continue
            if sid not in cand or cand[sid][1] < val:
                cand[sid] = t
        ww = []
        for t in sorted(cand.values(), key=lambda t: -t[5]):
            sem, val, _inc, _src, vc, _ord = t
            sid = id(sem)
            if know.get(sid, 0) >= val:
                continue
            ww.append((sem, val))
            know[sid] = val
            for s2, v2 in vc.items():
                if know.get(s2, 0) < v2:
                    know[s2] = v2
        if dma:
            n = self.dma_count[e]
            if n >= DMA_ROT:
                sem = self.dma_sems[e][n % DMA_ROT]
                val = 16 * (n // DMA_ROT)
                if know.get(id(sem), 0) < val:
                    know[id(sem)] = val
                    ww.append((sem, val))
        base = self._dma_token(e) if dma else self._compute_token(e)
        self.order += 1
        vc = dict(know)
        tok = base + (vc, self.order)
        if not dma:
            vc[id(base[0])] = max(vc.get(id(base[0]), 0), base[1])
        self.streams[e].append((ww, fn, tok))
        for k in reads:
            self.readers.setdefault(k, []).append(tok)
        for k in writes:
            self.last_w[k] = tok
            self.readers[k] = []
        return tok

    def wait_tokens(self, e, toks):
        wd = self.waited[e]
        waits = []
        for (sem, val, *_rest) in toks:
            if wd.get(id(sem), 0) >= val:
                continue
            wd[id(sem)] = val
            waits.append((sem, val))
        self.streams[e].append((waits, None, None))

    def emit(self):
        nc = self.nc
        with nc.Block() as block:
            def mk(e):
                def body(eng):
                    for waits, fn, tok in self.streams[e]:
                        if fn is None or not FUSE_WAIT or not waits:
                            for sem, val in waits:
                                eng.wait_ge(sem, val)
                            if fn is not None:
                                fn(eng).then_inc(tok[0], tok[2])
                        else:
                            for sem, val in waits[:-1]:
                                eng.wait_ge(sem, val)
                            ins = fn(eng)
                            ins._wait_ge(waits[-1][0], waits[-1][1])
                            ins.then_inc(tok[0], tok[2])
                return body
            for e in self.ENGS:
                if self.streams[e]:
                    getattr(block, e)(mk(e))


class V:
    __slots__ = ("ap", "keys")

    def __init__(self, ap, keys):
        self.ap = ap
        self.keys = keys


class Buf:
    def __init__(self, t, name):
        self.t = t
        self.name = name

    def v(self, idx=None, sub=None):
        ap = self.t[idx] if idx is not None else self.t[:]
        return V(ap, [(self.name, sub)])

    def w(self, ap, sub=None):
        return V(ap, [(self.name, sub)])


class G:
    def __init__(self, nc, st):
        self.nc = nc
        self.st = st
        self.kb = KB(nc, st)
        self.banks = [Buf(st.enter_context(nc.psum_tensor(f"psb{i}", [128, 512], F32)), f"ps{i}") for i in range(8)]
        self.bank_i = 0
        self.out_tokens = []

    def sb(self, name, shape, dt):
        return Buf(self.st.enter_context(self.nc.sbuf_tensor("sb_" + name, shape, dt)), name)

    def bank(self):
        b = self.banks[self.bank_i % 8]
        self.bank_i += 1
        return b

    @staticmethod
    def _k(vs):
        ks = []
        for v in vs:
            if isinstance(v, V):
                ks.extend(v.keys)
        return ks

    def mm(self, out, lhsT, rhs, start=True, stop=True):
        return self.kb.op("tensor", lambda e: e.matmul(out.ap, lhsT=lhsT.ap, rhs=rhs.ap, start=start, stop=stop),
                          reads=self._k([lhsT, rhs]), writes=out.keys)

    def tr(self, out, in_, ident):
        return self.kb.op("tensor", lambda e: e.transpose(out=out.ap, in_=in_.ap, identity=ident.ap),
                          reads=self._k([in_, ident]), writes=out.keys)

    def act(self, out, in_, func, bias=None, scale=1.0, accum=None, eng="scalar"):
        kw = {}
        if bias is not None:
            kw["bias"] = bias.ap if isinstance(bias, V) else bias
        if accum is not None:
            kw["accum_out"] = accum.ap
        sc = scale.ap if isinstance(scale, V) else scale
        return self.kb.op("scalar", lambda e: e.activation(out=out.ap, in_=in_.ap, func=func, scale=sc, **kw),
                          reads=self._k([in_, bias, scale]), writes=self._k([out, accum]))

    def tt(self, eng, out, a, b, op):
        return self.kb.op(eng, lambda e: e.tensor_tensor(out=out.ap, in0=a.ap, in1=b.ap, op=op),
                          reads=self._k([a, b]), writes=out.keys)

    def ts(self, eng, out, a, s1, op0, s2=None, op1=None):
        s1a = s1.ap if isinstance(s1, V) else s1
        s2a = s2.ap if isinstance(s2, V) else s2
        if op1 is None:
            fn = lambda e: e.tensor_scalar(out=out.ap, in0=a.ap, scalar1=s1a, scalar2=None, op0=op0)
        else:
            fn = lambda e: e.tensor_scalar(out=out.ap, in0=a.ap, scalar1=s1a, scalar2=s2a, op0=op0, op1=op1)
        return self.kb.op(eng, fn, reads=self._k([a, s1, s2]), writes=out.keys)

    def stt(self, eng, out, in0, scalar, in1, op0, op1):
        sa = scalar.ap if isinstance(scalar, V) else scalar
        return self.kb.op(eng, lambda e: e.scalar_tensor_tensor(out=out.ap, in0=in0.ap, scalar=sa, in1=in1.ap, op0=op0, op1=op1),
                          reads=self._k([in0, scalar, in1]), writes=out.keys)

    def copy(self, eng, out, in_):
        if eng == "scalar":
            return self.act(out, in_, AF.Copy)
        return self.kb.op(eng, lambda e: e.tensor_copy(out=out.ap, in_=in_.ap), reads=in_.keys, writes=out.keys)

    def memset(self, eng, out, val):
        return self.kb.op(eng, lambda e: e.memset(out.ap, val), writes=out.keys)

    def recip(self, out, in_):
        return self.kb.op("vector", lambda e: e.reciprocal(out=out.ap, in_=in_.ap), reads=in_.keys, writes=out.keys)

    def scan(self, out, d0, d1, init, op0, op1):
        return self.kb.op("vector", lambda e: e.tensor_tensor_scan(out=out.ap, data0=d0.ap, data1=d1.ap, initial=init, op0=op0, op1=op1),
                          reads=self._k([d0, d1]), writes=out.keys)

    def aselect(self, out, in_, pattern, cmp, fill, base, cm):
        return self.kb.op("gpsimd", lambda e: e.affine_select(out=out.ap, in_=in_.ap, pattern=pattern, compare_op=cmp,
                                                               fill=fill, base=base, channel_multiplier=cm),
                          reads=in_.keys, writes=out.keys)

    def dma_in(self, eng, out, in_ap, **kw):
        return self.kb.op(eng, lambda e: e.dma_start(out=out.ap, in_=in_ap, **kw), writes=out.keys, dma=True)

    def dma_out(self, eng, out_ap, in_, final=True):
        t = self.kb.op(eng, lambda e: e.dma_start(out=out_ap, in_=in_.ap), reads=in_.keys, dma=True)
        if final:
            self.out_tokens.append(t)
        return t

    def finish(self):
        self.kb.wait_tokens("sync", self.out_tokens)
        self.kb.emit()


S_LEN = 8192
D = 1024
TB = 512
NCH = TB // 128
GN_EPS = 64e-5
RMS_EPS = 1e-5
PROJ = [("r", 0, 256, 0), ("k", 2, 256, 256), ("v", 3, 256, 512), ("w1", 1, 64, 768), ("a1", 4, 64, 832), ("g1", 5, 160, 896)]
NCOL = 1056


BACK_W = 4


def build_phase1(n_tok=S_LEN, stop=None, stopargs=()):
    nc = bass.Bass("TRN2", target_bir_lowering=False)
    dr = lambda n, s, d=F32: nc.dram_tensor(n, s, d, kind="ExternalInput").ap()
    x_d = dr("x", [n_tok, D])
    gm_d = dr("gm", [128, 7, 8])
    wcat_d = dr("wcat", [D, NCOL])
    w2_d = dr("w2", [64, 256])
    a2_d = dr("a2", [64, 256])
    g2_d = dr("g2", [160, 256])
    vec_d = dr("vec", [128, 7, 2])
    yg_d = nc.dram_tensor("yg", [256, n_tok], BF16, kind="ExternalOutput").ap()
    dbg_d = nc.dram_tensor("dbg", [128, 2560], F32, kind="ExternalOutput").ap() if stop else None
    nblk = n_tok // TB

    class Stop(Exception):
        pass

    with ExitStack() as st:
        g = G(nc, st)
        sb = g.sb
        dbgt = sb("dbgt", [128, 2560], F32) if stop else None
        dbg_off = [0]

        def dump(v, n):
            o = dbg_off[0]
            p = v.ap.shape[0]
            g.copy("vector", dbgt.w(dbgt.t[0:p, o:o + n]), v)
            dbg_off[0] = o + n

        def chk(name):
            if stop == name:
                raise Stop()
        def emit_all():
            identf = sb("identf", [128, 128], F32)
            ident = sb("ident", [128, 128], BF16)
            ident4 = sb("ident4", [128, 4, 128], BF16)
            onesblk = sb("onesblk", [128, 128], F32)
            m_su = sb("m_su", [128, 4, 128], BF16)
            m_u = sb("m_u", [128, 4, 128], BF16)
            m_sl = sb("m_sl", [128, 4, 128], BF16)
            rmask = sb("rmask", [128, TB], F32)
            g.memset("gpsimd", identf.v(), 0.0)
            g.aselect(identf.v(), identf.v(), [[-1, 128]], ALU.not_equal, 1.0, 0, 1)
            g.copy("vector", ident.v(), identf.v())
            for h in range(4):
                g.copy("vector", ident4.v(np.s_[:, h, :]), identf.v())
            g.memset("gpsimd", onesblk.v(), 0.0)
            g.memset("gpsimd", onesblk.v(np.s_[0:64, 0:64]), 1.0)
            g.memset("gpsimd", onesblk.v(np.s_[64:128, 64:128]), 1.0)
            for m, cm, pat, cmp in ((m_su, -1, 1, ALU.is_gt), (m_u, -1, 1, ALU.is_ge), (m_sl, 1, -1, ALU.is_gt)):
                g.memset("gpsimd", m.v(), 1.0)
                g.aselect(m.v(), m.v(), [[0, 4], [pat, 128]], cmp, 0.0, 0, cm)
            g.memset("gpsimd", rmask.v(), 1.0)
            g.memset("gpsimd", rmask.w(rmask.t[:].rearrange("p (c t) -> p c t", t=128)[:, :, 0:1]), 0.0)

            if stop == "const":
                dump(identf.v(), 128); dump(m_su.v(np.s_[:, 1, :]), 128); dump(m_sl.v(np.s_[:, 2, :]), 128); dump(m_u.v(np.s_[:, 3, :]), 128)
                dump(onesblk.v(), 128); dump(rmask.v(), 512)
            chk("const")
            gm = sb("gm", [128, 7, 8], F32)
            coefA = sb("coefA", [128, 6, 8], F32)
            coefB = sb("coefB", [128, 6, 8], F32)
            vec = sb("vec", [128, 7, 2], F32)
            WA = sb("WA", [128, 8, NCOL], BF16)
            WB = sb("WB", [128, 8, NCOL], BF16)
            w2b = sb("w2b", [64, 256], BF16)
            a2b = sb("a2b", [128, 256], BF16)
            g2a = sb("g2a", [128, 256], BF16)
            g2b = sb("g2b", [32, 256], BF16)
            xin = sb("xin", [128, 4, 1024], F32)
            g.dma_in("sync", gm.v(), gm_d[:, :, :])
            g.dma_in("sync", vec.v(), vec_d[:, :, :])
            g.dma_in("gpsimd", w2b.v(), w2_d[:, :])
            g.dma_in("gpsimd", a2b.v(np.s_[64:128, :]), a2_d[:, :])
            g.dma_in("gpsimd", g2a.v(), g2_d[0:128, :])
            g.dma_in("gpsimd", g2b.v(), g2_d[128:160, :])
            g0b = gm.w(gm.t[:, 0:1, :].to_broadcast([128, 6, 8]))
            g.tt("vector", coefB.v(), gm.v(np.s_[:, 1:7, :]), g0b, ALU.mult)
            g.tt("vector", coefA.v(), g0b, coefB.v(), ALU.subtract)
            stage = xin
            wv = wcat_d.rearrange("(kc p) n -> p kc n", p=128)
            XK = [("xin", j) for j in range(4)]
            for (nm, mi, ncol, off) in PROJ:
                sv = stage.t[:].rearrange("p a b -> p (a b)")[:, 0:8 * ncol].rearrange("p (k n) -> p k n", k=8)
                g.kb.op("sync", lambda e, sv=sv, off=off, ncol=ncol: e.dma_start(out=sv, in_=wv[:, :, off:off + ncol]), writes=XK, dma=True)
                ca = coefA.w(coefA.t[:, mi, :].unsqueeze(2).to_broadcast([128, 8, ncol]))
                cb = coefB.w(coefB.t[:, mi, :].unsqueeze(2).to_broadcast([128, 8, ncol]))
                g.tt("vector", WA.v(np.s_[:, :, off:off + ncol]), V(sv, XK), ca, ALU.mult)
                g.tt("gpsimd", WB.v(np.s_[:, :, off:off + ncol]), V(sv, XK), cb, ALU.mult)

            if stop == "weights":
                dump(WA.v(np.s_[:, 3, 0:512]), 512); dump(WB.v(np.s_[:, 7, 544:1056]), 512); dump(coefA.v(np.s_[:, 2, :]), 8)
            chk("weights")
            ms = sb("ms", [128, 4], F32)
            xn = [sb(f"xn{i}", [128, 1024], BF16) for i in range(2)]
            hT = sb("hT", [128, 8, TB + 1], BF16)
            T = {n: sb("t_" + n, [128, TB], F32) for n in
                 ("rT", "kraw", "aT", "sgw", "cs", "Er", "En", "kk", "kp", "beta", "t1", "t2", "EC")}
            junk = Buf(T["EC"].t[:].bitcast(BF16), "t_EC")
            tw = sb("tw", [128, TB], BF16)
            sg1a = sb("sg1a", [128, TB], BF16)
            sg1b = sb("sg1b", [32, TB], BF16)
            vT = sb("vT", [128, 2, TB], F32)
            gT2 = [sb(f"gT{i}", [128, 2, TB], F32) for i in range(2)]
            bonusT2 = [sb(f"bonusT{i}", [128, 2, TB], F32) for i in range(2)]
            PCs2 = [sb(f"PCs{i}", [128, 2, NCH], F32) for i in range(2)]
            rt2 = [sb(f"rt_{i}", [128, 2, TB], BF16) for i in range(2)]
            kt2 = [sb(f"kt_{i}", [128, 2, TB], BF16) for i in range(2)]
            at2 = [sb(f"at_{i}", [128, 2, TB], BF16) for i in range(2)]
            bt2 = [sb(f"bt_{i}", [128, 2, TB], BF16) for i in range(2)]
            T2 = {n: sb("t2_" + n, [128, TB], F32) for n in ("o1", "o2")}
            khT = sb("khT", [128, 2, TB], BF16)
            bhT = sb("bhT", [128, 2, TB], BF16)
            XR2 = [sb(f"XR{i}", [128, NCH, 4, 128], BF16) for i in range(2)]
            Kpad2 = [sb(f"Kpad{i}", [128, NCH, 4, 128], BF16) for i in range(2)]
            Bpad2 = [sb(f"Bpad{i}", [128, NCH, 4, 128], BF16) for i in range(2)]
            Vpad2 = [sb(f"Vpad{i}", [128, NCH, 4, 128], BF16) for i in range(2)]
            NSL = 2
            Pb = [[sb(f"P{s}{i}", [128, 4, 128], BF16) for i in range(2)] for s in range(NSL)]
            SUt = [sb(f"SU{s}", [128, 2, 4, 128], BF16) for s in range(NSL)]
            UUt = [sb(f"UU{s}", [128, 2, 4, 128], BF16) for s in range(NSL)]
            PTb = [[Buf(SUt[s].t[:, 0], f"PT{s}0"), sb(f"PT{s}1", [128, 4, 128], BF16)] for s in range(NSL)]
            NTb = [[sb(f"NT{s}{i}", [128, 4, 128], BF16) for i in range(2)] for s in range(NSL)]
            AakT = [Buf(SUt[s].t[:, 1], f"AakT{s}") for s in range(NSL)]
            ArbT = [Buf(UUt[s].t[:, 0], f"ArbT{s}") for s in range(NSL)]
            ArkT = [Buf(UUt[s].t[:, 1], f"ArkT{s}") for s in range(NSL)]
            Apad = [sb(f"Apad{s}", [128, 4, 128], BF16) for s in range(NSL)]
            Wpad = [sb(f"Wpad{s}", [128, 4, 128], BF16) for s in range(NSL)]
            TTbd = [sb(f"TTbd{s}", [128, 2, 128], BF16) for s in range(NSL)]
            RhT = [sb(f"RhT{s}", [128, 2, 128], BF16) for s in range(NSL)]
            Sbd = [sb(f"Sbd{i}", [128, 2, 128], BF16) for i in range(2)]
            yraw = sb("yraw", [128, 2, TB], F32)
            ygo = sb("ygo", [128, 2, TB], BF16)
            for b_ in Kpad2 + Bpad2 + Vpad2:
                g.memset("gpsimd", b_.v(), 0.0)
            for s in range(NSL):
                g.memset("gpsimd", Apad[s].v(), 0.0)
                g.memset("gpsimd", Wpad[s].v(), 0.0)
            g.memset("gpsimd", Sbd[0].v(), 0.0)
            g.memset("vector", hT.v(np.s_[:, :, 0:1]), 0.0)
            s_cur = 0

            def bf(bank):
                return bank.t[:].bitcast(BF16)

            def hsl(h, j):
                cc, q = divmod(h, 2)
                return np.s_[q * 64:(q + 1) * 64, cc, j * 128:(j + 1) * 128]

            def load_x(blk):
                t0 = blk * TB
                for j in range(4):
                    g.dma_in("sync", xin.v(np.s_[:, j, :], sub=j), x_d[t0 + j * 128:t0 + (j + 1) * 128, :])

            def front(blk):
                F = blk % 2
                rt_, kt_, at_, bt_ = rt2[F], kt2[F], at2[F], bt2[F]
                XR, Kpad, Bpad, Vpad, PCs, gT, bonusT = XR2[F], Kpad2[F], Bpad2[F], Vpad2[F], PCs2[F], gT2[F], bonusT2[F]
                for j in range(4):
                    g.act(junk.v(), xin.v(np.s_[:, j, :], sub=j), AF.Square, scale=1.0 / 32, accum=ms.v(np.s_[:, j:j + 1]))
                g.ts("vector", ms.v(), ms.v(), RMS_EPS, ALU.add)
                g.act(ms.v(), ms.v(), AF.Sqrt)
                g.recip(ms.v(), ms.v())
                if blk > 0:
                    g.copy("vector", hT.v(np.s_[:, :, 0:1]), hT.v(np.s_[:, :, TB:TB + 1]))
                yield
                for j in range(4):
                    xj = xn[j % 2]
                    g.act(xj.v(), xin.v(np.s_[:, j, :], sub=j), AF.Copy, scale=ms.v(np.s_[:, j:j + 1]))
                    bk = g.bank()
                    bv = bf(bk)
                    for kc in range(8):
                        g.tr(bk.w(bv[:, kc * 128:(kc + 1) * 128]), xj.v(np.s_[:, kc * 128:(kc + 1) * 128]), ident.v())
                    g.copy("vector" if j % 2 == 0 else "scalar", hT.v(np.s_[:, :, 1 + 128 * j:1 + 128 * (j + 1)]),
                           bk.w(bv.rearrange("p (k t) -> p k t", k=8)))
                    yield
                if blk + 1 < nblk:
                    load_x(blk + 1)

                def proj_fm(off, ncol):
                    bk = g.bank()
                    for kc in range(8):
                        g.mm(bk.v(np.s_[0:ncol, :]), WA.v(np.s_[:, kc, off:off + ncol]), hT.v(np.s_[:, kc, 1:TB + 1]),
                             start=(kc == 0), stop=False)
                        g.mm(bk.v(np.s_[0:ncol, :]), WB.v(np.s_[:, kc, off:off + ncol]), hT.v(np.s_[:, kc, 0:TB]),
                             start=False, stop=(kc == 7))
                    return bk

                bk = proj_fm(768, 128)
                g.act(tw.v(np.s_[0:64, :]), bk.v(np.s_[0:64, :]), AF.Tanh)
                g.copy("vector", tw.v(np.s_[64:128, :]), bk.v(np.s_[64:128, :]))
                yield
                bk = proj_fm(896, 128)
                g.act(sg1a.v(), bk.v(), AF.Sigmoid)
                yield
                bk = proj_fm(1024, 32)
                g.act(sg1b.v(), bk.v(np.s_[0:32, :]), AF.Sigmoid)
                yield
                for j in range(4):
                    bk = g.bank()
                    for kc in range(8):
                        g.mm(bk.v(np.s_[:, 0:256]), hT.v(np.s_[:, kc, 1 + 128 * j:1 + 128 * (j + 1)]), WA.v(np.s_[:, kc, 512:768]),
                             start=(kc == 0), stop=False)
                        g.mm(bk.v(np.s_[:, 0:256]), hT.v(np.s_[:, kc, 128 * j:128 * (j + 1)]), WB.v(np.s_[:, kc, 512:768]),
                             start=False, stop=(kc == 7))
                    pv = bk.t[:, 0:256].rearrange("p (h c) -> p h c", h=4)
                    g.copy("vector", Vpad.v(np.s_[:, j, 0::2, 0:64]), bk.w(pv[:, 0::2, :]))
                    g.copy("scalar", Vpad.v(np.s_[:, j, 1::2, 64:128]), bk.w(pv[:, 1::2, :]))
                    yield

                for cc in range(2):
                    bk = proj_fm(0 + cc * 128, 128)
                    g.copy("scalar", T["rT"].v(), bk.v())
                    yield
                    bk = proj_fm(256 + cc * 128, 128)
                    g.copy("vector", T["kraw"].v(), bk.v())
                    yield
                    bk = proj_fm(512 + cc * 128, 128)
                    g.copy("scalar", vT.v(np.s_[:, cc, :]), bk.v())
                    yield
                    bk = g.bank()
                    g.mm(bk.v(), w2b.v(np.s_[0:64, cc * 128:(cc + 1) * 128]), tw.v(np.s_[0:64, :]))
                    g.act(T["sgw"].v(), bk.v(), AF.Sigmoid, bias=vec.v(np.s_[:, 0, cc:cc + 1]))
                    bk = g.bank()
                    g.mm(bk.v(), a2b.v(np.s_[64:128, cc * 128:(cc + 1) * 128]), tw.v(np.s_[64:128, :]))
                    g.act(T["aT"].v(), bk.v(), AF.Sigmoid, bias=vec.v(np.s_[:, 1, cc:cc + 1]))
                    bk = g.bank()
                    g.mm(bk.v(), g2a.v(np.s_[:, cc * 128:(cc + 1) * 128]), sg1a.v(), start=True, stop=False)
                    g.mm(bk.v(), g2b.v(np.s_[0:32, cc * 128:(cc + 1) * 128]), sg1b.v(np.s_[0:32, :]), start=False, stop=True)
                    g.copy("vector", gT.v(np.s_[:, cc, :]), bk.v())
                    yield

                    kk, kp, beta, t1, t2 = T["kk"], T["kp"], T["beta"], T["t1"], T["t2"]
                    Er, En, EC, cs = T["Er"], T["En"], T["EC"], T["cs"]
                    g.ts("gpsimd", kk.v(), T["kraw"].v(), vec.v(np.s_[:, 2, cc:cc + 1]), ALU.mult)
                    g.tt("gpsimd", t1.v(), kk.v(), kk.v(), ALU.mult)
                    bk = g.bank()
                    g.mm(bk.v(), onesblk.v(), t1.v())
                    g.scan(cs.v(), rmask.v(), T["sgw"].v(), 0.0, ALU.mult, ALU.add)
                    yield
                    g.ts("vector", t2.v(), bk.v(), 1e-24, ALU.max)
                    g.act(t2.v(), t2.v(), AF.Sqrt)
                    g.act(Er.v(), cs.v(), AF.Exp, scale=NEG_C)
                    g.act(En.v(), cs.v(), AF.Exp, scale=-NEG_C)
                    g.recip(t2.v(), t2.v())
                    yield
                    g.tt("gpsimd", kk.v(), kk.v(), t2.v(), ALU.mult)
                    g.ts("vector", t1.v(), T["aT"].v(), -1.0, ALU.add, vec.v(np.s_[:, 3, cc:cc + 1]), ALU.mult)
                    g.tt("gpsimd", beta.v(), kk.v(), T["aT"].v(), ALU.mult)
                    g.stt("vector", kp.v(), t1.v(), 1.0, T["kraw"].v(), ALU.add, ALU.mult)
                    yield
                    Er3 = Er.t[:].rearrange("p (c t) -> p c t", t=128)
                    g.copy("vector", PCs.v(np.s_[:, cc, :]), Er.w(Er3[:, :, 127]))
                    g.tt("vector", EC.w(EC.t[:].rearrange("p (c t) -> p c t", t=128)),
                         En.w(En.t[:].rearrange("p (c t) -> p c t", t=128)),
                         Er.w(Er3[:, :, 127:128].to_broadcast([128, NCH, 128])), ALU.mult)
                    g.tt("gpsimd", kt_.v(np.s_[:, cc, :]), kp.v(), En.v(), ALU.mult)
                    g.tt("gpsimd", rt_.v(np.s_[:, cc, :]), T["rT"].v(), Er.v(), ALU.mult)
                    yield
                    g.tt("gpsimd", bt_.v(np.s_[:, cc, :]), beta.v(), En.v(), ALU.mult)
                    g.tt("gpsimd", khT.v(np.s_[:, cc, :]), kp.v(), EC.v(), ALU.mult)
                    g.tt("gpsimd", bhT.v(np.s_[:, cc, :]), beta.v(), EC.v(), ALU.mult)
                    kk3 = kk.t[:].rearrange("p (c t) -> p c t", t=128)
                    at3 = at_.t[:, cc, :].rearrange("p (c t) -> p c t", t=128)
                    g.stt("vector", at_.w(at3[:, :, 1:128]), kk.w(kk3[:, :, 1:128]), -1.0, Er.w(Er3[:, :, 0:127]), ALU.mult, ALU.mult)
                    g.ts("vector", at_.w(at3[:, :, 0:1]), kk.w(kk3[:, :, 0:1]), -1.0, ALU.mult)
                    yield
                    g.stt("vector", t1.v(), T["rT"].v(), vec.v(np.s_[:, 4, cc:cc + 1]), kp.v(), ALU.mult, ALU.mult)
                    bk = g.bank()
                    g.mm(bk.v(), onesblk.v(), t1.v())
                    g.tt("vector", bonusT.v(np.s_[:, cc, :]), vT.v(np.s_[:, cc, :]), bk.v(), ALU.mult)
                    yield

                for (src, kind) in ((at_, "A"), (khT, "K"), (bhT, "B")):
                    bk = g.bank()
                    bv = bf(bk)
                    for j in range(NCH):
                        for cc in range(2):
                            sl = (j * 2 + cc) * 128
                            g.tr(bk.w(bv[:, sl:sl + 128]), src.v(np.s_[:, cc, j * 128:(j + 1) * 128]), ident.v())
                    if kind == "A":
                        g.copy("vector", XR.v(np.s_[:, :, :, 0:64]),
                               bk.w(bv.rearrange("p (j h c) -> p j h c", j=NCH, h=4)))
                    else:
                        dst = Kpad if kind == "K" else Bpad
                        b5 = bv.rearrange("p (j c q d) -> p j c q d", j=NCH, c=2, q=2)
                        g.copy("vector", dst.v(np.s_[:, :, 0::2, 0:64]), bk.w(b5[:, :, :, 0, :]))
                        g.copy("scalar", dst.v(np.s_[:, :, 1::2, 64:128]), bk.w(b5[:, :, :, 1, :]))
                    yield

            def back(blk):
                nonlocal s_cur
                t0 = blk * TB
                F = blk % 2
                rt_, kt_, at_, bt_ = rt2[F], kt2[F], at2[F], bt2[F]
                XR, Kpad, Bpad, Vpad, PCs, gT, bonusT = XR2[F], Kpad2[F], Bpad2[F], Vpad2[F], PCs2[F], gT2[F], bonusT2[F]

                def pre_a(j, s):
                    def grp(specs, msk, dsts_t, dkeys):
                        bks = [g.bank(), g.bank()]
                        for si, (lh, rh) in enumerate(specs):
                            for h in range(4):
                                cc, q = divmod(h, 2)
                                sl = (si * 2 + cc) * 128
                                g.mm(bks[q].v(np.s_[:, sl:sl + 128]), lh.v(hsl(h, j)), rh.v(hsl(h, j)))
                        n = len(specs)
                        for q in range(2):
                            if n == 2:
                                src = bks[q].w(bks[q].t[:].rearrange("p (a c t) -> p a c t", a=2, c=2))
                                dst = V(dsts_t[:, :, q::2, :], dkeys)
                                mk = msk.w(msk.t[:].rearrange("p (a c) t -> p a c t", a=2))
                            else:
                                src = bks[q].w(bks[q].t[:, 0:256].rearrange("p (c t) -> p c t", c=2))
                                dst = V(dsts_t[:, q::2, :], dkeys)
                                mk = msk.v(np.s_[:, 0:2, :])
                            g.tt("vector", dst, src, mk, ALU.mult)
                    grp([(bt_, at_), (kt_, at_)], m_su, SUt[s].t, [(PTb[s][0].name, None), (AakT[s].name, None)])
                    yield
                    grp([(bt_, rt_), (kt_, rt_)], m_u, UUt[s].t, [(ArbT[s].name, None), (ArkT[s].name, None)])
                    yield
                    grp([(at_, bt_)], m_sl, Pb[s][0].t, [(Pb[s][0].name, None)])
                    g.tt("gpsimd", NTb[s][0].v(), PTb[s][0].v(), ident4.v(), ALU.add)
                    yield

                def pre_dbl(s, it):
                    cur, nxt = it % 2, (it + 1) % 2
                    bk = g.bank()
                    for h in range(4):
                        g.mm(bk.v(np.s_[:, h * 128:(h + 1) * 128]), PTb[s][cur].v(np.s_[:, h, :]), Pb[s][cur].v(np.s_[:, h, :]))
                    g.copy("scalar", Pb[s][nxt].v(), bk.w(bk.t[:].rearrange("p (h t) -> p h t", h=4)))
                    if it < 5:
                        bk = g.bank()
                        for h in range(4):
                            g.mm(bk.v(np.s_[:, h * 128:(h + 1) * 128]), Pb[s][cur].v(np.s_[:, h, :]), PTb[s][cur].v(np.s_[:, h, :]))
                        g.copy("vector", PTb[s][nxt].v(), bk.w(bk.t[:].rearrange("p (h t) -> p h t", h=4)))
                    yield
                    bk = g.bank()
                    for h in range(4):
                        o = bk.v(np.s_[:, h * 128:(h + 1) * 128])
                        g.mm(o, ident.v(), NTb[s][cur].v(np.s_[:, h, :]), start=True, stop=False)
                        g.mm(o, Pb[s][nxt].v(np.s_[:, h, :]), NTb[s][cur].v(np.s_[:, h, :]), start=False, stop=True)
                    eng = "scalar" if it % 2 == 0 else "vector"
                    g.copy(eng, NTb[s][nxt].v(), bk.w(bk.t[:].rearrange("p (h t) -> p h t", h=4)))
                    yield

                def pre_b(j, s):
                    NTf = NTb[s][0]
                    bk = g.bank()
                    for h in range(4):
                        q = h % 2
                        g.mm(bk.v(np.s_[:, h * 64:(h + 1) * 64]), AakT[s].v(np.s_[:, h, :]), Vpad.v(np.s_[:, j, h, q * 64:(q + 1) * 64]))
                    g.copy("vector", XR.v(np.s_[:, j, :, 64:128]), bk.w(bk.t[:, 0:256].rearrange("p (h c) -> p h c", h=4)))
                    yield
                    bk = g.bank()
                    for h in range(4):
                        g.mm(bk.v(np.s_[:, h * 128:(h + 1) * 128]), NTf.v(np.s_[:, h, :]), XR.v(np.s_[:, j, h, :]))
                    x4 = bk.t[:].rearrange("p (h c) -> p h c", h=4)
                    g.copy("vector", Apad[s].v(np.s_[:, 0::2, 0:64]), bk.w(x4[:, 0::2, 0:64]))
                    g.copy("scalar", Apad[s].v(np.s_[:, 1::2, 64:128]), bk.w(x4[:, 1::2, 0:64]))
                    g.copy("vector", Wpad[s].v(np.s_[:, 0::2, 0:64]), bk.w(x4[:, 0::2, 64:128]))
                    g.copy("scalar", Wpad[s].v(np.s_[:, 1::2, 64:128]), bk.w(x4[:, 1::2, 64:128]))
                    yield
                    bk = g.bank()
                    for cc in range(2):
                        o = bk.v(np.s_[:, cc * 128:(cc + 1) * 128])
                        for q in range(2):
                            h = 2 * cc + q
                            g.mm(o, Apad[s].v(np.s_[:, h, :]), Bpad.v(np.s_[:, j, h, :]), start=(q == 0), stop=(q == 1))
                    for cc in range(2):
                        g.stt("vector", TTbd[s].v(np.s_[:, cc, :]), identf.v(), PCs.v(np.s_[:, cc, j:j + 1]),
                              bk.v(np.s_[:, cc * 128:(cc + 1) * 128]), ALU.mult, ALU.add)
                    bk = g.bank()
                    for cc in range(2):
                        o = bk.v(np.s_[:, cc * 128:(cc + 1) * 128])
                        for q in range(2):
                            h = 2 * cc + q
                            g.mm(o, Apad[s].v(np.s_[:, h, :]), ArbT[s].v(np.s_[:, h, :]), start=(q == 0), stop=(q == 1))
                    g.tt("vector", RhT[s].v(), bk.w(bk.t[:, 0:256].rearrange("p (c t) -> p c t", c=2)),
                         rt_.v(np.s_[:, :, j * 128:(j + 1) * 128]), ALU.add)
                    yield

                def seq(j, s):
                    nonlocal s_cur
                    Sc, Sn = Sbd[s_cur], Sbd[1 - s_cur]
                    bk = g.bank()
                    for cc in range(2):
                        o = bk.v(np.s_[:, cc * 128:(cc + 1) * 128])
                        g.mm(o, TTbd[s].v(np.s_[:, cc, :]), Sc.v(np.s_[:, cc, :]), start=True, stop=False)
                        for q in range(2):
                            h = 2 * cc + q
                            g.mm(o, Bpad.v(np.s_[:, j, h, :]), Wpad[s].v(np.s_[:, h, :]), start=False, stop=False)
                            g.mm(o, Kpad.v(np.s_[:, j, h, :]), Vpad.v(np.s_[:, j, h, :]), start=False, stop=(q == 1))
                    g.copy("vector", Sn.v(), bk.w(bk.t[:, 0:256].rearrange("p (c t) -> p c t", c=2)))
                    bk = g.bank()
                    for cc in range(2):
                        o = bk.v(np.s_[:, cc * 128:(cc + 1) * 128])
                        g.mm(o, Sc.v(np.s_[:, cc, :]), RhT[s].v(np.s_[:, cc, :]), start=True, stop=False)
                        for q in range(2):
                            h = 2 * cc + q
                            g.mm(o, Wpad[s].v(np.s_[:, h, :]), ArbT[s].v(np.s_[:, h, :]), start=False, stop=False)
                            g.mm(o, Vpad.v(np.s_[:, j, h, :]), ArkT[s].v(np.s_[:, h, :]), start=False, stop=(q == 1))
                    g.copy("scalar", yraw.v(np.s_[:, :, j * 128:(j + 1) * 128]), bk.w(bk.t[:, 0:256].rearrange("p (c t) -> p c t", c=2)))
                    s_cur = 1 - s_cur
                    yield

                def rr(gens):
                    gens = list(gens)
                    while gens:
                        for gn in list(gens):
                            try:
                                next(gn)
                            except StopIteration:
                                gens.remove(gn)
                            yield

                for jp in range(0, NCH, NSL):
                    yield from rr([pre_a(jp + s, s) for s in range(NSL)])
                    for it in range(6):
                        yield from rr([pre_dbl(s, it) for s in range(NSL)])
                    yield from rr([pre_b(jp + s, s) for s in range(NSL)])
                    for s in range(NSL):
                        yield from seq(jp + s, s)

                for cc in range(2):
                    t1, t2 = T2["o1"], T2["o2"]
                    yr = yraw.v(np.s_[:, cc, :])
                    bk = g.bank()
                    g.mm(bk.v(), onesblk.v(), yr)
                    g.stt("vector", t1.v(), bk.v(), -1.0 / 64, yr, ALU.mult, ALU.add)
                    g.tt("gpsimd", t2.v(), t1.v(), t1.v(), ALU.mult)
                    yield
                    bk = g.bank()
                    g.mm(bk.v(), onesblk.v(), t2.v())
                    g.ts("vector", t2.v(), bk.v(), 1.0 / 64, ALU.mult, GN_EPS, ALU.add)
                    g.act(t2.v(), t2.v(), AF.Sqrt)
                    g.recip(t2.v(), t2.v())
                    yield
                    g.tt("gpsimd", t1.v(), t1.v(), t2.v(), ALU.mult)
                    g.ts("vector", t1.v(), t1.v(), vec.v(np.s_[:, 5, cc:cc + 1]), ALU.mult, vec.v(np.s_[:, 6, cc:cc + 1]), ALU.add)
                    g.tt("gpsimd", t1.v(), t1.v(), bonusT.v(np.s_[:, cc, :]), ALU.add)
                    g.tt("vector", ygo.v(np.s_[:, cc, :]), t1.v(), gT.v(np.s_[:, cc, :]), ALU.mult)
                    g.dma_out("sync", yg_d[cc * 128:(cc + 1) * 128, t0:t0 + TB], ygo.v(np.s_[:, cc, :]))
                    yield

            def drive(gens, weights=None):
                gens = list(gens)
                weights = list(weights or [1] * len(gens))
                while gens:
                    for gn, w in list(zip(gens, weights)):
                        for _ in range(w):
                            try:
                                next(gn)
                            except StopIteration:
                                k = gens.index(gn)
                                gens.pop(k)
                                weights.pop(k)
                                break

            load_x(0)
            drive([front(0)])
            for blk in range(nblk):
                gs = [back(blk)]
                if blk + 1 < nblk:
                    gs.append(front(blk + 1))
                drive(gs, [BACK_W, 1])
        try:
            emit_all()
        except Stop:
            pass
        if stop:
            g.dma_out("sync", dbg_d[:, :], dbgt.v())
        g.finish()
    return nc


def phase1_inputs(inp, core):
    b, hg = divmod(core, 4)
    cs = slice(hg * 256, (hg + 1) * 256)
    pm = lambda v: np.ascontiguousarray(v.reshape(-1, 128).T)
    gm = np.stack([pm(inp["norm_mix_g"][0])] + [pm(inp["rwkv_mu"][0, i]) for i in range(6)], axis=1)
    wcat = np.concatenate([inp["rwkv_w_r"][0][:, cs], inp["rwkv_w_k"][0][:, cs], inp["rwkv_w_v"][0][:, cs],
                           inp["rwkv_w1"][0], inp["rwkv_a1"][0], inp["rwkv_g1"][0]], axis=1)
    vecs = [inp["rwkv_w0"][0][cs], inp["rwkv_a0"][0][cs], inp["rwkv_k_k"][0][cs], inp["rwkv_k_a"][0][cs],
            inp["rwkv_r_k"][0].reshape(-1)[cs], inp["rwkv_ln_w"][0][cs], inp["rwkv_ln_b"][0][cs]]
    vec = np.stack([pm(v) for v in vecs], axis=1)
    return {
        "x": np.ascontiguousarray(inp["x"][b]),
        "gm": np.ascontiguousarray(gm, dtype=np.float32),
        "wcat": np.ascontiguousarray(wcat, dtype=np.float32),
        "w2": np.ascontiguousarray(inp["rwkv_w2"][0][:, cs]),
        "a2": np.ascontiguousarray(inp["rwkv_a2"][0][:, cs]),
        "g2": np.ascontiguousarray(inp["rwkv_g2"][0][:, cs]),
        "vec": np.ascontiguousarray(vec, dtype=np.float32),
    }


NT2 = 17
NTOK2 = NT2 * 128
DFF = 4096
HG = 512
NGRP = DFF // HG
NEG_BIG = -30000.0
TWO_PI = 6.283185307179586
PI = 3.141592653589793
C1_2PI = 6.28125
C2_2PI = TWO_PI - 6.28125


def build_phase2():
    nc = bass.Bass("TRN2", target_bir_lowering=False)
    dr = lambda n, s, d=F32: nc.dram_tensor(n, s, d, kind="ExternalInput").ap()
    x_d = dr("x", [NTOK2, D])
    yg_d = dr("ygT", [D, NTOK2], BF16)
    pos_d = dr("pos", [1, NTOK2], I32)
    wo0_d = dr("wo0", [D, D])
    win_d = [dr("win0", [D, DFF]), dr("win1", [D, DFF])]
    wout_d = [dr("wout0", [DFF, D]), dr("wout1", [DFF, D])]
    wqkv_d = dr("wqkv", [D, 1280])
    wo1_d = dr("wo1", [D, D])
    gains_d = dr("gains", [128, 3, 8])
    bq_d = dr("bq", [128, 10])
    rowv_d = dr("rowv", [1, 1024 + 1024 + 128 + 16])
    cst_d = dr("cst", [128, 4])
    out_d = nc.dram_tensor("out", [16 * 128, D], F32, kind="ExternalOutput").ap()

    with ExitStack() as st:
        g = G(nc, st)
        sb = g.sb
        identf = sb("identf", [128, 128], F32)
        ident = sb("ident", [128, 128], BF16)
        g.memset("gpsimd", identf.v(), 0.0)
        g.aselect(identf.v(), identf.v(), [[-1, 128]], ALU.not_equal, 1.0, 0, 1)
        g.copy("vector", ident.v(), identf.v())
        rotf = sb("rotf", [128, 128], F32)
        rot = sb("rot", [128, 128], BF16)
        g.memset("gpsimd", rotf.v(), 0.0)
        for blk in range(2):
            o = blk * 64
            sub = rotf.v(np.s_[:, o:o + 32])
            g.aselect(sub, sub, [[-1, 32]], ALU.not_equal, -1.0, -(o + 32), 1)
            sub = rotf.v(np.s_[:, o + 32:o + 64])
            g.aselect(sub, sub, [[-1, 32]], ALU.not_equal, 1.0, -(o + 32) + 32, 1)
        g.copy("vector", rot.v(), rotf.v())
        mscr = sb("scr", [128, 1024], F32)
        maskb = Buf(mscr.t[:, 0:256], "scr_m0")
        mask1 = Buf(mscr.t[:, 256:512], "scr_m1")
        g.memset("gpsimd", maskb.v(), 0.0)
        g.aselect(maskb.v(), maskb.v(), [[1, 256]], ALU.is_gt, NEG_BIG, 0, -1)
        g.aselect(maskb.v(), maskb.v(), [[-1, 256]], ALU.is_ge, NEG_BIG, 128, 1)
        cst = sb("cst", [128, 4], F32)
        g.dma_in("sync", cst.v(), cst_d[:, :])
        g.copy("vector", mask1.v(), maskb.v())
        g.ts("vector", mask1.v(np.s_[:, 0:128]), maskb.v(np.s_[:, 0:128]), cst.v(np.s_[:, 2:3]), ALU.add)
        maskbf = sb("maskbf", [128, 256], BF16)
        mask1bf = sb("mask1bf", [128, 256], BF16)
        g.ts("vector", maskbf.v(), maskb.v(), 8.0, ALU.mult)
        g.ts("vector", mask1bf.v(), mask1.v(), 8.0, ALU.mult)
        ones_row = sb("ones_row", [1, 128], F32)
        g.memset("gpsimd", ones_row.v(), 1.0)
        gains = sb("gains", [128, 3, 8], F32)
        g.dma_in("sync", gains.v(), gains_d[:, :, :])
        bq = sb("bq", [128, 10], F32)
        g.dma_in("sync", bq.v(), bq_d[:, :])
        rowv = sb("rowv", [1, 1024], F32)
        g.dma_in("sync", rowv.v(), rowv_d[0:1, 1024:2048])
        bvb = sb("bvb", [128, 128], F32)
        sinkb = sb("sinkb", [128, 16], F32)
        g.dma_in("sync", bvb.v(), rowv_d[0:1, 2048:2176].to_broadcast([128, 128]))
        g.dma_in("sync", sinkb.v(), rowv_d[0:1, 2176:2192].to_broadcast([128, 16]))

        xres = sb("xres", [128, NT2, 1024], F32)
        hT = sb("hT", [128, 8, NTOK2], BF16)
        uT = sb("uT", [128, 4, NTOK2], BF16)
        arena = sb("arena", [128, 18432], BF16)
        junk = sb("junk", [128, 1024], BF16)
        ms = sb("ms", [128, NT2], F32)
        xn = [sb("xn0", [128, 1024], BF16)] * 2
        scr = mscr
        NU = 4
        asc = sb("asc", [128, NU, 2, 256], F32)
        relu_s = [Buf(asc.t[:, i].rearrange("p a b -> p (a b)").bitcast(BF16)[:, 0:512], f"asc_relu{i}") for i in range(2)]

        def bf(bank):
            return bank.t[:].bitcast(BF16)

        NTILES = [(0, 512), (512, 512), (1024, 512), (1536, 512), (2048, 128)]
        NTILES_M = {0: NTILES, 1: [(128, 512), (640, 512), (1152, 512), (1664, 512)]}

        def load_x_and_yg():
            for kc in range(8):
                g.dma_in("sync", hT.v(np.s_[:, kc, :]), yg_d[kc * 128:(kc + 1) * 128, :])
            for i in range(NT2):
                g.dma_in("sync", xres.v(np.s_[:, i, :], sub=i), x_d[i * 128:(i + 1) * 128, :])

        def wload(dst_ap, key, src_ap):
            g.kb.op("gpsimd", lambda e: e.dma_start(out=dst_ap, in_=src_ap), writes=[key], dma=True)

        def rstd_all(tiles):
            allk = [("ms", i) for i in tiles]
            lo, hi = tiles[0], tiles[-1] + 1
            for i in tiles:
                g.act(junk.v(), xres.v(np.s_[:, i, :], sub=i), AF.Square, scale=1.0 / 32, accum=ms.v(np.s_[:, i:i + 1], sub=i))
            mv = V(ms.t[:, lo:hi], allk)
            g.ts("vector", mv, mv, RMS_EPS, ALU.add)
            g.act(mv, mv, AF.Sqrt)
            g.recip(mv, mv)

        def norm_to_hT(gi, tiles):
            tiles = list(tiles)
            rstd_all(tiles)
            for i in tiles:
                xj = xn[i % 2]
                g.act(xj.v(), xres.v(np.s_[:, i, :], sub=i), AF.Copy, scale=ms.v(np.s_[:, i:i + 1], sub=i))
                bk = g.bank()
                bv = bf(bk)
                for kc in range(8):
                    g.tr(bk.w(bv[:, kc * 128:(kc + 1) * 128]), xj.v(np.s_[:, kc * 128:(kc + 1) * 128]), ident.v())
                g.tt("vector", hT.v(np.s_[:, :, i * 128:(i + 1) * 128]), bk.w(bv.rearrange("p (k t) -> p k t", k=8)),
                     gains.w(gains.t[:, gi, :].unsqueeze(2).to_broadcast([128, 8, 128])), ALU.mult)

        def mlp(layer, first_tile=0):
            win, wout = win_d[layer], wout_d[layer]
            ntiles = [(n0, nn) for (n0, nn) in NTILES_M[first_tile]]
            winv = win.rearrange("(kc p) n -> p kc n", p=128)
            woutv = wout.rearrange("(m p) n -> p m n", p=128)
            def WI(s):
                return arena.t[:, s * 4096:(s + 1) * 4096].rearrange("p (k n) -> p k n", k=8)

            def WO(s):
                return arena.t[:, 8192 + s * 4096:8192 + (s + 1) * 4096].rearrange("p (m n) -> p m n", m=4)

            ri = 0
            for grp in range(NGRP):
                s = grp % 2
                kwi, kwo = ("arena", "wi%d" % s), ("arena", "wo%d" % s)
                wload(WI(s), kwi, winv[:, :, grp * HG:(grp + 1) * HG])
                wload(WO(s), kwo, woutv[:, grp * 4:(grp + 1) * 4, :])
                for m in range(4):
                    for (n0, nn) in ntiles:
                        bk = g.bank()
                        for kc in range(8):
                            g.mm(bk.v(np.s_[:, 0:nn]), V(WI(s)[:, kc, m * 128:(m + 1) * 128], [kwi]), hT.v(np.s_[:, kc, n0:n0 + nn]),
                                 start=(kc == 0), stop=(kc == 7))
                        r = relu_s[ri % 2]
                        ri += 1
                        g.act(V(r.t[:, 0:nn], r.v().keys + [("asc", 0), ("asc", 1)]), bk.v(np.s_[:, 0:nn]), AF.Relu)
                        g.tt("gpsimd", uT.v(np.s_[:, m, n0:n0 + nn]), r.v(np.s_[:, 0:nn]), r.v(np.s_[:, 0:nn]), ALU.mult)
                for i in range(first_tile, NT2):
                    for half in range(2):
                        bk = g.bank()
                        for m in range(4):
                            g.mm(bk.v(), uT.v(np.s_[:, m, i * 128:(i + 1) * 128]), V(WO(s)[:, m, half * 512:(half + 1) * 512], [kwo]),
                                 start=(m == 0), stop=(m == 3))
                        xs = xres.v(np.s_[:, i, half * 512:(half + 1) * 512], sub=i)
                        g.tt("vector", xs, xs, bk.v(), ALU.add)

        load_x_and_yg()
        wo_v = arena.t[:, 0:8192].rearrange("p (k n) -> p k n", k=8)
        g.kb.op("gpsimd", lambda e: e.dma_start(out=wo_v, in_=wo0_d.rearrange("(kc p) n -> p kc n", p=128)),
                writes=[("arena", "wi0"), ("arena", "wi1")], dma=True)
        for i in range(NT2):
            for half in range(2):
                bk = g.bank()
                for kc in range(8):
                    g.mm(bk.v(), hT.v(np.s_[:, kc, i * 128:(i + 1) * 128]),
                         V(wo_v[:, kc, half * 512:(half + 1) * 512], [("arena", "wi0"), ("arena", "wi1")]),
                         start=(kc == 0), stop=(kc == 7))
                xs = xres.v(np.s_[:, i, half * 512:(half + 1) * 512], sub=i)
                g.tt("vector", xs, xs, bk.v(), ALU.add)

        norm_to_hT(0, range(NT2))
        mlp(0)

        norm_to_hT(1, range(NT2))
        wq_v = arena.t[:, 0:10240].rearrange("p (k n) -> p k n", k=8)
        wo1_v = arena.t[:, 10240:18432].rearrange("p (k n) -> p k n", k=8)
        KQ = [("arena", "wi0"), ("arena", "wi1"), ("arena", "wo0")]
        KO = [("arena", "wo0"), ("arena", "wo1"), ("arena", "x")]
        g.kb.op("gpsimd", lambda e: e.dma_start(out=wq_v, in_=wqkv_d.rearrange("(kc p) n -> p kc n", p=128)), writes=KQ, dma=True)
        g.kb.op("gpsimd", lambda e: e.dma_start(out=wo1_v, in_=wo1_d.rearrange("(kc p) n -> p kc n", p=128)), writes=KO, dma=True)
        tabs = uT.t[:].rearrange("p a b -> p (a b)").bitcast(F32)
        cosT = uT.w(tabs[:, 0:NTOK2])
        sinT = uT.w(tabs[:, NTOK2:2 * NTOK2])
        posi = V(scr.t[:, 0:512].bitcast(I32), [("scr", "a"), ("scr_m0", None), ("scr_m1", None)])
        for (n0, nn) in NTILES:
            pch = V(posi.ap[:, 0:nn], posi.keys)
            ach = scr.v(np.s_[:, 512:512 + nn], sub="b")
            g.dma_in("sync", pch, pos_d[0:1, n0:n0 + nn].to_broadcast([128, nn]))
            g.copy("vector", ach, pch)
            g.ts("vector", ach, ach, cst.v(np.s_[:, 0:1]), ALU.mult)
            sch = V(sinT.ap[:, n0:n0 + nn], sinT.keys)
            cch = V(cosT.ap[:, n0:n0 + nn], cosT.keys)
            T1 = V(xn[0].t[:].bitcast(F32)[:, 0:nn], [("xn0", None)])
            A2 = V(junk.t[:].bitcast(F32)[:, 0:nn], [("junk", None)])
            TI = pch
            for (src, dst, shift) in ((ach, sch, 0.0), (ach, cch, 0.5 * PI)):
                if shift:
                    g.ts("vector", A2, src, shift, ALU.add)
                    src = A2
                g.ts("vector", T1, src, 1.0 / TWO_PI, ALU.mult)
                g.copy("vector", TI, T1)
                g.copy("vector", T1, TI)
                g.stt("vector", dst, T1, -C1_2PI, src, ALU.mult, ALU.add)
                g.stt("vector", dst, T1, -C2_2PI, dst, ALU.mult, ALU.add)
                g.ts("vector", dst, dst, -PI, ALU.max, PI, ALU.min)
                g.act(dst, dst, AF.Sin)

        kr = sb("kr", [128, 2, NTOK2], BF16)
        vpad = [sb(f"vpad{i}", [128, 2, 2, 128], BF16) for i in range(3)]
        for v_ in vpad:
            g.memset("gpsimd", v_.v(), 0.0)
        qb16 = sb("qb16", [128, 512], BF16)
        SCALE = 0.125
        negsink = sb("negsink", [128, 16], F32)
        esink = sb("esink", [128, 16], F32)
        g.ts("vector", negsink.v(), sinkb.v(), -1.0, ALU.mult)
        g.act(esink.v(), sinkb.v(), AF.Exp)

        def rope_evac(bk, nn, bias_col, n0, dst, qf, qb, r2):
            g.act(qf, bk.v(np.s_[:, 0:nn]), AF.Identity, bias=bias_col)
            g.copy("gpsimd", qb, qf)
            b2 = g.bank()
            g.mm(b2.v(np.s_[:, 0:nn]), rot.v(), qb)
            g.tt("vector", r2, b2.v(np.s_[:, 0:nn]), V(sinT.ap[:, n0:n0 + nn], sinT.keys), ALU.mult)
            g.tt("gpsimd", qf, qf, V(cosT.ap[:, n0:n0 + nn], cosT.keys), ALU.mult)
            g.tt("vector", dst, qf, r2, ALU.add)

        wkd_ap = asc.t[:].rearrange("p a b c -> p (a b c)")[:, 0:1024].bitcast(BF16).rearrange("p (k j c) -> p k j c", k=8, j=2)
        WK = [("asc", u) for u in range(NU)] + [("asc_relu0", None), ("asc_relu1", None)]
        for j in range(2):
            for dup in range(2):
                g.copy("vector", V(wkd_ap[:, :, j, dup * 64:(dup + 1) * 64], WK), V(wq_v[:, :, 1024 + j * 64:1024 + (j + 1) * 64], KQ))
        for j in range(2):
            for (n0, nn) in NTILES:
                bk = g.bank()
                for kc in range(8):
                    g.mm(bk.v(np.s_[:, 0:nn]), V(wkd_ap[:, kc, j, :], WK), hT.v(np.s_[:, kc, n0:n0 + nn]), start=(kc == 0), stop=(kc == 7))
                rope_evac(bk, nn, bq.v(np.s_[:, 8 + j:9 + j]), n0, kr.v(np.s_[:, j, n0:n0 + nn]),
                          scr.v(np.s_[:, 0:nn], sub="a"), qb16.v(np.s_[:, 0:nn]), scr.v(np.s_[:, 512:512 + nn], sub="b"))

        def make_vpad(i):
            vp = vpad[i % 3]
            bk = g.bank()
            for kc in range(8):
                g.mm(bk.v(np.s_[:, 0:128]), hT.v(np.s_[:, kc, i * 128:(i + 1) * 128]), V(wq_v[:, kc, 1152:1280], KQ), start=(kc == 0), stop=(kc == 7))
            for q2 in range(2):
                g.tt("vector", vp.v(np.s_[:, :, q2, q2 * 64:(q2 + 1) * 64]), bk.w(bk.t[:, 0:128].rearrange("p (j d) -> p j d", j=2)),
                     bvb.w(bvb.t[:].rearrange("p (j d) -> p j d", j=2)), ALU.add)
            return vp

        qr2 = [sb(f"qr{i}", [128, 8, 128], BF16) for i in range(2)]
        oT = sb("oT", [128, 8, 128], BF16)
        pn = [sb(f"pn{i}", [128, 2, 256], BF16) for i in range(NU)]
        pT = [sb(f"pT{i}", [128, 2, 2, 128], BF16) for i in range(NU)]
        stat = [sb(f"stat{i}", [128, 8], F32) for i in range(NU)]
        def prep(i):
            make_vpad(i)
            qr = qr2[i % 2]
            yield
            for hh in range(2):
                bk = g.bank()
                for a4 in range(4):
                    hp = hh * 4 + a4
                    for kc in range(8):
                        g.mm(bk.v(np.s_[:, a4 * 128:(a4 + 1) * 128]), V(wq_v[:, kc, hp * 128:(hp + 1) * 128], KQ),
                             hT.v(np.s_[:, kc, i * 128:(i + 1) * 128]), start=(kc == 0), stop=(kc == 7))
                qf = scr.v(np.s_[:, 0:512], sub="a")
                r2 = scr.v(np.s_[:, 512:1024], sub="b")
                qb = qb16.v()
                qf3 = V(scr.t[:, 0:512].rearrange("p (a t) -> p a t", a=4), [("scr", "a")])
                r23 = V(scr.t[:, 512:1024].rearrange("p (a t) -> p a t", a=4), [("scr", "b")])
                g.tt("vector", qf3, bk.w(bk.t[:].rearrange("p (a t) -> p a t", a=4)),
                     bq.w(bq.t[:, hh * 4:hh * 4 + 4].unsqueeze(2).to_broadcast([128, 4, 128])), ALU.add)
                g.copy("gpsimd", qb, qf)
                b2 = g.bank()
                g.mm(b2.v(), rot.v(), qb)
                sin_b = V(sinT.ap[:, i * 128:(i + 1) * 128].unsqueeze(1).to_broadcast([128, 4, 128]), sinT.keys)
                cos_b = V(cosT.ap[:, i * 128:(i + 1) * 128].unsqueeze(1).to_broadcast([128, 4, 128]), cosT.keys)
                g.tt("vector", r23, b2.w(b2.t[:].rearrange("p (a t) -> p a t", a=4)), sin_b, ALU.mult)
                g.tt("gpsimd", qf3, qf3, cos_b, ALU.mult)
                g.tt("vector", qr.v(np.s_[:, hh * 4:hh * 4 + 4, :]), qf3, r23, ALU.add)
                yield

        def attn(i):
            vps = [vpad[(i - 1) % 3], vpad[i % 3]]
            qr = qr2[i % 2]
            mk = mask1bf if i == 1 else maskbf
            for j in range(2):
                sbank = {}
                for gp in range(2):
                    hp0 = 4 * j + 2 * gp
                    bks = [g.bank(), g.bank()]
                    mk2 = mk.w(mk.t[:].unsqueeze(1).to_broadcast([128, 2, 256]))
                    for q2 in range(2):
                        g.mm(bks[q2].w(bks[q2].t[:].rearrange("p (a k) -> p a k", a=2)), ident.v(), mk2, start=True, stop=False)
                    for a in range(2):
                        for q2 in range(2):
                            o = bks[q2].v(np.s_[:, a * 256:(a + 1) * 256])
                            g.mm(o, qr.v(np.s_[q2 * 64:(q2 + 1) * 64, hp0 + a, :]),
                                 kr.v(np.s_[q2 * 64:(q2 + 1) * 64, j, (i - 1) * 128:(i + 1) * 128]), start=False, stop=(a == 1))
                    for q2 in range(2):
                        sbank[gp * 2 + q2] = bks[q2]
                UN = range(4)
                yield
                for u in UN:
                    gp, q2 = divmod(u, 2)
                    st_ = stat[u]
                    ps3 = sbank[u].w(sbank[u].t[:].rearrange("p (a k) -> p a k", a=2))
                    g.kb.op("vector", lambda e, st_=st_, ps3=ps3: e.tensor_reduce(out=st_.t[:, 0:2], in_=ps3.ap, axis=mybir.AxisListType.X, op=ALU.max),
                            reads=ps3.keys, writes=st_.v().keys)
                    h0 = 2 * (4 * j + 2 * gp) + q2
                    g.stt("vector", st_.v(np.s_[:, 2:4]), st_.v(np.s_[:, 0:2]), -SCALE, negsink.w(negsink.t[:, h0:h0 + 3:2]), ALU.mult, ALU.min)
                yield
                for u in UN:
                    st_ = stat[u]
                    for a in range(2):
                        g.act(V(asc.t[:, u, a, :], [("asc", u)]), sbank[u].v(np.s_[:, a * 256:(a + 1) * 256]), AF.Exp, scale=SCALE,
                              bias=st_.v(np.s_[:, 2 + a:3 + a]), accum=st_.v(np.s_[:, 4 + a:5 + a]))
                    g.act(st_.v(np.s_[:, 6:8]), st_.v(np.s_[:, 2:4]), AF.Exp)
                yield
                for u in UN:
                    gp, q2 = divmod(u, 2)
                    st_ = stat[u]
                    h0 = 2 * (4 * j + 2 * gp) + q2
                    g.tt("vector", st_.v(np.s_[:, 6:8]), st_.v(np.s_[:, 6:8]), esink.w(esink.t[:, h0:h0 + 3:2]), ALU.mult)
                    g.tt("vector", st_.v(np.s_[:, 4:6]), st_.v(np.s_[:, 4:6]), st_.v(np.s_[:, 6:8]), ALU.add)
                    g.recip(st_.v(np.s_[:, 4:6]), st_.v(np.s_[:, 4:6]))
                for u in UN:
                    st_ = stat[u]
                    g.tt("gpsimd", pn[u].v(), V(asc.t[:, u], [("asc", u)]), st_.w(st_.t[:, 4:6].unsqueeze(2).to_broadcast([128, 2, 256])), ALU.mult)
                yield
                tbs = {}
                for u in UN:
                    tb = g.bank()
                    tv = bf(tb)
                    for a in range(2):
                        for kb in range(2):
                            sl = (a * 2 + kb) * 128
                            g.tr(tb.w(tv[:, sl:sl + 128]), pn[u].v(np.s_[:, a, kb * 128:(kb + 1) * 128]), ident.v())
                    tbs[u] = (tb, tv)
                for u in UN:
                    tb, tv = tbs[u]
                    g.copy("scalar" if u % 2 == 0 else "vector", pT[u].v(), tb.w(tv[:, 0:512].rearrange("p (a k q) -> p a k q", a=2, k=2)))
                yield
                for gp in range(2):
                    hp0 = 4 * j + 2 * gp
                    obk = g.bank()
                    first = True
                    for q2 in range(2):
                        for kb in range(2):
                            g.mm(obk.w(obk.t[:, 0:256].rearrange("p (a q) -> p a q", a=2)), vps[kb].v(np.s_[:, j, q2, :]),
                                 pT[gp * 2 + q2].v(np.s_[:, :, kb, :]), start=first, stop=(q2 == 1 and kb == 1))
                            first = False
                    g.copy("scalar", oT.v(np.s_[:, hp0:hp0 + 2, :]), obk.w(obk.t[:, 0:256].rearrange("p (a q) -> p a q", a=2)))
            for half in range(2):
                bk = g.bank()
                for hp in range(8):
                    g.mm(bk.v(), oT.v(np.s_[:, hp, :]), V(wo1_v[:, hp, half * 512:(half + 1) * 512], KO), start=(hp == 0), stop=False)
                g.mm(bk.v(), ones_row.v(), rowv.v(np.s_[0:1, half * 512:(half + 1) * 512]), start=False, stop=True)
                xs = xres.v(np.s_[:, i, half * 512:(half + 1) * 512], sub=i)
                g.tt("vector", xs, xs, bk.v(), ALU.add)

            yield

        def drive2(gens):
            gens = list(gens)
            while gens:
                for gn in list(gens):
                    try:
                        next(gn)
                    except StopIteration:
                        gens.remove(gn)

        make_vpad(0)
        drive2([prep(1)])
        for i in range(1, NT2):
            gs = [attn(i)]
            if i + 1 < NT2:
                gs.append(prep(i + 1))
            drive2(gs)

        norm_to_hT(2, range(1, NT2))
        mlp(1, first_tile=1)

        gfb = V(asc.t[:].rearrange("p a b c -> p (a b c)")[:, 0:1024], [("asc", u) for u in range(NU)] + [("asc_relu0", None), ("asc_relu1", None)])
        g.dma_in("sync", gfb, rowv_d[0:1, 0:1024].to_broadcast([128, 1024]))
        rstd_all(list(range(1, NT2)))
        for i in range(1, NT2):
            o_ = V(scr.t[:], [("scr", "a"), ("scr", "b")])
            g.stt("vector", o_, xres.v(np.s_[:, i, :], sub=i), ms.v(np.s_[:, i:i + 1], sub=i), gfb, ALU.mult, ALU.mult)
            g.dma_out("sync", out_d[(i - 1) * 128:i * 128, :], o_)
        g.finish()
    return nc


def _pm(v):
    return np.ascontiguousarray(np.asarray(v).reshape(-1, 128).T)


def phase2_inputs(inp, ygT_full, core):
    b, tq = divmod(core, 4)
    t0 = tq * 2048
    x = np.zeros((NTOK2, D), np.float32)
    yg = np.zeros((D, NTOK2), ml_dtypes.bfloat16)
    pos = np.zeros((1, NTOK2), np.int32)
    lo = t0 - 128
    if tq > 0:
        x[:] = inp["x"][b, lo:lo + NTOK2]
        yg[:] = ygT_full[b][:, lo:lo + NTOK2]
        pos[0] = inp["positions"][b, lo:lo + NTOK2]
    else:
        x[128:] = inp["x"][b, 0:2048]
        yg[:, 128:] = ygT_full[b][:, 0:2048]
        pos[0, 128:] = inp["positions"][b, 0:2048]
    gains = np.stack([_pm(inp["norm_mlp_g"][0]), _pm(inp["norm_mix_g"][1]), _pm(inp["norm_mlp_g"][1])], axis=1)
    bqkv = inp["attn_b_qkv"][0]
    bq = np.zeros((128, 10), np.float32)
    bq[:, 0:8] = _pm(bqkv[0:1024])
    for j in range(2):
        bk = bqkv[1024 + j * 64:1024 + (j + 1) * 64]
        bq[:, 8 + j] = np.concatenate([bk, bk])
    rowv = np.concatenate([inp["norm_final_g"], inp["attn_b_o"][0], bqkv[1152:1280], inp["attn_sinks"][0]])[None, :]
    cst = np.zeros((128, 4), np.float32)
    p = np.arange(128)
    cst[:, 0] = (10000.0 ** (-(np.arange(0, 64, 2, dtype=np.float32)) / 64.0))[p % 32]
    cst[:, 1] = np.where((p % 64) < 32, 1.0, 1.0)
    cst[:, 2] = 0.0 if tq > 0 else NEG_BIG
    return {
        "x": x, "ygT": yg, "pos": pos,
        "wo0": np.ascontiguousarray(inp["rwkv_w_o"][0]),
        "win0": np.ascontiguousarray(inp["mlp_w_in"][0]), "win1": np.ascontiguousarray(inp["mlp_w_in"][1]),
        "wout0": np.ascontiguousarray(inp["mlp_w_out"][0]), "wout1": np.ascontiguousarray(inp["mlp_w_out"][1]),
        "wqkv": np.ascontiguousarray(inp["attn_w_qkv"][0]), "wo1": np.ascontiguousarray(inp["attn_w_o"][0]),
        "gains": np.ascontiguousarray(gains, dtype=np.float32), "bq": bq,
        "rowv": np.ascontiguousarray(rowv, dtype=np.float32), "cst": cst,
    }


_NC_CACHE = {}


def kernel(**inputs):
    inp = {k: np.asarray(v) for k, v in inputs.items()}
    if "p1" not in _NC_CACHE:
        _NC_CACHE["p1"] = build_phase1()
        _NC_CACHE["p2"] = build_phase2()
    r1 = run_bass_kernel_spmd(_NC_CACHE["p1"], [phase1_inputs(inp, c) for c in range(8)], core_ids=list(range(8)))
    ygT = np.zeros((2, D, S_LEN), ml_dtypes.bfloat16)
    for c in range(8):
        b, hg = divmod(c, 4)
        ygT[b, hg * 256:(hg + 1) * 256] = r1.results[c]["yg"]
    r2 = run_bass_kernel_spmd(_NC_CACHE["p2"], [phase2_inputs(inp, ygT, c) for c in range(8)], core_ids=list(range(8)))
    out = np.zeros((2, S_LEN, D), np.float32)
    for c in range(8):
        b, tq = divmod(c, 4)
        out[b, tq * 2048:(tq + 1) * 2048] = r2.results[c]["out"]
    return out
```
